# Optimizing a Trainium2 kernel written in Bass

```python
import math
import jax
import jax.numpy as jnp
from jax import lax
import numpy as np

D_MODEL = 1024
BATCH = 2
SEQ = 8192
DEPTH = 2

N_BRANCH = 3
BRANCH_WIDTH = 512
HG_HEADS = 8
HG_DK = 128
HG_DV = BRANCH_WIDTH // HG_HEADS
HG_FDIM = HG_HEADS * HG_DK
HG_CHUNK = 64
HG_MIN_FORGET = 1e-6
DIL_GROUPS = ((128, 1), (512, 4), (2048, 16))
DIL_HEADS = 8
DIL_HEAD_DIM = BRANCH_WIDTH // DIL_HEADS
ALIBI_MAX_BIAS = 8.0
MASK_VALUE = -1e30
MLA_HEADS = 4
MLA_NOPE = 128
MLA_ROPE = 64
MLA_V = BRANCH_WIDTH // MLA_HEADS
MLA_Q_RANK = 384
MLA_KV_RANK = 256
MLA_Q_BLOCK = 128
ROPE_THETA = 10000.0
LN_EPS = 1e-5
RMS_EPS = 1e-6
DEEPNORM_ALPHA = (2 * DEPTH) ** 0.25
DEEPNORM_BETA = (8 * DEPTH) ** -0.25

IN_SPLITS = (
    ('hg_q', HG_FDIM), ('hg_f_fwd', HG_FDIM), ('hg_f_bwd', HG_FDIM),
    ('hg_i', BRANCH_WIDTH), ('hg_g', BRANCH_WIDTH),
    ('dil_q0', BRANCH_WIDTH), ('dil_k0', BRANCH_WIDTH), ('dil_v0', BRANCH_WIDTH),
    ('dil_q1', BRANCH_WIDTH), ('dil_k1', BRANCH_WIDTH), ('dil_v1', BRANCH_WIDTH),
    ('dil_q2', BRANCH_WIDTH), ('dil_k2', BRANCH_WIDTH), ('dil_v2', BRANCH_WIDTH),
    ('dil_g', BRANCH_WIDTH),
    ('mla_cq', MLA_Q_RANK), ('mla_ckv', MLA_KV_RANK), ('mla_kr', MLA_ROPE),
    ('mla_g', BRANCH_WIDTH),
    ('merge', N_BRANCH * D_MODEL),
)
N_IN = sum(n for _, n in IN_SPLITS)

kernel_name = 'hybrid_hgrn2_dilated_mla_encoder'

F32 = jnp.float32


def _split_in(h):
    sizes = [n for _, n in IN_SPLITS]
    idx = [int(i) for i in np.cumsum(sizes)[:-1]]
    pieces = jnp.split(h, idx, axis=-1)
    return {name: t for (name, _), t in zip(IN_SPLITS, pieces)}


def _layer_norm(x, g, b):
    xf = x.astype(F32)
    mu = jnp.mean(xf, -1, keepdims=True)
    var = jnp.mean(jnp.square(xf - mu), -1, keepdims=True)
    return ((xf - mu) * lax.rsqrt(var + LN_EPS) * g + b).astype(x.dtype)


def _rms_norm(x, g):
    xf = x.astype(F32)
    return xf * lax.rsqrt(jnp.mean(jnp.square(xf), -1, keepdims=True) + RMS_EPS) * g


def _lower_bounds(param):
    p = jax.nn.softmax(param.astype(F32), axis=0)
    return jnp.cumsum(p, axis=0) - p[0:1]


def _gla_scan(q, k, v, log_f):
    B, S, H, K = q.shape
    V = v.shape[-1]
    C = HG_CHUNK
    nc = S // C

    def to_chunks(t):
        return t.reshape(B, nc, C, H, t.shape[-1]).transpose(1, 0, 3, 2, 4)

    qc, kc, vc, fc = (to_chunks(t) for t in (q, k, v, log_f))
    causal = jnp.tril(jnp.ones((C, C), dtype=bool))[:, :, None]

    def step(state, inp):
        qi, ki, vi, fi = inp
        b = jnp.cumsum(fi, axis=2)
        o_inter = jnp.einsum('bhtk,bhkv->bhtv', qi * jnp.exp(b), state)
        diff = b[:, :, :, None, :] - b[:, :, None, :, :]
        decay = jnp.where(causal, jnp.exp(jnp.where(causal, diff, 0.0)), 0.0)
        attn = jnp.einsum('bhtk,bhsk,bhtsk->bhts', qi, ki, decay)
        o = o_inter + jnp.einsum('bhts,bhsv->bhtv', attn, vi)
        b_last = b[:, :, -1, :]
        state = (jnp.exp(b_last)[..., None] * state
                 + jnp.einsum('bhsk,bhsv->bhkv', ki * jnp.exp(b_last[:, :, None, :] - b), vi))
        return state, o

    state0 = jnp.zeros((B, H, K, V), F32)
    _, o = lax.scan(step, state0, (qc, kc, vc, fc))
    return o.transpose(1, 0, 3, 2, 4).reshape(B, S, H, V)


def _hgrn2(q_raw, f_fwd, f_bwd, i_raw, lb_fwd, lb_bwd):
    B, S, _ = q_raw.shape
    q = jax.nn.silu(q_raw.astype(F32)).reshape(B, S, HG_HEADS, HG_DK)
    v = i_raw.astype(F32).reshape(B, S, HG_HEADS, HG_DV)

    def gates(z, lb):
        z = z.astype(F32).reshape(B, S, HG_HEADS, HG_DK)
        lb = lb.astype(F32).reshape(HG_HEADS, HG_DK)
        f = lb + (1.0 - lb) * jax.nn.sigmoid(z)
        log_f = jnp.log(jnp.maximum(f, HG_MIN_FORGET))
        k = (1.0 - lb) * jax.nn.sigmoid(-z)
        return k, log_f

    k_f, lf_f = gates(f_fwd, lb_fwd)
    k_b, lf_b = gates(f_bwd, lb_bwd)
    o_f = _gla_scan(q, k_f, v, lf_f)
    flip = lambda t: jnp.flip(t, axis=1)
    o_b = flip(_gla_scan(flip(q), flip(k_b), flip(v), flip(lf_b)))
    return (o_f + o_b).reshape(B, S, HG_HEADS * HG_DV)


def _alibi_slopes(n):
    return jnp.asarray(2.0 ** (-ALIBI_MAX_BIAS * (np.arange(n) + 1) / n), dtype=F32)


def _dilated_group(q, k, v, slopes, window, dilation):
    B, S, H, Dh = q.shape
    W = window // (2 * dilation)
    L = S // dilation
    nb = -(-L // W)
    Lp = nb * W

    def subseq(t):
        return t.astype(F32).reshape(B, L, dilation, H, Dh).transpose(0, 2, 1, 3, 4)

    qs, ks, vs = subseq(q), subseq(k), subseq(v)
    qb = jnp.pad(qs, ((0, 0), (0, 0), (0, Lp - L), (0, 0), (0, 0))).reshape(B, dilation, nb, W, H, Dh)

    def key_blocks(t):
        tp = jnp.pad(t, ((0, 0), (0, 0), (W, Lp - L + W), (0, 0), (0, 0)))
        tp = tp.reshape(B, dilation, nb + 2, W, H, Dh)
        return jnp.concatenate([tp[:, :, :-2], tp[:, :, 1:-1], tp[:, :, 2:]], axis=3)

    kb, vb = key_blocks(ks), key_blocks(vs)
    qpos = jnp.arange(nb)[:, None] * W + jnp.arange(W)[None, :]
    kpos = jnp.arange(nb)[:, None] * W - W + jnp.arange(3 * W)[None, :]
    rel = kpos[:, None, :] - qpos[:, :, None]
    allowed = (jnp.abs(rel) <= W) & (kpos[:, None, :] >= 0) & (kpos[:, None, :] < L)
    dist = (jnp.abs(rel) * dilation).astype(F32)
    bias = -slopes[:, None, None, None] * dist[None]
    s = jnp.einsum('brnqhd,brnkhd->brhnqk', qb * (Dh ** -0.5), kb) + bias
    s = jnp.where(allowed, s, MASK_VALUE)
    m = jnp.max(s, axis=-1, keepdims=True)
    p = jnp.exp(s - m)
    den = jnp.sum(p, axis=-1, keepdims=True)
    o = jnp.einsum('brhnqk,brnkhd->brhnqd', p, vb) / den
    lse = (m + jnp.log(den))[..., 0]
    o = o.transpose(0, 1, 3, 4, 2, 5).reshape(B, dilation, Lp, H, Dh)[:, :, :L]
    o = o.transpose(0, 2, 1, 3, 4).reshape(B, S, H, Dh)
    lse = lse.transpose(0, 1, 3, 4, 2).reshape(B, dilation, Lp, H)[:, :, :L]
    lse = lse.transpose(0, 2, 1, 3).reshape(B, S, H)
    return o, lse


def _dilated_mixer(qkv_groups):
    B, S, _ = qkv_groups[0][0].shape
    slopes = _alibi_slopes(len(DIL_GROUPS) * DIL_HEADS)
    outs, lses = [], []
    for g, (window, dilation) in enumerate(DIL_GROUPS):
        q, k, v = (t.reshape(B, S, DIL_HEADS, DIL_HEAD_DIM) for t in qkv_groups[g])
        o, lse = _dilated_group(q, k, v, slopes[g * DIL_HEADS:(g + 1) * DIL_HEADS], window, dilation)
        outs.append(o)
        lses.append(lse)
    w = jax.nn.softmax(jnp.stack(lses), axis=0)
    o = jnp.sum(w[..., None] * jnp.stack(outs), axis=0)
    return o.reshape(B, S, DIL_HEADS * DIL_HEAD_DIM)


def _rope(x, pos):
    R = x.shape[-1]
    half = R // 2
    inv = 1.0 / (ROPE_THETA ** (jnp.arange(half, dtype=F32) / half))
    ang = pos.astype(F32)[..., None] * inv
    shape = ang.shape[:2] + (1,) * (x.ndim - 3) + (half,)
    cos, sin = jnp.cos(ang).reshape(shape), jnp.sin(ang).reshape(shape)
    x1, x2 = x[..., :half].astype(F32), x[..., half:].astype(F32)
    return jnp.concatenate([x1 * cos - x2 * sin, x2 * cos + x1 * sin], axis=-1)


def _mla(c_q, c_kv, k_rope_raw, pos, g_q, g_kv, w_uq, w_ukv):
    B, S, _ = c_q.shape
    H, Dqk = MLA_HEADS, MLA_NOPE + MLA_ROPE
    q = jnp.einsum('bsr,rn->bsn', _rms_norm(c_q, g_q), w_uq).reshape(B, S, H, Dqk)
    q = jnp.concatenate([q[..., :MLA_NOPE].astype(F32), _rope(q[..., MLA_NOPE:], pos)], axis=-1) * (Dqk ** -0.5)
    kv = jnp.einsum('bsr,rn->bsn', _rms_norm(c_kv, g_kv), w_ukv).reshape(B, S, H, MLA_NOPE + MLA_V)
    k_rope = _rope(k_rope_raw, pos)
    k = jnp.concatenate([kv[..., :MLA_NOPE].astype(F32), jnp.broadcast_to(k_rope[:, :, None, :], (B, S, H, MLA_ROPE))], axis=-1)
    v = kv[..., MLA_NOPE:].astype(F32)
    nq = S // MLA_Q_BLOCK
    qb = q.reshape(B, nq, MLA_Q_BLOCK, H, Dqk).transpose(1, 0, 2, 3, 4)

    def attend(qi):
        s = jnp.einsum('bqhd,bkhd->bhqk', qi, k)
        p = jax.nn.softmax(s, axis=-1)
        return jnp.einsum('bhqk,bkhd->bqhd', p, v)

    o = lax.map(attend, qb)
    return o.transpose(1, 0, 2, 3, 4).reshape(B, S, H * MLA_V)


def _layer(x, pos, w_in, b_in, lb_f, lb_b, hg_norm, g_q, g_kv, w_uq, w_ukv, w_branch, w_out, ln_g, ln_b):
    B, S, D = x.shape
    p = _split_in(jnp.einsum('bsd,dn->bsn', x, w_in) + b_in)
    y_a = _rms_norm(_hgrn2(p['hg_q'], p['hg_f_fwd'], p['hg_f_bwd'], p['hg_i'], lb_f, lb_b), hg_norm)
    y_a = y_a * jax.nn.silu(p['hg_g'].astype(F32))
    groups = [(p['dil_q%d' % g], p['dil_k%d' % g], p['dil_v%d' % g]) for g in range(len(DIL_GROUPS))]
    y_b = _dilated_mixer(groups) * jax.nn.silu(p['dil_g'].astype(F32))
    y_c = _mla(p['mla_cq'], p['mla_ckv'], p['mla_kr'], pos, g_q, g_kv, w_uq, w_ukv)
    y_c = y_c * jax.nn.silu(p['mla_g'].astype(F32))
    branch = jnp.einsum('nbsc,ncd->nbsd', jnp.stack([y_a, y_b, y_c]), w_branch)
    gates = jax.nn.sigmoid(p['merge'].astype(F32).reshape(B, S, N_BRANCH, D)).transpose(2, 0, 1, 3)
    merged = jnp.sum(gates * branch, axis=0)
    out = jnp.einsum('bsd,de->bse', merged, w_out)
    return _layer_norm(DEEPNORM_ALPHA * x.astype(F32) + out, ln_g, ln_b).astype(x.dtype)


def setup_inputs(seed: int = 0) -> dict:
    key = jax.random.key(seed)
    ks = jax.random.split(key, 16)
    n = jax.random.normal
    L = DEPTH
    return {
        'x': n(ks[0], (BATCH, SEQ, D_MODEL), F32),
        'positions': jnp.broadcast_to(jnp.arange(SEQ, dtype=jnp.int32)[None, :], (BATCH, SEQ)),
        'w_in': n(ks[1], (L, D_MODEL, N_IN), F32) * D_MODEL ** -0.5,
        'b_in': n(ks[2], (L, N_IN), F32) * 0.02,
        'hg_lb_fwd': n(ks[3], (L, HG_FDIM), F32) * 0.5,
        'hg_lb_bwd': n(ks[4], (L, HG_FDIM), F32) * 0.5,
        'hg_norm': 1.0 + 0.01 * n(ks[5], (L, BRANCH_WIDTH), F32),
        'mla_q_norm': 1.0 + 0.01 * n(ks[6], (L, MLA_Q_RANK), F32),
        'mla_kv_norm': 1.0 + 0.01 * n(ks[7], (L, MLA_KV_RANK), F32),
        'w_uq': n(ks[8], (L, MLA_Q_RANK, MLA_HEADS * (MLA_NOPE + MLA_ROPE)), F32) * MLA_Q_RANK ** -0.5,
        'w_ukv': n(ks[9], (L, MLA_KV_RANK, MLA_HEADS * (MLA_NOPE + MLA_V)), F32) * MLA_KV_RANK ** -0.5,
        'w_branch': n(ks[10], (L, N_BRANCH, BRANCH_WIDTH, D_MODEL), F32) * (DEEPNORM_BETA * BRANCH_WIDTH ** -0.5),
        'w_out': n(ks[11], (L, D_MODEL, D_MODEL), F32) * (DEEPNORM_BETA * D_MODEL ** -0.5),
        'ln_g': 1.0 + 0.01 * n(ks[12], (L, D_MODEL), F32),
        'ln_b': 0.01 * n(ks[13], (L, D_MODEL), F32),
    }


def reference(x, positions, w_in, b_in, hg_lb_fwd, hg_lb_bwd, hg_norm, mla_q_norm, mla_kv_norm,
              w_uq, w_ukv, w_branch, w_out, ln_g, ln_b):
    lb_f = _lower_bounds(hg_lb_fwd)
    lb_b = _lower_bounds(hg_lb_bwd)
    for l in range(DEPTH):
        x = _layer(x, positions, w_in[l], b_in[l], lb_f[l], lb_b[l], hg_norm[l],
                   mla_q_norm[l], mla_kv_norm[l], w_uq[l], w_ukv[l], w_branch[l], w_out[l],
                   ln_g[l], ln_b[l])
    return x
```

```python
import math
from contextlib import ExitStack
import numpy as np
from concourse.bass_utils import run_bass_kernel_spmd
import concourse.bass as bass
import concourse.mybir as mybir

F32 = mybir.dt.float32
BF16 = mybir.dt.bfloat16
I32 = mybir.dt.int32
AF = mybir.ActivationFunctionType
ALU = mybir.AluOpType
AX = mybir.AxisListType


class Buf:
    __slots__ = ("name", "w", "r")

    def __init__(self, name=""):
        self.name = name
        self.w = {}
        self.r = {}


class _Eng:
    def __init__(self, name, sem):
        self.name = name
        self.sem = sem
        self.count = 0
        self.waited = {}
        self.items = []


class Prog:
    ENGS = ("pe", "act", "dve", "pool", "sp")

    def __init__(self, nc):
        self.nc = nc
        self.e = {n: _Eng(n, nc.alloc_semaphore("prog_" + n)) for n in self.ENGS}
        self.chan = {}
        self.nops = 0
        self.retired = []

    def _need(self, eng, waits, ev, raw):
        sem, val, en = ev
        if en == eng.name:
            if eng.name == "pe" or not raw:
                return
        k = id(sem)
        if eng.waited.get(k, 0) >= val:
            return
        if k not in waits or waits[k][1] < val:
            waits[k] = (sem, val)

    def _deps(self, eng, reads, writes):
        waits = {}
        for b in reads:
            for ev in b.w.values():
                self._need(eng, waits, ev, True)
        for b in writes:
            for ev in b.w.values():
                self._need(eng, waits, ev, False)
            for ev in b.r.values():
                self._need(eng, waits, ev, False)
        for k, (sem, val) in waits.items():
            eng.waited[k] = val
        return list(waits.values())

    @staticmethod
    def _mark(ev, reads, writes):
        k = id(ev[0])
        for b in reads:
            o = b.r.get(k)
            if o is None or o[1] < ev[1]:
                b.r[k] = ev
        for b in writes:
            o = b.w.get(k)
            if o is None or o[1] < ev[1]:
                b.w[k] = ev

    def op(self, engname, fn, reads=(), writes=()):
        eng = self.e[engname]
        waits = self._deps(eng, reads, writes)
        if eng.count >= 30000:
            self.retired.append((eng.sem, eng.count))
            eng.sem = self.nc.alloc_semaphore("prog_%s_%d" % (engname, self.nops))
            eng.count = 0
        eng.count += 1
        ev = (eng.sem, eng.count, eng.name)
        eng.items.append((waits, fn, (eng.sem, 1)))
        self._mark(ev, reads, writes)
        self.nops += 1
        return ev

    def dma(self, qname, chan, out, in_, reads=(), writes=(), fn=None, inc=16):
        eng = self.e[qname]
        waits = self._deps(eng, reads, writes)
        if chan not in self.chan:
            self.chan[chan] = [self.nc.alloc_semaphore("ch_" + chan), 0]
        c = self.chan[chan]
        if c[1] >= 30000:
            self.retired.append((c[0], c[1]))
            c[0] = self.nc.alloc_semaphore("ch_%s_%d" % (chan, self.nops))
            c[1] = 0
        c[1] += inc
        ev = (c[0], c[1], "dma")
        if fn is None:
            fn = (lambda e, o=out, i=in_: e.dma_start(out=o, in_=i))
        eng.items.append((waits, fn, (c[0], inc)))
        self._mark(ev, reads, writes)
        self.nops += 1
        return ev

    def barrier(self):
        evs = list(self.retired)
        for n in self.ENGS:
            if self.e[n].count > 0:
                evs.append((self.e[n].sem, self.e[n].count))
        for c in self.chan.values():
            evs.append((c[0], c[1]))
        for n in self.ENGS:
            eng = self.e[n]
            waits = []
            for sem, val in evs:
                if eng.waited.get(id(sem), 0) < val:
                    waits.append((sem, val))
                    eng.waited[id(sem)] = val
            eng.items.append((waits, None, None))

    def wait_all(self, engname, bufs):
        eng = self.e[engname]
        waits = self._deps(eng, bufs, bufs)
        eng.items.append((waits, None, None))

    def emit(self):
        nc = self.nc
        with nc.Block() as block:
            def run(eng, h):
                for waits, fn, inc in eng.items:
                    for sem, val in waits:
                        h.wait_ge(sem, val)
                    if fn is not None:
                        ins = fn(h)
                        ins.then_inc(inc[0], inc[1])

            @block.tensor
            def _(h):
                run(self.e["pe"], h)

            @block.scalar
            def _(h):
                run(self.e["act"], h)

            @block.vector
            def _(h):
                run(self.e["dve"], h)

            @block.gpsimd
            def _(h):
                run(self.e["pool"], h)

            @block.sync
            def _(h):
                run(self.e["sp"], h)


ALPHA = 4.0 ** 0.25
LN_EPS = 1e-5


class PsumPool:
    def __init__(self, nc, n=8):
        self.t = [nc.alloc_psum_tensor("psb%d" % i, [128, 512], F32) for i in range(n)]
        self.b = [Buf("psb%d" % i) for i in range(n)]
        self.i = 0
        self.n = n

    def next(self):
        i = self.i
        self.i = (i + 1) % self.n
        return self.t[i], self.b[i]


def load_cast_weight(P, nc, q, dram2d, dst16, dstbuf, kchunks, ncols, stage, stage_bufs, ctr, colsplit):
    v = dram2d.rearrange("(k p) c -> p k c", p=128)
    for k in range(kchunks):
        for c0 in range(0, ncols, colsplit):
            cw = min(colsplit, ncols - c0)
            s = ctr[0] % len(stage)
            ctr[0] += 1
            P.dma(q, "wst%d" % s, stage[s][:, 0:cw], v[:, k, c0:c0 + cw], writes=[stage_bufs[s]])
            eng = "dve" if (ctr[0] % 2 == 0) else "pool"
            P.op(eng, (lambda e, s=s, k=k, c0=c0, cw=cw: e.tensor_copy(out=dst16[:, k, c0:c0 + cw], in_=stage[s][:, 0:cw])),
                 reads=[stage_bufs[s]], writes=[dstbuf])


def emit_B(nc, P, ps, layer, io):
    T = 2048
    TT = 256
    NT = T // TT
    xT = io["x32src"]; wg = io["wg"]; bg = io["bg"]; wbr = io["wbr"]; wo = io["wo"]; hgn = io["hgn"]; lng = io["lng"]; lnb = io["lnb"]
    orows = io["orows"]; oidx = io["oidx"]; Bodst = io["Bodst"]; Bxsrc = io["Bxsrc"]
    out32 = io["out32"]; out16 = io.get("out16")
    esb = ExitStack()
    sb = lambda n, s, dt=F32: esb.enter_context(nc.sbuf_tensor("%s_B%d" % (n, layer), s, dt))
    wg16 = sb("wg16", [128, 8, 4608], BF16); Bwg = Buf("wg16")
    wbr16 = sb("wbr16", [128, 12, 1024], BF16); Bwbr = Buf("wbr16")
    wo16 = sb("wo16", [128, 8, 1024], BF16); Bwo = Buf("wo16")
    stage = [sb("wstage%d" % i, [128, 1152], F32) for i in range(2)]
    stage_b = [Buf("wstage%d" % i) for i in range(2)]
    bgs = sb("bgs", [128, 36]); hgns = sb("hgns", [128, 4]); lngs = sb("lngs", [128, 8]); lnbs = sb("lnbs", [128, 8])
    oix = sb("oix", [128, 96], I32)
    Bc = Buf("consts")
    ones32 = sb("ones32", [128, 128]); Bones = Buf("ones")
    epsr = sb("epsr", [128, 1]); epsl = sb("epsl", [128, 1])
    x32 = sb("x32", [128, 8, TT]); Bx32 = Buf("x32")
    x16 = sb("x16", [128, 8, TT], BF16); Bx16 = Buf("x16")
    o32 = sb("o16", [128, 12, TT], BF16); Bo32 = Buf("o16")
    y16 = sb("y16", [128, 12, TT], BF16); By16 = [Buf("y16_%d" % i) for i in range(12)]
    gt = [sb("gt%d" % i, [128, TT]) for i in range(2)]; Bgt = [Buf("gt%d" % i) for i in range(2)]
    sq = sb("sq", [128, 8, TT]); Bsq = Buf("sq")
    rstd = sb("rstd", [128, TT]); Brstd = Buf("rstd")
    tmp = sb("tmp", [128, TT]); Btmp = Buf("tmp")
    sg = [sb("sg%d" % i, [128, 3, TT]) for i in range(2)]; Bsg = [Buf("sg%d" % i) for i in range(2)]
    mm = sb("mm", [128, TT]); Bmm = Buf("mm")
    tt2 = sb("tt2", [128, TT]); Btt2 = Buf("tt2")
    mg16 = sb("mg16", [128, 8, TT], BF16); Bmg = [Buf("mg%d" % i) for i in range(8)]
    r32 = sb("r32", [128, 8, TT]); Br = [Buf("r%d" % i) for i in range(8)]
    mean = sb("mean", [128, TT]); Bmean = Buf("mean")
    ob = [sb("ob%d" % i, [128, TT]) for i in range(2)]; Bob = [Buf("ob%d" % i) for i in range(2)]
    ob16 = [sb("ob16_%d" % i, [128, TT], BF16) for i in range(2)]; Bob16 = [Buf("ob16_%d" % i) for i in range(2)]
    Bout = Buf("xnT")
    xnT = out32

    P.dma("sp", "c0", bgs[:], bg, writes=[Bc])
    P.dma("sp", "c4", oix[:], oidx, writes=[Bc])
    P.dma("sp", "c1", hgns[:], hgn, writes=[Bc])
    P.dma("sp", "c2", lngs[:], lng, writes=[Bc])
    P.dma("sp", "c3", lnbs[:], lnb, writes=[Bc])
    P.op("pool", lambda e: e.memset(ones32[:], 1.0), writes=[Bones])
    P.op("pool", lambda e: e.memset(epsr[:], RMS_EPS), writes=[Bc])
    P.op("pool", lambda e: e.memset(epsl[:], LN_EPS), writes=[Bc])
    ctr = [0]
    load_cast_weight(P, nc, "sp", wg, wg16, Bwg, 8, 4608, stage, stage_b, ctr, 1152)
    load_cast_weight(P, nc, "sp", wbr, wbr16, Bwbr, 12, 1024, stage, stage_b, ctr, 1024)
    load_cast_weight(P, nc, "sp", wo, wo16, Bwo, 8, 1024, stage, stage_b, ctr, 1024)

    xv = xT.rearrange("(k p) t -> p k t", p=128)
    outv = xnT.rearrange("(k p) t -> p k t", p=128)
    gi = 0
    for tt in range(NT):
        c0 = tt * TT
        P.dma("act", "x32", x32[:], xv[:, :, c0:c0 + TT], reads=[Bxsrc], writes=[Bx32])
        for blk in range(12):
            P.dma("pool", "o16g", None, None, reads=[Bodst, Bc], writes=[Bo32],
                  fn=(lambda e, blk=blk, tt=tt: e.indirect_dma_start(out=o32[:, blk, :], out_offset=None, in_=orows,
                                                                   in_offset=bass.IndirectOffsetOnAxis(ap=oix[:, tt * 12 + blk:tt * 12 + blk + 1], axis=0))))
        P.op("pool", lambda e: e.tensor_copy(out=x16[:], in_=x32[:]), reads=[Bx32], writes=[Bx16])
        P.op("act", lambda e: e.activation(out=sq[:, 0:4, :], in_=o32[:, 0:4, :], func=AF.Square), reads=[Bo32], writes=[Bsq])
        pt, pb = ps.next()
        for j in range(4):
            P.op("pe", lambda e, j=j, pt=pt: e.matmul(pt[:, 0:TT], lhsT=ones32[:], rhs=sq[:, j, :], start=(j == 0), stop=(j == 3)),
                 reads=[Bones, Bsq], writes=[pb])
        P.op("act", lambda e, pt=pt: e.activation(out=tmp[:], in_=pt[:, 0:TT], func=AF.Sqrt, bias=epsr[:, 0:1], scale=1.0 / 512.0), reads=[pb, Bc], writes=[Btmp])
        P.op("dve", lambda e: e.reciprocal(out=rstd[:], in_=tmp[:]), reads=[Btmp], writes=[Brstd])
        for blk in range(12):
            pt, pb = ps.next()
            for k in range(8):
                P.op("pe", lambda e, k=k, pt=pt, blk=blk: e.matmul(pt[:, 0:TT], lhsT=wg16[:, k, blk * 128:(blk + 1) * 128], rhs=x16[:, k, :],
                                                              start=(k == 0), stop=(k == 7)), reads=[Bwg, Bx16], writes=[pb])
            g = gi % 2
            gi += 1
            P.op("act", lambda e, pt=pt, blk=blk, g=g: e.activation(out=gt[g][:], in_=pt[:, 0:TT], func=AF.Silu, bias=bgs[:, blk:blk + 1], scale=1.0),
                 reads=[pb, Bc], writes=[Bgt[g]])
            if blk < 4:
                P.op("dve", lambda e, g=g: e.tensor_tensor(out=gt[g][:], in0=gt[g][:], in1=rstd[:], op=ALU.mult),
                     reads=[Bgt[g], Brstd], writes=[Bgt[g]])
                P.op("dve", lambda e, g=g, blk=blk: e.scalar_tensor_tensor(out=y16[:, blk, :], in0=o32[:, blk, :], scalar=hgns[:, blk:blk + 1],
                                                                           in1=gt[g][:], op0=ALU.mult, op1=ALU.mult),
                     reads=[Bo32, Bgt[g], Bc], writes=[By16[blk]])
            else:
                P.op("dve", lambda e, g=g, blk=blk: e.tensor_tensor(out=y16[:, blk, :], in0=o32[:, blk, :], in1=gt[g][:], op=ALU.mult),
                     reads=[Bo32, Bgt[g]], writes=[By16[blk]])
        for db in range(8):
            s = db % 2
            pbs = []
            for n in range(3):
                pt, pb = ps.next()
                col = 1536 + n * 1024 + db * 128
                for k in range(8):
                    P.op("pe", lambda e, k=k, pt=pt, col=col: e.matmul(pt[:, 0:TT], lhsT=wg16[:, k, col:col + 128], rhs=x16[:, k, :],
                                                                  start=(k == 0), stop=(k == 7)), reads=[Bwg, Bx16], writes=[pb])
                bi = 12 + n * 8 + db
                P.op("act", lambda e, pt=pt, n=n, s=s, bi=bi: e.activation(out=sg[s][:, n, :], in_=pt[:, 0:TT], func=AF.Sigmoid, bias=bgs[:, bi:bi + 1], scale=1.0),
                     reads=[pb, Bc], writes=[Bsg[s]])
            for n in range(3):
                pt, pb = ps.next()
                for j in range(4):
                    P.op("pe", lambda e, j=j, n=n, pt=pt, db=db: e.matmul(pt[:, 0:TT], lhsT=wbr16[:, n * 4 + j, db * 128:(db + 1) * 128], rhs=y16[:, n * 4 + j, :],
                                                                     start=(j == 0), stop=(j == 3)), reads=[Bwbr, By16[n * 4 + j]], writes=[pb])
                pbs.append((pt, pb))
            P.op("dve", lambda e, s=s, p0=pbs[0][0]: e.tensor_tensor(out=mm[:], in0=p0[:, 0:TT], in1=sg[s][:, 0, :], op=ALU.mult),
                 reads=[pbs[0][1], Bsg[s]], writes=[Bmm])
            P.op("dve", lambda e, s=s, p1=pbs[1][0]: e.tensor_tensor(out=tt2[:], in0=p1[:, 0:TT], in1=sg[s][:, 1, :], op=ALU.mult),
                 reads=[pbs[1][1], Bsg[s]], writes=[Btt2])
            P.op("pool", lambda e: e.tensor_tensor(out=mm[:], in0=mm[:], in1=tt2[:], op=ALU.add), reads=[Bmm, Btt2], writes=[Bmm])
            P.op("dve", lambda e, s=s, p2=pbs[2][0]: e.tensor_tensor(out=tt2[:], in0=p2[:, 0:TT], in1=sg[s][:, 2, :], op=ALU.mult),
                 reads=[pbs[2][1], Bsg[s]], writes=[Btt2])
            P.op("pool", lambda e, db=db: e.tensor_tensor(out=mg16[:, db, :], in0=mm[:], in1=tt2[:], op=ALU.add),
                 reads=[Bmm, Btt2], writes=[Bmg[db]])
        for eb in range(8):
            pt, pb = ps.next()
            for d in range(8):
                P.op("pe", lambda e, d=d, pt=pt, eb=eb: e.matmul(pt[:, 0:TT], lhsT=wo16[:, d, eb * 128:(eb + 1) * 128], rhs=mg16[:, d, :],
                                                            start=(d == 0), stop=(d == 7)), reads=[Bwo, Bmg[d]], writes=[pb])
            P.op("dve", lambda e, pt=pt, eb=eb: e.scalar_tensor_tensor(out=r32[:, eb, :], in0=x32[:, eb, :], scalar=ALPHA, in1=pt[:, 0:TT],
                                                                        op0=ALU.mult, op1=ALU.add), reads=[Bx32, pb], writes=[Br[eb]])
        pt, pb = ps.next()
        for eb in range(8):
            P.op("pe", lambda e, eb=eb, pt=pt: e.matmul(pt[:, 0:TT], lhsT=ones32[:], rhs=r32[:, eb, :], start=(eb == 0), stop=(eb == 7)),
                 reads=[Bones, Br[eb]], writes=[pb])
        P.op("act", lambda e, pt=pt: e.activation(out=mean[:], in_=pt[:, 0:TT], func=AF.Copy, scale=1.0 / 1024.0), reads=[pb], writes=[Bmean])
        for eb in range(8):
            P.op("dve", lambda e, eb=eb: e.tensor_tensor(out=r32[:, eb, :], in0=r32[:, eb, :], in1=mean[:], op=ALU.subtract),
                 reads=[Br[eb], Bmean], writes=[Br[eb]])
        P.op("act", lambda e: e.activation(out=sq[:], in_=r32[:], func=AF.Square), reads=Br, writes=[Bsq])
        pt, pb = ps.next()
        for eb in range(8):
            P.op("pe", lambda e, eb=eb, pt=pt: e.matmul(pt[:, 0:TT], lhsT=ones32[:], rhs=sq[:, eb, :], start=(eb == 0), stop=(eb == 7)),
                 reads=[Bones, Bsq], writes=[pb])
        P.op("act", lambda e, pt=pt: e.activation(out=tmp[:], in_=pt[:, 0:TT], func=AF.Sqrt, bias=epsl[:, 0:1], scale=1.0 / 1024.0), reads=[pb, Bc], writes=[Btmp])
        P.op("dve", lambda e: e.reciprocal(out=rstd[:], in_=tmp[:]), reads=[Btmp], writes=[Brstd])
        for eb in range(8):
            s = eb % 2
            P.op("dve", lambda e, eb=eb: e.tensor_tensor(out=r32[:, eb, :], in0=r32[:, eb, :], in1=rstd[:], op=ALU.mult),
                 reads=[Br[eb], Brstd], writes=[Br[eb]])
            P.op("act", lambda e, eb=eb, s=s: e.activation(out=ob[s][:], in_=r32[:, eb, :], func=AF.Identity, bias=lnbs[:, eb:eb + 1], scale=lngs[:, eb:eb + 1]),
                 reads=[Br[eb], Bc], writes=[Bob[s]])
            P.dma("sp", "ob%d" % s, outv[:, eb, c0:c0 + TT], ob[s][:], reads=[Bob[s]], writes=[Bout])
            if out16 is not None:
                P.op("pool", lambda e, s=s: e.tensor_copy(out=ob16[s][:], in_=ob[s][:]), reads=[Bob[s]], writes=[Bob16[s]])
                P.dma("sp", "ob16_%d" % s, out16.rearrange("(k p) t -> p k t", p=128)[:, eb, c0:c0 + TT], ob16[s][:], reads=[Bob16[s]], writes=[Bout])
    P.barrier()
    esb.close()


RMS_EPS = 1e-6
S = 8192
TT = 512
NT = S // TT
QSCALE = 192.0 ** -0.5
TWO_PI = 2.0 * math.pi
C1 = 6.28125
C2 = TWO_PI - C1
DILS = (1, 4, 16)
LN_MINF = math.log(1e-6)


def make_scratch(nc):
    dr = lambda n, s, dt=BF16: nc.dram_tensor(n, s, dt, kind="Internal").ap()
    scr = {}
    scr["dsub"] = [[dr("dsub%d_%d" % (g, t), [128, DILS[g], S // DILS[g]]) for t in range(3)] for g in range(3)]
    scr["hq_d"] = [[dr("hq%d_%d" % (h, d), [128, S]) for d in range(2)] for h in range(2)]
    scr["hk_d"] = [[dr("hk%d_%d" % (h, d), [128, S]) for d in range(2)] for h in range(2)]
    scr["hv_d"] = dr("hv", [128, S])
    scr["qd1"] = dr("qd1", [128, S]); scr["qd2"] = dr("qd2", [64, S])
    return scr


def emit_A(nc, P, ps, layer, io, scr, xsrc16=None, phases=("mla", "dil", "hg")):
    debug = False
    xT = io.get("xT"); pos = io["pos"]
    wA = io["wA"]; bA = io["bA"]; lbraw = io["lbraw"]
    wuq = io["wuq"]; gq = io["gq"]; wukv = io["wukv"]; gkv = io["gkv"]
    etab = io["etab"]; ropec = io["ropec"]; ident = io["ident"]; masks = io["masks"]; scanmask = io["scanmask"]
    osrc = io["osrc"]
    dsub = scr["dsub"]; hq_d = scr["hq_d"]; hk_d = scr["hk_d"]; hv_d = scr["hv_d"]; qd1 = scr["qd1"]; qd2 = scr["qd2"]
    Bdsub = Buf("dsub"); Bhqk = Buf("hqk"); BoT = Buf("oT"); Bqd = Buf("qd")
    es0 = ExitStack()
    sb = lambda n, s, dt=F32: es0.enter_context(nc.sbuf_tensor("%s_A%d" % (n, layer), s, dt))

    Bc = Buf("consts")
    bAs = sb("bAs", [128, 23]); lbr = sb("lbr", [128, 4, 2]); lbt = sb("lbt", [128, 4, 3])
    gqs = sb("gqs", [128, 3]); gkvs = sb("gkvs", [128, 2]); ropecs = sb("ropecs", [64, 2])
    ones32 = sb("ones32", [128, 128]); epsr = sb("epsr", [128, 1]); id32 = sb("id32", [128, 128]); id16 = sb("id16", [128, 128], BF16)
    ones16 = sb("ones16", [128, 128], BF16)
    msk = sb("msk", [128, 2, 128]); smask = sb("smask", [128, TT])
    P.dma("sp", "c0", bAs[:], bA, writes=[Bc])
    P.dma("sp", "c1", lbr[:], lbraw, writes=[Bc])
    P.dma("sp", "c2", gqs[:], gq, writes=[Bc])
    P.dma("sp", "c3", gkvs[:], gkv, writes=[Bc])
    P.dma("sp", "c4", ropecs[:], ropec, writes=[Bc])
    P.dma("sp", "c5", id32[:], ident, writes=[Bc])
    P.dma("sp", "c6", msk[:], masks, writes=[Bc])
    P.dma("sp", "c7", smask[:], scanmask, writes=[Bc])
    P.op("pool", lambda e: e.memset(ones32[:], 1.0), writes=[Bc])
    P.op("pool", lambda e: e.memset(ones16[:], 1.0), writes=[Bc])
    P.op("pool", lambda e: e.memset(epsr[:], RMS_EPS), writes=[Bc])
    P.op("pool", lambda e: e.tensor_copy(out=id16[:], in_=id32[:]), reads=[Bc], writes=[Bc])
    for blk in (7, 10, 13):
        P.op("dve", lambda e, blk=blk: e.tensor_scalar(out=bAs[:, blk:blk + 1], in0=bAs[:, blk:blk + 1], scalar1=0.125, scalar2=None, op0=ALU.mult),
             reads=[Bc], writes=[Bc])
    if layer == 0:
        P.op("dve", lambda e: e.memset(lbt[:, :, 0], 0.0), writes=[Bc])
    else:
        P.op("dve", lambda e: e.tensor_tensor(out=lbt[:, :, 1], in0=lbr[:, :, 1], in1=lbr[:, :, 0], op=ALU.subtract), reads=[Bc], writes=[Bc])
        P.op("act", lambda e: e.activation(out=lbt[:, :, 0], in_=lbt[:, :, 1], func=AF.Sigmoid), reads=[Bc], writes=[Bc])
    P.op("dve", lambda e: e.tensor_scalar(out=lbt[:, :, 1], in0=lbt[:, :, 0], scalar1=-1.0, scalar2=1.0, op0=ALU.mult, op1=ALU.add), reads=[Bc], writes=[Bc])
    P.op("dve", lambda e: e.tensor_scalar(out=lbt[:, :, 2], in0=lbt[:, :, 1], scalar1=-1.0, scalar2=None, op0=ALU.mult), reads=[Bc], writes=[Bc])

    K1T = sb("K1T", [128, S], BF16); K2T = sb("K2T", [64, S], BF16)
    Vtok = sb("Vtok", [128, S // 128, 128], BF16)
    BQ = Buf("Q"); BK = Buf("K"); BV = Buf("V")
    mT = [[sb("mT%d%d" % (h, d), [128, 128]) for d in range(2)] for h in range(2)]
    BTt = [[sb("BT%d%d" % (h, d), [128, 128]) for d in range(2)] for h in range(2)]
    BmB = Buf("mB")

    es1 = ExitStack()
    sb = lambda n, s, dt=F32: es1.enter_context(nc.sbuf_tensor("%s_A%d" % (n, layer), s, dt))
    win16 = sb("win16", [128, 8, 2816], BF16); Bwin = Buf("win16")
    stage = [sb("wstage%d" % i, [128, 704], F32) for i in range(2)]
    stage_b = [Buf("wstage%d" % i) for i in range(2)]
    wv = wA.rearrange("(k p) c -> p k c", p=128)
    ci = 0
    for k in range(8):
        for c0 in (0, 704, 1408, 2112):
            s = ci % 2
            P.dma("sp", "wst%d" % s, stage[s][:], wv[:, k, c0:c0 + 704], writes=[stage_b[s]])
            P.op("dve" if ci % 2 == 0 else "pool", lambda e, s=s, k=k, c0=c0: e.tensor_copy(out=win16[:, k, c0:c0 + 704], in_=stage[s][:]),
                 reads=[stage_b[s]], writes=[Bwin])
            ci += 1
    wuq16 = sb("wuq16", [128, 3, 256], BF16); wukv16 = sb("wukv16", [128, 2, 256], BF16); Bwu = Buf("wu")
    wuv = wuq.rearrange("(k p) c -> p k c", p=128)
    wkv = wukv.rearrange("(k p) c -> p k c", p=128)
    for j in range(3):
        s = ci % 2
        P.dma("sp", "wst%d" % s, stage[s][:, 0:256], wuv[:, j, :], writes=[stage_b[s]])
        P.op("dve", lambda e, s=s, j=j: e.tensor_scalar(out=wuq16[:, j, :], in0=stage[s][:, 0:256], scalar1=gqs[:, j:j + 1], scalar2=None, op0=ALU.mult),
             reads=[stage_b[s], Bc], writes=[Bwu])
        ci += 1
    for j in range(2):
        s = ci % 2
        P.dma("sp", "wst%d" % s, stage[s][:, 0:256], wkv[:, j, :], writes=[stage_b[s]])
        P.op("dve", lambda e, s=s, j=j: e.tensor_scalar(out=wukv16[:, j, :], in0=stage[s][:, 0:256], scalar1=gkvs[:, j:j + 1], scalar2=None, op0=ALU.mult),
             reads=[stage_b[s], Bc], writes=[Bwu])
        ci += 1

    x32 = sb("x32", [128, 4, TT]); Bx32 = Buf("x32"); x16 = sb("x16", [128, 8, TT], BF16); Bx16 = Buf("x16")
    posi = sb("posi", [64, TT], I32); Bposi = Buf("posi")
    ang = sb("ang", [64, TT]); Bang = Buf("ang")
    ru = sb("ru", [64, TT]); Bru = Buf("ru"); rki = sb("rki", [64, TT], I32); Brki = Buf("rki"); rkf = sb("rkf", [64, TT]); Brkf = Buf("rkf")
    cs = sb("cs", [64, TT]); sn = sb("sn", [64, TT]); Bcs = Buf("cs"); Bsn = Buf("sn")
    c32 = sb("c32", [128, 3, TT]); Bc32 = Buf("c32"); csq = sb("csq", [128, 3, TT]); Bcsq = Buf("csq")
    cn16 = sb("cn16", [128, 3, TT], BF16); Bcn = Buf("cn16")
    rt = sb("rt", [128, TT]); Brt = Buf("rt"); rr = sb("rr", [128, TT]); Brr = Buf("rr")
    t1 = sb("t1", [64, TT]); t2 = sb("t2", [64, TT]); Bt1 = Buf("t1"); Bt2 = Buf("t2")
    q1s = sb("q1s", [128, TT], BF16); q2s = sb("q2s", [64, TT], BF16); Bq1s = Buf("q1s"); Bq2s = Buf("q2s")
    dd16 = [sb("dd16_%d" % i, [128, TT], BF16) for i in range(2)]; Bdd = [Buf("dd16_%d" % i) for i in range(2)]
    qs = [sb("qs%d" % h, [128, TT]) for h in range(2)]; Bqs = [Buf("qs%d" % h) for h in range(2)]
    sig = sb("sig", [128, TT]); Bsig = Buf("sig"); ff = sb("ff", [128, TT]); Bff = Buf("ff")
    bb = sb("bb", [128, TT]); Bbb = Buf("bb"); eq = sb("eq", [128, TT]); Beq = Buf("eq"); ek = sb("ek", [128, TT]); Bek = Buf("ek")
    kk = sb("kk", [128, TT]); Bkk = Buf("kk")
    hq16 = [sb("hq16_%d" % i, [128, TT], BF16) for i in range(2)]; Bhq16 = [Buf("hq16_%d" % i) for i in range(2)]
    hk16 = [sb("hk16_%d" % i, [128, TT], BF16) for i in range(2)]; Bhk16 = [Buf("hk16_%d" % i) for i in range(2)]
    hv16 = sb("hv16", [128, TT], BF16); Bhv16 = Buf("hv16")

    if xsrc16 is None:
        xv = xT.rearrange("(k p) t -> p k t", p=128)
    else:
        xg = xsrc16.rearrange("(c r h p) t -> p r c h t", c=4, r=4, h=2, p=128)

    def inproj(col, m, rhs_cols=None):
        pt, pb = ps.next()
        for k in range(8):
            P.op("pe", lambda e, k=k, pt=pt: e.matmul(pt[0:m, :], lhsT=win16[:, k, col:col + m], rhs=x16[:, k, :], start=(k == 0), stop=(k == 7)),
                 reads=[Bwin, Bx16], writes=[pb])
        return pt, pb

    def sintab(dst, Bdst, shift):
        P.op("dve", lambda e: e.tensor_scalar(out=ru[:], in0=ang[:], scalar1=1.0 / TWO_PI, scalar2=shift / TWO_PI + 0.5, op0=ALU.mult, op1=ALU.add),
             reads=[Bang], writes=[Bru])
        P.op("dve", lambda e: e.tensor_copy(out=rki[:], in_=ru[:]), reads=[Bru], writes=[Brki])
        P.op("dve", lambda e: e.tensor_copy(out=rkf[:], in_=rki[:]), reads=[Brki], writes=[Brkf])
        P.op("dve", lambda e: e.tensor_scalar(out=ru[:], in0=ang[:], scalar1=shift, scalar2=None, op0=ALU.add), reads=[Bang], writes=[Bru])
        P.op("dve", lambda e: e.scalar_tensor_tensor(out=ru[:], in0=rkf[:], scalar=-C1, in1=ru[:], op0=ALU.mult, op1=ALU.add),
             reads=[Brkf, Bru], writes=[Bru])
        P.op("dve", lambda e: e.scalar_tensor_tensor(out=ru[:], in0=rkf[:], scalar=-C2, in1=ru[:], op0=ALU.mult, op1=ALU.add),
             reads=[Brkf, Bru], writes=[Bru])
        P.op("dve", lambda e: e.tensor_scalar(out=rkf[:], in0=ru[:], scalar1=math.pi, scalar2=None, op0=ALU.is_gt), reads=[Bru], writes=[Brkf])
        P.op("dve", lambda e: e.scalar_tensor_tensor(out=ru[:], in0=rkf[:], scalar=-TWO_PI, in1=ru[:], op0=ALU.mult, op1=ALU.add),
             reads=[Brkf, Bru], writes=[Bru])
        P.op("dve", lambda e: e.tensor_scalar(out=rkf[:], in0=ru[:], scalar1=-math.pi, scalar2=None, op0=ALU.is_lt), reads=[Bru], writes=[Brkf])
        P.op("dve", lambda e: e.scalar_tensor_tensor(out=ru[:], in0=rkf[:], scalar=TWO_PI, in1=ru[:], op0=ALU.mult, op1=ALU.add),
             reads=[Brkf, Bru], writes=[Bru])
        P.op("dve", lambda e: e.tensor_scalar(out=ru[:], in0=ru[:], scalar1=math.pi, scalar2=-math.pi, op0=ALU.min, op1=ALU.max), reads=[Bru], writes=[Bru])
        P.op("act", lambda e: e.activation(out=dst[:], in_=ru[:], func=AF.Sin), reads=[Bru], writes=[Bdst])

    def rms_norm(nblk, rank):
        P.op("act", lambda e: e.activation(out=csq[:, 0:nblk, :], in_=c32[:, 0:nblk, :], func=AF.Square), reads=[Bc32], writes=[Bcsq])
        pt, pb = ps.next()
        for j in range(nblk):
            P.op("pe", lambda e, j=j, pt=pt: e.matmul(pt[:], lhsT=ones32[:], rhs=csq[:, j, :], start=(j == 0), stop=(j == nblk - 1)),
                 reads=[Bc, Bcsq], writes=[pb])
        P.op("act", lambda e, pt=pt: e.activation(out=rt[:], in_=pt[:], func=AF.Sqrt, bias=epsr[:, 0:1], scale=1.0 / rank), reads=[pb, Bc], writes=[Brt])
        P.op("dve", lambda e: e.reciprocal(out=rr[:], in_=rt[:]), reads=[Brt], writes=[Brr])
        for j in range(nblk):
            P.op("dve", lambda e, j=j: e.tensor_tensor(out=cn16[:, j, :], in0=c32[:, j, :], in1=rr[:], op=ALU.mult), reads=[Bc32, Brr], writes=[Bcn])

    ddi = 0
    hi = 0
    for tt in range(NT):
        c0 = tt * TT
        if xsrc16 is None:
            for hf in range(2):
                P.dma("sp", "x32", x32[:], xv[:, hf * 4:(hf + 1) * 4, c0:c0 + TT], writes=[Bx32])
                P.op("pool", lambda e, hf=hf: e.tensor_copy(out=x16[:, hf * 4:(hf + 1) * 4, :], in_=x32[:]), reads=[Bx32], writes=[Bx16])
        else:
            for cch in range(4):
                P.dma("sp", "x32", x16[:, cch * 2:cch * 2 + 2, :], xg[:, c0 // 2048, cch, :, (c0 % 2048):(c0 % 2048) + TT], reads=[io["Bxg"]], writes=[Bx16])
        P.dma("sp", "posi", posi[:], pos[0:1, c0:c0 + TT].partition_broadcast(64), writes=[Bposi])
        P.op("dve", lambda e: e.tensor_copy(out=ang[:], in_=posi[:]), reads=[Bposi], writes=[Bang])
        P.op("dve", lambda e: e.tensor_scalar(out=ang[:], in0=ang[:], scalar1=ropecs[:, 0:1], scalar2=None, op0=ALU.mult), reads=[Bang, Bc], writes=[Bang])
        sintab(sn, Bsn, 0.0)
        sintab(cs, Bcs, math.pi / 2)
        P.op("dve", lambda e: e.tensor_scalar(out=sn[:], in0=sn[:], scalar1=ropecs[:, 1:2], scalar2=None, op0=ALU.mult), reads=[Bsn, Bc], writes=[Bsn])
        for j in range(3):
            pt, pb = inproj((16 + j) * 128, 128)
            P.op("act", lambda e, pt=pt, j=j: e.activation(out=c32[:, j, :], in_=pt[:], func=AF.Identity, bias=bAs[:, 16 + j:17 + j], scale=1.0),
                 reads=[pb, Bc], writes=[Bc32])
        rms_norm(3, 384.0)
        pt, pb = ps.next()
        for j in range(3):
            P.op("pe", lambda e, j=j, pt=pt: e.matmul(pt[:], lhsT=wuq16[:, j, 0:128], rhs=cn16[:, j, :], start=(j == 0), stop=(j == 2)),
                 reads=[Bwu, Bcn], writes=[pb])
        P.op("act", lambda e, pt=pt: e.activation(out=q1s[:], in_=pt[:], func=AF.Copy, scale=QSCALE), reads=[pb], writes=[Bq1s])
        P.dma("sp", "q1s", qd1[:, c0:c0 + TT], q1s[:], reads=[Bq1s], writes=[Bqd])
        pA, pbA = ps.next()
        for j in range(3):
            P.op("pe", lambda e, j=j, pA=pA: e.matmul(pA[0:64, :], lhsT=wuq16[:, j, 128:192], rhs=cn16[:, j, :], start=(j == 0), stop=(j == 2)),
                 reads=[Bwu, Bcn], writes=[pbA])
        pB, pbB = ps.next()
        for j in range(3):
            P.op("pe", lambda e, j=j, pB=pB: e.matmul(pB[0:64, :], lhsT=wuq16[:, j, 192:256], rhs=cn16[:, j, :], start=(j == 0), stop=(j == 2)),
                 reads=[Bwu, Bcn], writes=[pbB])
        P.op("dve", lambda e, pA=pA: e.scalar_tensor_tensor(out=t1[:], in0=pA[0:64, :], scalar=QSCALE, in1=cs[:], op0=ALU.mult, op1=ALU.mult),
             reads=[pbA, Bcs], writes=[Bt1])
        P.op("dve", lambda e, pB=pB: e.scalar_tensor_tensor(out=t2[:], in0=pB[0:64, :], scalar=QSCALE, in1=sn[:], op0=ALU.mult, op1=ALU.mult),
             reads=[pbB, Bsn], writes=[Bt2])
        P.op("pool", lambda e: e.tensor_tensor(out=q2s[:], in0=t1[:], in1=t2[:], op=ALU.add), reads=[Bt1, Bt2], writes=[Bq2s])
        P.dma("sp", "q2s", qd2[:, c0:c0 + TT], q2s[:], reads=[Bq2s], writes=[Bqd])
        for j in range(2):
            pt, pb = inproj((19 + j) * 128, 128)
            P.op("act", lambda e, pt=pt, j=j: e.activation(out=c32[:, j, :], in_=pt[:], func=AF.Identity, bias=bAs[:, 19 + j:20 + j], scale=1.0),
                 reads=[pb, Bc], writes=[Bc32])
        rms_norm(2, 256.0)
        pt, pb = ps.next()
        for j in range(2):
            P.op("pe", lambda e, j=j, pt=pt: e.matmul(pt[:], lhsT=wukv16[:, j, 0:128], rhs=cn16[:, j, :], start=(j == 0), stop=(j == 1)),
                 reads=[Bwu, Bcn], writes=[pb])
        P.op("act", lambda e, pt=pt, c0=c0: e.activation(out=K1T[:, c0:c0 + TT], in_=pt[:], func=AF.Copy), reads=[pb], writes=[BK])
        pt, pb = ps.next()
        for i in range(4):
            for j in range(2):
                P.op("pe", lambda e, i=i, j=j, pt=pt: e.matmul(pt[:, i * 128:(i + 1) * 128], lhsT=cn16[:, j, i * 128:(i + 1) * 128], rhs=wukv16[:, j, 128:256],
                                                          start=(j == 0), stop=(j == 1)), reads=[Bwu, Bcn], writes=[pb])
        P.op("act", lambda e, pt=pt, tt=tt: e.activation(out=Vtok[:, tt * 4:(tt + 1) * 4, :], in_=pt[:].rearrange("p (a b) -> p a b", a=4), func=AF.Copy),
             reads=[pb], writes=[BV])
        pA, pbA = inproj(21 * 128, 64)
        pB, pbB = inproj(21 * 128 + 64, 64)
        P.op("dve", lambda e, pA=pA: e.scalar_tensor_tensor(out=t1[:], in0=pA[0:64, :], scalar=bAs[0:64, 21:22], in1=cs[:], op0=ALU.add, op1=ALU.mult),
             reads=[pbA, Bcs, Bc], writes=[Bt1])
        P.op("dve", lambda e, pB=pB: e.scalar_tensor_tensor(out=t2[:], in0=pB[0:64, :], scalar=bAs[0:64, 22:23], in1=sn[:], op0=ALU.add, op1=ALU.mult),
             reads=[pbB, Bsn, Bc], writes=[Bt2])
        P.op("pool", lambda e, c0=c0: e.tensor_tensor(out=K2T[:, c0:c0 + TT], in0=t1[:], in1=t2[:], op=ALU.add), reads=[Bt1, Bt2], writes=[BK])
        if "dil" in phases:
            for g in range(3):
                d = DILS[g]
                for t in range(3):
                    blk = 7 + g * 3 + t
                    pt, pb = inproj(blk * 128, 128)
                    s = ddi % 2
                    ddi += 1
                    P.op("act", lambda e, pt=pt, s=s, d=d, blk=blk, t=t: e.activation(
                        out=dd16[s][:].rearrange("p (r j) -> p r j", r=d), in_=pt[:].rearrange("p (j r) -> p r j", r=d),
                        func=AF.Identity, bias=bAs[:, blk:blk + 1], scale=(0.125 if t == 0 else 1.0)), reads=[pb, Bc], writes=[Bdd[s]])
                    P.dma("sp", "dd%d" % s, dsub[g][t][:, :, c0 // d:(c0 + TT) // d], dd16[s][:].rearrange("p (r j) -> p r j", r=d),
                          reads=[Bdd[s]], writes=[Bdsub])
        if "hg" in phases:
            for h in range(2):
                pt, pb = inproj(h * 128, 128)
                P.op("act", lambda e, pt=pt, h=h: e.activation(out=qs[h][:], in_=pt[:], func=AF.Silu, bias=bAs[:, h:h + 1], scale=1.0),
                     reads=[pb, Bc], writes=[Bqs[h]])
            pt, pb = inproj(6 * 128, 128)
            P.op("act", lambda e, pt=pt: e.activation(out=hv16[:], in_=pt[:], func=AF.Identity, bias=bAs[:, 6:7], scale=1.0), reads=[pb, Bc], writes=[Bhv16])
            P.dma("sp", "hv16", hv_d[:, c0:c0 + TT], hv16[:], reads=[Bhv16], writes=[Bhqk])
            for dr_ in range(2):
                for h in range(2):
                    idx = h * 2 + dr_
                    blk = 2 + dr_ * 2 + h
                    pt, pb = inproj(blk * 128, 128)
                    P.op("act", lambda e, pt=pt, blk=blk: e.activation(out=sig[:], in_=pt[:], func=AF.Sigmoid, bias=bAs[:, blk:blk + 1], scale=1.0),
                         reads=[pb, Bc], writes=[Bsig])
                    P.op("dve", lambda e, idx=idx: e.tensor_scalar(out=ff[:], in0=sig[:], scalar1=lbt[:, idx, 1:2], scalar2=lbt[:, idx, 0:1], op0=ALU.mult, op1=ALU.add),
                         reads=[Bsig, Bc], writes=[Bff])
                    P.op("act", lambda e: e.activation(out=ff[:], in_=ff[:], func=AF.Ln), reads=[Bff], writes=[Bff])
                    P.op("pool", lambda e: e.tensor_scalar(out=ff[:], in0=ff[:], scalar1=LN_MINF, scalar2=None, op0=ALU.max), reads=[Bff], writes=[Bff])
                    if dr_ == 0:
                        P.op("dve", lambda e: e.tensor_tensor_scan(out=bb[:], data0=smask[:], data1=ff[:], initial=0.0, op0=ALU.mult, op1=ALU.add),
                             reads=[Bff, Bc], writes=[Bbb])
                        mcol, bcol = 31, 63
                    else:
                        P.op("dve", lambda e: e.tensor_tensor_scan(out=bb[:, ::-1], data0=smask[:], data1=ff[:, ::-1], initial=0.0, op0=ALU.mult, op1=ALU.add),
                             reads=[Bff, Bc], writes=[Bbb])
                        mcol, bcol = 32, 0
                    b3 = bb[:].rearrange("p (c t) -> p c t", t=64)
                    P.op("pool", lambda e, h=h, dr_=dr_, tt=tt, b3=b3, mcol=mcol: e.tensor_copy(out=mT[h][dr_][:, tt * 8:(tt + 1) * 8], in_=b3[:, :, mcol]),
                         reads=[Bbb], writes=[BmB])
                    P.op("pool", lambda e, h=h, dr_=dr_, tt=tt, b3=b3, bcol=bcol: e.tensor_copy(out=BTt[h][dr_][:, tt * 8:(tt + 1) * 8], in_=b3[:, :, bcol]),
                         reads=[Bbb], writes=[BmB])
                    mb = b3[:, :, mcol:mcol + 1]
                    mbc = bass.AP(mb.tensor, mb.offset, [list(mb.ap[0]), list(mb.ap[1]), [0, 64]])
                    P.op("dve", lambda e, b3=b3, mbc=mbc: e.tensor_tensor(out=eq[:].rearrange("p (c t) -> p c t", t=64), in0=b3, in1=mbc, op=ALU.subtract),
                         reads=[Bbb], writes=[Beq])
                    P.op("act", lambda e: e.activation(out=ek[:], in_=eq[:], func=AF.Exp, scale=-1.0), reads=[Beq], writes=[Bek])
                    P.op("act", lambda e: e.activation(out=eq[:], in_=eq[:], func=AF.Exp), reads=[Beq], writes=[Beq])
                    s = hi % 2
                    hi += 1
                    P.op("dve", lambda e, s=s, h=h: e.tensor_tensor(out=hq16[s][:], in0=qs[h][:], in1=eq[:], op=ALU.mult), reads=[Bqs[h], Beq], writes=[Bhq16[s]])
                    P.dma("sp", "hq16_%d" % s, hq_d[h][dr_][:, c0:c0 + TT], hq16[s][:], reads=[Bhq16[s]], writes=[Bhqk])
                    P.op("dve", lambda e, idx=idx: e.tensor_scalar(out=kk[:], in0=sig[:], scalar1=lbt[:, idx, 2:3], scalar2=lbt[:, idx, 1:2], op0=ALU.mult, op1=ALU.add),
                         reads=[Bsig, Bc], writes=[Bkk])
                    P.op("pool", lambda e, s=s: e.tensor_tensor(out=hk16[s][:], in0=kk[:], in1=ek[:], op=ALU.mult), reads=[Bkk, Bek], writes=[Bhk16[s]])
                    P.dma("sp", "hk16_%d" % s, hk_d[h][dr_][:, c0:c0 + TT], hk16[s][:], reads=[Bhk16[s]], writes=[Bhqk])

    P.barrier()
    es1.close()
    es2 = ExitStack()
    sb = lambda n, s, dt=F32: es2.enter_context(nc.sbuf_tensor("%s_A%d" % (n, layer), s, dt))
    if "mla" in phases:
        pT = [sb("pT%d" % i, [128, TT], BF16) for i in range(3)]; BpT = [Buf("pT%d" % i) for i in range(3)]
        dacc = sb("dacc", [128, TT]); Bdacc = Buf("dacc")
        rden = sb("rden", [128, TT]); Brden = Buf("rden")
        oc = sb("oc", [128, TT], BF16); Boc = Buf("oc")
        po_t = nc.alloc_psum_tensor("po_mla", [128, 512], F32) if False else None
        pi = 0
        Q1 = [sb("Q1_%d" % i, [128, TT], BF16) for i in range(2)]; Q2 = [sb("Q2_%d" % i, [64, TT], BF16) for i in range(2)]
        BQs = [Buf("Qs%d" % i) for i in range(2)]
        for qt in range(NT):
            q0 = qt * TT
            qi = qt % 2
            P.dma("sp", "Q1_%d" % qi, Q1[qi][:], qd1[:, q0:q0 + TT], reads=[Bqd], writes=[BQs[qi]])
            P.dma("sp", "Q2_%d" % qi, Q2[qi][:], qd2[:, q0:q0 + TT], reads=[Bqd], writes=[BQs[qi]])
            BQ = BQs[qi]
            po, pbo = ps.next()
            for kb in range(S // 128):
                k0 = kb * 128
                pt, pb = ps.next()
                if pt is po:
                    pt, pb = ps.next()
                P.op("pe", lambda e, pt=pt, k0=k0, qi=qi: e.matmul(pt[:], lhsT=K1T[:, k0:k0 + 128], rhs=Q1[qi][:], start=True, stop=False),
                     reads=[BK, BQ], writes=[pb])
                P.op("pe", lambda e, pt=pt, k0=k0, qi=qi: e.matmul(pt[:], lhsT=K2T[:, k0:k0 + 128], rhs=Q2[qi][:], start=False, stop=True),
                     reads=[BK, BQ], writes=[pb])
                s = pi % 3
                pi += 1
                P.op("act", lambda e, pt=pt, s=s: e.activation(out=pT[s][:], in_=pt[:], func=AF.Exp), reads=[pb], writes=[BpT[s]])
                P.op("pe", lambda e, po=po, kb=kb, s=s: e.matmul(po[:], lhsT=Vtok[:, kb, :], rhs=pT[s][:], start=(kb == 0), stop=(kb == S // 128 - 1)),
                     reads=[BV, BpT[s]], writes=[pbo])
                if kb == 0:
                    P.op("dve", lambda e, s=s: e.tensor_copy(out=dacc[:], in_=pT[s][:]), reads=[BpT[s]], writes=[Bdacc])
                else:
                    P.op("dve", lambda e, s=s: e.tensor_tensor(out=dacc[:], in0=dacc[:], in1=pT[s][:], op=ALU.add), reads=[BpT[s], Bdacc], writes=[Bdacc])
            pd, pbd = ps.next()
            if pd is po:
                pd, pbd = ps.next()
            P.op("pe", lambda e, pd=pd: e.matmul(pd[:], lhsT=ones32[:], rhs=dacc[:], start=True, stop=True), reads=[Bc, Bdacc], writes=[pbd])
            P.op("dve", lambda e, pd=pd: e.reciprocal(out=rden[:], in_=pd[:]), reads=[pbd], writes=[Brden])
            P.op("dve", lambda e, po=po: e.tensor_tensor(out=oc[:], in0=po[:], in1=rden[:], op=ALU.mult), reads=[pbo, Brden], writes=[Boc])
            P.dma("sp", "oc", osrc[(q0 // 2048) * 384 + 256:(q0 // 2048) * 384 + 384, (q0 % 2048):(q0 % 2048) + TT], oc[:], reads=[Boc], writes=[BoT])


    P.barrier()
    es2.close()
    es2 = ExitStack()
    sb = lambda n, s, dt=F32: es2.enter_context(nc.sbuf_tensor("%s_A%d" % (n, layer), s, dt))
    if "dil" in phases:
        RG = 2048
        ets = sb("ets", [128, 18, 256]); Bets = Buf("ets")
        P.dma("sp", "ets", ets[:], etab, writes=[Bets])
        Qs = sb("Qs", [128, RG], BF16); Ks = sb("Ks", [128, RG + 128], BF16); Vs = sb("Vs", [128, RG + 128], BF16)
        BQs_ = Buf("Qs"); BKs = Buf("Ks"); BVs = Buf("Vs")
        Vp = sb("Vp", [128, 17, 2, 65], BF16); BVp = [Buf("Vp%d" % i) for i in range(17)]
        pe32 = [sb("pe32_%d" % i, [128, 256]) for i in range(2)]; Bpe = [Buf("pe32_%d" % i) for i in range(2)]
        pt16 = [sb("pt16_%d" % i, [128, 256], BF16) for i in range(2)]; Bpt16 = [Buf("pt16_%d" % i) for i in range(2)]
        acc = [sb("dacc%d" % h, [65, RG]) for h in range(2)]; Bacc = [Buf("dacc%d" % h) for h in range(2)]
        obd = sb("obd", [64, 512], BF16); Bobd = Buf("obd")
        P.op("pool", lambda e: e.memset(Vp[:], 1.0), writes=BVp)
        ui = 0
        for rg in range(S // RG):
            R0 = rg * RG
            for h in range(2):
                P.op("pool", lambda e, h=h: e.memset(acc[h][:], 0.0), writes=[Bacc[h]])
            for g in range(3):
                d = DILS[g]
                J = S // d
                nj = RG // d
                nb = nj // 128
                j0 = R0 // d
                for r in range(d):
                    lo = j0 - 64
                    hi_ = j0 + nj + 64
                    clo = max(lo, 0)
                    chi = min(hi_, J)
                    if lo < 0:
                        P.op("pool", lambda e: e.memset(Ks[:, 0:64], 0.0), writes=[BKs])
                        P.op("pool", lambda e: e.memset(Vs[:, 0:64], 0.0), writes=[BVs])
                    if hi_ > J:
                        P.op("pool", lambda e, nj=nj: e.memset(Ks[:, nj + 64:nj + 128], 0.0), writes=[BKs])
                        P.op("pool", lambda e, nj=nj: e.memset(Vs[:, nj + 64:nj + 128], 0.0), writes=[BVs])
                    P.dma("sp", "Qs", Qs[:, 0:nj], dsub[g][0][:, r, j0:j0 + nj], reads=[Bdsub], writes=[BQs_])
                    P.dma("sp", "Ks", Ks[:, clo - lo:chi - lo], dsub[g][1][:, r, clo:chi], reads=[Bdsub], writes=[BKs])
                    P.dma("sp", "Vs", Vs[:, clo - lo:chi - lo], dsub[g][2][:, r, clo:chi], reads=[Bdsub], writes=[BVs])
                    for n in range(nb + 1):
                        ptr, pbr = ps.next()
                        ptr16 = ptr[:].bitcast(BF16)
                        P.op("pe", lambda e, n=n, ptr16=ptr16: e.transpose(out=ptr16[:, 0:128], in_=Vs[:, n * 128:(n + 1) * 128], identity=id16[:]),
                             reads=[BVs, Bc], writes=[pbr])
                        P.op("act", lambda e, n=n, ptr16=ptr16: e.activation(out=Vp[:, n, :, 0:64], in_=ptr16[:, 0:128].rearrange("p (h v) -> p h v", h=2), func=AF.Copy),
                             reads=[pbr], writes=[BVp[n]])
                    for qb in range(nb):
                        jb = j0 + qb * 128
                        var = 1 if jb == 0 else (2 if jb + 128 == J else 0)
                        for h in range(2):
                            u = ui % 2
                            ui += 1
                            pt, pb = ps.next()
                            for kc in range(2):
                                P.op("pe", lambda e, pt=pt, kc=kc, qb=qb, h=h: e.matmul(pt[:, kc * 128:(kc + 1) * 128],
                                     lhsT=Ks[h * 64:(h + 1) * 64, (qb + kc) * 128:(qb + kc + 1) * 128], rhs=Qs[h * 64:(h + 1) * 64, qb * 128:(qb + 1) * 128],
                                     start=True, stop=True), reads=[BKs, BQs_], writes=[pb])
                            P.op("act", lambda e, pt=pt, u=u: e.activation(out=pe32[u][:], in_=pt[:, 0:256], func=AF.Exp), reads=[pb], writes=[Bpe[u]])
                            ei = (g * 2 + h) * 3 + var
                            P.op("dve", lambda e, u=u, ei=ei: e.tensor_tensor(out=pt16[u][:], in0=pe32[u][:], in1=ets[:, ei, :], op=ALU.mult),
                                 reads=[Bpe[u], Bets], writes=[Bpt16[u]])
                            po, pbo = ps.next()
                            for kc in range(2):
                                P.op("pe", lambda e, po=po, kc=kc, qb=qb, h=h, u=u: e.matmul(po[0:65, 0:128], lhsT=Vp[:, qb + kc, h, :], rhs=pt16[u][:, kc * 128:(kc + 1) * 128],
                                     start=(kc == 0), stop=(kc == 1)), reads=[BVp[qb + kc], Bpt16[u]], writes=[pbo])
                            st = qb * 128 * d + r
                            av = acc[h][:, st:st + 127 * d + 1:d]
                            P.op("dve", lambda e, po=po, av=av: e.tensor_tensor(out=av, in0=av, in1=po[0:65, 0:128], op=ALU.add), reads=[pbo, Bacc[h]], writes=[Bacc[h]])
            for h in range(2):
                P.op("dve", lambda e, h=h: e.reciprocal(out=acc[h][64:65, :], in_=acc[h][64:65, :]), reads=[Bacc[h]], writes=[Bacc[h]])
                for cc in range(RG // 512):
                    pt, pb = ps.next()
                    P.op("pe", lambda e, pt=pt, h=h, cc=cc: e.matmul(pt[0:64, :], lhsT=ones32[64:65, 0:64], rhs=acc[h][64:65, cc * 512:(cc + 1) * 512], start=True, stop=True),
                         reads=[Bc, Bacc[h]], writes=[pb])
                    P.op("dve", lambda e, pt=pt, h=h, cc=cc: e.tensor_tensor(out=obd[:], in0=acc[h][0:64, cc * 512:(cc + 1) * 512], in1=pt[0:64, :], op=ALU.mult),
                         reads=[pb, Bacc[h]], writes=[Bobd])
                    P.dma("sp", "obd", osrc[rg * 384 + 128 + h * 64:rg * 384 + 128 + (h + 1) * 64, cc * 512:(cc + 1) * 512], obd[:], reads=[Bobd], writes=[BoT])

    P.barrier()
    es2.close()
    es2 = ExitStack()
    sb = lambda n, s, dt=F32: es2.enter_context(nc.sbuf_tensor("%s_A%d" % (n, layer), s, dt))
    if "hg" in phases:
        oac = [sb("oac%d" % h, [64, S]) for h in range(2)]; Boac = [Buf("oac%d" % h) for h in range(2)]
        for h in range(2):
            P.op("pool", lambda e, h=h: e.memset(oac[h][:], 0.0), writes=[Boac[h]])
        gam = [[sb("gam%d%d" % (h, d), [128, 128]) for d in range(2)] for h in range(2)]; Bgam = Buf("gam")
        for h in range(2):
            for d in range(2):
                if d == 0:
                    dst, mn, bc, mc = gam[h][d][:, 0:127], mT[h][d][:, 1:128], BTt[h][d][:, 0:127], mT[h][d][:, 0:127]
                else:
                    dst, mn, bc, mc = gam[h][d][:, 1:128], mT[h][d][:, 0:127], BTt[h][d][:, 1:128], mT[h][d][:, 1:128]
                P.op("dve", lambda e, dst=dst, mn=mn, bc=bc: e.tensor_tensor(out=dst, in0=mn, in1=bc, op=ALU.add), reads=[BmB], writes=[Bgam])
                P.op("dve", lambda e, dst=dst, mc=mc: e.tensor_tensor(out=dst, in0=dst, in1=mc, op=ALU.subtract), reads=[BmB, Bgam], writes=[Bgam])
                P.op("act", lambda e, dst=dst: e.activation(out=dst, in_=dst, func=AF.Exp), reads=[Bgam], writes=[Bgam])
        ch = [(h, d) for d in range(2) for h in range(2)]
        qT = {c: sb("hqT%d%d" % c, [128, TT], BF16) for c in ch}; kT = {c: sb("hkT%d%d" % c, [128, TT], BF16) for c in ch}
        vT = {c: sb("hvT%d%d" % c, [128, TT], BF16) for c in ch}
        Bld = {c: Buf("hld%d%d" % c) for c in ch}
        ktok = {c: sb("ktok%d%d" % c, [128, 128], BF16) for c in ch}; Bktok = {c: Buf("ktok%d%d" % c) for c in ch}
        vtok = {c: sb("vtok%d%d" % c, [128, 128], BF16) for c in ch}; Bvtok = {c: Buf("vtok%d%d" % c) for c in ch}
        at16 = {c: sb("at16%d%d" % c, [128, 128], BF16) for c in ch}; Bat = {c: Buf("at%d%d" % c) for c in ch}
        S32 = {c: sb("S32%d%d" % c, [128, 64]) for c in ch}; S16 = {c: sb("S16%d%d" % c, [128, 64], BF16) for c in ch}
        Sh = {c: sb("Sh%d%d" % c, [128, 64]) for c in ch}
        BS32 = {c: Buf("S32%d%d" % c) for c in ch}; BS16 = {c: Buf("S16%d%d" % c) for c in ch}; BSh = {c: Buf("Sh%d%d" % c) for c in ch}
        for c in ch:
            P.op("pool", lambda e, c=c: e.memset(S32[c][:], 0.0), writes=[BS32[c]])
            P.op("pool", lambda e, c=c: e.memset(S16[c][:], 0.0), writes=[BS16[c]])
        for step in range(NT):
            for c in ch:
                h, d = c
                ti = step if d == 0 else NT - 1 - step
                c0 = ti * TT
                P.dma("sp", "hl%d%d" % c, qT[c][:], hq_d[h][d][:, c0:c0 + TT], reads=[Bhqk], writes=[Bld[c]])
                P.dma("sp", "hl%d%d" % c, kT[c][:], hk_d[h][d][:, c0:c0 + TT], reads=[Bhqk], writes=[Bld[c]])
                P.dma("sp", "hl%d%d" % c, vT[c][:], hv_d[:, c0:c0 + TT], reads=[Bhqk], writes=[Bld[c]])
            for pp in range(4):
                for c in ch:
                    h, d = c
                    ti = step if d == 0 else NT - 1 - step
                    pr = pp if d == 0 else 3 - pp
                    p0 = pr * 128
                    ptr, pbr = ps.next()
                    ptr16 = ptr[:].bitcast(BF16)
                    P.op("pe", lambda e, c=c, p0=p0, ptr16=ptr16: e.transpose(out=ptr16[:, 0:128], in_=kT[c][:, p0:p0 + 128], identity=id16[:]),
                         reads=[Bld[c], Bc], writes=[pbr])
                    P.op("act", lambda e, c=c, ptr16=ptr16: e.activation(out=ktok[c][:], in_=ptr16[:, 0:128], func=AF.Copy), reads=[pbr], writes=[Bktok[c]])
                    ptr2, pbr2 = ps.next()
                    ptr216 = ptr2[:].bitcast(BF16)
                    P.op("pe", lambda e, c=c, p0=p0, ptr216=ptr216: e.transpose(out=ptr216[:, 0:128], in_=vT[c][:, p0:p0 + 128], identity=id16[:]),
                         reads=[Bld[c], Bc], writes=[pbr2])
                    P.op("act", lambda e, c=c, ptr216=ptr216: e.activation(out=vtok[c][:], in_=ptr216[:, 0:128], func=AF.Copy), reads=[pbr2], writes=[Bvtok[c]])
                    pa, pba = ps.next()
                    P.op("pe", lambda e, c=c, p0=p0, pa=pa: e.matmul(pa[:, 0:128], lhsT=kT[c][:, p0:p0 + 128], rhs=qT[c][:, p0:p0 + 128], start=True, stop=True),
                         reads=[Bld[c]], writes=[pba])
                    P.op("dve", lambda e, c=c, d=d, pa=pa: e.tensor_tensor(out=at16[c][:], in0=pa[:, 0:128], in1=msk[:, d, :], op=ALU.mult),
                         reads=[pba, Bc], writes=[Bat[c]])
                    po, pbo = ps.next()
                    P.op("pe", lambda e, c=c, h=h, po=po: e.matmul(po[0:64, 0:128], lhsT=vtok[c][:, h * 64:(h + 1) * 64], rhs=at16[c][:], start=True, stop=False),
                         reads=[Bvtok[c], Bat[c]], writes=[pbo])
                    for cc in range(2):
                        ck = cc if d == 0 else 1 - cc
                        cidx = ti * 8 + pr * 2 + ck
                        q0 = p0 + ck * 64
                        P.op("pe", lambda e, c=c, po=po, ck=ck, q0=q0, cc=cc: e.matmul(po[0:64, ck * 64:(ck + 1) * 64], lhsT=S16[c][:], rhs=qT[c][:, q0:q0 + 64],
                             start=False, stop=(cc == 1)), reads=[BS16[c], Bld[c]], writes=[pbo])
                        pu, pbu = ps.next()
                        P.op("pe", lambda e, c=c, h=h, pu=pu, ck=ck: e.matmul(pu[:, 0:64], lhsT=ktok[c][ck * 64:(ck + 1) * 64, :], rhs=vtok[c][ck * 64:(ck + 1) * 64, h * 64:(h + 1) * 64],
                             start=True, stop=True), reads=[Bktok[c], Bvtok[c]], writes=[pbu])
                        P.op("dve", lambda e, c=c, pu=pu: e.tensor_tensor(out=Sh[c][:], in0=pu[:, 0:64], in1=S32[c][:], op=ALU.add), reads=[pbu, BS32[c]], writes=[BSh[c]])
                        P.op("dve", lambda e, c=c, h=h, d=d, cidx=cidx: e.tensor_scalar(out=S32[c][:], in0=Sh[c][:], scalar1=gam[h][d][:, cidx:cidx + 1], scalar2=None, op0=ALU.mult),
                             reads=[BSh[c], Bgam], writes=[BS32[c]])
                        P.op("act", lambda e, c=c, h=h, d=d, cidx=cidx: e.activation(out=S16[c][:], in_=Sh[c][:], func=AF.Copy, scale=gam[h][d][:, cidx:cidx + 1]),
                             reads=[BSh[c], Bgam], writes=[BS16[c]])
                    t0_ = ti * TT + p0
                    P.op("dve", lambda e, h=h, po=po, t0_=t0_: e.tensor_tensor(out=oac[h][:, t0_:t0_ + 128], in0=oac[h][:, t0_:t0_ + 128], in1=po[0:64, 0:128], op=ALU.add),
                         reads=[pbo, Boac[h]], writes=[Boac[h]])
        o16c = [sb("o16c%d" % i, [64, 2048], BF16) for i in range(2)]; Bo16c = [Buf("o16c%d" % i) for i in range(2)]
        for h in range(2):
            for tq in range(4):
                u = (h * 4 + tq) % 2
                P.op("act" if u == 0 else "dve", (lambda e, h=h, tq=tq, u=u: e.activation(out=o16c[u][:], in_=oac[h][:, tq * 2048:(tq + 1) * 2048], func=AF.Copy)) if u == 0 else
                     (lambda e, h=h, tq=tq, u=u: e.tensor_copy(out=o16c[u][:], in_=oac[h][:, tq * 2048:(tq + 1) * 2048])), reads=[Boac[h]], writes=[Bo16c[u]])
                P.dma("sp", "o16c%d" % u, osrc[tq * 384 + h * 64:tq * 384 + (h + 1) * 64, :], o16c[u][:], reads=[Bo16c[u]], writes=[BoT])

    P.barrier()
    es2.close()
    es0.close()


RG4 = [[0, 1, 2, 3], [4, 5, 6, 7]]


def build_fused():
    nc = bass.Bass("TRN2", target_bir_lowering=False)
    dr = lambda n, s, kind="ExternalInput", dt=F32: nc.dram_tensor(n, s, dt, kind=kind).ap()
    shared = {"pos": dr("pos", [1, S], dt=I32), "etab": dr("etab", [128, 18, 256]), "ropec": dr("ropec", [64, 2]),
              "ident": dr("ident", [128, 128]), "masks": dr("masks", [128, 2, 128]), "scanmask": dr("scanmask", [128, 512]),
              "lbraw": dr("lbraw", [128, 4, 2])}
    xT = dr("xT", [1024, S]); xTq = dr("xTq", [1024, 2048]); oidx = dr("oidx", [128, 96], dt=I32)
    ioA, ioB = [], []
    for l in range(2):
        a = dict(shared)
        a.update({"wA": dr("wA%d" % l, [1024, 2816]), "bA": dr("bA%d" % l, [128, 23]), "wuq": dr("wuq%d" % l, [384, 256]), "gq": dr("gq%d" % l, [128, 3]),
                  "wukv": dr("wukv%d" % l, [256, 256]), "gkv": dr("gkv%d" % l, [128, 2])})
        ioA.append(a)
        ioB.append({"wg": dr("wg%d" % l, [1024, 4608]), "bg": dr("bg%d" % l, [128, 36]), "wbr": dr("wbr%d" % l, [1536, 1024]), "wo": dr("wo%d" % l, [1024, 1024]),
                    "hgn": dr("hgn%d" % l, [128, 4]), "lng": dr("lng%d" % l, [128, 8]), "lnb": dr("lnb%d" % l, [128, 8]), "oidx": oidx})
    outT = dr("outT", [1024, 2048], kind="ExternalOutput")
    cco_src = [nc.dram_tensor("cco_src%d" % l, [1536, 2048], BF16) for l in range(2)]
    cco_dst = [nc.dram_tensor("cco_dst%d" % l, [6 * 1024, 2048], BF16) for l in range(2)]
    ccx_src = nc.dram_tensor("ccx_src", [1024, 2048], BF16)
    ccx_dst = nc.dram_tensor("ccx_dst", [4096, 2048], BF16)
    xn32 = nc.dram_tensor("xn32", [1024, 2048], F32).ap()
    scr = make_scratch(nc)
    P = Prog(nc)
    ps = PsumPool(nc)
    Bxn = Buf("xn32"); Bxg = Buf("xg"); Bnone = Buf("none")
    for l in range(2):
        ioA[l]["osrc"] = cco_src[l].ap()
        if l == 0:
            ioA[l]["xT"] = xT
            emit_A(nc, P, ps, l, ioA[l], scr)
        else:
            ioA[l]["Bxg"] = Bxg
            emit_A(nc, P, ps, l, ioA[l], scr, xsrc16=ccx_dst.ap())
        P.barrier()
        Bod = Buf("cco_dst%d" % l)
        for k in range(6):
            P.dma("pool", "cc_o%d" % l, None, None, reads=[Bnone], writes=[Bod], inc=1,
                  fn=(lambda e, l=l, k=k: e.collective_compute("AllGather", ALU.bypass, replica_groups=RG4,
                                                             ins=[cco_src[l].ap()[k * 256:(k + 1) * 256, :].opt()],
                                                             outs=[cco_dst[l].ap()[k * 1024:(k + 1) * 1024, :].opt()])))
        b = ioB[l]
        b["orows"] = cco_dst[l].ap().rearrange("r (a c) -> (r a) c", c=256)
        b["Bodst"] = Bod
        if l == 0:
            b["x32src"] = xTq; b["Bxsrc"] = Bnone; b["out32"] = xn32; b["out16"] = ccx_src.ap()
        else:
            b["x32src"] = xn32; b["Bxsrc"] = Bxn; b["out32"] = outT
        emit_B(nc, P, ps, l, b)
        P.barrier()
        if l == 0:
            for k in range(4):
                P.dma("pool", "cc_x", None, None, reads=[Bnone], writes=[Bxg], inc=1,
                      fn=(lambda e, k=k: e.collective_compute("AllGather", ALU.bypass, replica_groups=RG4,
                                                            ins=[ccx_src.ap()[k * 256:(k + 1) * 256, :].opt()],
                                                            outs=[ccx_dst.ap()[k * 1024:(k + 1) * 1024, :].opt()])))
    P.barrier()
    P.emit()
    return nc


SPL = [1024,1024,1024,512,512] + [512]*10 + [384,256,64,512,3072]
NAMES = ['hg_q','hg_f_fwd','hg_f_bwd','hg_i','hg_g','dil_q0','dil_k0','dil_v0','dil_q1','dil_k1','dil_v1','dil_q2','dil_k2','dil_v2','dil_g','mla_cq','mla_ckv','mla_kr','mla_g','merge']
OFF = dict(zip(NAMES, [int(v) for v in np.cumsum([0]+SPL[:-1])]))
def a_cols(hq):
    ar = np.arange
    c = []
    c += [OFF['hg_q'] + (2*hq)*128 + ar(128), OFF['hg_q'] + (2*hq+1)*128 + ar(128)]
    c += [OFF['hg_f_fwd'] + (2*hq)*128 + ar(128), OFF['hg_f_fwd'] + (2*hq+1)*128 + ar(128)]
    c += [OFF['hg_f_bwd'] + (2*hq)*128 + ar(128), OFF['hg_f_bwd'] + (2*hq+1)*128 + ar(128)]
    c += [OFF['hg_i'] + hq*128 + ar(128)]
    for g in range(3):
        for t in 'qkv':
            c += [OFF['dil_%s%d' % (t, g)] + hq*128 + ar(128)]
    c += [OFF['mla_cq'] + ar(384), OFF['mla_ckv'] + ar(256)]
    kr = OFF['mla_kr'] + ar(64)
    c += [kr, np.concatenate([kr[32:], kr[:32]])]
    return np.concatenate(c)
def etab_np(hq):
    slopes = 2.0 ** (-8.0 * (np.arange(24) + 1) / 24)
    kk = np.arange(128)[:, None]; qq = np.arange(128)[None, :]
    E = np.zeros((128, 18, 256), np.float32)
    for g, d in enumerate((1, 4, 16)):
        for hh in range(2):
            sl = slopes[g*8 + 2*hq + hh]
            for var in range(3):
                for kc in range(2):
                    rel = (kk + 128*kc - 64) - qq
                    e = np.where(np.abs(rel) <= 64, np.exp(-sl * d * np.abs(rel)), 0.0)
                    if var == 1 and kc == 0: e = np.where(kk < 64, 0.0, e)
                    if var == 2 and kc == 1: e = np.where(kk >= 64, 0.0, e)
                    E[:, (g*2+hh)*3 + var, kc*128:(kc+1)*128] = e
    return E
def a_inputs(inp, l, b, hq, xT_b):
    cols = a_cols(hq)
    w_in = inp['w_in'][l]; b_in = inp['b_in'][l]
    bsel = b_in[cols]
    bA = np.zeros((128, 23), np.float32)
    bA[:, :22] = bsel.reshape(22, 128).T
    bA[:64, 22] = bsel[21*128+64: 22*128]
    lbraw = np.zeros((128, 4, 2), np.float32)
    for h in range(2):
        for d, nm in enumerate(('hg_lb_fwd', 'hg_lb_bwd')):
            lbraw[:, h*2+d, :] = inp[nm][:, (2*hq+h)*128:(2*hq+h+1)*128].T
    wuq = inp['w_uq'][l]
    qc = hq*192 + np.arange(192)
    rope = qc[128:]
    wuq_sel = np.concatenate([wuq[:, qc[:128]], wuq[:, rope], wuq[:, np.concatenate([rope[32:], rope[:32]])]], 1)
    wukv = inp['w_ukv'][l]
    wukv_sel = wukv[:, hq*256:(hq+1)*256]
    inv = (1.0 / (10000.0 ** (np.arange(32, dtype=np.float32) / 32))).astype(np.float32)
    ropec = np.zeros((64, 2), np.float32); ropec[:, 0] = np.concatenate([inv, inv]); ropec[:32, 1] = -1.0; ropec[32:, 1] = 1.0
    masks = np.zeros((128, 2, 128), np.float32)
    ss = np.arange(128)[:, None]; tq = np.arange(128)[None, :]
    same = (ss // 64) == (tq // 64)
    masks[:, 0, :] = (same & (ss <= tq)).astype(np.float32)
    masks[:, 1, :] = (same & (ss >= tq)).astype(np.float32)
    sm = np.ones((128, 512), np.float32); sm[:, ::64] = 0.0
    return {"pos": np.ascontiguousarray(inp['positions'][b:b+1].astype(np.int32)), "wA": np.ascontiguousarray(w_in[:, cols]), "bA": bA, "lbraw": lbraw,
            "wuq": np.ascontiguousarray(wuq_sel), "gq": np.ascontiguousarray(inp['mla_q_norm'][l].reshape(3,128).T),
            "wukv": np.ascontiguousarray(wukv_sel), "gkv": np.ascontiguousarray(inp['mla_kv_norm'][l].reshape(2,128).T),
            "etab": etab_np(hq), "ropec": ropec, "ident": np.eye(128, dtype=np.float32), "masks": masks, "scanmask": sm}


_CACHE = {}


def _b_inputs(inp, l):
    w_in = inp['w_in'][l]; b_in = inp['b_in'][l]
    cols = np.concatenate([np.arange(OFF['hg_g'], OFF['hg_g'] + 512), np.arange(OFF['dil_g'], OFF['dil_g'] + 512),
                           np.arange(OFF['mla_g'], OFF['mla_g'] + 512), np.arange(OFF['merge'], OFF['merge'] + 3072)])
    return {"wg%d" % l: np.ascontiguousarray(w_in[:, cols]), "bg%d" % l: np.ascontiguousarray(b_in[cols].reshape(36, 128).T),
            "wbr%d" % l: np.ascontiguousarray(inp['w_branch'][l].reshape(1536, 1024)), "wo%d" % l: np.ascontiguousarray(inp['w_out'][l]),
            "hgn%d" % l: np.ascontiguousarray(inp['hg_norm'][l].reshape(4, 128).T),
            "lng%d" % l: np.ascontiguousarray(inp['ln_g'][l].reshape(8, 128).T), "lnb%d" % l: np.ascontiguousarray(inp['ln_b'][l].reshape(8, 128).T)}


def _oidx(tq):
    p = np.arange(128)[:, None, None, None]; tt = np.arange(8)[None, :, None, None]
    n = np.arange(3)[None, None, :, None]; r = np.arange(4)[None, None, None, :]
    rho = tq * 384 + n * 128 + p
    g = (rho // 256) * 1024 + r * 256 + (rho % 256)
    v = g * 8 + tt
    return np.ascontiguousarray(v.reshape(128, 96).astype(np.int32))


def kernel(**inputs):
    inp = {k: np.asarray(v) for k, v in inputs.items()}
    inp['positions'] = inp['positions'].astype(np.int32)
    for k in inp:
        if k != 'positions':
            inp[k] = inp[k].astype(np.float32, copy=False)
    B = 2
    xT = [np.ascontiguousarray(inp['x'][b].T) for b in range(B)]
    if "nc" not in _CACHE:
        _CACHE["nc"] = build_fused()
    nc = _CACHE["nc"]
    bl = [_b_inputs(inp, l) for l in range(2)]
    in_maps = []
    for c in range(8):
        b, q = c // 4, c % 4
        m = {"xT": xT[b], "xTq": np.ascontiguousarray(xT[b][:, q * 2048:(q + 1) * 2048]), "oidx": _oidx(q)}
        for l in range(2):
            a = a_inputs(inp, l, b, q, None)
            for k in ("pos", "etab", "ropec", "ident", "masks", "scanmask", "lbraw"):
                m[k] = a[k]
            for k in ("wA", "bA", "wuq", "gq", "wukv", "gkv"):
                m["%s%d" % (k, l)] = a[k]
            m.update(bl[l])
        in_maps.append(m)
    res = run_bass_kernel_spmd(nc, in_maps, core_ids=list(range(8))).results
    out = np.empty((B, 8192, 1024), np.float32)
    for c in range(8):
        b, q = c // 4, c % 4
        out[b, q * 2048:(q + 1) * 2048, :] = np.asarray(res[c]["outT"]).T
    return out
```

```python
import math
from contextlib import ExitStack
import numpy as np
from concourse.bass_utils import run_bass_kernel_spmd
import concourse.bass as bass
import concourse.mybir as mybir

F32 = mybir.dt.float32
BF16 = mybir.dt.bfloat16
I32 = mybir.dt.int32
AF = mybir.ActivationFunctionType
ALU = mybir.AluOpType
AX = mybir.AxisListType


class Buf:
    __slots__ = ("name", "w", "r")

    def __init__(self, name=""):
        self.name = name
        self.w = {}
        self.r = {}


class _Eng:
    def __init__(self, name, sem):
        self.name = name
        self.sem = sem
        self.count = 0
        self.waited = {}
        self.items = []


class Prog:
    ENGS = ("pe", "act", "dve", "pool", "sp")

    def __init__(self, nc):
        self.nc = nc
        self.e = {n: _Eng(n, nc.alloc_semaphore("prog_" + n)) for n in self.ENGS}
        self.chan = {}
        self.nops = 0
        self.retired = []

    def _need(self, eng, waits, ev, raw):
        sem, val, en = ev
        if en == eng.name:
            if eng.name == "pe" or not raw:
                return
        k = id(sem)
        if eng.waited.get(k, 0) >= val:
            return
        if k not in waits or waits[k][1] < val:
            waits[k] = (sem, val)

    def _deps(self, eng, reads, writes):
        waits = {}
        for b in reads:
            for ev in b.w.values():
                self._need(eng, waits, ev, True)
        for b in writes:
            for ev in b.w.values():
                self._need(eng, waits, ev, False)
            for ev in b.r.values():
                self._need(eng, waits, ev, False)
        for k, (sem, val) in waits.items():
            eng.waited[k] = val
        return list(waits.values())

    @staticmethod
    def _mark(ev, reads, writes):
        k = id(ev[0])
        for b in reads:
            o = b.r.get(k)
            if o is None or o[1] < ev[1]:
                b.r[k] = ev
        for b in writes:
            o = b.w.get(k)
            if o is None or o[1] < ev[1]:
                b.w[k] = ev

    def op(self, engname, fn, reads=(), writes=()):
        eng = self.e[engname]
        waits = self._deps(eng, reads, writes)
        if eng.count >= 30000:
            self.retired.append((eng.sem, eng.count))
            eng.sem = self.nc.alloc_semaphore("prog_%s_%d" % (engname, self.nops))
            eng.count = 0
        eng.count += 1
        ev = (eng.sem, eng.count, eng.name)
        eng.items.append((waits, fn, (eng.sem, 1)))
        self._mark(ev, reads, writes)
        self.nops += 1
        return ev

    def dma(self, qname, chan, out, in_, reads=(), writes=(), fn=None, inc=16):
        eng = self.e[qname]
        waits = self._deps(eng, reads, writes)
        if chan not in self.chan:
            self.chan[chan] = [self.nc.alloc_semaphore("ch_" + chan), 0]
        c = self.chan[chan]
        if c[1] >= 30000:
            self.retired.append((c[0], c[1]))
            c[0] = self.nc.alloc_semaphore("ch_%s_%d" % (chan, self.nops))
            c[1] = 0
        c[1] += inc
        ev = (c[0], c[1], "dma")
        if fn is None:
            fn = (lambda e, o=out, i=in_: e.dma_start(out=o, in_=i))
        eng.items.append((waits, fn, (c[0], inc)))
        self._mark(ev, reads, writes)
        self.nops += 1
        return ev

    def barrier(self):
        evs = list(self.retired)
        for n in self.ENGS:
            if self.e[n].count > 0:
                evs.append((self.e[n].sem, self.e[n].count))
        for c in self.chan.values():
            evs.append((c[0], c[1]))
        for n in self.ENGS:
            eng = self.e[n]
            waits = []
            for sem, val in evs:
                if eng.waited.get(id(sem), 0) < val:
                    waits.append((sem, val))
                    eng.waited[id(sem)] = val
            eng.items.append((waits, None, None))

    def wait_all(self, engname, bufs):
        eng = self.e[engname]
        waits = self._deps(eng, bufs, bufs)
        eng.items.append((waits, None, None))

    def emit(self):
        nc = self.nc
        with nc.Block() as block:
            def run(eng, h):
                for waits, fn, inc in eng.items:
                    for sem, val in waits:
                        h.wait_ge(sem, val)
                    if fn is not None:
                        ins = fn(h)
                        ins.then_inc(inc[0], inc[1])

            @block.tensor
            def _(h):
                run(self.e["pe"], h)

            @block.scalar
            def _(h):
                run(self.e["act"], h)

            @block.vector
            def _(h):
                run(self.e["dve"], h)

            @block.gpsimd
            def _(h):
                run(self.e["pool"], h)

            @block.sync
            def _(h):
                run(self.e["sp"], h)


ALPHA = 4.0 ** 0.25
LN_EPS = 1e-5


class PsumPool:
    def __init__(self, nc, n=8):
        self.t = [nc.alloc_psum_tensor("psb%d" % i, [128, 512], F32) for i in range(n)]
        self.b = [Buf("psb%d" % i) for i in range(n)]
        self.i = 0
        self.n = n

    def next(self):
        i = self.i
        self.i = (i + 1) % self.n
        return self.t[i], self.b[i]


def load_cast_weight(P, nc, q, dram2d, dst16, dstbuf, kchunks, ncols, stage, stage_bufs, ctr, colsplit):
    v = dram2d.rearrange("(k p) c -> p k c", p=128)
    for k in range(kchunks):
        for c0 in range(0, ncols, colsplit):
            cw = min(colsplit, ncols - c0)
            s = ctr[0] % len(stage)
            ctr[0] += 1
            P.dma(q, "wst%d" % s, stage[s][:, 0:cw], v[:, k, c0:c0 + cw], writes=[stage_bufs[s]])
            eng = "dve" if (ctr[0] % 2 == 0) else "pool"
            P.op(eng, (lambda e, s=s, k=k, c0=c0, cw=cw: e.tensor_copy(out=dst16[:, k, c0:c0 + cw], in_=stage[s][:, 0:cw])),
                 reads=[stage_bufs[s]], writes=[dstbuf])


def emit_B(nc, P, ps, layer, io):
    T = 2048
    TT = 256
    NT = T // TT
    xT = io["x32src"]; wg = io["wg"]; bg = io["bg"]; wbr = io["wbr"]; wo = io["wo"]; hgn = io["hgn"]; lng = io["lng"]; lnb = io["lnb"]
    orows = io["orows"]; oidx = io["oidx"]; Bodst = io["Bodst"]; Bxsrc = io["Bxsrc"]
    out32 = io["out32"]; out16 = io.get("out16")
    esb = ExitStack()
    sb = lambda n, s, dt=F32: esb.enter_context(nc.sbuf_tensor("%s_B%d" % (n, layer), s, dt))
    wg16 = sb("wg16", [128, 8, 4608], BF16); Bwg = Buf("wg16")
    wbr16 = sb("wbr16", [128, 12, 1024], BF16); Bwbr = Buf("wbr16")
    wo16 = sb("wo16", [128, 8, 1024], BF16); Bwo = Buf("wo16")
    stage = [sb("wstage%d" % i, [128, 1152], F32) for i in range(2)]
    stage_b = [Buf("wstage%d" % i) for i in range(2)]
    bgs = sb("bgs", [128, 36]); hgns = sb("hgns", [128, 4]); lngs = sb("lngs", [128, 8]); lnbs = sb("lnbs", [128, 8])
    oix = sb("oix", [128, 96], I32)
    Bc = Buf("consts")
    ones32 = sb("ones32", [128, 128]); Bones = Buf("ones")
    epsr = sb("epsr", [128, 1]); epsl = sb("epsl", [128, 1])
    x32 = sb("x32", [128, 8, TT]); Bx32 = Buf("x32")
    x16 = sb("x16", [128, 8, TT], BF16); Bx16 = Buf("x16")
    o32 = sb("o16", [128, 12, TT], BF16); Bo32 = Buf("o16")
    y16 = sb("y16", [128, 12, TT], BF16); By16 = [Buf("y16_%d" % i) for i in range(12)]
    gt = [sb("gt%d" % i, [128, TT]) for i in range(2)]; Bgt = [Buf("gt%d" % i) for i in range(2)]
    sq = sb("sq", [128, 8, TT]); Bsq = Buf("sq")
    rstd = sb("rstd", [128, TT]); Brstd = Buf("rstd")
    tmp = sb("tmp", [128, TT]); Btmp = Buf("tmp")
    sg = [sb("sg%d" % i, [128, 3, TT]) for i in range(2)]; Bsg = [Buf("sg%d" % i) for i in range(2)]
    mm = sb("mm", [128, TT]); Bmm = Buf("mm")
    tt2 = sb("tt2", [128, TT]); Btt2 = Buf("tt2")
    mg16 = sb("mg16", [128, 8, TT], BF16); Bmg = [Buf("mg%d" % i) for i in range(8)]
    r32 = sb("r32", [128, 8, TT]); Br = [Buf("r%d" % i) for i in range(8)]
    mean = sb("mean", [128, TT]); Bmean = Buf("mean")
    ob = [sb("ob%d" % i, [128, TT]) for i in range(2)]; Bob = [Buf("ob%d" % i) for i in range(2)]
    ob16 = [sb("ob16_%d" % i, [128, TT], BF16) for i in range(2)]; Bob16 = [Buf("ob16_%d" % i) for i in range(2)]
    Bout = Buf("xnT")
    xnT = out32

    P.dma("sp", "c0", bgs[:], bg, writes=[Bc])
    P.dma("sp", "c4", oix[:], oidx, writes=[Bc])
    P.dma("sp", "c1", hgns[:], hgn, writes=[Bc])
    P.dma("sp", "c2", lngs[:], lng, writes=[Bc])
    P.dma("sp", "c3", lnbs[:], lnb, writes=[Bc])
    P.op("pool", lambda e: e.memset(ones32[:], 1.0), writes=[Bones])
    P.op("pool", lambda e: e.memset(epsr[:], RMS_EPS), writes=[Bc])
    P.op("pool", lambda e: e.memset(epsl[:], LN_EPS), writes=[Bc])
    ctr = [0]
    load_cast_weight(P, nc, "sp", wg, wg16, Bwg, 8, 4608, stage, stage_b, ctr, 1152)
    load_cast_weight(P, nc, "sp", wbr, wbr16, Bwbr, 12, 1024, stage, stage_b, ctr, 1024)
    load_cast_weight(P, nc, "sp", wo, wo16, Bwo, 8, 1024, stage, stage_b, ctr, 1024)

    xv = xT.rearrange("(k p) t -> p k t", p=128)
    outv = xnT.rearrange("(k p) t -> p k t", p=128)
    gi = 0
    for tt in range(NT):
        c0 = tt * TT
        P.dma("act", "x32", x32[:], xv[:, :, c0:c0 + TT], reads=[Bxsrc], writes=[Bx32])
        for blk in range(12):
            P.dma("pool", "o16g", None, None, reads=[Bodst, Bc], writes=[Bo32],
                  fn=(lambda e, blk=blk, tt=tt: e.indirect_dma_start(out=o32[:, blk, :], out_offset=None, in_=orows,
                                                                   in_offset=bass.IndirectOffsetOnAxis(ap=oix[:, tt * 12 + blk:tt * 12 + blk + 1], axis=0))))
        P.op("pool", lambda e: e.tensor_copy(out=x16[:], in_=x32[:]), reads=[Bx32], writes=[Bx16])
        P.op("act", lambda e: e.activation(out=sq[:, 0:4, :], in_=o32[:, 0:4, :], func=AF.Square), reads=[Bo32], writes=[Bsq])
        pt, pb = ps.next()
        for j in range(4):
            P.op("pe", lambda e, j=j, pt=pt: e.matmul(pt[:, 0:TT], lhsT=ones32[:], rhs=sq[:, j, :], start=(j == 0), stop=(j == 3)),
                 reads=[Bones, Bsq], writes=[pb])
        P.op("act", lambda e, pt=pt: e.activation(out=tmp[:], in_=pt[:, 0:TT], func=AF.Sqrt, bias=epsr[:, 0:1], scale=1.0 / 512.0), reads=[pb, Bc], writes=[Btmp])
        P.op("dve", lambda e: e.reciprocal(out=rstd[:], in_=tmp[:]), reads=[Btmp], writes=[Brstd])
        for blk in range(12):
            pt, pb = ps.next()
            for k in range(8):
                P.op("pe", lambda e, k=k, pt=pt, blk=blk: e.matmul(pt[:, 0:TT], lhsT=wg16[:, k, blk * 128:(blk + 1) * 128], rhs=x16[:, k, :],
                                                              start=(k == 0), stop=(k == 7)), reads=[Bwg, Bx16], writes=[pb])
            g = gi % 2
            gi += 1
            P.op("act", lambda e, pt=pt, blk=blk, g=g: e.activation(out=gt[g][:], in_=pt[:, 0:TT], func=AF.Silu, bias=bgs[:, blk:blk + 1], scale=1.0),
                 reads=[pb, Bc], writes=[Bgt[g]])
            if blk < 4:
                P.op("dve", lambda e, g=g: e.tensor_tensor(out=gt[g][:], in0=gt[g][:], in1=rstd[:], op=ALU.mult),
                     reads=[Bgt[g], Brstd], writes=[Bgt[g]])
                P.op("dve", lambda e, g=g, blk=blk: e.scalar_tensor_tensor(out=y16[:, blk, :], in0=o32[:, blk, :], scalar=hgns[:, blk:blk + 1],
                                                                           in1=gt[g][:], op0=ALU.mult, op1=ALU.mult),
                     reads=[Bo32, Bgt[g], Bc], writes=[By16[blk]])
            else:
                P.op("dve", lambda e, g=g, blk=blk: e.tensor_tensor(out=y16[:, blk, :], in0=o32[:, blk, :], in1=gt[g][:], op=ALU.mult),
                     reads=[Bo32, Bgt[g]], writes=[By16[blk]])
        for db in range(8):
            s = db % 2
            pbs = []
            for n in range(3):
                pt, pb = ps.next()
                col = 1536 + n * 1024 + db * 128
                for k in range(8):
                    P.op("pe", lambda e, k=k, pt=pt, col=col: e.matmul(pt[:, 0:TT], lhsT=wg16[:, k, col:col + 128], rhs=x16[:, k, :],
                                                                  start=(k == 0), stop=(k == 7)), reads=[Bwg, Bx16], writes=[pb])
                bi = 12 + n * 8 + db
                P.op("act", lambda e, pt=pt, n=n, s=s, bi=bi: e.activation(out=sg[s][:, n, :], in_=pt[:, 0:TT], func=AF.Sigmoid, bias=bgs[:, bi:bi + 1], scale=1.0),
                     reads=[pb, Bc], writes=[Bsg[s]])
            for n in range(3):
                pt, pb = ps.next()
                for j in range(4):
                    P.op("pe", lambda e, j=j, n=n, pt=pt, db=db: e.matmul(pt[:, 0:TT], lhsT=wbr16[:, n * 4 + j, db * 128:(db + 1) * 128], rhs=y16[:, n * 4 + j, :],
                                                                     start=(j == 0), stop=(j == 3)), reads=[Bwbr, By16[n * 4 + j]], writes=[pb])
                pbs.append((pt, pb))
            P.op("dve", lambda e, s=s, p0=pbs[0][0]: e.tensor_tensor(out=mm[:], in0=p0[:, 0:TT], in1=sg[s][:, 0, :], op=ALU.mult),
                 reads=[pbs[0][1], Bsg[s]], writes=[Bmm])
            P.op("dve", lambda e, s=s, p1=pbs[1][0]: e.tensor_tensor(out=tt2[:], in0=p1[:, 0:TT], in1=sg[s][:, 1, :], op=ALU.mult),
                 reads=[pbs[1][1], Bsg[s]], writes=[Btt2])
            P.op("pool", lambda e: e.tensor_tensor(out=mm[:], in0=mm[:], in1=tt2[:], op=ALU.add), reads=[Bmm, Btt2], writes=[Bmm])
            P.op("dve", lambda e, s=s, p2=pbs[2][0]: e.tensor_tensor(out=tt2[:], in0=p2[:, 0:TT], in1=sg[s][:, 2, :], op=ALU.mult),
                 reads=[pbs[2][1], Bsg[s]], writes=[Btt2])
            P.op("pool", lambda e, db=db: e.tensor_tensor(out=mg16[:, db, :], in0=mm[:], in1=tt2[:], op=ALU.add),
                 reads=[Bmm, Btt2], writes=[Bmg[db]])
        for eb in range(8):
            pt, pb = ps.next()
            for d in range(8):
                P.op("pe", lambda e, d=d, pt=pt, eb=eb: e.matmul(pt[:, 0:TT], lhsT=wo16[:, d, eb * 128:(eb + 1) * 128], rhs=mg16[:, d, :],
                                                            start=(d == 0), stop=(d == 7)), reads=[Bwo, Bmg[d]], writes=[pb])
            P.op("dve", lambda e, pt=pt, eb=eb: e.scalar_tensor_tensor(out=r32[:, eb, :], in0=x32[:, eb, :], scalar=ALPHA, in1=pt[:, 0:TT],
                                                                        op0=ALU.mult, op1=ALU.add), reads=[Bx32, pb], writes=[Br[eb]])
        pt, pb = ps.next()
        for eb in range(8):
            P.op("pe", lambda e, eb=eb, pt=pt: e.matmul(pt[:, 0:TT], lhsT=ones32[:], rhs=r32[:, eb, :], start=(eb == 0), stop=(eb == 7)),
                 reads=[Bones, Br[eb]], writes=[pb])
        P.op("act", lambda e, pt=pt: e.activation(out=mean[:], in_=pt[:, 0:TT], func=AF.Copy, scale=1.0 / 1024.0), reads=[pb], writes=[Bmean])
        for eb in range(8):
            P.op("dve", lambda e, eb=eb: e.tensor_tensor(out=r32[:, eb, :], in0=r32[:, eb, :], in1=mean[:], op=ALU.subtract),
                 reads=[Br[eb], Bmean], writes=[Br[eb]])
        P.op("act", lambda e: e.activation(out=sq[:], in_=r32[:], func=AF.Square), reads=Br, writes=[Bsq])
        pt, pb = ps.next()
        for eb in range(8):
            P.op("pe", lambda e, eb=eb, pt=pt: e.matmul(pt[:, 0:TT], lhsT=ones32[:], rhs=sq[:, eb, :], start=(eb == 0), stop=(eb == 7)),
                 reads=[Bones, Bsq], writes=[pb])
        P.op("act", lambda e, pt=pt: e.activation(out=tmp[:], in_=pt[:, 0:TT], func=AF.Sqrt, bias=epsl[:, 0:1], scale=1.0 / 1024.0), reads=[pb, Bc], writes=[Btmp])
        P.op("dve", lambda e: e.reciprocal(out=rstd[:], in_=tmp[:]), reads=[Btmp], writes=[Brstd])
        for eb in range(8):
            s = eb % 2
            P.op("dve", lambda e, eb=eb: e.tensor_tensor(out=r32[:, eb, :], in0=r32[:, eb, :], in1=rstd[:], op=ALU.mult),
                 reads=[Br[eb], Brstd], writes=[Br[eb]])
            P.op("act", lambda e, eb=eb, s=s: e.activation(out=ob[s][:], in_=r32[:, eb, :], func=AF.Identity, bias=lnbs[:, eb:eb + 1], scale=lngs[:, eb:eb + 1]),
                 reads=[Br[eb], Bc], writes=[Bob[s]])
            P.dma("sp", "ob%d" % s, outv[:, eb, c0:c0 + TT], ob[s][:], reads=[Bob[s]], writes=[Bout])
            if out16 is not None:
                P.op("pool", lambda e, s=s: e.tensor_copy(out=ob16[s][:], in_=ob[s][:]), reads=[Bob[s]], writes=[Bob16[s]])
                P.dma("sp", "ob16_%d" % s, out16.rearrange("(k p) t -> p k t", p=128)[:, eb, c0:c0 + TT], ob16[s][:], reads=[Bob16[s]], writes=[Bout])
    P.barrier()
    esb.close()


RMS_EPS = 1e-6
S = 8192
TT = 512
NT = S // TT
QSCALE = 192.0 ** -0.5
TWO_PI = 2.0 * math.pi
C1 = 6.28125
C2 = TWO_PI - C1
DILS = (1, 4, 16)
LN_MINF = math.log(1e-6)


def make_scratch(nc):
    dr = lambda n, s, dt=BF16: nc.dram_tensor(n, s, dt, kind="Internal").ap()
    scr = {}
    scr["dsub"] = [[dr("dsub%d_%d" % (g, t), [128, DILS[g], S // DILS[g]]) for t in range(3)] for g in range(3)]
    scr["hq_d"] = [[dr("hq%d_%d" % (h, d), [128, S]) for d in range(2)] for h in range(2)]
    scr["hk_d"] = [[dr("hk%d_%d" % (h, d), [128, S]) for d in range(2)] for h in range(2)]
    scr["hv_d"] = dr("hv", [128, S])
    scr["qd1"] = dr("qd1", [128, S]); scr["qd2"] = dr("qd2", [64, S])
    return scr


def emit_A(nc, P, ps, layer, io, scr, xsrc16=None, phases=("mla", "dil", "hg")):
    debug = False
    xT = io.get("xT"); pos = io["pos"]
    wA = io["wA"]; bA = io["bA"]; lbraw = io["lbraw"]
    wuq = io["wuq"]; gq = io["gq"]; wukv = io["wukv"]; gkv = io["gkv"]
    etab = io["etab"]; ropec = io["ropec"]; ident = io["ident"]; masks = io["masks"]; scanmask = io["scanmask"]
    osrc = io["osrc"]
    dsub = scr["dsub"]; hq_d = scr["hq_d"]; hk_d = scr["hk_d"]; hv_d = scr["hv_d"]; qd1 = scr["qd1"]; qd2 = scr["qd2"]
    Bdsub = Buf("dsub"); Bhqk = Buf("hqk"); BoT = Buf("oT"); Bqd = Buf("qd")
    es0 = ExitStack()
    sb = lambda n, s, dt=F32: es0.enter_context(nc.sbuf_tensor("%s_A%d" % (n, layer), s, dt))

    Bc = Buf("consts")
    bAs = sb("bAs", [128, 23]); lbr = sb("lbr", [128, 4, 2]); lbt = sb("lbt", [128, 4, 3])
    gqs = sb("gqs", [128, 3]); gkvs = sb("gkvs", [128, 2]); ropecs = sb("ropecs", [64, 2])
    ones32 = sb("ones32", [128, 128]); epsr = sb("epsr", [128, 1]); id32 = sb("id32", [128, 128]); id16 = sb("id16", [128, 128], BF16)
    ones16 = sb("ones16", [128, 128], BF16)
    msk = sb("msk", [128, 2, 128]); smask = sb("smask", [128, TT])
    P.dma("sp", "c0", bAs[:], bA, writes=[Bc])
    P.dma("sp", "c1", lbr[:], lbraw, writes=[Bc])
    P.dma("sp", "c2", gqs[:], gq, writes=[Bc])
    P.dma("sp", "c3", gkvs[:], gkv, writes=[Bc])
    P.dma("sp", "c4", ropecs[:], ropec, writes=[Bc])
    P.dma("sp", "c5", id32[:], ident, writes=[Bc])
    P.dma("sp", "c6", msk[:], masks, writes=[Bc])
    P.dma("sp", "c7", smask[:], scanmask, writes=[Bc])
    P.op("pool", lambda e: e.memset(ones32[:], 1.0), writes=[Bc])
    P.op("pool", lambda e: e.memset(ones16[:], 1.0), writes=[Bc])
    P.op("pool", lambda e: e.memset(epsr[:], RMS_EPS), writes=[Bc])
    P.op("pool", lambda e: e.tensor_copy(out=id16[:], in_=id32[:]), reads=[Bc], writes=[Bc])
    for blk in (7, 10, 13):
        P.op("dve", lambda e, blk=blk: e.tensor_scalar(out=bAs[:, blk:blk + 1], in0=bAs[:, blk:blk + 1], scalar1=0.125, scalar2=None, op0=ALU.mult),
             reads=[Bc], writes=[Bc])
    if layer == 0:
        P.op("dve", lambda e: e.memset(lbt[:, :, 0], 0.0), writes=[Bc])
    else:
        P.op("dve", lambda e: e.tensor_tensor(out=lbt[:, :, 1], in0=lbr[:, :, 1], in1=lbr[:, :, 0], op=ALU.subtract), reads=[Bc], writes=[Bc])
        P.op("act", lambda e: e.activation(out=lbt[:, :, 0], in_=lbt[:, :, 1], func=AF.Sigmoid), reads=[Bc], writes=[Bc])
    P.op("dve", lambda e: e.tensor_scalar(out=lbt[:, :, 1], in0=lbt[:, :, 0], scalar1=-1.0, scalar2=1.0, op0=ALU.mult, op1=ALU.add), reads=[Bc], writes=[Bc])
    P.op("dve", lambda e: e.tensor_scalar(out=lbt[:, :, 2], in0=lbt[:, :, 1], scalar1=-1.0, scalar2=None, op0=ALU.mult), reads=[Bc], writes=[Bc])

    K1T = sb("K1T", [128, S], BF16); K2T = sb("K2T", [64, S], BF16)
    Vtok = sb("Vtok", [128, S // 128, 128], BF16)
    BQ = Buf("Q"); BK = Buf("K"); BV = Buf("V")
    mT = [[sb("mT%d%d" % (h, d), [128, 128]) for d in range(2)] for h in range(2)]
    BTt = [[sb("BT%d%d" % (h, d), [128, 128]) for d in range(2)] for h in range(2)]
    BmB = Buf("mB")

    es1 = ExitStack()
    sb = lambda n, s, dt=F32: es1.enter_context(nc.sbuf_tensor("%s_A%d" % (n, layer), s, dt))
    win16 = sb("win16", [128, 8, 2816], BF16); Bwin = Buf("win16")
    stage = [sb("wstage%d" % i, [128, 704], F32) for i in range(2)]
    stage_b = [Buf("wstage%d" % i) for i in range(2)]
    wv = wA.rearrange("(k p) c -> p k c", p=128)
    ci = 0
    for k in range(8):
        for c0 in (0, 704, 1408, 2112):
            s = ci % 2
            P.dma("sp", "wst%d" % s, stage[s][:], wv[:, k, c0:c0 + 704], writes=[stage_b[s]])
            P.op("dve" if ci % 2 == 0 else "pool", lambda e, s=s, k=k, c0=c0: e.tensor_copy(out=win16[:, k, c0:c0 + 704], in_=stage[s][:]),
                 reads=[stage_b[s]], writes=[Bwin])
            ci += 1
    wuq16 = sb("wuq16", [128, 3, 256], BF16); wukv16 = sb("wukv16", [128, 2, 256], BF16); Bwu = Buf("wu")
    wuv = wuq.rearrange("(k p) c -> p k c", p=128)
    wkv = wukv.rearrange("(k p) c -> p k c", p=128)
    for j in range(3):
        s = ci % 2
        P.dma("sp", "wst%d" % s, stage[s][:, 0:256], wuv[:, j, :], writes=[stage_b[s]])
        P.op("dve", lambda e, s=s, j=j: e.tensor_scalar(out=wuq16[:, j, :], in0=stage[s][:, 0:256], scalar1=gqs[:, j:j + 1], scalar2=None, op0=ALU.mult),
             reads=[stage_b[s], Bc], writes=[Bwu])
        ci += 1
    for j in range(2):
        s = ci % 2
        P.dma("sp", "wst%d" % s, stage[s][:, 0:256], wkv[:, j, :], writes=[stage_b[s]])
        P.op("dve", lambda e, s=s, j=j: e.tensor_scalar(out=wukv16[:, j, :], in0=stage[s][:, 0:256], scalar1=gkvs[:, j:j + 1], scalar2=None, op0=ALU.mult),
             reads=[stage_b[s], Bc], writes=[Bwu])
        ci += 1

    x32 = sb("x32", [128, 4, TT]); Bx32 = Buf("x32"); x16 = sb("x16", [128, 8, TT], BF16); Bx16 = Buf("x16")
    posi = sb("posi", [64, TT], I32); Bposi = Buf("posi")
    ang = sb("ang", [64, TT]); Bang = Buf("ang")
    ru = sb("ru", [64, TT]); Bru = Buf("ru"); rki = sb("rki", [64, TT], I32); Brki = Buf("rki"); rkf = sb("rkf", [64, TT]); Brkf = Buf("rkf")
    cs = sb("cs", [64, TT]); sn = sb("sn", [64, TT]); Bcs = Buf("cs"); Bsn = Buf("sn")
    c32 = sb("c32", [128, 3, TT]); Bc32 = Buf("c32"); csq = sb("csq", [128, 3, TT]); Bcsq = Buf("csq")
    cn16 = sb("cn16", [128, 3, TT], BF16); Bcn = Buf("cn16")
    rt = sb("rt", [128, TT]); Brt = Buf("rt"); rr = sb("rr", [128, TT]); Brr = Buf("rr")
    t1 = sb("t1", [64, TT]); t2 = sb("t2", [64, TT]); Bt1 = Buf("t1"); Bt2 = Buf("t2")
    q1s = sb("q1s", [128, TT], BF16); q2s = sb("q2s", [64, TT], BF16); Bq1s = Buf("q1s"); Bq2s = Buf("q2s")
    dd16 = [sb("dd16_%d" % i, [128, TT], BF16) for i in range(2)]; Bdd = [Buf("dd16_%d" % i) for i in range(2)]
    qs = [sb("qs%d" % h, [128, TT]) for h in range(2)]; Bqs = [Buf("qs%d" % h) for h in range(2)]
    sig = sb("sig", [128, TT]); Bsig = Buf("sig"); ff = sb("ff", [128, TT]); Bff = Buf("ff")
    bb = sb("bb", [128, TT]); Bbb = Buf("bb"); eq = sb("eq", [128, TT]); Beq = Buf("eq"); ek = sb("ek", [128, TT]); Bek = Buf("ek")
    kk = sb("kk", [128, TT]); Bkk = Buf("kk")
    hq16 = [sb("hq16_%d" % i, [128, TT], BF16) for i in range(2)]; Bhq16 = [Buf("hq16_%d" % i) for i in range(2)]
    hk16 = [sb("hk16_%d" % i, [128, TT], BF16) for i in range(2)]; Bhk16 = [Buf("hk16_%d" % i) for i in range(2)]
    hv16 = sb("hv16", [128, TT], BF16); Bhv16 = Buf("hv16")

    if xsrc16 is None:
        xv = xT.rearrange("(k p) t -> p k t", p=128)
    else:
        xg = xsrc16.rearrange("(c r h p) t -> p r c h t", c=4, r=4, h=2, p=128)

    def inproj(col, m, rhs_cols=None):
        pt, pb = ps.next()
        for k in range(8):
            P.op("pe", lambda e, k=k, pt=pt: e.matmul(pt[0:m, :], lhsT=win16[:, k, col:col + m], rhs=x16[:, k, :], start=(k == 0), stop=(k == 7)),
                 reads=[Bwin, Bx16], writes=[pb])
        return pt, pb

    def sintab(dst, Bdst, shift):
        P.op("dve", lambda e: e.tensor_scalar(out=ru[:], in0=ang[:], scalar1=1.0 / TWO_PI, scalar2=shift / TWO_PI + 0.5, op0=ALU.mult, op1=ALU.add),
             reads=[Bang], writes=[Bru])
        P.op("dve", lambda e: e.tensor_copy(out=rki[:], in_=ru[:]), reads=[Bru], writes=[Brki])
        P.op("dve", lambda e: e.tensor_copy(out=rkf[:], in_=rki[:]), reads=[Brki], writes=[Brkf])
        P.op("dve", lambda e: e.tensor_scalar(out=ru[:], in0=ang[:], scalar1=shift, scalar2=None, op0=ALU.add), reads=[Bang], writes=[Bru])
        P.op("dve", lambda e: e.scalar_tensor_tensor(out=ru[:], in0=rkf[:], scalar=-C1, in1=ru[:], op0=ALU.mult, op1=ALU.add),
             reads=[Brkf, Bru], writes=[Bru])
        P.op("dve", lambda e: e.scalar_tensor_tensor(out=ru[:], in0=rkf[:], scalar=-C2, in1=ru[:], op0=ALU.mult, op1=ALU.add),
             reads=[Brkf, Bru], writes=[Bru])
        P.op("dve", lambda e: e.tensor_scalar(out=rkf[:], in0=ru[:], scalar1=math.pi, scalar2=None, op0=ALU.is_gt), reads=[Bru], writes=[Brkf])
        P.op("dve", lambda e: e.scalar_tensor_tensor(out=ru[:], in0=rkf[:], scalar=-TWO_PI, in1=ru[:], op0=ALU.mult, op1=ALU.add),
             reads=[Brkf, Bru], writes=[Bru])
        P.op("dve", lambda e: e.tensor_scalar(out=rkf[:], in0=ru[:], scalar1=-math.pi, scalar2=None, op0=ALU.is_lt), reads=[Bru], writes=[Brkf])
        P.op("dve", lambda e: e.scalar_tensor_tensor(out=ru[:], in0=rkf[:], scalar=TWO_PI, in1=ru[:], op0=ALU.mult, op1=ALU.add),
             reads=[Brkf, Bru], writes=[Bru])
        P.op("dve", lambda e: e.tensor_scalar(out=ru[:], in0=ru[:], scalar1=math.pi, scalar2=-math.pi, op0=ALU.min, op1=ALU.max), reads=[Bru], writes=[Bru])
        P.op("act", lambda e: e.activation(out=dst[:], in_=ru[:], func=AF.Sin), reads=[Bru], writes=[Bdst])

    def rms_norm(nblk, rank):
        P.op("act", lambda e: e.activation(out=csq[:, 0:nblk, :], in_=c32[:, 0:nblk, :], func=AF.Square), reads=[Bc32], writes=[Bcsq])
        pt, pb = ps.next()
        for j in range(nblk):
            P.op("pe", lambda e, j=j, pt=pt: e.matmul(pt[:], lhsT=ones32[:], rhs=csq[:, j, :], start=(j == 0), stop=(j == nblk - 1)),
                 reads=[Bc, Bcsq], writes=[pb])
        P.op("act", lambda e, pt=pt: e.activation(out=rt[:], in_=pt[:], func=AF.Sqrt, bias=epsr[:, 0:1], scale=1.0 / rank), reads=[pb, Bc], writes=[Brt])
        P.op("dve", lambda e: e.reciprocal(out=rr[:], in_=rt[:]), reads=[Brt], writes=[Brr])
        for j in range(nblk):
            P.op("dve", lambda e, j=j: e.tensor_tensor(out=cn16[:, j, :], in0=c32[:, j, :], in1=rr[:], op=ALU.mult), reads=[Bc32, Brr], writes=[Bcn])

    ddi = 0
    hi = 0
    for tt in range(NT):
        c0 = tt * TT
        if xsrc16 is None:
            for hf in range(2):
                P.dma("sp", "x32", x32[:], xv[:, hf * 4:(hf + 1) * 4, c0:c0 + TT], writes=[Bx32])
                P.op("pool", lambda e, hf=hf: e.tensor_copy(out=x16[:, hf * 4:(hf + 1) * 4, :], in_=x32[:]), reads=[Bx32], writes=[Bx16])
        else:
            for cch in range(4):
                P.dma("sp", "x32", x16[:, cch * 2:cch * 2 + 2, :], xg[:, c0 // 2048, cch, :, (c0 % 2048):(c0 % 2048) + TT], reads=[io["Bxg"]], writes=[Bx16])
        P.dma("sp", "posi", posi[:], pos[0:1, c0:c0 + TT].partition_broadcast(64), writes=[Bposi])
        P.op("dve", lambda e: e.tensor_copy(out=ang[:], in_=posi[:]), reads=[Bposi], writes=[Bang])
        P.op("dve", lambda e: e.tensor_scalar(out=ang[:], in0=ang[:], scalar1=ropecs[:, 0:1], scalar2=None, op0=ALU.mult), reads=[Bang, Bc], writes=[Bang])
        sintab(sn, Bsn, 0.0)
        sintab(cs, Bcs, math.pi / 2)
        P.op("dve", lambda e: e.tensor_scalar(out=sn[:], in0=sn[:], scalar1=ropecs[:, 1:2], scalar2=None, op0=ALU.mult), reads=[Bsn, Bc], writes=[Bsn])
        for j in range(3):
            pt, pb = inproj((16 + j) * 128, 128)
            P.op("act", lambda e, pt=pt, j=j: e.activation(out=c32[:, j, :], in_=pt[:], func=AF.Identity, bias=bAs[:, 16 + j:17 + j], scale=1.0),
                 reads=[pb, Bc], writes=[Bc32])
        rms_norm(3, 384.0)
        pt, pb = ps.next()
        for j in range(3):
            P.op("pe", lambda e, j=j, pt=pt: e.matmul(pt[:], lhsT=wuq16[:, j, 0:128], rhs=cn16[:, j, :], start=(j == 0), stop=(j == 2)),
                 reads=[Bwu, Bcn], writes=[pb])
        P.op("act", lambda e, pt=pt: e.activation(out=q1s[:], in_=pt[:], func=AF.Copy, scale=QSCALE), reads=[pb], writes=[Bq1s])
        P.dma("sp", "q1s", qd1[:, c0:c0 + TT], q1s[:], reads=[Bq1s], writes=[Bqd])
        pA, pbA = ps.next()
        for j in range(3):
            P.op("pe", lambda e, j=j, pA=pA: e.matmul(pA[0:64, :], lhsT=wuq16[:, j, 128:192], rhs=cn16[:, j, :], start=(j == 0), stop=(j == 2)),
                 reads=[Bwu, Bcn], writes=[pbA])
        pB, pbB = ps.next()
        for j in range(3):
            P.op("pe", lambda e, j=j, pB=pB: e.matmul(pB[0:64, :], lhsT=wuq16[:, j, 192:256], rhs=cn16[:, j, :], start=(j == 0), stop=(j == 2)),
                 reads=[Bwu, Bcn], writes=[pbB])
        P.op("dve", lambda e, pA=pA: e.scalar_tensor_tensor(out=t1[:], in0=pA[0:64, :], scalar=QSCALE, in1=cs[:], op0=ALU.mult, op1=ALU.mult),
             reads=[pbA, Bcs], writes=[Bt1])
        P.op("dve", lambda e, pB=pB: e.scalar_tensor_tensor(out=t2[:], in0=pB[0:64, :], scalar=QSCALE, in1=sn[:], op0=ALU.mult, op1=ALU.mult),
             reads=[pbB, Bsn], writes=[Bt2])
        P.op("pool", lambda e: e.tensor_tensor(out=q2s[:], in0=t1[:], in1=t2[:], op=ALU.add), reads=[Bt1, Bt2], writes=[Bq2s])
        P.dma("sp", "q2s", qd2[:, c0:c0 + TT], q2s[:], reads=[Bq2s], writes=[Bqd])
        for j in range(2):
            pt, pb = inproj((19 + j) * 128, 128)
            P.op("act", lambda e, pt=pt, j=j: e.activation(out=c32[:, j, :], in_=pt[:], func=AF.Identity, bias=bAs[:, 19 + j:20 + j], scale=1.0),
                 reads=[pb, Bc], writes=[Bc32])
        rms_norm(2, 256.0)
        pt, pb = ps.next()
        for j in range(2):
            P.op("pe", lambda e, j=j, pt=pt: e.matmul(pt[:], lhsT=wukv16[:, j, 0:128], rhs=cn16[:, j, :], start=(j == 0), stop=(j == 1)),
                 reads=[Bwu, Bcn], writes=[pb])
        P.op("act", lambda e, pt=pt, c0=c0: e.activation(out=K1T[:, c0:c0 + TT], in_=pt[:], func=AF.Copy), reads=[pb], writes=[BK])
        pt, pb = ps.next()
        for i in range(4):
            for j in range(2):
                P.op("pe", lambda e, i=i, j=j, pt=pt: e.matmul(pt[:, i * 128:(i + 1) * 128], lhsT=cn16[:, j, i * 128:(i + 1) * 128], rhs=wukv16[:, j, 128:256],
                                                          start=(j == 0), stop=(j == 1)), reads=[Bwu, Bcn], writes=[pb])
        P.op("act", lambda e, pt=pt, tt=tt: e.activation(out=Vtok[:, tt * 4:(tt + 1) * 4, :], in_=pt[:].rearrange("p (a b) -> p a b", a=4), func=AF.Copy),
             reads=[pb], writes=[BV])
        pA, pbA = inproj(21 * 128, 64)
        pB, pbB = inproj(21 * 128 + 64, 64)
        P.op("dve", lambda e, pA=pA: e.scalar_tensor_tensor(out=t1[:], in0=pA[0:64, :], scalar=bAs[0:64, 21:22], in1=cs[:], op0=ALU.add, op1=ALU.mult),
             reads=[pbA, Bcs, Bc], writes=[Bt1])
        P.op("dve", lambda e, pB=pB: e.scalar_tensor_tensor(out=t2[:], in0=pB[0:64, :], scalar=bAs[0:64, 22:23], in1=sn[:], op0=ALU.add, op1=ALU.mult),
             reads=[pbB, Bsn, Bc], writes=[Bt2])
        P.op("pool", lambda e, c0=c0: e.tensor_tensor(out=K2T[:, c0:c0 + TT], in0=t1[:], in1=t2[:], op=ALU.add), reads=[Bt1, Bt2], writes=[BK])
        if "dil" in phases:
            for g in range(3):
                d = DILS[g]
                for t in range(3):
                    blk = 7 + g * 3 + t
                    pt, pb = inproj(blk * 128, 128)
                    s = ddi % 2
                    ddi += 1
                    P.op("act", lambda e, pt=pt, s=s, d=d, blk=blk, t=t: e.activation(
                        out=dd16[s][:].rearrange("p (r j) -> p r j", r=d), in_=pt[:].rearrange("p (j r) -> p r j", r=d),
                        func=AF.Identity, bias=bAs[:, blk:blk + 1], scale=(0.125 if t == 0 else 1.0)), reads=[pb, Bc], writes=[Bdd[s]])
                    P.dma("sp", "dd%d" % s, dsub[g][t][:, :, c0 // d:(c0 + TT) // d], dd16[s][:].rearrange("p (r j) -> p r j", r=d),
                          reads=[Bdd[s]], writes=[Bdsub])
        if "hg" in phases:
            for h in range(2):
                pt, pb = inproj(h * 128, 128)
                P.op("act", lambda e, pt=pt, h=h: e.activation(out=qs[h][:], in_=pt[:], func=AF.Silu, bias=bAs[:, h:h + 1], scale=1.0),
                     reads=[pb, Bc], writes=[Bqs[h]])
            pt, pb = inproj(6 * 128, 128)
            P.op("act", lambda e, pt=pt: e.activation(out=hv16[:], in_=pt[:], func=AF.Identity, bias=bAs[:, 6:7], scale=1.0), reads=[pb, Bc], writes=[Bhv16])
            P.dma("sp", "hv16", hv_d[:, c0:c0 + TT], hv16[:], reads=[Bhv16], writes=[Bhqk])
            for dr_ in range(2):
                for h in range(2):
                    idx = h * 2 + dr_
                    blk = 2 + dr_ * 2 + h
                    pt, pb = inproj(blk * 128, 128)
                    P.op("act", lambda e, pt=pt, blk=blk: e.activation(out=sig[:], in_=pt[:], func=AF.Sigmoid, bias=bAs[:, blk:blk + 1], scale=1.0),
                         reads=[pb, Bc], writes=[Bsig])
                    P.op("dve", lambda e, idx=idx: e.tensor_scalar(out=ff[:], in0=sig[:], scalar1=lbt[:, idx, 1:2], scalar2=lbt[:, idx, 0:1], op0=ALU.mult, op1=ALU.add),
                         reads=[Bsig, Bc], writes=[Bff])
                    P.op("act", lambda e: e.activation(out=ff[:], in_=ff[:], func=AF.Ln), reads=[Bff], writes=[Bff])
                    P.op("pool", lambda e: e.tensor_scalar(out=ff[:], in0=ff[:], scalar1=LN_MINF, scalar2=None, op0=ALU.max), reads=[Bff], writes=[Bff])
                    if dr_ == 0:
                        P.op("dve", lambda e: e.tensor_tensor_scan(out=bb[:], data0=smask[:], data1=ff[:], initial=0.0, op0=ALU.mult, op1=ALU.add),
                             reads=[Bff, Bc], writes=[Bbb])
                        mcol, bcol = 31, 63
                    else:
                        P.op("dve", lambda e: e.tensor_tensor_scan(out=bb[:, ::-1], data0=smask[:], data1=ff[:, ::-1], initial=0.0, op0=ALU.mult, op1=ALU.add),
                             reads=[Bff, Bc], writes=[Bbb])
                        mcol, bcol = 32, 0
                    b3 = bb[:].rearrange("p (c t) -> p c t", t=64)
                    P.op("pool", lambda e, h=h, dr_=dr_, tt=tt, b3=b3, mcol=mcol: e.tensor_copy(out=mT[h][dr_][:, tt * 8:(tt + 1) * 8], in_=b3[:, :, mcol]),
                         reads=[Bbb], writes=[BmB])
                    P.op("pool", lambda e, h=h, dr_=dr_, tt=tt, b3=b3, bcol=bcol: e.tensor_copy(out=BTt[h][dr_][:, tt * 8:(tt + 1) * 8], in_=b3[:, :, bcol]),
                         reads=[Bbb], writes=[BmB])
                    mb = b3[:, :, mcol:mcol + 1]
                    mbc = bass.AP(mb.tensor, mb.offset, [list(mb.ap[0]), list(mb.ap[1]), [0, 64]])
                    P.op("dve", lambda e, b3=b3, mbc=mbc: e.tensor_tensor(out=eq[:].rearrange("p (c t) -> p c t", t=64), in0=b3, in1=mbc, op=ALU.subtract),
                         reads=[Bbb], writes=[Beq])
                    P.op("act", lambda e: e.activation(out=ek[:], in_=eq[:], func=AF.Exp, scale=-1.0), reads=[Beq], writes=[Bek])
                    P.op("act", lambda e: e.activation(out=eq[:], in_=eq[:], func=AF.Exp), reads=[Beq], writes=[Beq])
                    s = hi % 2
                    hi += 1
                    P.op("dve", lambda e, s=s, h=h: e.tensor_tensor(out=hq16[s][:], in0=qs[h][:], in1=eq[:], op=ALU.mult), reads=[Bqs[h], Beq], writes=[Bhq16[s]])
                    P.dma("sp", "hq16_%d" % s, hq_d[h][dr_][:, c0:c0 + TT], hq16[s][:], reads=[Bhq16[s]], writes=[Bhqk])
                    P.op("dve", lambda e, idx=idx: e.tensor_scalar(out=kk[:], in0=sig[:], scalar1=lbt[:, idx, 2:3], scalar2=lbt[:, idx, 1:2], op0=ALU.mult, op1=ALU.add),
                         reads=[Bsig, Bc], writes=[Bkk])
                    P.op("pool", lambda e, s=s: e.tensor_tensor(out=hk16[s][:], in0=kk[:], in1=ek[:], op=ALU.mult), reads=[Bkk, Bek], writes=[Bhk16[s]])
                    P.dma("sp", "hk16_%d" % s, hk_d[h][dr_][:, c0:c0 + TT], hk16[s][:], reads=[Bhk16[s]], writes=[Bhqk])

    P.barrier()
    es1.close()
    es2 = ExitStack()
    sb = lambda n, s, dt=F32: es2.enter_context(nc.sbuf_tensor("%s_A%d" % (n, layer), s, dt))
    if "mla" in phases:
        pT = [sb("pT%d" % i, [128, TT], BF16) for i in range(4)]; BpT = [Buf("pT%d" % i) for i in range(4)]
        dacc = sb("dacc", [128, TT]); Bdacc = Buf("dacc")
        rden = sb("rden", [128, TT]); Brden = Buf("rden")
        oc = sb("oc", [128, TT], BF16); Boc = Buf("oc")
        po_t = nc.alloc_psum_tensor("po_mla", [128, 512], F32) if False else None
        pi = 0
        Q1 = [sb("Q1_%d" % i, [128, TT], BF16) for i in range(2)]; Q2 = [sb("Q2_%d" % i, [64, TT], BF16) for i in range(2)]
        BQs = [Buf("Qs%d" % i) for i in range(2)]
        for qt in range(NT):
            q0 = qt * TT
            qi = qt % 2
            P.dma("sp", "Q1_%d" % qi, Q1[qi][:], qd1[:, q0:q0 + TT], reads=[Bqd], writes=[BQs[qi]])
            P.dma("sp", "Q2_%d" % qi, Q2[qi][:], qd2[:, q0:q0 + TT], reads=[Bqd], writes=[BQs[qi]])
            BQ = BQs[qi]
            po, pbo = ps.next()

            def mla_a(kb, qi=qi, BQ=BQ, po=po):
                nonlocal pi
                k0 = kb * 128
                pt, pb = ps.next()
                if pt is po:
                    pt, pb = ps.next()
                P.op("pe", lambda e, pt=pt, k0=k0, qi=qi: e.matmul(pt[:], lhsT=K1T[:, k0:k0 + 128], rhs=Q1[qi][:], start=True, stop=False),
                     reads=[BK, BQ], writes=[pb])
                P.op("pe", lambda e, pt=pt, k0=k0, qi=qi: e.matmul(pt[:], lhsT=K2T[:, k0:k0 + 128], rhs=Q2[qi][:], start=False, stop=True),
                     reads=[BK, BQ], writes=[pb])
                s = pi % 4
                pi += 1
                P.op("act", lambda e, pt=pt, s=s: e.activation(out=pT[s][:], in_=pt[:], func=AF.Exp), reads=[pb], writes=[BpT[s]])
                return s

            def mla_b(kb, s, po=po, pbo=pbo):
                P.op("pe", lambda e, po=po, kb=kb, s=s: e.matmul(po[:], lhsT=Vtok[:, kb, :], rhs=pT[s][:], start=(kb == 0), stop=(kb == S // 128 - 1)),
                     reads=[BV, BpT[s]], writes=[pbo])
                if kb == 0:
                    P.op("dve", lambda e, s=s: e.tensor_copy(out=dacc[:], in_=pT[s][:]), reads=[BpT[s]], writes=[Bdacc])
                else:
                    P.op("dve", lambda e, s=s: e.tensor_tensor(out=dacc[:], in0=dacc[:], in1=pT[s][:], op=ALU.add), reads=[BpT[s], Bdacc], writes=[Bdacc])

            pend = []
            for kb in range(S // 128):
                pend.append((kb, mla_a(kb)))
                if len(pend) > 2:
                    mla_b(*pend.pop(0))
            while pend:
                mla_b(*pend.pop(0))
            pd, pbd = ps.next()
            if pd is po:
                pd, pbd = ps.next()
            P.op("pe", lambda e, pd=pd: e.matmul(pd[:], lhsT=ones32[:], rhs=dacc[:], start=True, stop=True), reads=[Bc, Bdacc], writes=[pbd])
            P.op("dve", lambda e, pd=pd: e.reciprocal(out=rden[:], in_=pd[:]), reads=[pbd], writes=[Brden])
            P.op("dve", lambda e, po=po: e.tensor_tensor(out=oc[:], in0=po[:], in1=rden[:], op=ALU.mult), reads=[pbo, Brden], writes=[Boc])
            P.dma("sp", "oc", osrc[(q0 // 2048) * 384 + 256:(q0 // 2048) * 384 + 384, (q0 % 2048):(q0 % 2048) + TT], oc[:], reads=[Boc], writes=[BoT])


    P.barrier()
    es2.close()
    es2 = ExitStack()
    sb = lambda n, s, dt=F32: es2.enter_context(nc.sbuf_tensor("%s_A%d" % (n, layer), s, dt))
    if "dil" in phases:
        RG = 2048
        ets = sb("ets", [128, 18, 256]); Bets = Buf("ets")
        P.dma("sp", "ets", ets[:], etab, writes=[Bets])
        NSET = 2
        Qs_ = [sb("Qs%d" % i, [128, RG], BF16) for i in range(NSET)]; Ks_ = [sb("Ks%d" % i, [128, RG + 128], BF16) for i in range(NSET)]
        Vs_ = [sb("Vs%d" % i, [128, RG + 128], BF16) for i in range(NSET)]
        BQs_l = [Buf("Qs%d" % i) for i in range(NSET)]; BKs_l = [Buf("Ks%d" % i) for i in range(NSET)]; BVs_l = [Buf("Vs%d" % i) for i in range(NSET)]
        Vp_ = [sb("Vp%d" % i, [128, 17, 2, 65], BF16) for i in range(NSET)]; BVp_l = [[Buf("Vp%d_%d" % (i, j)) for j in range(17)] for i in range(NSET)]
        NU = 4
        pe32 = [sb("pe32_%d" % i, [128, 256]) for i in range(NU)]; Bpe = [Buf("pe32_%d" % i) for i in range(NU)]
        pt16 = [sb("pt16_%d" % i, [128, 256], BF16) for i in range(NU)]; Bpt16 = [Buf("pt16_%d" % i) for i in range(NU)]
        acc = [sb("dacc%d" % h, [65, RG]) for h in range(2)]; Bacc = [Buf("dacc%d" % h) for h in range(2)]
        obd = sb("obd", [64, 512], BF16); Bobd = Buf("obd")
        for i in range(NSET):
            P.op("pool", lambda e, i=i: e.memset(Vp_[i][:], 1.0), writes=BVp_l[i])
        ui = 0
        si = 0
        for rg in range(S // RG):
            R0 = rg * RG
            for h in range(2):
                P.op("pool", lambda e, h=h: e.memset(acc[h][:], 0.0), writes=[Bacc[h]])
            for g in range(3):
                d = DILS[g]
                J = S // d
                nj = RG // d
                nb = nj // 128
                j0 = R0 // d
                for r in range(d):
                    ss = si % NSET
                    si += 1
                    Qs, Ks, Vs, Vp = Qs_[ss], Ks_[ss], Vs_[ss], Vp_[ss]
                    BQs_, BKs, BVs, BVp = BQs_l[ss], BKs_l[ss], BVs_l[ss], BVp_l[ss]
                    lo = j0 - 64
                    hi_ = j0 + nj + 64
                    clo = max(lo, 0)
                    chi = min(hi_, J)
                    if lo < 0:
                        P.op("pool", lambda e, Ks=Ks: e.memset(Ks[:, 0:64], 0.0), writes=[BKs])
                        P.op("pool", lambda e, Vs=Vs: e.memset(Vs[:, 0:64], 0.0), writes=[BVs])
                    if hi_ > J:
                        P.op("pool", lambda e, nj=nj, Ks=Ks: e.memset(Ks[:, nj + 64:nj + 128], 0.0), writes=[BKs])
                        P.op("pool", lambda e, nj=nj, Vs=Vs: e.memset(Vs[:, nj + 64:nj + 128], 0.0), writes=[BVs])
                    P.dma("sp", "Qs%d" % ss, Qs[:, 0:nj], dsub[g][0][:, r, j0:j0 + nj], reads=[Bdsub], writes=[BQs_])
                    P.dma("sp", "Ks%d" % ss, Ks[:, clo - lo:chi - lo], dsub[g][1][:, r, clo:chi], reads=[Bdsub], writes=[BKs])
                    P.dma("sp", "Vs%d" % ss, Vs[:, clo - lo:chi - lo], dsub[g][2][:, r, clo:chi], reads=[Bdsub], writes=[BVs])
                    for n in range(nb + 1):
                        ptr, pbr = ps.next()
                        ptr16 = ptr[:].bitcast(BF16)
                        P.op("pe", lambda e, n=n, ptr16=ptr16, Vs=Vs: e.transpose(out=ptr16[:, 0:128], in_=Vs[:, n * 128:(n + 1) * 128], identity=id16[:]),
                             reads=[BVs, Bc], writes=[pbr])
                        P.op("act", lambda e, n=n, ptr16=ptr16, Vp=Vp: e.activation(out=Vp[:, n, :, 0:64], in_=ptr16[:, 0:128].rearrange("p (h v) -> p h v", h=2), func=AF.Copy),
                             reads=[pbr], writes=[BVp[n]])

                    def dil_a(qb, h, Qs=Qs, Ks=Ks, BQs_=BQs_, BKs=BKs, g=g, j0=j0, J=J):
                        nonlocal ui
                        jb = j0 + qb * 128
                        var = 1 if jb == 0 else (2 if jb + 128 == J else 0)
                        u = ui % NU
                        ui += 1
                        pt, pb = ps.next()
                        for kc in range(2):
                            P.op("pe", lambda e, pt=pt, kc=kc, qb=qb, h=h: e.matmul(pt[:, kc * 128:(kc + 1) * 128],
                                 lhsT=Ks[h * 64:(h + 1) * 64, (qb + kc) * 128:(qb + kc + 1) * 128], rhs=Qs[h * 64:(h + 1) * 64, qb * 128:(qb + 1) * 128],
                                 start=True, stop=True), reads=[BKs, BQs_], writes=[pb])
                        P.op("act", lambda e, pt=pt, u=u: e.activation(out=pe32[u][:], in_=pt[:, 0:256], func=AF.Exp), reads=[pb], writes=[Bpe[u]])
                        ei = (g * 2 + h) * 3 + var
                        P.op("dve", lambda e, u=u, ei=ei: e.tensor_tensor(out=pt16[u][:], in0=pe32[u][:], in1=ets[:, ei, :], op=ALU.mult),
                             reads=[Bpe[u], Bets], writes=[Bpt16[u]])
                        return (qb, h, u)

                    def dil_b(qb, h, u, Vp=Vp, BVp=BVp, d=d, r=r):
                        po, pbo = ps.next()
                        for kc in range(2):
                            P.op("pe", lambda e, po=po, kc=kc, qb=qb, h=h, u=u: e.matmul(po[0:65, 0:128], lhsT=Vp[:, qb + kc, h, :], rhs=pt16[u][:, kc * 128:(kc + 1) * 128],
                                 start=(kc == 0), stop=(kc == 1)), reads=[BVp[qb + kc], Bpt16[u]], writes=[pbo])
                        st = qb * 128 * d + r
                        av = acc[h][:, st:st + 127 * d + 1:d]
                        P.op("dve", lambda e, po=po, av=av: e.tensor_tensor(out=av, in0=av, in1=po[0:65, 0:128], op=ALU.add), reads=[pbo, Bacc[h]], writes=[Bacc[h]])

                    pend = []
                    for qb in range(nb):
                        for h in range(2):
                            pend.append(dil_a(qb, h))
                            if len(pend) > 2:
                                dil_b(*pend.pop(0))
                    while pend:
                        dil_b(*pend.pop(0))
            for h in range(2):
                P.op("dve", lambda e, h=h: e.reciprocal(out=acc[h][64:65, :], in_=acc[h][64:65, :]), reads=[Bacc[h]], writes=[Bacc[h]])
                for cc in range(RG // 512):
                    pt, pb = ps.next()
                    P.op("pe", lambda e, pt=pt, h=h, cc=cc: e.matmul(pt[0:64, :], lhsT=ones32[64:65, 0:64], rhs=acc[h][64:65, cc * 512:(cc + 1) * 512], start=True, stop=True),
                         reads=[Bc, Bacc[h]], writes=[pb])
                    P.op("dve", lambda e, pt=pt, h=h, cc=cc: e.tensor_tensor(out=obd[:], in0=acc[h][0:64, cc * 512:(cc + 1) * 512], in1=pt[0:64, :], op=ALU.mult),
                         reads=[pb, Bacc[h]], writes=[Bobd])
                    P.dma("sp", "obd", osrc[rg * 384 + 128 + h * 64:rg * 384 + 128 + (h + 1) * 64, cc * 512:(cc + 1) * 512], obd[:], reads=[Bobd], writes=[BoT])

    P.barrier()
    es2.close()
    es2 = ExitStack()
    sb = lambda n, s, dt=F32: es2.enter_context(nc.sbuf_tensor("%s_A%d" % (n, layer), s, dt))
    if "hg" in phases:
        oac = [sb("oac%d" % h, [64, S]) for h in range(2)]; Boac = [Buf("oac%d" % h) for h in range(2)]
        for h in range(2):
            P.op("pool", lambda e, h=h: e.memset(oac[h][:], 0.0), writes=[Boac[h]])
        gam = [[sb("gam%d%d" % (h, d), [128, 128]) for d in range(2)] for h in range(2)]; Bgam = Buf("gam")
        for h in range(2):
            for d in range(2):
                if d == 0:
                    dst, mn, bc, mc = gam[h][d][:, 0:127], mT[h][d][:, 1:128], BTt[h][d][:, 0:127], mT[h][d][:, 0:127]
                else:
                    dst, mn, bc, mc = gam[h][d][:, 1:128], mT[h][d][:, 0:127], BTt[h][d][:, 1:128], mT[h][d][:, 1:128]
                P.op("dve", lambda e, dst=dst, mn=mn, bc=bc: e.tensor_tensor(out=dst, in0=mn, in1=bc, op=ALU.add), reads=[BmB], writes=[Bgam])
                P.op("dve", lambda e, dst=dst, mc=mc: e.tensor_tensor(out=dst, in0=dst, in1=mc, op=ALU.subtract), reads=[BmB, Bgam], writes=[Bgam])
                P.op("act", lambda e, dst=dst: e.activation(out=dst, in_=dst, func=AF.Exp), reads=[Bgam], writes=[Bgam])
        ch = [(h, d) for d in range(2) for h in range(2)]
        qT = {c: sb("hqT%d%d" % c, [128, TT], BF16) for c in ch}; kT = {c: sb("hkT%d%d" % c, [128, TT], BF16) for c in ch}
        vT = {c: sb("hvT%d%d" % c, [128, TT], BF16) for c in ch}
        Bld = {c: Buf("hld%d%d" % c) for c in ch}
        kvtok = {c: sb("kvtok%d%d" % c, [128, 256], BF16) for c in ch}; Bkv = {c: Buf("kvtok%d%d" % c) for c in ch}
        at16 = {c: sb("at16%d%d" % c, [128, 128], BF16) for c in ch}; Bat = {c: Buf("at%d%d" % c) for c in ch}
        S32 = {c: sb("S32%d%d" % c, [128, 64]) for c in ch}; S16 = {c: sb("S16%d%d" % c, [128, 64], BF16) for c in ch}
        Sh = {c: sb("Sh%d%d" % c, [128, 64]) for c in ch}
        BS32 = {c: Buf("S32%d%d" % c) for c in ch}; BS16 = {c: Buf("S16%d%d" % c) for c in ch}; BSh = {c: Buf("Sh%d%d" % c) for c in ch}
        for c in ch:
            P.op("pool", lambda e, c=c: e.memset(S32[c][:], 0.0), writes=[BS32[c]])
            P.op("pool", lambda e, c=c: e.memset(S16[c][:], 0.0), writes=[BS16[c]])
        for step in range(NT):
            for c in ch:
                h, d = c
                ti = step if d == 0 else NT - 1 - step
                c0 = ti * TT
                P.dma("sp", "hl%d%d" % c, qT[c][:], hq_d[h][d][:, c0:c0 + TT], reads=[Bhqk], writes=[Bld[c]])
                P.dma("sp", "hl%d%d" % c, kT[c][:], hk_d[h][d][:, c0:c0 + TT], reads=[Bhqk], writes=[Bld[c]])
                P.dma("sp", "hl%d%d" % c, vT[c][:], hv_d[:, c0:c0 + TT], reads=[Bhqk], writes=[Bld[c]])
            for pp in range(4):
                inf = {}
                for c in ch:
                    h, d = c
                    ti = step if d == 0 else NT - 1 - step
                    pr = pp if d == 0 else 3 - pp
                    inf[c] = (h, d, ti, pr, pr * 128)
                trb = {}
                for ci, c in enumerate(ch):
                    h, d, ti, pr, p0 = inf[c]
                    bank, bb_ = ps.t[ci], ps.b[ci]
                    b16 = bank[:].bitcast(BF16)
                    P.op("pe", lambda e, c=c, p0=p0, b16=b16: e.transpose(out=b16[:, 0:128], in_=kT[c][:, p0:p0 + 128], identity=id16[:]),
                         reads=[Bld[c], Bc], writes=[bb_])
                    P.op("pe", lambda e, c=c, p0=p0, b16=b16: e.transpose(out=b16[:, 128:256], in_=vT[c][:, p0:p0 + 128], identity=id16[:]),
                         reads=[Bld[c], Bc], writes=[bb_])
                    trb[c] = (b16, bb_)
                for c in ch:
                    b16, bb_ = trb[c]
                    P.op("act", lambda e, c=c, b16=b16: e.activation(out=kvtok[c][:], in_=b16[:, 0:256], func=AF.Copy), reads=[bb_], writes=[Bkv[c]])
                pab = {}
                for ci, c in enumerate(ch):
                    h, d, ti, pr, p0 = inf[c]
                    pa, pba = ps.t[ci], ps.b[ci]
                    P.op("pe", lambda e, c=c, p0=p0, pa=pa: e.matmul(pa[:, 0:128], lhsT=kT[c][:, p0:p0 + 128], rhs=qT[c][:, p0:p0 + 128], start=True, stop=True),
                         reads=[Bld[c]], writes=[pba])
                    pab[c] = (pa, pba)
                for c in ch:
                    h, d, ti, pr, p0 = inf[c]
                    pa, pba = pab[c]
                    P.op("dve", lambda e, c=c, d=d, pa=pa: e.tensor_tensor(out=at16[c][:], in0=pa[:, 0:128], in1=msk[:, d, :], op=ALU.mult),
                         reads=[pba, Bc], writes=[Bat[c]])
                for cc in range(2):
                    pub = {}
                    for ci, c in enumerate(ch):
                        h, d, ti, pr, p0 = inf[c]
                        po, pbo = ps.t[4 + ci], ps.b[4 + ci]
                        ck = cc if d == 0 else 1 - cc
                        q0 = p0 + ck * 64
                        if cc == 0:
                            P.op("pe", lambda e, c=c, h=h, po=po: e.matmul(po[0:64, 0:128], lhsT=kvtok[c][:, 128 + h * 64:128 + (h + 1) * 64], rhs=at16[c][:], start=True, stop=False),
                                 reads=[Bkv[c], Bat[c]], writes=[pbo])
                        P.op("pe", lambda e, c=c, po=po, ck=ck, q0=q0, cc=cc: e.matmul(po[0:64, ck * 64:(ck + 1) * 64], lhsT=S16[c][:], rhs=qT[c][:, q0:q0 + 64],
                             start=False, stop=(cc == 1)), reads=[BS16[c], Bld[c]], writes=[pbo])
                        pu, pbu = ps.t[ci], ps.b[ci]
                        P.op("pe", lambda e, c=c, h=h, pu=pu, ck=ck: e.matmul(pu[:, 0:64], lhsT=kvtok[c][ck * 64:(ck + 1) * 64, 0:128],
                             rhs=kvtok[c][ck * 64:(ck + 1) * 64, 128 + h * 64:128 + (h + 1) * 64], start=True, stop=True), reads=[Bkv[c]], writes=[pbu])
                        pub[c] = (pu, pbu, ti * 8 + pr * 2 + ck)
                    for c in ch:
                        pu, pbu, cidx = pub[c]
                        P.op("dve", lambda e, c=c, pu=pu: e.tensor_tensor(out=Sh[c][:], in0=pu[:, 0:64], in1=S32[c][:], op=ALU.add), reads=[pbu, BS32[c]], writes=[BSh[c]])
                    for c in ch:
                        h, d = c
                        pu, pbu, cidx = pub[c]
                        P.op("dve", lambda e, c=c, h=h, d=d, cidx=cidx: e.tensor_scalar(out=S32[c][:], in0=Sh[c][:], scalar1=gam[h][d][:, cidx:cidx + 1], scalar2=None, op0=ALU.mult),
                             reads=[BSh[c], Bgam], writes=[BS32[c]])
                        P.op("act", lambda e, c=c, h=h, d=d, cidx=cidx: e.activation(out=S16[c][:], in_=Sh[c][:], func=AF.Copy, scale=gam[h][d][:, cidx:cidx + 1]),
                             reads=[BSh[c], Bgam], writes=[BS16[c]])
                for ci, c in enumerate(ch):
                    h, d, ti, pr, p0 = inf[c]
                    po, pbo = ps.t[4 + ci], ps.b[4 + ci]
                    t0_ = ti * TT + p0
                    P.op("dve", lambda e, h=h, po=po, t0_=t0_: e.tensor_tensor(out=oac[h][:, t0_:t0_ + 128], in0=oac[h][:, t0_:t0_ + 128], in1=po[0:64, 0:128], op=ALU.add),
                         reads=[pbo, Boac[h]], writes=[Boac[h]])
        o16c = [sb("o16c%d" % i, [64, 2048], BF16) for i in range(2)]; Bo16c = [Buf("o16c%d" % i) for i in range(2)]
        for h in range(2):
            for tq in range(4):
                u = (h * 4 + tq) % 2
                P.op("act" if u == 0 else "dve", (lambda e, h=h, tq=tq, u=u: e.activation(out=o16c[u][:], in_=oac[h][:, tq * 2048:(tq + 1) * 2048], func=AF.Copy)) if u == 0 else
                     (lambda e, h=h, tq=tq, u=u: e.tensor_copy(out=o16c[u][:], in_=oac[h][:, tq * 2048:(tq + 1) * 2048])), reads=[Boac[h]], writes=[Bo16c[u]])
                P.dma("sp", "o16c%d" % u, osrc[tq * 384 + h * 64:tq * 384 + (h + 1) * 64, :], o16c[u][:], reads=[Bo16c[u]], writes=[BoT])

    P.barrier()
    es2.close()
    es0.close()


RG4 = [[0, 1, 2, 3], [4, 5, 6, 7]]


def build_fused():
    nc = bass.Bass("TRN2", target_bir_lowering=False)
    dr = lambda n, s, kind="ExternalInput", dt=F32: nc.dram_tensor(n, s, dt, kind=kind).ap()
    shared = {"pos": dr("pos", [1, S], dt=I32), "etab": dr("etab", [128, 18, 256]), "ropec": dr("ropec", [64, 2]),
              "ident": dr("ident", [128, 128]), "masks": dr("masks", [128, 2, 128]), "scanmask": dr("scanmask", [128, 512]),
              "lbraw": dr("lbraw", [128, 4, 2])}
    xT = dr("xT", [1024, S]); xTq = dr("xTq", [1024, 2048]); oidx = dr("oidx", [128, 96], dt=I32)
    ioA, ioB = [], []
    for l in range(2):
        a = dict(shared)
        a.update({"wA": dr("wA%d" % l, [1024, 2816]), "bA": dr("bA%d" % l, [128, 23]), "wuq": dr("wuq%d" % l, [384, 256]), "gq": dr("gq%d" % l, [128, 3]),
                  "wukv": dr("wukv%d" % l, [256, 256]), "gkv": dr("gkv%d" % l, [128, 2])})
        ioA.append(a)
        ioB.append({"wg": dr("wg%d" % l, [1024, 4608]), "bg": dr("bg%d" % l, [128, 36]), "wbr": dr("wbr%d" % l, [1536, 1024]), "wo": dr("wo%d" % l, [1024, 1024]),
                    "hgn": dr("hgn%d" % l, [128, 4]), "lng": dr("lng%d" % l, [128, 8]), "lnb": dr("lnb%d" % l, [128, 8]), "oidx": oidx})
    outT = dr("outT", [1024, 2048], kind="ExternalOutput")
    cco_src = [nc.dram_tensor("cco_src%d" % l, [1536, 2048], BF16) for l in range(2)]
    cco_dst = [nc.dram_tensor("cco_dst%d" % l, [6 * 1024, 2048], BF16) for l in range(2)]
    ccx_src = nc.dram_tensor("ccx_src", [1024, 2048], BF16)
    ccx_dst = nc.dram_tensor("ccx_dst", [4096, 2048], BF16)
    xn32 = nc.dram_tensor("xn32", [1024, 2048], F32).ap()
    scr = make_scratch(nc)
    P = Prog(nc)
    ps = PsumPool(nc)
    Bxn = Buf("xn32"); Bxg = Buf("xg"); Bnone = Buf("none")
    for l in range(2):
        ioA[l]["osrc"] = cco_src[l].ap()
        if l == 0:
            ioA[l]["xT"] = xT
            emit_A(nc, P, ps, l, ioA[l], scr)
        else:
            ioA[l]["Bxg"] = Bxg
            emit_A(nc, P, ps, l, ioA[l], scr, xsrc16=ccx_dst.ap())
        P.barrier()
        Bod = Buf("cco_dst%d" % l)
        for k in range(6):
            P.dma("pool", "cc_o%d" % l, None, None, reads=[Bnone], writes=[Bod], inc=1,
                  fn=(lambda e, l=l, k=k: e.collective_compute("AllGather", ALU.bypass, replica_groups=RG4,
                                                             ins=[cco_src[l].ap()[k * 256:(k + 1) * 256, :].opt()],
                                                             outs=[cco_dst[l].ap()[k * 1024:(k + 1) * 1024, :].opt()])))
        b = ioB[l]
        b["orows"] = cco_dst[l].ap().rearrange("r (a c) -> (r a) c", c=256)
        b["Bodst"] = Bod
        if l == 0:
            b["x32src"] = xTq; b["Bxsrc"] = Bnone; b["out32"] = xn32; b["out16"] = ccx_src.ap()
        else:
            b["x32src"] = xn32; b["Bxsrc"] = Bxn; b["out32"] = outT
        emit_B(nc, P, ps, l, b)
        P.barrier()
        if l == 0:
            for k in range(4):
                P.dma("pool", "cc_x", None, None, reads=[Bnone], writes=[Bxg], inc=1,
                      fn=(lambda e, k=k: e.collective_compute("AllGather", ALU.bypass, replica_groups=RG4,
                                                            ins=[ccx_src.ap()[k * 256:(k + 1) * 256, :].opt()],
                                                            outs=[ccx_dst.ap()[k * 1024:(k + 1) * 1024, :].opt()])))
    P.barrier()
    P.emit()
    return nc


SPL = [1024,1024,1024,512,512] + [512]*10 + [384,256,64,512,3072]
NAMES = ['hg_q','hg_f_fwd','hg_f_bwd','hg_i','hg_g','dil_q0','dil_k0','dil_v0','dil_q1','dil_k1','dil_v1','dil_q2','dil_k2','dil_v2','dil_g','mla_cq','mla_ckv','mla_kr','mla_g','merge']
OFF = dict(zip(NAMES, [int(v) for v in np.cumsum([0]+SPL[:-1])]))
def a_cols(hq):
    ar = np.arange
    c = []
    c += [OFF['hg_q'] + (2*hq)*128 + ar(128), OFF['hg_q'] + (2*hq+1)*128 + ar(128)]
    c += [OFF['hg_f_fwd'] + (2*hq)*128 + ar(128), OFF['hg_f_fwd'] + (2*hq+1)*128 + ar(128)]
    c += [OFF['hg_f_bwd'] + (2*hq)*128 + ar(128), OFF['hg_f_bwd'] + (2*hq+1)*128 + ar(128)]
    c += [OFF['hg_i'] + hq*128 + ar(128)]
    for g in range(3):
        for t in 'qkv':
            c += [OFF['dil_%s%d' % (t, g)] + hq*128 + ar(128)]
    c += [OFF['mla_cq'] + ar(384), OFF['mla_ckv'] + ar(256)]
    kr = OFF['mla_kr'] + ar(64)
    c += [kr, np.concatenate([kr[32:], kr[:32]])]
    return np.concatenate(c)
def etab_np(hq):
    slopes = 2.0 ** (-8.0 * (np.arange(24) + 1) / 24)
    kk = np.arange(128)[:, None]; qq = np.arange(128)[None, :]
    E = np.zeros((128, 18, 256), np.float32)
    for g, d in enumerate((1, 4, 16)):
        for hh in range(2):
            sl = slopes[g*8 + 2*hq + hh]
            for var in range(3):
                for kc in range(2):
                    rel = (kk + 128*kc - 64) - qq
                    e = np.where(np.abs(rel) <= 64, np.exp(-sl * d * np.abs(rel)), 0.0)
                    if var == 1 and kc == 0: e = np.where(kk < 64, 0.0, e)
                    if var == 2 and kc == 1: e = np.where(kk >= 64, 0.0, e)
                    E[:, (g*2+hh)*3 + var, kc*128:(kc+1)*128] = e
    return E
def a_inputs(inp, l, b, hq, xT_b):
    cols = a_cols(hq)
    w_in = inp['w_in'][l]; b_in = inp['b_in'][l]
    bsel = b_in[cols]
    bA = np.zeros((128, 23), np.float32)
    bA[:, :22] = bsel.reshape(22, 128).T
    bA[:64, 22] = bsel[21*128+64: 22*128]
    lbraw = np.zeros((128, 4, 2), np.float32)
    for h in range(2):
        for d, nm in enumerate(('hg_lb_fwd', 'hg_lb_bwd')):
            lbraw[:, h*2+d, :] = inp[nm][:, (2*hq+h)*128:(2*hq+h+1)*128].T
    wuq = inp['w_uq'][l]
    qc = hq*192 + np.arange(192)
    rope = qc[128:]
    wuq_sel = np.concatenate([wuq[:, qc[:128]], wuq[:, rope], wuq[:, np.concatenate([rope[32:], rope[:32]])]], 1)
    wukv = inp['w_ukv'][l]
    wukv_sel = wukv[:, hq*256:(hq+1)*256]
    inv = (1.0 / (10000.0 ** (np.arange(32, dtype=np.float32) / 32))).astype(np.float32)
    ropec = np.zeros((64, 2), np.float32); ropec[:, 0] = np.concatenate([inv, inv]); ropec[:32, 1] = -1.0; ropec[32:, 1] = 1.0
    masks = np.zeros((128, 2, 128), np.float32)
    ss = np.arange(128)[:, None]; tq = np.arange(128)[None, :]
    same = (ss // 64) == (tq // 64)
    masks[:, 0, :] = (same & (ss <= tq)).astype(np.float32)
    masks[:, 1, :] = (same & (ss >= tq)).astype(np.float32)
    sm = np.ones((128, 512), np.float32); sm[:, ::64] = 0.0
    return {"pos": np.ascontiguousarray(inp['positions'][b:b+1].astype(np.int32)), "wA": np.ascontiguousarray(w_in[:, cols]), "bA": bA, "lbraw": lbraw,
            "wuq": np.ascontiguousarray(wuq_sel), "gq": np.ascontiguousarray(inp['mla_q_norm'][l].reshape(3,128).T),
            "wukv": np.ascontiguousarray(wukv_sel), "gkv": np.ascontiguousarray(inp['mla_kv_norm'][l].reshape(2,128).T),
            "etab": etab_np(hq), "ropec": ropec, "ident": np.eye(128, dtype=np.float32), "masks": masks, "scanmask": sm}


_CACHE = {}


def _b_inputs(inp, l):
    w_in = inp['w_in'][l]; b_in = inp['b_in'][l]
    cols = np.concatenate([np.arange(OFF['hg_g'], OFF['hg_g'] + 512), np.arange(OFF['dil_g'], OFF['dil_g'] + 512),
                           np.arange(OFF['mla_g'], OFF['mla_g'] + 512), np.arange(OFF['merge'], OFF['merge'] + 3072)])
    return {"wg%d" % l: np.ascontiguousarray(w_in[:, cols]), "bg%d" % l: np.ascontiguousarray(b_in[cols].reshape(36, 128).T),
            "wbr%d" % l: np.ascontiguousarray(inp['w_branch'][l].reshape(1536, 1024)), "wo%d" % l: np.ascontiguousarray(inp['w_out'][l]),
            "hgn%d" % l: np.ascontiguousarray(inp['hg_norm'][l].reshape(4, 128).T),
            "lng%d" % l: np.ascontiguousarray(inp['ln_g'][l].reshape(8, 128).T), "lnb%d" % l: np.ascontiguousarray(inp['ln_b'][l].reshape(8, 128).T)}


def _oidx(tq):
    p = np.arange(128)[:, None, None, None]; tt = np.arange(8)[None, :, None, None]
    n = np.arange(3)[None, None, :, None]; r = np.arange(4)[None, None, None, :]
    rho = tq * 384 + n * 128 + p
    g = (rho // 256) * 1024 + r * 256 + (rho % 256)
    v = g * 8 + tt
    return np.ascontiguousarray(v.reshape(128, 96).astype(np.int32))


def kernel(**inputs):
    inp = {k: np.asarray(v) for k, v in inputs.items()}
    inp['positions'] = inp['positions'].astype(np.int32)
    for k in inp:
        if k != 'positions':
            inp[k] = inp[k].astype(np.float32, copy=False)
    B = 2
    xT = [np.ascontiguousarray(inp['x'][b].T) for b in range(B)]
    if "nc" not in _CACHE:
        _CACHE["nc"] = build_fused()
    nc = _CACHE["nc"]
    bl = [_b_inputs(inp, l) for l in range(2)]
    in_maps = []
    for c in range(8):
        b, q = c // 4, c % 4
        m = {"xT": xT[b], "xTq": np.ascontiguousarray(xT[b][:, q * 2048:(q + 1) * 2048]), "oidx": _oidx(q)}
        for l in range(2):
            a = a_inputs(inp, l, b, q, None)
            for k in ("pos", "etab", "ropec", "ident", "masks", "scanmask", "lbraw"):
                m[k] = a[k]
            for k in ("wA", "bA", "wuq", "gq", "wukv", "gkv"):
                m["%s%d" % (k, l)] = a[k]
            m.update(bl[l])
        in_maps.append(m)
    res = run_bass_kernel_spmd(nc, in_maps, core_ids=list(range(8))).results
    out = np.empty((B, 8192, 1024), np.float32)
    for c in range(8):
        b, q = c // 4, c % 4
        out[b, q * 2048:(q + 1) * 2048, :] = np.asarray(res[c]["outT"]).T
    return out
```

```python
import math
from contextlib import ExitStack
import numpy as np
from concourse.bass_utils import run_bass_kernel_spmd
import concourse.bass as bass
import concourse.mybir as mybir

F32 = mybir.dt.float32
BF16 = mybir.dt.bfloat16
I32 = mybir.dt.int32
AF = mybir.ActivationFunctionType
ALU = mybir.AluOpType
AX = mybir.AxisListType


class Buf:
    __slots__ = ("name", "w", "r")

    def __init__(self, name=""):
        self.name = name
        self.w = {}
        self.r = {}


class _Eng:
    def __init__(self, name, sem):
        self.name = name
        self.sem = sem
        self.count = 0
        self.waited = {}
        self.items = []


class Prog:
    ENGS = ("pe", "act", "dve", "pool", "sp")

    def __init__(self, nc):
        self.nc = nc
        self.e = {n: _Eng(n, nc.alloc_semaphore("prog_" + n)) for n in self.ENGS}
        self.chan = {}
        self.nops = 0
        self.retired = []

    def _need(self, eng, waits, ev, raw):
        sem, val, en = ev
        if en == eng.name:
            if eng.name == "pe" or not raw:
                return
        k = id(sem)
        if eng.waited.get(k, 0) >= val:
            return
        if k not in waits or waits[k][1] < val:
            waits[k] = (sem, val)

    def _deps(self, eng, reads, writes):
        waits = {}
        for b in reads:
            for ev in b.w.values():
                self._need(eng, waits, ev, True)
        for b in writes:
            for ev in b.w.values():
                self._need(eng, waits, ev, False)
            for ev in b.r.values():
                self._need(eng, waits, ev, False)
        for k, (sem, val) in waits.items():
            eng.waited[k] = val
        return list(waits.values())

    @staticmethod
    def _mark(ev, reads, writes):
        k = id(ev[0])
        for b in reads:
            o = b.r.get(k)
            if o is None or o[1] < ev[1]:
                b.r[k] = ev
        for b in writes:
            o = b.w.get(k)
            if o is None or o[1] < ev[1]:
                b.w[k] = ev

    def op(self, engname, fn, reads=(), writes=()):
        eng = self.e[engname]
        waits = self._deps(eng, reads, writes)
        if eng.count >= 30000:
            self.retired.append((eng.sem, eng.count))
            eng.sem = self.nc.alloc_semaphore("prog_%s_%d" % (engname, self.nops))
            eng.count = 0
        eng.count += 1
        ev = (eng.sem, eng.count, eng.name)
        eng.items.append((waits, fn, (eng.sem, 1)))
        self._mark(ev, reads, writes)
        self.nops += 1
        return ev

    def dma(self, qname, chan, out, in_, reads=(), writes=(), fn=None, inc=16):
        eng = self.e[qname]
        waits = self._deps(eng, reads, writes)
        if chan not in self.chan:
            self.chan[chan] = [self.nc.alloc_semaphore("ch_" + chan), 0]
        c = self.chan[chan]
        if c[1] >= 30000:
            self.retired.append((c[0], c[1]))
            c[0] = self.nc.alloc_semaphore("ch_%s_%d" % (chan, self.nops))
            c[1] = 0
        c[1] += inc
        ev = (c[0], c[1], "dma")
        if fn is None:
            fn = (lambda e, o=out, i=in_: e.dma_start(out=o, in_=i))
        eng.items.append((waits, fn, (c[0], inc)))
        self._mark(ev, reads, writes)
        self.nops += 1
        return ev

    def barrier(self):
        evs = list(self.retired)
        for n in self.ENGS:
            if self.e[n].count > 0:
                evs.append((self.e[n].sem, self.e[n].count))
        for c in self.chan.values():
            evs.append((c[0], c[1]))
        for n in self.ENGS:
            eng = self.e[n]
            waits = []
            for sem, val in evs:
                if eng.waited.get(id(sem), 0) < val:
                    waits.append((sem, val))
                    eng.waited[id(sem)] = val
            eng.items.append((waits, None, None))

    def wait_all(self, engname, bufs):
        eng = self.e[engname]
        waits = self._deps(eng, bufs, bufs)
        eng.items.append((waits, None, None))

    def emit(self):
        nc = self.nc
        with nc.Block() as block:
            def run(eng, h):
                for waits, fn, inc in eng.items:
                    for sem, val in waits:
                        h.wait_ge(sem, val)
                    if fn is not None:
                        ins = fn(h)
                        ins.then_inc(inc[0], inc[1])

            @block.tensor
            def _(h):
                run(self.e["pe"], h)

            @block.scalar
            def _(h):
                run(self.e["act"], h)

            @block.vector
            def _(h):
                run(self.e["dve"], h)

            @block.gpsimd
            def _(h):
                run(self.e["pool"], h)

            @block.sync
            def _(h):
                run(self.e["sp"], h)


ALPHA = 4.0 ** 0.25
LN_EPS = 1e-5


class PsumPool:
    def __init__(self, nc, n=8):
        self.t = [nc.alloc_psum_tensor("psb%d" % i, [128, 512], F32) for i in range(n)]
        self.b = [Buf("psb%d" % i) for i in range(n)]
        self.i = 0
        self.n = n

    def next(self):
        i = self.i
        self.i = (i + 1) % self.n
        return self.t[i], self.b[i]


def load_cast_weight(P, nc, q, dram2d, dst16, dstbuf, kchunks, ncols, stage, stage_bufs, ctr, colsplit):
    v = dram2d.rearrange("(k p) c -> p k c", p=128)
    for k in range(kchunks):
        for c0 in range(0, ncols, colsplit):
            cw = min(colsplit, ncols - c0)
            s = ctr[0] % len(stage)
            ctr[0] += 1
            P.dma(q, "wst%d" % s, stage[s][:, 0:cw], v[:, k, c0:c0 + cw], writes=[stage_bufs[s]])
            eng = "dve" if (ctr[0] % 2 == 0) else "pool"
            P.op(eng, (lambda e, s=s, k=k, c0=c0, cw=cw: e.tensor_copy(out=dst16[:, k, c0:c0 + cw], in_=stage[s][:, 0:cw])),
                 reads=[stage_bufs[s]], writes=[dstbuf])


def emit_B(nc, P, ps, layer, io):
    T = 2048
    TT = 256
    NT = T // TT
    xT = io["x32src"]; wg = io["wg"]; bg = io["bg"]; wbr = io["wbr"]; wo = io["wo"]; hgn = io["hgn"]; lng = io["lng"]; lnb = io["lnb"]
    orows = io["orows"]; oidx = io["oidx"]; Bodst = io["Bodst"]; Bxsrc = io["Bxsrc"]
    out32 = io["out32"]; out16 = io.get("out16")
    esb = ExitStack()
    sb = lambda n, s, dt=F32: esb.enter_context(nc.sbuf_tensor("%s_B%d" % (n, layer), s, dt))
    wg16 = sb("wg16", [128, 8, 4608], BF16); Bwg = Buf("wg16")
    wbr16 = sb("wbr16", [128, 12, 1024], BF16); Bwbr = Buf("wbr16")
    wo16 = sb("wo16", [128, 8, 1024], BF16); Bwo = Buf("wo16")
    stage = [sb("wstage%d" % i, [128, 1152], F32) for i in range(2)]
    stage_b = [Buf("wstage%d" % i) for i in range(2)]
    bgs = sb("bgs", [128, 36]); hgns = sb("hgns", [128, 4]); lngs = sb("lngs", [128, 8]); lnbs = sb("lnbs", [128, 8])
    oix = sb("oix", [128, 96], I32)
    Bc = Buf("consts")
    ones32 = sb("ones32", [128, 128]); Bones = Buf("ones")
    epsr = sb("epsr", [128, 1]); epsl = sb("epsl", [128, 1])
    x32 = sb("x32", [128, 8, TT]); Bx32 = Buf("x32")
    x16 = sb("x16", [128, 8, TT], BF16); Bx16 = Buf("x16")
    o32 = sb("o16", [128, 12, TT], BF16); Bo32 = Buf("o16")
    y16 = sb("y16", [128, 12, TT], BF16); By16 = [Buf("y16_%d" % i) for i in range(12)]
    gt = [sb("gt%d" % i, [128, TT]) for i in range(2)]; Bgt = [Buf("gt%d" % i) for i in range(2)]
    sq = sb("sq", [128, 8, TT]); Bsq = Buf("sq")
    rstd = sb("rstd", [128, TT]); Brstd = Buf("rstd")
    tmp = sb("tmp", [128, TT]); Btmp = Buf("tmp")
    sg = [sb("sg%d" % i, [128, 3, TT]) for i in range(2)]; Bsg = [Buf("sg%d" % i) for i in range(2)]
    mm = sb("mm", [128, TT]); Bmm = Buf("mm")
    tt2 = sb("tt2", [128, TT]); Btt2 = Buf("tt2")
    mg16 = sb("mg16", [128, 8, TT], BF16); Bmg = [Buf("mg%d" % i) for i in range(8)]
    r32 = sb("r32", [128, 8, TT]); Br = [Buf("r%d" % i) for i in range(8)]
    mean = sb("mean", [128, TT]); Bmean = Buf("mean")
    ob = [sb("ob%d" % i, [128, TT]) for i in range(2)]; Bob = [Buf("ob%d" % i) for i in range(2)]
    ob16 = [sb("ob16_%d" % i, [128, TT], BF16) for i in range(2)]; Bob16 = [Buf("ob16_%d" % i) for i in range(2)]
    Bout = Buf("xnT")
    xnT = out32

    P.dma("sp", "c0", bgs[:], bg, writes=[Bc])
    P.dma("sp", "c4", oix[:], oidx, writes=[Bc])
    P.dma("sp", "c1", hgns[:], hgn, writes=[Bc])
    P.dma("sp", "c2", lngs[:], lng, writes=[Bc])
    P.dma("sp", "c3", lnbs[:], lnb, writes=[Bc])
    P.op("pool", lambda e: e.memset(ones32[:], 1.0), writes=[Bones])
    P.op("pool", lambda e: e.memset(epsr[:], RMS_EPS), writes=[Bc])
    P.op("pool", lambda e: e.memset(epsl[:], LN_EPS), writes=[Bc])
    ctr = [0]
    load_cast_weight(P, nc, "sp", wg, wg16, Bwg, 8, 4608, stage, stage_b, ctr, 1152)
    load_cast_weight(P, nc, "sp", wbr, wbr16, Bwbr, 12, 1024, stage, stage_b, ctr, 1024)
    load_cast_weight(P, nc, "sp", wo, wo16, Bwo, 8, 1024, stage, stage_b, ctr, 1024)

    xv = xT.rearrange("(k p) t -> p k t", p=128)
    outv = xnT.rearrange("(k p) t -> p k t", p=128)
    gi = 0
    for tt in range(NT):
        c0 = tt * TT
        P.dma("act", "x32", x32[:], xv[:, :, c0:c0 + TT], reads=[Bxsrc], writes=[Bx32])
        for blk in range(12):
            P.dma("pool", "o16g", None, None, reads=[Bodst, Bc], writes=[Bo32],
                  fn=(lambda e, blk=blk, tt=tt: e.indirect_dma_start(out=o32[:, blk, :], out_offset=None, in_=orows,
                                                                   in_offset=bass.IndirectOffsetOnAxis(ap=oix[:, tt * 12 + blk:tt * 12 + blk + 1], axis=0))))
        P.op("pool", lambda e: e.tensor_copy(out=x16[:], in_=x32[:]), reads=[Bx32], writes=[Bx16])
        P.op("act", lambda e: e.activation(out=sq[:, 0:4, :], in_=o32[:, 0:4, :], func=AF.Square), reads=[Bo32], writes=[Bsq])
        pt, pb = ps.next()
        for j in range(4):
            P.op("pe", lambda e, j=j, pt=pt: e.matmul(pt[:, 0:TT], lhsT=ones32[:], rhs=sq[:, j, :], start=(j == 0), stop=(j == 3)),
                 reads=[Bones, Bsq], writes=[pb])
        P.op("act", lambda e, pt=pt: e.activation(out=tmp[:], in_=pt[:, 0:TT], func=AF.Sqrt, bias=epsr[:, 0:1], scale=1.0 / 512.0), reads=[pb, Bc], writes=[Btmp])
        P.op("dve", lambda e: e.reciprocal(out=rstd[:], in_=tmp[:]), reads=[Btmp], writes=[Brstd])
        for blk in range(12):
            pt, pb = ps.next()
            for k in range(8):
                P.op("pe", lambda e, k=k, pt=pt, blk=blk: e.matmul(pt[:, 0:TT], lhsT=wg16[:, k, blk * 128:(blk + 1) * 128], rhs=x16[:, k, :],
                                                              start=(k == 0), stop=(k == 7)), reads=[Bwg, Bx16], writes=[pb])
            g = gi % 2
            gi += 1
            P.op("act", lambda e, pt=pt, blk=blk, g=g: e.activation(out=gt[g][:], in_=pt[:, 0:TT], func=AF.Silu, bias=bgs[:, blk:blk + 1], scale=1.0),
                 reads=[pb, Bc], writes=[Bgt[g]])
            if blk < 4:
                P.op("dve", lambda e, g=g: e.tensor_tensor(out=gt[g][:], in0=gt[g][:], in1=rstd[:], op=ALU.mult),
                     reads=[Bgt[g], Brstd], writes=[Bgt[g]])
                P.op("dve", lambda e, g=g, blk=blk: e.scalar_tensor_tensor(out=y16[:, blk, :], in0=o32[:, blk, :], scalar=hgns[:, blk:blk + 1],
                                                                           in1=gt[g][:], op0=ALU.mult, op1=ALU.mult),
                     reads=[Bo32, Bgt[g], Bc], writes=[By16[blk]])
            else:
                P.op("dve", lambda e, g=g, blk=blk: e.tensor_tensor(out=y16[:, blk, :], in0=o32[:, blk, :], in1=gt[g][:], op=ALU.mult),
                     reads=[Bo32, Bgt[g]], writes=[By16[blk]])
        for db in range(8):
            s = db % 2
            pbs = []
            for n in range(3):
                pt, pb = ps.next()
                col = 1536 + n * 1024 + db * 128
                for k in range(8):
                    P.op("pe", lambda e, k=k, pt=pt, col=col: e.matmul(pt[:, 0:TT], lhsT=wg16[:, k, col:col + 128], rhs=x16[:, k, :],
                                                                  start=(k == 0), stop=(k == 7)), reads=[Bwg, Bx16], writes=[pb])
                bi = 12 + n * 8 + db
                P.op("act", lambda e, pt=pt, n=n, s=s, bi=bi: e.activation(out=sg[s][:, n, :], in_=pt[:, 0:TT], func=AF.Sigmoid, bias=bgs[:, bi:bi + 1], scale=1.0),
                     reads=[pb, Bc], writes=[Bsg[s]])
            for n in range(3):
                pt, pb = ps.next()
                for j in range(4):
                    P.op("pe", lambda e, j=j, n=n, pt=pt, db=db: e.matmul(pt[:, 0:TT], lhsT=wbr16[:, n * 4 + j, db * 128:(db + 1) * 128], rhs=y16[:, n * 4 + j, :],
                                                                     start=(j == 0), stop=(j == 3)), reads=[Bwbr, By16[n * 4 + j]], writes=[pb])
                pbs.append((pt, pb))
            P.op("dve", lambda e, s=s, p0=pbs[0][0]: e.tensor_tensor(out=mm[:], in0=p0[:, 0:TT], in1=sg[s][:, 0, :], op=ALU.mult),
                 reads=[pbs[0][1], Bsg[s]], writes=[Bmm])
            P.op("dve", lambda e, s=s, p1=pbs[1][0]: e.tensor_tensor(out=tt2[:], in0=p1[:, 0:TT], in1=sg[s][:, 1, :], op=ALU.mult),
                 reads=[pbs[1][1], Bsg[s]], writes=[Btt2])
            P.op("pool", lambda e: e.tensor_tensor(out=mm[:], in0=mm[:], in1=tt2[:], op=ALU.add), reads=[Bmm, Btt2], writes=[Bmm])
            P.op("dve", lambda e, s=s, p2=pbs[2][0]: e.tensor_tensor(out=tt2[:], in0=p2[:, 0:TT], in1=sg[s][:, 2, :], op=ALU.mult),
                 reads=[pbs[2][1], Bsg[s]], writes=[Btt2])
            P.op("pool", lambda e, db=db: e.tensor_tensor(out=mg16[:, db, :], in0=mm[:], in1=tt2[:], op=ALU.add),
                 reads=[Bmm, Btt2], writes=[Bmg[db]])
        for eb in range(8):
            pt, pb = ps.next()
            for d in range(8):
                P.op("pe", lambda e, d=d, pt=pt, eb=eb: e.matmul(pt[:, 0:TT], lhsT=wo16[:, d, eb * 128:(eb + 1) * 128], rhs=mg16[:, d, :],
                                                            start=(d == 0), stop=(d == 7)), reads=[Bwo, Bmg[d]], writes=[pb])
            P.op("dve", lambda e, pt=pt, eb=eb: e.scalar_tensor_tensor(out=r32[:, eb, :], in0=x32[:, eb, :], scalar=ALPHA, in1=pt[:, 0:TT],
                                                                        op0=ALU.mult, op1=ALU.add), reads=[Bx32, pb], writes=[Br[eb]])
        pt, pb = ps.next()
        for eb in range(8):
            P.op("pe", lambda e, eb=eb, pt=pt: e.matmul(pt[:, 0:TT], lhsT=ones32[:], rhs=r32[:, eb, :], start=(eb == 0), stop=(eb == 7)),
                 reads=[Bones, Br[eb]], writes=[pb])
        P.op("act", lambda e, pt=pt: e.activation(out=mean[:], in_=pt[:, 0:TT], func=AF.Copy, scale=1.0 / 1024.0), reads=[pb], writes=[Bmean])
        for eb in range(8):
            P.op("dve", lambda e, eb=eb: e.tensor_tensor(out=r32[:, eb, :], in0=r32[:, eb, :], in1=mean[:], op=ALU.subtract),
                 reads=[Br[eb], Bmean], writes=[Br[eb]])
        P.op("act", lambda e: e.activation(out=sq[:], in_=r32[:], func=AF.Square), reads=Br, writes=[Bsq])
        pt, pb = ps.next()
        for eb in range(8):
            P.op("pe", lambda e, eb=eb, pt=pt: e.matmul(pt[:, 0:TT], lhsT=ones32[:], rhs=sq[:, eb, :], start=(eb == 0), stop=(eb == 7)),
                 reads=[Bones, Bsq], writes=[pb])
        P.op("act", lambda e, pt=pt: e.activation(out=tmp[:], in_=pt[:, 0:TT], func=AF.Sqrt, bias=epsl[:, 0:1], scale=1.0 / 1024.0), reads=[pb, Bc], writes=[Btmp])
        P.op("dve", lambda e: e.reciprocal(out=rstd[:], in_=tmp[:]), reads=[Btmp], writes=[Brstd])
        for eb in range(8):
            s = eb % 2
            P.op("dve", lambda e, eb=eb: e.tensor_tensor(out=r32[:, eb, :], in0=r32[:, eb, :], in1=rstd[:], op=ALU.mult),
                 reads=[Br[eb], Brstd], writes=[Br[eb]])
            P.op("act", lambda e, eb=eb, s=s: e.activation(out=ob[s][:], in_=r32[:, eb, :], func=AF.Identity, bias=lnbs[:, eb:eb + 1], scale=lngs[:, eb:eb + 1]),
                 reads=[Br[eb], Bc], writes=[Bob[s]])
            P.dma("sp", "ob%d" % s, outv[:, eb, c0:c0 + TT], ob[s][:], reads=[Bob[s]], writes=[Bout])
            if out16 is not None:
                P.op("pool", lambda e, s=s: e.tensor_copy(out=ob16[s][:], in_=ob[s][:]), reads=[Bob[s]], writes=[Bob16[s]])
                P.dma("sp", "ob16_%d" % s, out16.rearrange("(k p) t -> p k t", p=128)[:, eb, c0:c0 + TT], ob16[s][:], reads=[Bob16[s]], writes=[Bout])
    P.barrier()
    esb.close()


RMS_EPS = 1e-6
S = 8192
TT = 512
NT = S // TT
QSCALE = 192.0 ** -0.5
TWO_PI = 2.0 * math.pi
C1 = 6.28125
C2 = TWO_PI - C1
DILS = (1, 4, 16)
LN_MINF = math.log(1e-6)


def make_scratch(nc):
    dr = lambda n, s, dt=BF16: nc.dram_tensor(n, s, dt, kind="Internal").ap()
    scr = {}
    scr["dsub"] = [[dr("dsub%d_%d" % (g, t), [128, DILS[g], S // DILS[g]]) for t in range(3)] for g in range(3)]
    scr["hq_d"] = [[dr("hq%d_%d" % (h, d), [128, S]) for d in range(2)] for h in range(2)]
    scr["hk_d"] = [[dr("hk%d_%d" % (h, d), [128, S]) for d in range(2)] for h in range(2)]
    scr["hv_d"] = dr("hv", [128, S])
    scr["qd1"] = dr("qd1", [128, S]); scr["qd2"] = dr("qd2", [64, S])
    return scr


def emit_A(nc, P, ps, layer, io, scr, xsrc16=None, phases=("mla", "dil", "hg")):
    debug = False
    xT = io.get("xT"); pos = io["pos"]
    wA = io["wA"]; bA = io["bA"]; lbraw = io["lbraw"]
    wuq = io["wuq"]; gq = io["gq"]; wukv = io["wukv"]; gkv = io["gkv"]
    etab = io["etab"]; ropec = io["ropec"]; ident = io["ident"]; masks = io["masks"]; scanmask = io["scanmask"]
    osrc = io["osrc"]
    dsub = scr["dsub"]; hq_d = scr["hq_d"]; hk_d = scr["hk_d"]; hv_d = scr["hv_d"]; qd1 = scr["qd1"]; qd2 = scr["qd2"]
    Bdsub = Buf("dsub"); Bhqk = Buf("hqk"); BoT = Buf("oT"); Bqd = Buf("qd")
    es0 = ExitStack()
    sb = lambda n, s, dt=F32: es0.enter_context(nc.sbuf_tensor("%s_A%d" % (n, layer), s, dt))

    Bc = Buf("consts")
    bAs = sb("bAs", [128, 23]); lbr = sb("lbr", [128, 4, 2]); lbt = sb("lbt", [128, 4, 3])
    gqs = sb("gqs", [128, 3]); gkvs = sb("gkvs", [128, 2]); ropecs = sb("ropecs", [64, 2])
    ones32 = sb("ones32", [128, 128]); epsr = sb("epsr", [128, 1]); id32 = sb("id32", [128, 128]); id16 = sb("id16", [128, 128], BF16)
    ones16 = sb("ones16", [128, 128], BF16)
    msk = sb("msk", [128, 2, 128]); smask = sb("smask", [128, TT])
    P.dma("sp", "c0", bAs[:], bA, writes=[Bc])
    P.dma("sp", "c1", lbr[:], lbraw, writes=[Bc])
    P.dma("sp", "c2", gqs[:], gq, writes=[Bc])
    P.dma("sp", "c3", gkvs[:], gkv, writes=[Bc])
    P.dma("sp", "c4", ropecs[:], ropec, writes=[Bc])
    P.dma("sp", "c5", id32[:], ident, writes=[Bc])
    P.dma("sp", "c6", msk[:], masks, writes=[Bc])
    P.dma("sp", "c7", smask[:], scanmask, writes=[Bc])
    P.op("pool", lambda e: e.memset(ones32[:], 1.0), writes=[Bc])
    P.op("pool", lambda e: e.memset(ones16[:], 1.0), writes=[Bc])
    P.op("pool", lambda e: e.memset(epsr[:], RMS_EPS), writes=[Bc])
    P.op("pool", lambda e: e.tensor_copy(out=id16[:], in_=id32[:]), reads=[Bc], writes=[Bc])
    for blk in (7, 10, 13):
        P.op("dve", lambda e, blk=blk: e.tensor_scalar(out=bAs[:, blk:blk + 1], in0=bAs[:, blk:blk + 1], scalar1=0.125, scalar2=None, op0=ALU.mult),
             reads=[Bc], writes=[Bc])
    if layer == 0:
        P.op("dve", lambda e: e.memset(lbt[:, :, 0], 0.0), writes=[Bc])
    else:
        P.op("dve", lambda e: e.tensor_tensor(out=lbt[:, :, 1], in0=lbr[:, :, 1], in1=lbr[:, :, 0], op=ALU.subtract), reads=[Bc], writes=[Bc])
        P.op("act", lambda e: e.activation(out=lbt[:, :, 0], in_=lbt[:, :, 1], func=AF.Sigmoid), reads=[Bc], writes=[Bc])
    P.op("dve", lambda e: e.tensor_scalar(out=lbt[:, :, 1], in0=lbt[:, :, 0], scalar1=-1.0, scalar2=1.0, op0=ALU.mult, op1=ALU.add), reads=[Bc], writes=[Bc])
    P.op("dve", lambda e: e.tensor_scalar(out=lbt[:, :, 2], in0=lbt[:, :, 1], scalar1=-1.0, scalar2=None, op0=ALU.mult), reads=[Bc], writes=[Bc])

    K1T = sb("K1T", [128, S], BF16); K2T = sb("K2T", [64, S], BF16)
    Vtok = sb("Vtok", [128, S // 128, 128], BF16)
    BQ = Buf("Q"); BK = Buf("K"); BV = Buf("V")
    mT = [[sb("mT%d%d" % (h, d), [128, 128]) for d in range(2)] for h in range(2)]
    BTt = [[sb("BT%d%d" % (h, d), [128, 128]) for d in range(2)] for h in range(2)]
    BmB = Buf("mB")

    es1 = ExitStack()
    sb = lambda n, s, dt=F32: es1.enter_context(nc.sbuf_tensor("%s_A%d" % (n, layer), s, dt))
    win16 = sb("win16", [128, 8, 2816], BF16); Bwin = Buf("win16")
    stage = [sb("wstage%d" % i, [128, 704], F32) for i in range(2)]
    stage_b = [Buf("wstage%d" % i) for i in range(2)]
    wv = wA.rearrange("(k p) c -> p k c", p=128)
    ci = 0
    for k in range(8):
        for c0 in (0, 704, 1408, 2112):
            s = ci % 2
            P.dma("sp", "wst%d" % s, stage[s][:], wv[:, k, c0:c0 + 704], writes=[stage_b[s]])
            P.op("dve" if ci % 2 == 0 else "pool", lambda e, s=s, k=k, c0=c0: e.tensor_copy(out=win16[:, k, c0:c0 + 704], in_=stage[s][:]),
                 reads=[stage_b[s]], writes=[Bwin])
            ci += 1
    wuq16 = sb("wuq16", [128, 3, 256], BF16); wukv16 = sb("wukv16", [128, 2, 256], BF16); Bwu = Buf("wu")
    wuv = wuq.rearrange("(k p) c -> p k c", p=128)
    wkv = wukv.rearrange("(k p) c -> p k c", p=128)
    for j in range(3):
        s = ci % 2
        P.dma("sp", "wst%d" % s, stage[s][:, 0:256], wuv[:, j, :], writes=[stage_b[s]])
        P.op("dve", lambda e, s=s, j=j: e.tensor_scalar(out=wuq16[:, j, :], in0=stage[s][:, 0:256], scalar1=gqs[:, j:j + 1], scalar2=None, op0=ALU.mult),
             reads=[stage_b[s], Bc], writes=[Bwu])
        ci += 1
    for j in range(2):
        s = ci % 2
        P.dma("sp", "wst%d" % s, stage[s][:, 0:256], wkv[:, j, :], writes=[stage_b[s]])
        P.op("dve", lambda e, s=s, j=j: e.tensor_scalar(out=wukv16[:, j, :], in0=stage[s][:, 0:256], scalar1=gkvs[:, j:j + 1], scalar2=None, op0=ALU.mult),
             reads=[stage_b[s], Bc], writes=[Bwu])
        ci += 1

    x32s = [sb("x32_%d" % i, [128, 4, TT]) for i in range(2)]; Bx32s = [Buf("x32_%d" % i) for i in range(2)]
    x16s = [sb("x16_%d" % i, [128, 8, TT], BF16) for i in range(2)]; Bx16s = [Buf("x16_%d" % i) for i in range(2)]
    xcur = [None, None]
    posi = sb("posi", [64, TT], I32); Bposi = Buf("posi")
    ang = sb("ang", [64, TT]); Bang = Buf("ang")
    ru = sb("ru", [64, TT]); Bru = Buf("ru"); rki = sb("rki", [64, TT], I32); Brki = Buf("rki"); rkf = sb("rkf", [64, TT]); Brkf = Buf("rkf")
    cs = sb("cs", [64, TT]); sn = sb("sn", [64, TT]); Bcs = Buf("cs"); Bsn = Buf("sn")
    c32 = sb("c32", [128, 3, TT]); Bc32 = Buf("c32"); csq = sb("csq", [128, 3, TT]); Bcsq = Buf("csq")
    cn16 = sb("cn16", [128, 3, TT], BF16); Bcn = Buf("cn16")
    rt = sb("rt", [128, TT]); Brt = Buf("rt"); rr = sb("rr", [128, TT]); Brr = Buf("rr")
    t1 = sb("t1", [64, TT]); t2 = sb("t2", [64, TT]); Bt1 = Buf("t1"); Bt2 = Buf("t2")
    q1s = sb("q1s", [128, TT], BF16); q2s = sb("q2s", [64, TT], BF16); Bq1s = Buf("q1s"); Bq2s = Buf("q2s")
    dd16 = [sb("dd16_%d" % i, [128, TT], BF16) for i in range(2)]; Bdd = [Buf("dd16_%d" % i) for i in range(2)]
    qs = [sb("qs%d" % h, [128, TT]) for h in range(2)]; Bqs = [Buf("qs%d" % h) for h in range(2)]
    sig = sb("sig", [128, TT]); Bsig = Buf("sig"); ff = sb("ff", [128, TT]); Bff = Buf("ff")
    bb = sb("bb", [128, TT]); Bbb = Buf("bb"); eq = sb("eq", [128, TT]); Beq = Buf("eq"); ek = sb("ek", [128, TT]); Bek = Buf("ek")
    kk = sb("kk", [128, TT]); Bkk = Buf("kk")
    hq16 = [sb("hq16_%d" % i, [128, TT], BF16) for i in range(2)]; Bhq16 = [Buf("hq16_%d" % i) for i in range(2)]
    hk16 = [sb("hk16_%d" % i, [128, TT], BF16) for i in range(2)]; Bhk16 = [Buf("hk16_%d" % i) for i in range(2)]
    hv16 = sb("hv16", [128, TT], BF16); Bhv16 = Buf("hv16")

    if xsrc16 is None:
        xv = xT.rearrange("(k p) t -> p k t", p=128)
    else:
        xg = xsrc16.rearrange("(c r h p) t -> p r c h t", c=4, r=4, h=2, p=128)

    def inproj(col, m, rhs_cols=None):
        pt, pb = ps.next()
        xx, bxx = xcur[0], xcur[1]
        for k in range(8):
            P.op("pe", lambda e, k=k, pt=pt, xx=xx: e.matmul(pt[0:m, :], lhsT=win16[:, k, col:col + m], rhs=xx[:, k, :], start=(k == 0), stop=(k == 7)),
                 reads=[Bwin, bxx], writes=[pb])
        return pt, pb

    def sintab(dst, Bdst, shift):
        P.op("dve", lambda e: e.tensor_scalar(out=ru[:], in0=ang[:], scalar1=1.0 / TWO_PI, scalar2=shift / TWO_PI + 0.5, op0=ALU.mult, op1=ALU.add),
             reads=[Bang], writes=[Bru])
        P.op("dve", lambda e: e.tensor_copy(out=rki[:], in_=ru[:]), reads=[Bru], writes=[Brki])
        P.op("dve", lambda e: e.tensor_copy(out=rkf[:], in_=rki[:]), reads=[Brki], writes=[Brkf])
        P.op("dve", lambda e: e.tensor_scalar(out=ru[:], in0=ang[:], scalar1=shift, scalar2=None, op0=ALU.add), reads=[Bang], writes=[Bru])
        P.op("dve", lambda e: e.scalar_tensor_tensor(out=ru[:], in0=rkf[:], scalar=-C1, in1=ru[:], op0=ALU.mult, op1=ALU.add),
             reads=[Brkf, Bru], writes=[Bru])
        P.op("dve", lambda e: e.scalar_tensor_tensor(out=ru[:], in0=rkf[:], scalar=-C2, in1=ru[:], op0=ALU.mult, op1=ALU.add),
             reads=[Brkf, Bru], writes=[Bru])
        P.op("dve", lambda e: e.tensor_scalar(out=rkf[:], in0=ru[:], scalar1=math.pi, scalar2=None, op0=ALU.is_gt), reads=[Bru], writes=[Brkf])
        P.op("dve", lambda e: e.scalar_tensor_tensor(out=ru[:], in0=rkf[:], scalar=-TWO_PI, in1=ru[:], op0=ALU.mult, op1=ALU.add),
             reads=[Brkf, Bru], writes=[Bru])
        P.op("dve", lambda e: e.tensor_scalar(out=rkf[:], in0=ru[:], scalar1=-math.pi, scalar2=None, op0=ALU.is_lt), reads=[Bru], writes=[Brkf])
        P.op("dve", lambda e: e.scalar_tensor_tensor(out=ru[:], in0=rkf[:], scalar=TWO_PI, in1=ru[:], op0=ALU.mult, op1=ALU.add),
             reads=[Brkf, Bru], writes=[Bru])
        P.op("dve", lambda e: e.tensor_scalar(out=ru[:], in0=ru[:], scalar1=math.pi, scalar2=-math.pi, op0=ALU.min, op1=ALU.max), reads=[Bru], writes=[Bru])
        P.op("act", lambda e: e.activation(out=dst[:], in_=ru[:], func=AF.Sin), reads=[Bru], writes=[Bdst])

    def rms_norm(nblk, rank):
        P.op("act", lambda e: e.activation(out=csq[:, 0:nblk, :], in_=c32[:, 0:nblk, :], func=AF.Square), reads=[Bc32], writes=[Bcsq])
        pt, pb = ps.next()
        for j in range(nblk):
            P.op("pe", lambda e, j=j, pt=pt: e.matmul(pt[:], lhsT=ones32[:], rhs=csq[:, j, :], start=(j == 0), stop=(j == nblk - 1)),
                 reads=[Bc, Bcsq], writes=[pb])
        P.op("act", lambda e, pt=pt: e.activation(out=rt[:], in_=pt[:], func=AF.Sqrt, bias=epsr[:, 0:1], scale=1.0 / rank), reads=[pb, Bc], writes=[Brt])
        P.op("dve", lambda e: e.reciprocal(out=rr[:], in_=rt[:]), reads=[Brt], writes=[Brr])
        for j in range(nblk):
            P.op("dve", lambda e, j=j: e.tensor_tensor(out=cn16[:, j, :], in0=c32[:, j, :], in1=rr[:], op=ALU.mult), reads=[Bc32, Brr], writes=[Bcn])

    ddi = 0
    hi = 0

    def load_x(tt):
        c0 = tt * TT
        xb, bxb = x16s[tt % 2], Bx16s[tt % 2]
        if xsrc16 is None:
            for hf in range(2):
                P.dma("sp", "x32_%d" % hf, x32s[hf][:], xv[:, hf * 4:(hf + 1) * 4, c0:c0 + TT], writes=[Bx32s[hf]])
                P.op("pool", lambda e, hf=hf, xb=xb: e.tensor_copy(out=xb[:, hf * 4:(hf + 1) * 4, :], in_=x32s[hf][:]), reads=[Bx32s[hf]], writes=[bxb])
        else:
            for cch in range(4):
                P.dma("sp", "x32_0", xb[:, cch * 2:cch * 2 + 2, :], xg[:, c0 // 2048, cch, :, (c0 % 2048):(c0 % 2048) + TT], reads=[io["Bxg"]], writes=[bxb])

    load_x(0)
    for tt in range(NT):
        c0 = tt * TT
        xcur[0], xcur[1] = x16s[tt % 2], Bx16s[tt % 2]
        if tt + 1 < NT:
            load_x(tt + 1)
        P.dma("sp", "posi", posi[:], pos[0:1, c0:c0 + TT].partition_broadcast(64), writes=[Bposi])
        P.op("dve", lambda e: e.tensor_copy(out=ang[:], in_=posi[:]), reads=[Bposi], writes=[Bang])
        P.op("dve", lambda e: e.tensor_scalar(out=ang[:], in0=ang[:], scalar1=ropecs[:, 0:1], scalar2=None, op0=ALU.mult), reads=[Bang, Bc], writes=[Bang])
        sintab(sn, Bsn, 0.0)
        sintab(cs, Bcs, math.pi / 2)
        P.op("dve", lambda e: e.tensor_scalar(out=sn[:], in0=sn[:], scalar1=ropecs[:, 1:2], scalar2=None, op0=ALU.mult), reads=[Bsn, Bc], writes=[Bsn])
        for j in range(3):
            pt, pb = inproj((16 + j) * 128, 128)
            P.op("act", lambda e, pt=pt, j=j: e.activation(out=c32[:, j, :], in_=pt[:], func=AF.Identity, bias=bAs[:, 16 + j:17 + j], scale=1.0),
                 reads=[pb, Bc], writes=[Bc32])
        rms_norm(3, 384.0)
        if "dil" in phases:
            for g in range(3):
                d = DILS[g]
                for t in range(3):
                    blk = 7 + g * 3 + t
                    pt, pb = inproj(blk * 128, 128)
                    s = ddi % 2
                    ddi += 1
                    P.op("act", lambda e, pt=pt, s=s, d=d, blk=blk, t=t: e.activation(
                        out=dd16[s][:].rearrange("p (r j) -> p r j", r=d), in_=pt[:].rearrange("p (j r) -> p r j", r=d),
                        func=AF.Identity, bias=bAs[:, blk:blk + 1], scale=(0.125 if t == 0 else 1.0)), reads=[pb, Bc], writes=[Bdd[s]])
                    P.dma("sp", "dd%d" % s, dsub[g][t][:, :, c0 // d:(c0 + TT) // d], dd16[s][:].rearrange("p (r j) -> p r j", r=d),
                          reads=[Bdd[s]], writes=[Bdsub])
        pt, pb = ps.next()
        for j in range(3):
            P.op("pe", lambda e, j=j, pt=pt: e.matmul(pt[:], lhsT=wuq16[:, j, 0:128], rhs=cn16[:, j, :], start=(j == 0), stop=(j == 2)),
                 reads=[Bwu, Bcn], writes=[pb])
        P.op("act", lambda e, pt=pt: e.activation(out=q1s[:], in_=pt[:], func=AF.Copy, scale=QSCALE), reads=[pb], writes=[Bq1s])
        P.dma("sp", "q1s", qd1[:, c0:c0 + TT], q1s[:], reads=[Bq1s], writes=[Bqd])
        pA, pbA = ps.next()
        for j in range(3):
            P.op("pe", lambda e, j=j, pA=pA: e.matmul(pA[0:64, :], lhsT=wuq16[:, j, 128:192], rhs=cn16[:, j, :], start=(j == 0), stop=(j == 2)),
                 reads=[Bwu, Bcn], writes=[pbA])
        pB, pbB = ps.next()
        for j in range(3):
            P.op("pe", lambda e, j=j, pB=pB: e.matmul(pB[0:64, :], lhsT=wuq16[:, j, 192:256], rhs=cn16[:, j, :], start=(j == 0), stop=(j == 2)),
                 reads=[Bwu, Bcn], writes=[pbB])
        P.op("dve", lambda e, pA=pA: e.scalar_tensor_tensor(out=t1[:], in0=pA[0:64, :], scalar=QSCALE, in1=cs[:], op0=ALU.mult, op1=ALU.mult),
             reads=[pbA, Bcs], writes=[Bt1])
        P.op("dve", lambda e, pB=pB: e.scalar_tensor_tensor(out=t2[:], in0=pB[0:64, :], scalar=QSCALE, in1=sn[:], op0=ALU.mult, op1=ALU.mult),
             reads=[pbB, Bsn], writes=[Bt2])
        P.op("pool", lambda e: e.tensor_tensor(out=q2s[:], in0=t1[:], in1=t2[:], op=ALU.add), reads=[Bt1, Bt2], writes=[Bq2s])
        P.dma("sp", "q2s", qd2[:, c0:c0 + TT], q2s[:], reads=[Bq2s], writes=[Bqd])
        for j in range(2):
            pt, pb = inproj((19 + j) * 128, 128)
            P.op("act", lambda e, pt=pt, j=j: e.activation(out=c32[:, j, :], in_=pt[:], func=AF.Identity, bias=bAs[:, 19 + j:20 + j], scale=1.0),
                 reads=[pb, Bc], writes=[Bc32])
        rms_norm(2, 256.0)
        if "hg" in phases:
            for h in range(2):
                pt, pb = inproj(h * 128, 128)
                P.op("act", lambda e, pt=pt, h=h: e.activation(out=qs[h][:], in_=pt[:], func=AF.Silu, bias=bAs[:, h:h + 1], scale=1.0),
                     reads=[pb, Bc], writes=[Bqs[h]])
            pt, pb = inproj(6 * 128, 128)
            P.op("act", lambda e, pt=pt: e.activation(out=hv16[:], in_=pt[:], func=AF.Identity, bias=bAs[:, 6:7], scale=1.0), reads=[pb, Bc], writes=[Bhv16])
            P.dma("sp", "hv16", hv_d[:, c0:c0 + TT], hv16[:], reads=[Bhv16], writes=[Bhqk])
            for dr_ in range(2):
                for h in range(2):
                    idx = h * 2 + dr_
                    blk = 2 + dr_ * 2 + h
                    pt, pb = inproj(blk * 128, 128)
                    P.op("act", lambda e, pt=pt, blk=blk: e.activation(out=sig[:], in_=pt[:], func=AF.Sigmoid, bias=bAs[:, blk:blk + 1], scale=1.0),
                         reads=[pb, Bc], writes=[Bsig])
                    P.op("dve", lambda e, idx=idx: e.tensor_scalar(out=ff[:], in0=sig[:], scalar1=lbt[:, idx, 1:2], scalar2=lbt[:, idx, 0:1], op0=ALU.mult, op1=ALU.add),
                         reads=[Bsig, Bc], writes=[Bff])
                    P.op("act", lambda e: e.activation(out=ff[:], in_=ff[:], func=AF.Ln), reads=[Bff], writes=[Bff])
                    P.op("pool", lambda e: e.tensor_scalar(out=ff[:], in0=ff[:], scalar1=LN_MINF, scalar2=None, op0=ALU.max), reads=[Bff], writes=[Bff])
                    if dr_ == 0:
                        P.op("dve", lambda e: e.tensor_tensor_scan(out=bb[:], data0=smask[:], data1=ff[:], initial=0.0, op0=ALU.mult, op1=ALU.add),
                             reads=[Bff, Bc], writes=[Bbb])
                        mcol, bcol = 31, 63
                    else:
                        P.op("dve", lambda e: e.tensor_tensor_scan(out=bb[:, ::-1], data0=smask[:], data1=ff[:, ::-1], initial=0.0, op0=ALU.mult, op1=ALU.add),
                             reads=[Bff, Bc], writes=[Bbb])
                        mcol, bcol = 32, 0
                    b3 = bb[:].rearrange("p (c t) -> p c t", t=64)
                    P.op("pool", lambda e, h=h, dr_=dr_, tt=tt, b3=b3, mcol=mcol: e.tensor_copy(out=mT[h][dr_][:, tt * 8:(tt + 1) * 8], in_=b3[:, :, mcol]),
                         reads=[Bbb], writes=[BmB])
                    P.op("pool", lambda e, h=h, dr_=dr_, tt=tt, b3=b3, bcol=bcol: e.tensor_copy(out=BTt[h][dr_][:, tt * 8:(tt + 1) * 8], in_=b3[:, :, bcol]),
                         reads=[Bbb], writes=[BmB])
                    mb = b3[:, :, mcol:mcol + 1]
                    mbc = bass.AP(mb.tensor, mb.offset, [list(mb.ap[0]), list(mb.ap[1]), [0, 64]])
                    P.op("dve", lambda e, b3=b3, mbc=mbc: e.tensor_tensor(out=eq[:].rearrange("p (c t) -> p c t", t=64), in0=b3, in1=mbc, op=ALU.subtract),
                         reads=[Bbb], writes=[Beq])
                    P.op("act", lambda e: e.activation(out=ek[:], in_=eq[:], func=AF.Exp, scale=-1.0), reads=[Beq], writes=[Bek])
                    P.op("act", lambda e: e.activation(out=eq[:], in_=eq[:], func=AF.Exp), reads=[Beq], writes=[Beq])
                    s = hi % 2
                    hi += 1
                    P.op("dve", lambda e, s=s, h=h: e.tensor_tensor(out=hq16[s][:], in0=qs[h][:], in1=eq[:], op=ALU.mult), reads=[Bqs[h], Beq], writes=[Bhq16[s]])
                    P.dma("sp", "hq16_%d" % s, hq_d[h][dr_][:, c0:c0 + TT], hq16[s][:], reads=[Bhq16[s]], writes=[Bhqk])
                    P.op("dve", lambda e, idx=idx: e.tensor_scalar(out=kk[:], in0=sig[:], scalar1=lbt[:, idx, 2:3], scalar2=lbt[:, idx, 1:2], op0=ALU.mult, op1=ALU.add),
                         reads=[Bsig, Bc], writes=[Bkk])
                    P.op("pool", lambda e, s=s: e.tensor_tensor(out=hk16[s][:], in0=kk[:], in1=ek[:], op=ALU.mult), reads=[Bkk, Bek], writes=[Bhk16[s]])
                    P.dma("sp", "hk16_%d" % s, hk_d[h][dr_][:, c0:c0 + TT], hk16[s][:], reads=[Bhk16[s]], writes=[Bhqk])

        pt, pb = ps.next()
        for j in range(2):
            P.op("pe", lambda e, j=j, pt=pt: e.matmul(pt[:], lhsT=wukv16[:, j, 0:128], rhs=cn16[:, j, :], start=(j == 0), stop=(j == 1)),
                 reads=[Bwu, Bcn], writes=[pb])
        P.op("act", lambda e, pt=pt, c0=c0: e.activation(out=K1T[:, c0:c0 + TT], in_=pt[:], func=AF.Copy), reads=[pb], writes=[BK])
        pt, pb = ps.next()
        for i in range(4):
            for j in range(2):
                P.op("pe", lambda e, i=i, j=j, pt=pt: e.matmul(pt[:, i * 128:(i + 1) * 128], lhsT=cn16[:, j, i * 128:(i + 1) * 128], rhs=wukv16[:, j, 128:256],
                                                          start=(j == 0), stop=(j == 1)), reads=[Bwu, Bcn], writes=[pb])
        P.op("act", lambda e, pt=pt, tt=tt: e.activation(out=Vtok[:, tt * 4:(tt + 1) * 4, :], in_=pt[:].rearrange("p (a b) -> p a b", a=4), func=AF.Copy),
             reads=[pb], writes=[BV])
        pA, pbA = inproj(21 * 128, 64)
        pB, pbB = inproj(21 * 128 + 64, 64)
        P.op("dve", lambda e, pA=pA: e.scalar_tensor_tensor(out=t1[:], in0=pA[0:64, :], scalar=bAs[0:64, 21:22], in1=cs[:], op0=ALU.add, op1=ALU.mult),
             reads=[pbA, Bcs, Bc], writes=[Bt1])
        P.op("dve", lambda e, pB=pB: e.scalar_tensor_tensor(out=t2[:], in0=pB[0:64, :], scalar=bAs[0:64, 22:23], in1=sn[:], op0=ALU.add, op1=ALU.mult),
             reads=[pbB, Bsn, Bc], writes=[Bt2])
        P.op("pool", lambda e, c0=c0: e.tensor_tensor(out=K2T[:, c0:c0 + TT], in0=t1[:], in1=t2[:], op=ALU.add), reads=[Bt1, Bt2], writes=[BK])
    P.barrier()
    es1.close()
    es2 = ExitStack()
    sb = lambda n, s, dt=F32: es2.enter_context(nc.sbuf_tensor("%s_A%d" % (n, layer), s, dt))
    if "mla" in phases:
        pT = [sb("pT%d" % i, [128, TT], BF16) for i in range(4)]; BpT = [Buf("pT%d" % i) for i in range(4)]
        dacc = sb("dacc", [128, TT]); Bdacc = Buf("dacc")
        daccs = [sb("daccs%d" % i, [128, TT]) for i in range(3)]; Bdaccs = [Buf("daccs%d" % i) for i in range(3)]
        rden = sb("rden", [128, TT]); Brden = Buf("rden")
        oc = sb("oc", [128, TT], BF16); Boc = Buf("oc")
        po_t = nc.alloc_psum_tensor("po_mla", [128, 512], F32) if False else None
        pi = 0
        Q1 = [sb("Q1_%d" % i, [128, TT], BF16) for i in range(2)]; Q2 = [sb("Q2_%d" % i, [64, TT], BF16) for i in range(2)]
        BQs = [Buf("Qs%d" % i) for i in range(2)]
        for qt in range(NT):
            q0 = qt * TT
            qi = qt % 2
            P.dma("sp", "Q1_%d" % qi, Q1[qi][:], qd1[:, q0:q0 + TT], reads=[Bqd], writes=[BQs[qi]])
            P.dma("sp", "Q2_%d" % qi, Q2[qi][:], qd2[:, q0:q0 + TT], reads=[Bqd], writes=[BQs[qi]])
            BQ = BQs[qi]
            po, pbo = ps.next()

            def mla_a(kb, qi=qi, BQ=BQ, po=po):
                nonlocal pi
                k0 = kb * 128
                pt, pb = ps.next()
                if pt is po:
                    pt, pb = ps.next()
                P.op("pe", lambda e, pt=pt, k0=k0, qi=qi: e.matmul(pt[:], lhsT=K1T[:, k0:k0 + 128], rhs=Q1[qi][:], start=True, stop=False),
                     reads=[BK, BQ], writes=[pb])
                P.op("pe", lambda e, pt=pt, k0=k0, qi=qi: e.matmul(pt[:], lhsT=K2T[:, k0:k0 + 128], rhs=Q2[qi][:], start=False, stop=True),
                     reads=[BK, BQ], writes=[pb])
                s = pi % 4
                pi += 1
                P.op("act", lambda e, pt=pt, s=s: e.activation(out=pT[s][:], in_=pt[:], func=AF.Exp), reads=[pb], writes=[BpT[s]])
                return s

            def mla_b(kb, s, po=po, pbo=pbo):
                P.op("pe", lambda e, po=po, kb=kb, s=s: e.matmul(po[:], lhsT=Vtok[:, kb, :], rhs=pT[s][:], start=(kb == 0), stop=(kb == S // 128 - 1)),
                     reads=[BV, BpT[s]], writes=[pbo])
                ai = kb % 3
                eng = "pool" if ai == 2 else "dve"
                if kb < 3:
                    P.op(eng, lambda e, s=s, ai=ai: e.tensor_copy(out=daccs[ai][:], in_=pT[s][:]), reads=[BpT[s]], writes=[Bdaccs[ai]])
                else:
                    P.op(eng, lambda e, s=s, ai=ai: e.tensor_tensor(out=daccs[ai][:], in0=daccs[ai][:], in1=pT[s][:], op=ALU.add),
                         reads=[BpT[s], Bdaccs[ai]], writes=[Bdaccs[ai]])

            pend = []
            for kb in range(S // 128):
                pend.append((kb, mla_a(kb)))
                if len(pend) > 2:
                    mla_b(*pend.pop(0))
            while pend:
                mla_b(*pend.pop(0))
            P.op("dve", lambda e: e.tensor_tensor(out=dacc[:], in0=daccs[0][:], in1=daccs[1][:], op=ALU.add), reads=[Bdaccs[0], Bdaccs[1]], writes=[Bdacc])
            P.op("dve", lambda e: e.tensor_tensor(out=dacc[:], in0=dacc[:], in1=daccs[2][:], op=ALU.add), reads=[Bdacc, Bdaccs[2]], writes=[Bdacc])
            pd, pbd = ps.next()
            if pd is po:
                pd, pbd = ps.next()
            P.op("pe", lambda e, pd=pd: e.matmul(pd[:], lhsT=ones32[:], rhs=dacc[:], start=True, stop=True), reads=[Bc, Bdacc], writes=[pbd])
            P.op("dve", lambda e, pd=pd: e.reciprocal(out=rden[:], in_=pd[:]), reads=[pbd], writes=[Brden])
            P.op("dve", lambda e, po=po: e.tensor_tensor(out=oc[:], in0=po[:], in1=rden[:], op=ALU.mult), reads=[pbo, Brden], writes=[Boc])
            P.dma("sp", "oc", osrc[(q0 // 2048) * 384 + 256:(q0 // 2048) * 384 + 384, (q0 % 2048):(q0 % 2048) + TT], oc[:], reads=[Boc], writes=[BoT])


    P.barrier()
    es2.close()
    es2 = ExitStack()
    sb = lambda n, s, dt=F32: es2.enter_context(nc.sbuf_tensor("%s_A%d" % (n, layer), s, dt))
    if "dil" in phases:
        RG = 2048
        ets = sb("ets", [128, 18, 256]); Bets = Buf("ets")
        P.dma("sp", "ets", ets[:], etab, writes=[Bets])
        NSET = 2
        Qs_ = [sb("Qs%d" % i, [128, RG], BF16) for i in range(NSET)]; Ks_ = [sb("Ks%d" % i, [128, RG + 128], BF16) for i in range(NSET)]
        Vs_ = [sb("Vs%d" % i, [128, RG + 128], BF16) for i in range(NSET)]
        BQs_l = [Buf("Qs%d" % i) for i in range(NSET)]; BKs_l = [Buf("Ks%d" % i) for i in range(NSET)]; BVs_l = [Buf("Vs%d" % i) for i in range(NSET)]
        Vp_ = [sb("Vp%d" % i, [128, 17, 2, 65], BF16) for i in range(NSET)]; BVp_l = [[Buf("Vp%d_%d" % (i, j)) for j in range(17)] for i in range(NSET)]
        NU = 4
        pe32 = [sb("pe32_%d" % i, [128, 256]) for i in range(NU)]; Bpe = [Buf("pe32_%d" % i) for i in range(NU)]
        pt16 = [sb("pt16_%d" % i, [128, 256], BF16) for i in range(NU)]; Bpt16 = [Buf("pt16_%d" % i) for i in range(NU)]
        acc = [sb("dacc%d" % h, [65, RG]) for h in range(2)]; Bacc = [Buf("dacc%d" % h) for h in range(2)]
        obd = sb("obd", [64, 512], BF16); Bobd = Buf("obd")
        for i in range(NSET):
            P.op("pool", lambda e, i=i: e.memset(Vp_[i][:], 1.0), writes=BVp_l[i])
        ui = 0
        si = 0
        for rg in range(S // RG):
            R0 = rg * RG
            for h in range(2):
                P.op("pool", lambda e, h=h: e.memset(acc[h][:], 0.0), writes=[Bacc[h]])
            for g in range(3):
                d = DILS[g]
                J = S // d
                nj = RG // d
                nb = nj // 128
                j0 = R0 // d
                for r in range(d):
                    ss = si % NSET
                    si += 1
                    Qs, Ks, Vs, Vp = Qs_[ss], Ks_[ss], Vs_[ss], Vp_[ss]
                    BQs_, BKs, BVs, BVp = BQs_l[ss], BKs_l[ss], BVs_l[ss], BVp_l[ss]
                    lo = j0 - 64
                    hi_ = j0 + nj + 64
                    clo = max(lo, 0)
                    chi = min(hi_, J)
                    if lo < 0:
                        P.op("pool", lambda e, Ks=Ks: e.memset(Ks[:, 0:64], 0.0), writes=[BKs])
                        P.op("pool", lambda e, Vs=Vs: e.memset(Vs[:, 0:64], 0.0), writes=[BVs])
                    if hi_ > J:
                        P.op("pool", lambda e, nj=nj, Ks=Ks: e.memset(Ks[:, nj + 64:nj + 128], 0.0), writes=[BKs])
                        P.op("pool", lambda e, nj=nj, Vs=Vs: e.memset(Vs[:, nj + 64:nj + 128], 0.0), writes=[BVs])
                    P.dma("sp", "Qs%d" % ss, Qs[:, 0:nj], dsub[g][0][:, r, j0:j0 + nj], reads=[Bdsub], writes=[BQs_])
                    P.dma("sp", "Ks%d" % ss, Ks[:, clo - lo:chi - lo], dsub[g][1][:, r, clo:chi], reads=[Bdsub], writes=[BKs])
                    P.dma("sp", "Vs%d" % ss, Vs[:, clo - lo:chi - lo], dsub[g][2][:, r, clo:chi], reads=[Bdsub], writes=[BVs])
                    for n in range(nb + 1):
                        ptr, pbr = ps.next()
                        ptr16 = ptr[:].bitcast(BF16)
                        P.op("pe", lambda e, n=n, ptr16=ptr16, Vs=Vs: e.transpose(out=ptr16[:, 0:128], in_=Vs[:, n * 128:(n + 1) * 128], identity=id16[:]),
                             reads=[BVs, Bc], writes=[pbr])
                        P.op("act", lambda e, n=n, ptr16=ptr16, Vp=Vp: e.activation(out=Vp[:, n, :, 0:64], in_=ptr16[:, 0:128].rearrange("p (h v) -> p h v", h=2), func=AF.Copy),
                             reads=[pbr], writes=[BVp[n]])

                    def dil_a(qb, h, Qs=Qs, Ks=Ks, BQs_=BQs_, BKs=BKs, g=g, j0=j0, J=J):
                        nonlocal ui
                        jb = j0 + qb * 128
                        var = 1 if jb == 0 else (2 if jb + 128 == J else 0)
                        u = ui % NU
                        ui += 1
                        pt, pb = ps.next()
                        for kc in range(2):
                            P.op("pe", lambda e, pt=pt, kc=kc, qb=qb, h=h: e.matmul(pt[:, kc * 128:(kc + 1) * 128],
                                 lhsT=Ks[h * 64:(h + 1) * 64, (qb + kc) * 128:(qb + kc + 1) * 128], rhs=Qs[h * 64:(h + 1) * 64, qb * 128:(qb + 1) * 128],
                                 start=True, stop=True), reads=[BKs, BQs_], writes=[pb])
                        P.op("act", lambda e, pt=pt, u=u: e.activation(out=pe32[u][:], in_=pt[:, 0:256], func=AF.Exp), reads=[pb], writes=[Bpe[u]])
                        ei = (g * 2 + h) * 3 + var
                        P.op("dve", lambda e, u=u, ei=ei: e.tensor_tensor(out=pt16[u][:], in0=pe32[u][:], in1=ets[:, ei, :], op=ALU.mult),
                             reads=[Bpe[u], Bets], writes=[Bpt16[u]])
                        return (qb, h, u)

                    def dil_b(qb, h, u, Vp=Vp, BVp=BVp, d=d, r=r):
                        po, pbo = ps.next()
                        for kc in range(2):
                            P.op("pe", lambda e, po=po, kc=kc, qb=qb, h=h, u=u: e.matmul(po[0:65, 0:128], lhsT=Vp[:, qb + kc, h, :], rhs=pt16[u][:, kc * 128:(kc + 1) * 128],
                                 start=(kc == 0), stop=(kc == 1)), reads=[BVp[qb + kc], Bpt16[u]], writes=[pbo])
                        st = qb * 128 * d + r
                        av = acc[h][:, st:st + 127 * d + 1:d]
                        P.op("dve", lambda e, po=po, av=av: e.tensor_tensor(out=av, in0=av, in1=po[0:65, 0:128], op=ALU.add), reads=[pbo, Bacc[h]], writes=[Bacc[h]])

                    pend = []
                    for qb in range(nb):
                        for h in range(2):
                            pend.append(dil_a(qb, h))
                            if len(pend) > 2:
                                dil_b(*pend.pop(0))
                    while pend:
                        dil_b(*pend.pop(0))
            for h in range(2):
                P.op("dve", lambda e, h=h: e.reciprocal(out=acc[h][64:65, :], in_=acc[h][64:65, :]), reads=[Bacc[h]], writes=[Bacc[h]])
                for cc in range(RG // 512):
                    pt, pb = ps.next()
                    P.op("pe", lambda e, pt=pt, h=h, cc=cc: e.matmul(pt[0:64, :], lhsT=ones32[64:65, 0:64], rhs=acc[h][64:65, cc * 512:(cc + 1) * 512], start=True, stop=True),
                         reads=[Bc, Bacc[h]], writes=[pb])
                    P.op("dve", lambda e, pt=pt, h=h, cc=cc: e.tensor_tensor(out=obd[:], in0=acc[h][0:64, cc * 512:(cc + 1) * 512], in1=pt[0:64, :], op=ALU.mult),
                         reads=[pb, Bacc[h]], writes=[Bobd])
                    P.dma("sp", "obd", osrc[rg * 384 + 128 + h * 64:rg * 384 + 128 + (h + 1) * 64, cc * 512:(cc + 1) * 512], obd[:], reads=[Bobd], writes=[BoT])

    P.barrier()
    es2.close()
    es2 = ExitStack()
    sb = lambda n, s, dt=F32: es2.enter_context(nc.sbuf_tensor("%s_A%d" % (n, layer), s, dt))
    if "hg" in phases:
        oac = [sb("oac%d" % h, [64, S]) for h in range(2)]; Boac = [Buf("oac%d" % h) for h in range(2)]
        for h in range(2):
            P.op("pool", lambda e, h=h: e.memset(oac[h][:], 0.0), writes=[Boac[h]])
        gam = [[sb("gam%d%d" % (h, d), [128, 128]) for d in range(2)] for h in range(2)]; Bgam = Buf("gam")
        for h in range(2):
            for d in range(2):
                if d == 0:
                    dst, mn, bc, mc = gam[h][d][:, 0:127], mT[h][d][:, 1:128], BTt[h][d][:, 0:127], mT[h][d][:, 0:127]
                else:
                    dst, mn, bc, mc = gam[h][d][:, 1:128], mT[h][d][:, 0:127], BTt[h][d][:, 1:128], mT[h][d][:, 1:128]
                P.op("dve", lambda e, dst=dst, mn=mn, bc=bc: e.tensor_tensor(out=dst, in0=mn, in1=bc, op=ALU.add), reads=[BmB], writes=[Bgam])
                P.op("dve", lambda e, dst=dst, mc=mc: e.tensor_tensor(out=dst, in0=dst, in1=mc, op=ALU.subtract), reads=[BmB, Bgam], writes=[Bgam])
                P.op("act", lambda e, dst=dst: e.activation(out=dst, in_=dst, func=AF.Exp), reads=[Bgam], writes=[Bgam])
        ch = [(h, d) for d in range(2) for h in range(2)]
        qT = {c: sb("hqT%d%d" % c, [128, TT], BF16) for c in ch}; kT = {c: sb("hkT%d%d" % c, [128, TT], BF16) for c in ch}
        vT = {c: sb("hvT%d%d" % c, [128, TT], BF16) for c in ch}
        Bld = {c: Buf("hld%d%d" % c) for c in ch}
        kvtok = {c: sb("kvtok%d%d" % c, [128, 256], BF16) for c in ch}; Bkv = {c: Buf("kvtok%d%d" % c) for c in ch}
        at16 = {c: sb("at16%d%d" % c, [128, 128], BF16) for c in ch}; Bat = {c: Buf("at%d%d" % c) for c in ch}
        S32 = {c: sb("S32%d%d" % c, [128, 64]) for c in ch}; S16 = {c: sb("S16%d%d" % c, [128, 64], BF16) for c in ch}
        Sh = {c: sb("Sh%d%d" % c, [128, 64]) for c in ch}
        BS32 = {c: Buf("S32%d%d" % c) for c in ch}; BS16 = {c: Buf("S16%d%d" % c) for c in ch}; BSh = {c: Buf("Sh%d%d" % c) for c in ch}
        for c in ch:
            P.op("pool", lambda e, c=c: e.memset(S32[c][:], 0.0), writes=[BS32[c]])
            P.op("pool", lambda e, c=c: e.memset(S16[c][:], 0.0), writes=[BS16[c]])
        for step in range(NT):
            for c in ch:
                h, d = c
                ti = step if d == 0 else NT - 1 - step
                c0 = ti * TT
                P.dma("sp", "hl%d%d" % c, qT[c][:], hq_d[h][d][:, c0:c0 + TT], reads=[Bhqk], writes=[Bld[c]])
                P.dma("sp", "hl%d%d" % c, kT[c][:], hk_d[h][d][:, c0:c0 + TT], reads=[Bhqk], writes=[Bld[c]])
                P.dma("sp", "hl%d%d" % c, vT[c][:], hv_d[:, c0:c0 + TT], reads=[Bhqk], writes=[Bld[c]])
            for pp in range(4):
                inf = {}
                for c in ch:
                    h, d = c
                    ti = step if d == 0 else NT - 1 - step
                    pr = pp if d == 0 else 3 - pp
                    inf[c] = (h, d, ti, pr, pr * 128)
                trb = {}
                for ci, c in enumerate(ch):
                    h, d, ti, pr, p0 = inf[c]
                    bank, bb_ = ps.t[ci], ps.b[ci]
                    b16 = bank[:].bitcast(BF16)
                    P.op("pe", lambda e, c=c, p0=p0, b16=b16: e.transpose(out=b16[:, 0:128], in_=kT[c][:, p0:p0 + 128], identity=id16[:]),
                         reads=[Bld[c], Bc], writes=[bb_])
                    P.op("pe", lambda e, c=c, p0=p0, b16=b16: e.transpose(out=b16[:, 128:256], in_=vT[c][:, p0:p0 + 128], identity=id16[:]),
                         reads=[Bld[c], Bc], writes=[bb_])
                    trb[c] = (b16, bb_)
                for c in ch:
                    b16, bb_ = trb[c]
                    P.op("act", lambda e, c=c, b16=b16: e.activation(out=kvtok[c][:], in_=b16[:, 0:256], func=AF.Copy), reads=[bb_], writes=[Bkv[c]])
                pab = {}
                for ci, c in enumerate(ch):
                    h, d, ti, pr, p0 = inf[c]
                    pa, pba = ps.t[ci], ps.b[ci]
                    P.op("pe", lambda e, c=c, p0=p0, pa=pa: e.matmul(pa[:, 0:128], lhsT=kT[c][:, p0:p0 + 128], rhs=qT[c][:, p0:p0 + 128], start=True, stop=True),
                         reads=[Bld[c]], writes=[pba])
                    pab[c] = (pa, pba)
                for c in ch:
                    h, d, ti, pr, p0 = inf[c]
                    pa, pba = pab[c]
                    P.op("dve", lambda e, c=c, d=d, pa=pa: e.tensor_tensor(out=at16[c][:], in0=pa[:, 0:128], in1=msk[:, d, :], op=ALU.mult),
                         reads=[pba, Bc], writes=[Bat[c]])
                for cc in range(2):
                    pub = {}
                    for ci, c in enumerate(ch):
                        h, d, ti, pr, p0 = inf[c]
                        po, pbo = ps.t[4 + ci], ps.b[4 + ci]
                        ck = cc if d == 0 else 1 - cc
                        q0 = p0 + ck * 64
                        if cc == 0:
                            P.op("pe", lambda e, c=c, h=h, po=po: e.matmul(po[0:64, 0:128], lhsT=kvtok[c][:, 128 + h * 64:128 + (h + 1) * 64], rhs=at16[c][:], start=True, stop=False),
                                 reads=[Bkv[c], Bat[c]], writes=[pbo])
                        P.op("pe", lambda e, c=c, po=po, ck=ck, q0=q0, cc=cc: e.matmul(po[0:64, ck * 64:(ck + 1) * 64], lhsT=S16[c][:], rhs=qT[c][:, q0:q0 + 64],
                             start=False, stop=(cc == 1)), reads=[BS16[c], Bld[c]], writes=[pbo])
                        pu, pbu = ps.t[ci], ps.b[ci]
                        P.op("pe", lambda e, c=c, h=h, pu=pu, ck=ck: e.matmul(pu[:, 0:64], lhsT=kvtok[c][ck * 64:(ck + 1) * 64, 0:128],
                             rhs=kvtok[c][ck * 64:(ck + 1) * 64, 128 + h * 64:128 + (h + 1) * 64], start=True, stop=True), reads=[Bkv[c]], writes=[pbu])
                        pub[c] = (pu, pbu, ti * 8 + pr * 2 + ck)
                    for c in ch:
                        pu, pbu, cidx = pub[c]
                        P.op("dve", lambda e, c=c, pu=pu: e.tensor_tensor(out=Sh[c][:], in0=pu[:, 0:64], in1=S32[c][:], op=ALU.add), reads=[pbu, BS32[c]], writes=[BSh[c]])
                    for c in ch:
                        h, d = c
                        pu, pbu, cidx = pub[c]
                        P.op("dve", lambda e, c=c, h=h, d=d, cidx=cidx: e.tensor_scalar(out=S32[c][:], in0=Sh[c][:], scalar1=gam[h][d][:, cidx:cidx + 1], scalar2=None, op0=ALU.mult),
                             reads=[BSh[c], Bgam], writes=[BS32[c]])
                        P.op("act", lambda e, c=c, h=h, d=d, cidx=cidx: e.activation(out=S16[c][:], in_=Sh[c][:], func=AF.Copy, scale=gam[h][d][:, cidx:cidx + 1]),
                             reads=[BSh[c], Bgam], writes=[BS16[c]])
                for ci, c in enumerate(ch):
                    h, d, ti, pr, p0 = inf[c]
                    po, pbo = ps.t[4 + ci], ps.b[4 + ci]
                    t0_ = ti * TT + p0
                    P.op("dve", lambda e, h=h, po=po, t0_=t0_: e.tensor_tensor(out=oac[h][:, t0_:t0_ + 128], in0=oac[h][:, t0_:t0_ + 128], in1=po[0:64, 0:128], op=ALU.add),
                         reads=[pbo, Boac[h]], writes=[Boac[h]])
        o16c = [sb("o16c%d" % i, [64, 2048], BF16) for i in range(2)]; Bo16c = [Buf("o16c%d" % i) for i in range(2)]
        for h in range(2):
            for tq in range(4):
                u = (h * 4 + tq) % 2
                P.op("act" if u == 0 else "dve", (lambda e, h=h, tq=tq, u=u: e.activation(out=o16c[u][:], in_=oac[h][:, tq * 2048:(tq + 1) * 2048], func=AF.Copy)) if u == 0 else
                     (lambda e, h=h, tq=tq, u=u: e.tensor_copy(out=o16c[u][:], in_=oac[h][:, tq * 2048:(tq + 1) * 2048])), reads=[Boac[h]], writes=[Bo16c[u]])
                P.dma("sp", "o16c%d" % u, osrc[tq * 384 + h * 64:tq * 384 + (h + 1) * 64, :], o16c[u][:], reads=[Bo16c[u]], writes=[BoT])

    P.barrier()
    es2.close()
    es0.close()


RG4 = [[0, 1, 2, 3], [4, 5, 6, 7]]


def build_fused():
    nc = bass.Bass("TRN2", target_bir_lowering=False)
    dr = lambda n, s, kind="ExternalInput", dt=F32: nc.dram_tensor(n, s, dt, kind=kind).ap()
    shared = {"pos": dr("pos", [1, S], dt=I32), "etab": dr("etab", [128, 18, 256]), "ropec": dr("ropec", [64, 2]),
              "ident": dr("ident", [128, 128]), "masks": dr("masks", [128, 2, 128]), "scanmask": dr("scanmask", [128, 512]),
              "lbraw": dr("lbraw", [128, 4, 2])}
    xT = dr("xT", [1024, S]); xTq = dr("xTq", [1024, 2048]); oidx = dr("oidx", [128, 96], dt=I32)
    ioA, ioB = [], []
    for l in range(2):
        a = dict(shared)
        a.update({"wA": dr("wA%d" % l, [1024, 2816]), "bA": dr("bA%d" % l, [128, 23]), "wuq": dr("wuq%d" % l, [384, 256]), "gq": dr("gq%d" % l, [128, 3]),
                  "wukv": dr("wukv%d" % l, [256, 256]), "gkv": dr("gkv%d" % l, [128, 2])})
        ioA.append(a)
        ioB.append({"wg": dr("wg%d" % l, [1024, 4608]), "bg": dr("bg%d" % l, [128, 36]), "wbr": dr("wbr%d" % l, [1536, 1024]), "wo": dr("wo%d" % l, [1024, 1024]),
                    "hgn": dr("hgn%d" % l, [128, 4]), "lng": dr("lng%d" % l, [128, 8]), "lnb": dr("lnb%d" % l, [128, 8]), "oidx": oidx})
    outT = dr("outT", [1024, 2048], kind="ExternalOutput")
    cco_src = [nc.dram_tensor("cco_src%d" % l, [1536, 2048], BF16) for l in range(2)]
    cco_dst = [nc.dram_tensor("cco_dst%d" % l, [6 * 1024, 2048], BF16) for l in range(2)]
    ccx_src = nc.dram_tensor("ccx_src", [1024, 2048], BF16)
    ccx_dst = nc.dram_tensor("ccx_dst", [4096, 2048], BF16)
    xn32 = nc.dram_tensor("xn32", [1024, 2048], F32).ap()
    scr = make_scratch(nc)
    P = Prog(nc)
    ps = PsumPool(nc)
    Bxn = Buf("xn32"); Bxg = Buf("xg"); Bnone = Buf("none")
    for l in range(2):
        ioA[l]["osrc"] = cco_src[l].ap()
        if l == 0:
            ioA[l]["xT"] = xT
            emit_A(nc, P, ps, l, ioA[l], scr)
        else:
            ioA[l]["Bxg"] = Bxg
            emit_A(nc, P, ps, l, ioA[l], scr, xsrc16=ccx_dst.ap())
        P.barrier()
        Bod = Buf("cco_dst%d" % l)
        for k in range(6):
            P.dma("pool", "cc_o%d" % l, None, None, reads=[Bnone], writes=[Bod], inc=1,
                  fn=(lambda e, l=l, k=k: e.collective_compute("AllGather", ALU.bypass, replica_groups=RG4,
                                                             ins=[cco_src[l].ap()[k * 256:(k + 1) * 256, :].opt()],
                                                             outs=[cco_dst[l].ap()[k * 1024:(k + 1) * 1024, :].opt()])))
        b = ioB[l]
        b["orows"] = cco_dst[l].ap().rearrange("r (a c) -> (r a) c", c=256)
        b["Bodst"] = Bod
        if l == 0:
            b["x32src"] = xTq; b["Bxsrc"] = Bnone; b["out32"] = xn32; b["out16"] = ccx_src.ap()
        else:
            b["x32src"] = xn32; b["Bxsrc"] = Bxn; b["out32"] = outT
        emit_B(nc, P, ps, l, b)
        P.barrier()
        if l == 0:
            for k in range(4):
                P.dma("pool", "cc_x", None, None, reads=[Bnone], writes=[Bxg], inc=1,
                      fn=(lambda e, k=k: e.collective_compute("AllGather", ALU.bypass, replica_groups=RG4,
                                                            ins=[ccx_src.ap()[k * 256:(k + 1) * 256, :].opt()],
                                                            outs=[ccx_dst.ap()[k * 1024:(k + 1) * 1024, :].opt()])))
    P.barrier()
    P.emit()
    return nc


SPL = [1024,1024,1024,512,512] + [512]*10 + [384,256,64,512,3072]
NAMES = ['hg_q','hg_f_fwd','hg_f_bwd','hg_i','hg_g','dil_q0','dil_k0','dil_v0','dil_q1','dil_k1','dil_v1','dil_q2','dil_k2','dil_v2','dil_g','mla_cq','mla_ckv','mla_kr','mla_g','merge']
OFF = dict(zip(NAMES, [int(v) for v in np.cumsum([0]+SPL[:-1])]))
def a_cols(hq):
    ar = np.arange
    c = []
    c += [OFF['hg_q'] + (2*hq)*128 + ar(128), OFF['hg_q'] + (2*hq+1)*128 + ar(128)]
    c += [OFF['hg_f_fwd'] + (2*hq)*128 + ar(128), OFF['hg_f_fwd'] + (2*hq+1)*128 + ar(128)]
    c += [OFF['hg_f_bwd'] + (2*hq)*128 + ar(128), OFF['hg_f_bwd'] + (2*hq+1)*128 + ar(128)]
    c += [OFF['hg_i'] + hq*128 + ar(128)]
    for g in range(3):
        for t in 'qkv':
            c += [OFF['dil_%s%d' % (t, g)] + hq*128 + ar(128)]
    c += [OFF['mla_cq'] + ar(384), OFF['mla_ckv'] + ar(256)]
    kr = OFF['mla_kr'] + ar(64)
    c += [kr, np.concatenate([kr[32:], kr[:32]])]
    return np.concatenate(c)
def etab_np(hq):
    slopes = 2.0 ** (-8.0 * (np.arange(24) + 1) / 24)
    kk = np.arange(128)[:, None]; qq = np.arange(128)[None, :]
    E = np.zeros((128, 18, 256), np.float32)
    for g, d in enumerate((1, 4, 16)):
        for hh in range(2):
            sl = slopes[g*8 + 2*hq + hh]
            for var in range(3):
                for kc in range(2):
                    rel = (kk + 128*kc - 64) - qq
                    e = np.where(np.abs(rel) <= 64, np.exp(-sl * d * np.abs(rel)), 0.0)
                    if var == 1 and kc == 0: e = np.where(kk < 64, 0.0, e)
                    if var == 2 and kc == 1: e = np.where(kk >= 64, 0.0, e)
                    E[:, (g*2+hh)*3 + var, kc*128:(kc+1)*128] = e
    return E
def a_inputs(inp, l, b, hq, xT_b):
    cols = a_cols(hq)
    w_in = inp['w_in'][l]; b_in = inp['b_in'][l]
    bsel = b_in[cols]
    bA = np.zeros((128, 23), np.float32)
    bA[:, :22] = bsel.reshape(22, 128).T
    bA[:64, 22] = bsel[21*128+64: 22*128]
    lbraw = np.zeros((128, 4, 2), np.float32)
    for h in range(2):
        for d, nm in enumerate(('hg_lb_fwd', 'hg_lb_bwd')):
            lbraw[:, h*2+d, :] = inp[nm][:, (2*hq+h)*128:(2*hq+h+1)*128].T
    wuq = inp['w_uq'][l]
    qc = hq*192 + np.arange(192)
    rope = qc[128:]
    wuq_sel = np.concatenate([wuq[:, qc[:128]], wuq[:, rope], wuq[:, np.concatenate([rope[32:], rope[:32]])]], 1)
    wukv = inp['w_ukv'][l]
    wukv_sel = wukv[:, hq*256:(hq+1)*256]
    inv = (1.0 / (10000.0 ** (np.arange(32, dtype=np.float32) / 32))).astype(np.float32)
    ropec = np.zeros((64, 2), np.float32); ropec[:, 0] = np.concatenate([inv, inv]); ropec[:32, 1] = -1.0; ropec[32:, 1] = 1.0
    masks = np.zeros((128, 2, 128), np.float32)
    ss = np.arange(128)[:, None]; tq = np.arange(128)[None, :]
    same = (ss // 64) == (tq // 64)
    masks[:, 0, :] = (same & (ss <= tq)).astype(np.float32)
    masks[:, 1, :] = (same & (ss >= tq)).astype(np.float32)
    sm = np.ones((128, 512), np.float32); sm[:, ::64] = 0.0
    return {"pos": np.ascontiguousarray(inp['positions'][b:b+1].astype(np.int32)), "wA": np.ascontiguousarray(w_in[:, cols]), "bA": bA, "lbraw": lbraw,
            "wuq": np.ascontiguousarray(wuq_sel), "gq": np.ascontiguousarray(inp['mla_q_norm'][l].reshape(3,128).T),
            "wukv": np.ascontiguousarray(wukv_sel), "gkv": np.ascontiguousarray(inp['mla_kv_norm'][l].reshape(2,128).T),
            "etab": etab_np(hq), "ropec": ropec, "ident": np.eye(128, dtype=np.float32), "masks": masks, "scanmask": sm}


_CACHE = {}


def _b_inputs(inp, l):
    w_in = inp['w_in'][l]; b_in = inp['b_in'][l]
    cols = np.concatenate([np.arange(OFF['hg_g'], OFF['hg_g'] + 512), np.arange(OFF['dil_g'], OFF['dil_g'] + 512),
                           np.arange(OFF['mla_g'], OFF['mla_g'] + 512), np.arange(OFF['merge'], OFF['merge'] + 3072)])
    return {"wg%d" % l: np.ascontiguousarray(w_in[:, cols]), "bg%d" % l: np.ascontiguousarray(b_in[cols].reshape(36, 128).T),
            "wbr%d" % l: np.ascontiguousarray(inp['w_branch'][l].reshape(1536, 1024)), "wo%d" % l: np.ascontiguousarray(inp['w_out'][l]),
            "hgn%d" % l: np.ascontiguousarray(inp['hg_norm'][l].reshape(4, 128).T),
            "lng%d" % l: np.ascontiguousarray(inp['ln_g'][l].reshape(8, 128).T), "lnb%d" % l: np.ascontiguousarray(inp['ln_b'][l].reshape(8, 128).T)}


def _oidx(tq):
    p = np.arange(128)[:, None, None, None]; tt = np.arange(8)[None, :, None, None]
    n = np.arange(3)[None, None, :, None]; r = np.arange(4)[None, None, None, :]
    rho = tq * 384 + n * 128 + p
    g = (rho // 256) * 1024 + r * 256 + (rho % 256)
    v = g * 8 + tt
    return np.ascontiguousarray(v.reshape(128, 96).astype(np.int32))


def kernel(**inputs):
    inp = {k: np.asarray(v) for k, v in inputs.items()}
    inp['positions'] = inp['positions'].astype(np.int32)
    for k in inp:
        if k != 'positions':
            inp[k] = inp[k].astype(np.float32, copy=False)
    B = 2
    xT = [np.ascontiguousarray(inp['x'][b].T) for b in range(B)]
    if "nc" not in _CACHE:
        _CACHE["nc"] = build_fused()
    nc = _CACHE["nc"]
    bl = [_b_inputs(inp, l) for l in range(2)]
    in_maps = []
    for c in range(8):
        b, q = c // 4, c % 4
        m = {"xT": xT[b], "xTq": np.ascontiguousarray(xT[b][:, q * 2048:(q + 1) * 2048]), "oidx": _oidx(q)}
        for l in range(2):
            a = a_inputs(inp, l, b, q, None)
            for k in ("pos", "etab", "ropec", "ident", "masks", "scanmask", "lbraw"):
                m[k] = a[k]
            for k in ("wA", "bA", "wuq", "gq", "wukv", "gkv"):
                m["%s%d" % (k, l)] = a[k]
            m.update(bl[l])
        in_maps.append(m)
    res = run_bass_kernel_spmd(nc, in_maps, core_ids=list(range(8))).results
    out = np.empty((B, 8192, 1024), np.float32)
    for c in range(8):
        b, q = c // 4, c % 4
        out[b, q * 2048:(q + 1) * 2048, :] = np.asarray(res[c]["outT"]).T
    return out
```

```python
import math
from contextlib import ExitStack
import numpy as np
from concourse.bass_utils import run_bass_kernel_spmd
import concourse.bass as bass
import concourse.mybir as mybir

F32 = mybir.dt.float32
BF16 = mybir.dt.bfloat16
I32 = mybir.dt.int32
AF = mybir.ActivationFunctionType
ALU = mybir.AluOpType
AX = mybir.AxisListType


class Buf:
    __slots__ = ("name", "w", "r")

    def __init__(self, name=""):
        self.name = name
        self.w = {}
        self.r = {}


class _Eng:
    def __init__(self, name, sem):
        self.name = name
        self.sem = sem
        self.count = 0
        self.waited = {}
        self.items = []


class Prog:
    ENGS = ("pe", "act", "dve", "pool", "sp")

    def __init__(self, nc):
        self.nc = nc
        self.e = {n: _Eng(n, nc.alloc_semaphore("prog_" + n)) for n in self.ENGS}
        self.chan = {}
        self.nops = 0
        self.retired = []

    def _need(self, eng, waits, ev, raw):
        sem, val, en = ev
        if en == eng.name:
            if eng.name == "pe" or not raw:
                return
        k = id(sem)
        if eng.waited.get(k, 0) >= val:
            return
        if k not in waits or waits[k][1] < val:
            waits[k] = (sem, val)

    def _deps(self, eng, reads, writes):
        waits = {}
        for b in reads:
            for ev in b.w.values():
                self._need(eng, waits, ev, True)
        for b in writes:
            for ev in b.w.values():
                self._need(eng, waits, ev, False)
            for ev in b.r.values():
                self._need(eng, waits, ev, False)
        for k, (sem, val) in waits.items():
            eng.waited[k] = val
        return list(waits.values())

    @staticmethod
    def _mark(ev, reads, writes):
        k = id(ev[0])
        for b in reads:
            o = b.r.get(k)
            if o is None or o[1] < ev[1]:
                b.r[k] = ev
        for b in writes:
            o = b.w.get(k)
            if o is None or o[1] < ev[1]:
                b.w[k] = ev

    def op(self, engname, fn, reads=(), writes=()):
        eng = self.e[engname]
        waits = self._deps(eng, reads, writes)
        if eng.count >= 30000:
            self.retired.append((eng.sem, eng.count))
            eng.sem = self.nc.alloc_semaphore("prog_%s_%d" % (engname, self.nops))
            eng.count = 0
        eng.count += 1
        ev = (eng.sem, eng.count, eng.name)
        eng.items.append((waits, fn, (eng.sem, 1)))
        self._mark(ev, reads, writes)
        self.nops += 1
        return ev

    def dma(self, qname, chan, out, in_, reads=(), writes=(), fn=None, inc=16):
        eng = self.e[qname]
        waits = self._deps(eng, reads, writes)
        if chan not in self.chan:
            self.chan[chan] = [self.nc.alloc_semaphore("ch_" + chan), 0]
        c = self.chan[chan]
        if c[1] >= 30000:
            self.retired.append((c[0], c[1]))
            c[0] = self.nc.alloc_semaphore("ch_%s_%d" % (chan, self.nops))
            c[1] = 0
        c[1] += inc
        ev = (c[0], c[1], "dma")
        if fn is None:
            fn = (lambda e, o=out, i=in_: e.dma_start(out=o, in_=i))
        eng.items.append((waits, fn, (c[0], inc)))
        self._mark(ev, reads, writes)
        self.nops += 1
        return ev

    def barrier(self):
        evs = list(self.retired)
        for n in self.ENGS:
            if self.e[n].count > 0:
                evs.append((self.e[n].sem, self.e[n].count))
        for c in self.chan.values():
            evs.append((c[0], c[1]))
        for n in self.ENGS:
            eng = self.e[n]
            waits = []
            for sem, val in evs:
                if eng.waited.get(id(sem), 0) < val:
                    waits.append((sem, val))
                    eng.waited[id(sem)] = val
            eng.items.append((waits, None, None))

    def wait_all(self, engname, bufs):
        eng = self.e[engname]
        waits = self._deps(eng, bufs, bufs)
        eng.items.append((waits, None, None))

    def emit(self):
        nc = self.nc
        with nc.Block() as block:
            def run(eng, h):
                for waits, fn, inc in eng.items:
                    for sem, val in waits:
                        h.wait_ge(sem, val)
                    if fn is not None:
                        ins = fn(h)
                        ins.then_inc(inc[0], inc[1])

            @block.tensor
            def _(h):
                run(self.e["pe"], h)

            @block.scalar
            def _(h):
                run(self.e["act"], h)

            @block.vector
            def _(h):
                run(self.e["dve"], h)

            @block.gpsimd
            def _(h):
                run(self.e["pool"], h)

            @block.sync
            def _(h):
                run(self.e["sp"], h)


ALPHA = 4.0 ** 0.25
LN_EPS = 1e-5


class PsumPool:
    def __init__(self, nc, n=8):
        self.t = [nc.alloc_psum_tensor("psb%d" % i, [128, 512], F32) for i in range(n)]
        self.b = [Buf("psb%d" % i) for i in range(n)]
        self.i = 0
        self.n = n

    def next(self):
        i = self.i
        self.i = (i + 1) % self.n
        return self.t[i], self.b[i]


def load_cast_weight(P, nc, q, dram2d, dst16, dstbuf, kchunks, ncols, stage, stage_bufs, ctr, colsplit):
    v = dram2d.rearrange("(k p) c -> p k c", p=128)
    for k in range(kchunks):
        for c0 in range(0, ncols, colsplit):
            cw = min(colsplit, ncols - c0)
            s = ctr[0] % len(stage)
            ctr[0] += 1
            P.dma(q, "wst%d" % s, stage[s][:, 0:cw], v[:, k, c0:c0 + cw], writes=[stage_bufs[s]])
            eng = "dve" if (ctr[0] % 2 == 0) else "pool"
            P.op(eng, (lambda e, s=s, k=k, c0=c0, cw=cw: e.tensor_copy(out=dst16[:, k, c0:c0 + cw], in_=stage[s][:, 0:cw])),
                 reads=[stage_bufs[s]], writes=[dstbuf])


def emit_B(nc, P, ps, layer, io):
    T = 2048
    TT = 256
    NT = T // TT
    xT = io["x32src"]; wg = io["wg"]; bg = io["bg"]; wbr = io["wbr"]; wo = io["wo"]; hgn = io["hgn"]; lng = io["lng"]; lnb = io["lnb"]
    orows = io["orows"]; oidx = io["oidx"]; Bodst = io["Bodst"]; Bxsrc = io["Bxsrc"]
    out32 = io["out32"]; out16 = io.get("out16")
    esb = ExitStack()
    sb = lambda n, s, dt=F32: esb.enter_context(nc.sbuf_tensor("%s_B%d" % (n, layer), s, dt))
    wg16 = sb("wg16", [128, 8, 4608], BF16); Bwg = Buf("wg16")
    wbr16 = sb("wbr16", [128, 12, 1024], BF16); Bwbr = Buf("wbr16")
    wo16 = sb("wo16", [128, 8, 1024], BF16); Bwo = Buf("wo16")
    stage = [sb("wstage%d" % i, [128, 1152], F32) for i in range(2)]
    stage_b = [Buf("wstage%d" % i) for i in range(2)]
    bgs = sb("bgs", [128, 36]); hgns = sb("hgns", [128, 4]); lngs = sb("lngs", [128, 8]); lnbs = sb("lnbs", [128, 8])
    oix = sb("oix", [128, 96], I32)
    Bc = Buf("consts")
    ones32 = sb("ones32", [128, 128]); Bones = Buf("ones")
    epsr = sb("epsr", [128, 1]); epsl = sb("epsl", [128, 1])
    x32 = sb("x32", [128, 8, TT]); Bx32 = Buf("x32")
    x16 = sb("x16", [128, 8, TT], BF16); Bx16 = Buf("x16")
    o32 = sb("o16", [128, 12, TT], BF16); Bo32 = Buf("o16")
    y16 = sb("y16", [128, 12, TT], BF16); By16 = [Buf("y16_%d" % i) for i in range(12)]
    gt = [sb("gt%d" % i, [128, TT]) for i in range(2)]; Bgt = [Buf("gt%d" % i) for i in range(2)]
    sq = sb("sq", [128, 8, TT]); Bsq = Buf("sq")
    rstd = sb("rstd", [128, TT]); Brstd = Buf("rstd")
    tmp = sb("tmp", [128, TT]); Btmp = Buf("tmp")
    sg = [sb("sg%d" % i, [128, 3, TT]) for i in range(2)]; Bsg = [Buf("sg%d" % i) for i in range(2)]
    mm = sb("mm", [128, TT]); Bmm = Buf("mm")
    tt2 = sb("tt2", [128, TT]); Btt2 = Buf("tt2")
    mg16 = sb("mg16", [128, 8, TT], BF16); Bmg = [Buf("mg%d" % i) for i in range(8)]
    r32 = sb("r32", [128, 8, TT]); Br = [Buf("r%d" % i) for i in range(8)]
    mean = sb("mean", [128, TT]); Bmean = Buf("mean")
    ob = [sb("ob%d" % i, [128, TT]) for i in range(2)]; Bob = [Buf("ob%d" % i) for i in range(2)]
    ob16 = [sb("ob16_%d" % i, [128, TT], BF16) for i in range(2)]; Bob16 = [Buf("ob16_%d" % i) for i in range(2)]
    Bout = Buf("xnT")
    xnT = out32

    P.dma("sp", "c0", bgs[:], bg, writes=[Bc])
    P.dma("sp", "c4", oix[:], oidx, writes=[Bc])
    P.dma("sp", "c1", hgns[:], hgn, writes=[Bc])
    P.dma("sp", "c2", lngs[:], lng, writes=[Bc])
    P.dma("sp", "c3", lnbs[:], lnb, writes=[Bc])
    P.op("pool", lambda e: e.memset(ones32[:], 1.0), writes=[Bones])
    P.op("pool", lambda e: e.memset(epsr[:], RMS_EPS), writes=[Bc])
    P.op("pool", lambda e: e.memset(epsl[:], LN_EPS), writes=[Bc])
    ctr = [0]
    load_cast_weight(P, nc, "sp", wg, wg16, Bwg, 8, 4608, stage, stage_b, ctr, 1152)
    load_cast_weight(P, nc, "sp", wbr, wbr16, Bwbr, 12, 1024, stage, stage_b, ctr, 1024)
    load_cast_weight(P, nc, "sp", wo, wo16, Bwo, 8, 1024, stage, stage_b, ctr, 1024)

    xv = xT.rearrange("(k p) t -> p k t", p=128)
    outv = xnT.rearrange("(k p) t -> p k t", p=128)
    gi = 0
    for tt in range(NT):
        c0 = tt * TT
        P.dma("act", "x32", x32[:], xv[:, :, c0:c0 + TT], reads=[Bxsrc], writes=[Bx32])
        for blk in range(12):
            P.dma("pool", "o16g", None, None, reads=[Bodst, Bc], writes=[Bo32],
                  fn=(lambda e, blk=blk, tt=tt: e.indirect_dma_start(out=o32[:, blk, :], out_offset=None, in_=orows,
                                                                   in_offset=bass.IndirectOffsetOnAxis(ap=oix[:, tt * 12 + blk:tt * 12 + blk + 1], axis=0))))
        P.op("pool", lambda e: e.tensor_copy(out=x16[:], in_=x32[:]), reads=[Bx32], writes=[Bx16])
        P.op("act", lambda e: e.activation(out=sq[:, 0:4, :], in_=o32[:, 0:4, :], func=AF.Square), reads=[Bo32], writes=[Bsq])
        pt, pb = ps.next()
        for j in range(4):
            P.op("pe", lambda e, j=j, pt=pt: e.matmul(pt[:, 0:TT], lhsT=ones32[:], rhs=sq[:, j, :], start=(j == 0), stop=(j == 3)),
                 reads=[Bones, Bsq], writes=[pb])
        P.op("act", lambda e, pt=pt: e.activation(out=tmp[:], in_=pt[:, 0:TT], func=AF.Sqrt, bias=epsr[:, 0:1], scale=1.0 / 512.0), reads=[pb, Bc], writes=[Btmp])
        P.op("dve", lambda e: e.reciprocal(out=rstd[:], in_=tmp[:]), reads=[Btmp], writes=[Brstd])
        for blk in range(12):
            pt, pb = ps.next()
            for k in range(8):
                P.op("pe", lambda e, k=k, pt=pt, blk=blk: e.matmul(pt[:, 0:TT], lhsT=wg16[:, k, blk * 128:(blk + 1) * 128], rhs=x16[:, k, :],
                                                              start=(k == 0), stop=(k == 7)), reads=[Bwg, Bx16], writes=[pb])
            g = gi % 2
            gi += 1
            P.op("act", lambda e, pt=pt, blk=blk, g=g: e.activation(out=gt[g][:], in_=pt[:, 0:TT], func=AF.Silu, bias=bgs[:, blk:blk + 1], scale=1.0),
                 reads=[pb, Bc], writes=[Bgt[g]])
            if blk < 4:
                P.op("dve", lambda e, g=g: e.tensor_tensor(out=gt[g][:], in0=gt[g][:], in1=rstd[:], op=ALU.mult),
                     reads=[Bgt[g], Brstd], writes=[Bgt[g]])
                P.op("dve", lambda e, g=g, blk=blk: e.scalar_tensor_tensor(out=y16[:, blk, :], in0=o32[:, blk, :], scalar=hgns[:, blk:blk + 1],
                                                                           in1=gt[g][:], op0=ALU.mult, op1=ALU.mult),
                     reads=[Bo32, Bgt[g], Bc], writes=[By16[blk]])
            else:
                P.op("dve", lambda e, g=g, blk=blk: e.tensor_tensor(out=y16[:, blk, :], in0=o32[:, blk, :], in1=gt[g][:], op=ALU.mult),
                     reads=[Bo32, Bgt[g]], writes=[By16[blk]])
        for db in range(8):
            s = db % 2
            pbs = []
            for n in range(3):
                pt, pb = ps.next()
                col = 1536 + n * 1024 + db * 128
                for k in range(8):
                    P.op("pe", lambda e, k=k, pt=pt, col=col: e.matmul(pt[:, 0:TT], lhsT=wg16[:, k, col:col + 128], rhs=x16[:, k, :],
                                                                  start=(k == 0), stop=(k == 7)), reads=[Bwg, Bx16], writes=[pb])
                bi = 12 + n * 8 + db
                P.op("act", lambda e, pt=pt, n=n, s=s, bi=bi: e.activation(out=sg[s][:, n, :], in_=pt[:, 0:TT], func=AF.Sigmoid, bias=bgs[:, bi:bi + 1], scale=1.0),
                     reads=[pb, Bc], writes=[Bsg[s]])
            for n in range(3):
                pt, pb = ps.next()
                for j in range(4):
                    P.op("pe", lambda e, j=j, n=n, pt=pt, db=db: e.matmul(pt[:, 0:TT], lhsT=wbr16[:, n * 4 + j, db * 128:(db + 1) * 128], rhs=y16[:, n * 4 + j, :],
                                                                     start=(j == 0), stop=(j == 3)), reads=[Bwbr, By16[n * 4 + j]], writes=[pb])
                pbs.append((pt, pb))
            P.op("dve", lambda e, s=s, p0=pbs[0][0]: e.tensor_tensor(out=mm[:], in0=p0[:, 0:TT], in1=sg[s][:, 0, :], op=ALU.mult),
                 reads=[pbs[0][1], Bsg[s]], writes=[Bmm])
            P.op("dve", lambda e, s=s, p1=pbs[1][0]: e.tensor_tensor(out=tt2[:], in0=p1[:, 0:TT], in1=sg[s][:, 1, :], op=ALU.mult),
                 reads=[pbs[1][1], Bsg[s]], writes=[Btt2])
            P.op("pool", lambda e: e.tensor_tensor(out=mm[:], in0=mm[:], in1=tt2[:], op=ALU.add), reads=[Bmm, Btt2], writes=[Bmm])
            P.op("dve", lambda e, s=s, p2=pbs[2][0]: e.tensor_tensor(out=tt2[:], in0=p2[:, 0:TT], in1=sg[s][:, 2, :], op=ALU.mult),
                 reads=[pbs[2][1], Bsg[s]], writes=[Btt2])
            P.op("pool", lambda e, db=db: e.tensor_tensor(out=mg16[:, db, :], in0=mm[:], in1=tt2[:], op=ALU.add),
                 reads=[Bmm, Btt2], writes=[Bmg[db]])
        for eb in range(8):
            pt, pb = ps.next()
            for d in range(8):
                P.op("pe", lambda e, d=d, pt=pt, eb=eb: e.matmul(pt[:, 0:TT], lhsT=wo16[:, d, eb * 128:(eb + 1) * 128], rhs=mg16[:, d, :],
                                                            start=(d == 0), stop=(d == 7)), reads=[Bwo, Bmg[d]], writes=[pb])
            P.op("dve", lambda e, pt=pt, eb=eb: e.scalar_tensor_tensor(out=r32[:, eb, :], in0=x32[:, eb, :], scalar=ALPHA, in1=pt[:, 0:TT],
                                                                        op0=ALU.mult, op1=ALU.add), reads=[Bx32, pb], writes=[Br[eb]])
        pt, pb = ps.next()
        for eb in range(8):
            P.op("pe", lambda e, eb=eb, pt=pt: e.matmul(pt[:, 0:TT], lhsT=ones32[:], rhs=r32[:, eb, :], start=(eb == 0), stop=(eb == 7)),
                 reads=[Bones, Br[eb]], writes=[pb])
        P.op("act", lambda e, pt=pt: e.activation(out=mean[:], in_=pt[:, 0:TT], func=AF.Copy, scale=1.0 / 1024.0), reads=[pb], writes=[Bmean])
        for eb in range(8):
            P.op("dve", lambda e, eb=eb: e.tensor_tensor(out=r32[:, eb, :], in0=r32[:, eb, :], in1=mean[:], op=ALU.subtract),
                 reads=[Br[eb], Bmean], writes=[Br[eb]])
        P.op("act", lambda e: e.activation(out=sq[:], in_=r32[:], func=AF.Square), reads=Br, writes=[Bsq])
        pt, pb = ps.next()
        for eb in range(8):
            P.op("pe", lambda e, eb=eb, pt=pt: e.matmul(pt[:, 0:TT], lhsT=ones32[:], rhs=sq[:, eb, :], start=(eb == 0), stop=(eb == 7)),
                 reads=[Bones, Bsq], writes=[pb])
        P.op("act", lambda e, pt=pt: e.activation(out=tmp[:], in_=pt[:, 0:TT], func=AF.Sqrt, bias=epsl[:, 0:1], scale=1.0 / 1024.0), reads=[pb, Bc], writes=[Btmp])
        P.op("dve", lambda e: e.reciprocal(out=rstd[:], in_=tmp[:]), reads=[Btmp], writes=[Brstd])
        for eb in range(8):
            s = eb % 2
            P.op("dve", lambda e, eb=eb: e.tensor_tensor(out=r32[:, eb, :], in0=r32[:, eb, :], in1=rstd[:], op=ALU.mult),
                 reads=[Br[eb], Brstd], writes=[Br[eb]])
            P.op("act", lambda e, eb=eb, s=s: e.activation(out=ob[s][:], in_=r32[:, eb, :], func=AF.Identity, bias=lnbs[:, eb:eb + 1], scale=lngs[:, eb:eb + 1]),
                 reads=[Br[eb], Bc], writes=[Bob[s]])
            P.dma("sp", "ob%d" % s, outv[:, eb, c0:c0 + TT], ob[s][:], reads=[Bob[s]], writes=[Bout])
            if out16 is not None:
                P.op("pool", lambda e, s=s: e.tensor_copy(out=ob16[s][:], in_=ob[s][:]), reads=[Bob[s]], writes=[Bob16[s]])
                jx = c0 // 512
                P.dma("sp", "ob16_%d" % s, out16[jx * 1024 + eb * 128:jx * 1024 + (eb + 1) * 128, (c0 % 512):(c0 % 512) + TT], ob16[s][:],
                      reads=[Bob16[s]], writes=[Bout, io["Bccx"][jx]])
        if out16 is not None and (c0 % 512) + TT == 512:
            io["cc_x"](c0 // 512)
    P.barrier()
    esb.close()


RMS_EPS = 1e-6
S = 8192
TT = 512
NT = S // TT
QSCALE = 192.0 ** -0.5
TWO_PI = 2.0 * math.pi
C1 = 6.28125
C2 = TWO_PI - C1
DILS = (1, 4, 16)
LN_MINF = math.log(1e-6)


def make_scratch(nc):
    dr = lambda n, s, dt=BF16: nc.dram_tensor(n, s, dt, kind="Internal").ap()
    scr = {}
    scr["dsub"] = [[dr("dsub%d_%d" % (g, t), [128, DILS[g], S // DILS[g]]) for t in range(3)] for g in range(3)]
    scr["hq_d"] = [[dr("hq%d_%d" % (h, d), [128, S]) for d in range(2)] for h in range(2)]
    scr["hk_d"] = [[dr("hk%d_%d" % (h, d), [128, S]) for d in range(2)] for h in range(2)]
    scr["hv_d"] = dr("hv", [128, S])
    scr["qd1"] = dr("qd1", [128, S]); scr["qd2"] = dr("qd2", [64, S])
    return scr


def emit_A(nc, P, ps, layer, io, scr, xsrc16=None, phases=("mla", "dil", "hg")):
    debug = False
    xT = io.get("xT"); pos = io["pos"]
    wA = io["wA"]; bA = io["bA"]; lbraw = io["lbraw"]
    wuq = io["wuq"]; gq = io["gq"]; wukv = io["wukv"]; gkv = io["gkv"]
    etab = io["etab"]; ropec = io["ropec"]; ident = io["ident"]; masks = io["masks"]; scanmask = io["scanmask"]
    osrc = io["osrc"]
    dsub = scr["dsub"]; hq_d = scr["hq_d"]; hk_d = scr["hk_d"]; hv_d = scr["hv_d"]; qd1 = scr["qd1"]; qd2 = scr["qd2"]
    Bdsub = Buf("dsub"); Bhqk = Buf("hqk"); BoT = Buf("oT"); Bqd = Buf("qd")
    es0 = ExitStack()
    sb = lambda n, s, dt=F32: es0.enter_context(nc.sbuf_tensor("%s_A%d" % (n, layer), s, dt))

    Bc = Buf("consts")
    bAs = sb("bAs", [128, 23]); lbr = sb("lbr", [128, 4, 2]); lbt = sb("lbt", [128, 4, 3])
    gqs = sb("gqs", [128, 3]); gkvs = sb("gkvs", [128, 2]); ropecs = sb("ropecs", [64, 2])
    ones32 = sb("ones32", [128, 128]); epsr = sb("epsr", [128, 1]); id32 = sb("id32", [128, 128]); id16 = sb("id16", [128, 128], BF16)
    ones16 = sb("ones16", [128, 128], BF16)
    msk = sb("msk", [128, 2, 128]); smask = sb("smask", [128, TT])
    P.dma("sp", "c0", bAs[:], bA, writes=[Bc])
    P.dma("sp", "c1", lbr[:], lbraw, writes=[Bc])
    P.dma("sp", "c2", gqs[:], gq, writes=[Bc])
    P.dma("sp", "c3", gkvs[:], gkv, writes=[Bc])
    P.dma("sp", "c4", ropecs[:], ropec, writes=[Bc])
    P.dma("sp", "c5", id32[:], ident, writes=[Bc])
    P.dma("sp", "c6", msk[:], masks, writes=[Bc])
    P.dma("sp", "c7", smask[:], scanmask, writes=[Bc])
    P.op("pool", lambda e: e.memset(ones32[:], 1.0), writes=[Bc])
    P.op("pool", lambda e: e.memset(ones16[:], 1.0), writes=[Bc])
    P.op("pool", lambda e: e.memset(epsr[:], RMS_EPS), writes=[Bc])
    P.op("pool", lambda e: e.tensor_copy(out=id16[:], in_=id32[:]), reads=[Bc], writes=[Bc])
    for blk in (7, 10, 13):
        P.op("dve", lambda e, blk=blk: e.tensor_scalar(out=bAs[:, blk:blk + 1], in0=bAs[:, blk:blk + 1], scalar1=0.125, scalar2=None, op0=ALU.mult),
             reads=[Bc], writes=[Bc])
    if layer == 0:
        P.op("dve", lambda e: e.memset(lbt[:, :, 0], 0.0), writes=[Bc])
    else:
        P.op("dve", lambda e: e.tensor_tensor(out=lbt[:, :, 1], in0=lbr[:, :, 1], in1=lbr[:, :, 0], op=ALU.subtract), reads=[Bc], writes=[Bc])
        P.op("act", lambda e: e.activation(out=lbt[:, :, 0], in_=lbt[:, :, 1], func=AF.Sigmoid), reads=[Bc], writes=[Bc])
    P.op("dve", lambda e: e.tensor_scalar(out=lbt[:, :, 1], in0=lbt[:, :, 0], scalar1=-1.0, scalar2=1.0, op0=ALU.mult, op1=ALU.add), reads=[Bc], writes=[Bc])
    P.op("dve", lambda e: e.tensor_scalar(out=lbt[:, :, 2], in0=lbt[:, :, 1], scalar1=-1.0, scalar2=None, op0=ALU.mult), reads=[Bc], writes=[Bc])

    K1T = sb("K1T", [128, S], BF16); K2T = sb("K2T", [64, S], BF16)
    Vtok = sb("Vtok", [128, S // 128, 128], BF16)
    BQ = Buf("Q"); BK = Buf("K"); BV = Buf("V")
    mT = [[sb("mT%d%d" % (h, d), [128, 128]) for d in range(2)] for h in range(2)]
    BTt = [[sb("BT%d%d" % (h, d), [128, 128]) for d in range(2)] for h in range(2)]
    BmB = Buf("mB")

    es1 = ExitStack()
    sb = lambda n, s, dt=F32: es1.enter_context(nc.sbuf_tensor("%s_A%d" % (n, layer), s, dt))
    win16 = sb("win16", [128, 8, 2816], BF16); Bwin = Buf("win16")
    stage = [sb("wstage%d" % i, [128, 704], F32) for i in range(2)]
    stage_b = [Buf("wstage%d" % i) for i in range(2)]
    wv = wA.rearrange("(k p) c -> p k c", p=128)
    ci = 0
    for k in range(8):
        for c0 in (0, 704, 1408, 2112):
            s = ci % 2
            P.dma("sp", "wst%d" % s, stage[s][:], wv[:, k, c0:c0 + 704], writes=[stage_b[s]])
            P.op("dve" if ci % 2 == 0 else "pool", lambda e, s=s, k=k, c0=c0: e.tensor_copy(out=win16[:, k, c0:c0 + 704], in_=stage[s][:]),
                 reads=[stage_b[s]], writes=[Bwin])
            ci += 1
    wuq16 = sb("wuq16", [128, 3, 256], BF16); wukv16 = sb("wukv16", [128, 2, 256], BF16); Bwu = Buf("wu")
    wuv = wuq.rearrange("(k p) c -> p k c", p=128)
    wkv = wukv.rearrange("(k p) c -> p k c", p=128)
    for j in range(3):
        s = ci % 2
        P.dma("sp", "wst%d" % s, stage[s][:, 0:256], wuv[:, j, :], writes=[stage_b[s]])
        P.op("dve", lambda e, s=s, j=j: e.tensor_scalar(out=wuq16[:, j, :], in0=stage[s][:, 0:256], scalar1=gqs[:, j:j + 1], scalar2=None, op0=ALU.mult),
             reads=[stage_b[s], Bc], writes=[Bwu])
        ci += 1
    for j in range(2):
        s = ci % 2
        P.dma("sp", "wst%d" % s, stage[s][:, 0:256], wkv[:, j, :], writes=[stage_b[s]])
        P.op("dve", lambda e, s=s, j=j: e.tensor_scalar(out=wukv16[:, j, :], in0=stage[s][:, 0:256], scalar1=gkvs[:, j:j + 1], scalar2=None, op0=ALU.mult),
             reads=[stage_b[s], Bc], writes=[Bwu])
        ci += 1

    x32s = [sb("x32_%d" % i, [128, 4, TT]) for i in range(2)]; Bx32s = [Buf("x32_%d" % i) for i in range(2)]
    x16s = [sb("x16_%d" % i, [128, 8, TT], BF16) for i in range(2)]; Bx16s = [Buf("x16_%d" % i) for i in range(2)]
    xcur = [None, None]
    posi = sb("posi", [64, TT], I32); Bposi = Buf("posi")
    ang = sb("ang", [64, TT]); Bang = Buf("ang")
    ru = sb("ru", [64, TT]); Bru = Buf("ru"); rki = sb("rki", [64, TT], I32); Brki = Buf("rki"); rkf = sb("rkf", [64, TT]); Brkf = Buf("rkf")
    cs = sb("cs", [64, TT]); sn = sb("sn", [64, TT]); Bcs = Buf("cs"); Bsn = Buf("sn")
    c32 = sb("c32", [128, 3, TT]); Bc32 = Buf("c32"); csq = sb("csq", [128, 3, TT]); Bcsq = Buf("csq")
    cn16 = sb("cn16", [128, 3, TT], BF16); Bcn = Buf("cn16")
    rt = sb("rt", [128, TT]); Brt = Buf("rt"); rr = sb("rr", [128, TT]); Brr = Buf("rr")
    t1 = sb("t1", [64, TT]); t2 = sb("t2", [64, TT]); Bt1 = Buf("t1"); Bt2 = Buf("t2")
    q1s = sb("q1s", [128, TT], BF16); q2s = sb("q2s", [64, TT], BF16); Bq1s = Buf("q1s"); Bq2s = Buf("q2s")
    dd16 = [sb("dd16_%d" % i, [128, TT], BF16) for i in range(2)]; Bdd = [Buf("dd16_%d" % i) for i in range(2)]
    qs = [sb("qs%d" % h, [128, TT]) for h in range(2)]; Bqs = [Buf("qs%d" % h) for h in range(2)]
    sig = sb("sig", [128, TT]); Bsig = Buf("sig"); ff = sb("ff", [128, TT]); Bff = Buf("ff")
    bb = sb("bb", [128, TT]); Bbb = Buf("bb"); eq = sb("eq", [128, TT]); Beq = Buf("eq"); ek = sb("ek", [128, TT]); Bek = Buf("ek")
    kk = sb("kk", [128, TT]); Bkk = Buf("kk")
    hq16 = [sb("hq16_%d" % i, [128, TT], BF16) for i in range(2)]; Bhq16 = [Buf("hq16_%d" % i) for i in range(2)]
    hk16 = [sb("hk16_%d" % i, [128, TT], BF16) for i in range(2)]; Bhk16 = [Buf("hk16_%d" % i) for i in range(2)]
    hv16 = sb("hv16", [128, TT], BF16); Bhv16 = Buf("hv16")

    if xsrc16 is None:
        xv = xT.rearrange("(k p) t -> p k t", p=128)
    else:
        xg = xsrc16.rearrange("(j r k p) t -> p j r k t", j=4, r=4, k=8, p=128)

    def inproj(col, m, rhs_cols=None):
        pt, pb = ps.next()
        xx, bxx = xcur[0], xcur[1]
        for k in range(8):
            P.op("pe", lambda e, k=k, pt=pt, xx=xx: e.matmul(pt[0:m, :], lhsT=win16[:, k, col:col + m], rhs=xx[:, k, :], start=(k == 0), stop=(k == 7)),
                 reads=[Bwin, bxx], writes=[pb])
        return pt, pb

    def sintab(dst, Bdst, shift):
        P.op("dve", lambda e: e.tensor_scalar(out=ru[:], in0=ang[:], scalar1=1.0 / TWO_PI, scalar2=shift / TWO_PI + 0.5, op0=ALU.mult, op1=ALU.add),
             reads=[Bang], writes=[Bru])
        P.op("dve", lambda e: e.tensor_copy(out=rki[:], in_=ru[:]), reads=[Bru], writes=[Brki])
        P.op("dve", lambda e: e.tensor_copy(out=rkf[:], in_=rki[:]), reads=[Brki], writes=[Brkf])
        P.op("dve", lambda e: e.tensor_scalar(out=ru[:], in0=ang[:], scalar1=shift, scalar2=None, op0=ALU.add), reads=[Bang], writes=[Bru])
        P.op("dve", lambda e: e.scalar_tensor_tensor(out=ru[:], in0=rkf[:], scalar=-C1, in1=ru[:], op0=ALU.mult, op1=ALU.add),
             reads=[Brkf, Bru], writes=[Bru])
        P.op("dve", lambda e: e.scalar_tensor_tensor(out=ru[:], in0=rkf[:], scalar=-C2, in1=ru[:], op0=ALU.mult, op1=ALU.add),
             reads=[Brkf, Bru], writes=[Bru])
        P.op("dve", lambda e: e.tensor_scalar(out=rkf[:], in0=ru[:], scalar1=math.pi, scalar2=None, op0=ALU.is_gt), reads=[Bru], writes=[Brkf])
        P.op("dve", lambda e: e.scalar_tensor_tensor(out=ru[:], in0=rkf[:], scalar=-TWO_PI, in1=ru[:], op0=ALU.mult, op1=ALU.add),
             reads=[Brkf, Bru], writes=[Bru])
        P.op("dve", lambda e: e.tensor_scalar(out=rkf[:], in0=ru[:], scalar1=-math.pi, scalar2=None, op0=ALU.is_lt), reads=[Bru], writes=[Brkf])
        P.op("dve", lambda e: e.scalar_tensor_tensor(out=ru[:], in0=rkf[:], scalar=TWO_PI, in1=ru[:], op0=ALU.mult, op1=ALU.add),
             reads=[Brkf, Bru], writes=[Bru])
        P.op("dve", lambda e: e.tensor_scalar(out=ru[:], in0=ru[:], scalar1=math.pi, scalar2=-math.pi, op0=ALU.min, op1=ALU.max), reads=[Bru], writes=[Bru])
        P.op("act", lambda e: e.activation(out=dst[:], in_=ru[:], func=AF.Sin), reads=[Bru], writes=[Bdst])

    def rms_norm(nblk, rank):
        P.op("act", lambda e: e.activation(out=csq[:, 0:nblk, :], in_=c32[:, 0:nblk, :], func=AF.Square), reads=[Bc32], writes=[Bcsq])
        pt, pb = ps.next()
        for j in range(nblk):
            P.op("pe", lambda e, j=j, pt=pt: e.matmul(pt[:], lhsT=ones32[:], rhs=csq[:, j, :], start=(j == 0), stop=(j == nblk - 1)),
                 reads=[Bc, Bcsq], writes=[pb])
        P.op("act", lambda e, pt=pt: e.activation(out=rt[:], in_=pt[:], func=AF.Sqrt, bias=epsr[:, 0:1], scale=1.0 / rank), reads=[pb, Bc], writes=[Brt])
        P.op("dve", lambda e: e.reciprocal(out=rr[:], in_=rt[:]), reads=[Brt], writes=[Brr])
        for j in range(nblk):
            P.op("dve", lambda e, j=j: e.tensor_tensor(out=cn16[:, j, :], in0=c32[:, j, :], in1=rr[:], op=ALU.mult), reads=[Bc32, Brr], writes=[Bcn])

    ddi = 0
    hi = 0

    def load_x(tt):
        c0 = tt * TT
        xb, bxb = x16s[tt % 2], Bx16s[tt % 2]
        if xsrc16 is None:
            for hf in range(2):
                P.dma("sp", "x32_%d" % hf, x32s[hf][:], xv[:, hf * 4:(hf + 1) * 4, c0:c0 + TT], writes=[Bx32s[hf]])
                P.op("pool", lambda e, hf=hf, xb=xb: e.tensor_copy(out=xb[:, hf * 4:(hf + 1) * 4, :], in_=x32s[hf][:]), reads=[Bx32s[hf]], writes=[bxb])
        else:
            P.dma("sp", "x32_0", xb[:], xg[:, (c0 % 2048) // 512, c0 // 2048, :, :], reads=[io["Bxg"][(c0 % 2048) // 512]], writes=[bxb])

    load_x(0)
    for tt in range(NT):
        c0 = tt * TT
        xcur[0], xcur[1] = x16s[tt % 2], Bx16s[tt % 2]
        if tt + 1 < NT:
            load_x(tt + 1)
        P.dma("sp", "posi", posi[:], pos[0:1, c0:c0 + TT].partition_broadcast(64), writes=[Bposi])
        P.op("dve", lambda e: e.tensor_copy(out=ang[:], in_=posi[:]), reads=[Bposi], writes=[Bang])
        P.op("dve", lambda e: e.tensor_scalar(out=ang[:], in0=ang[:], scalar1=ropecs[:, 0:1], scalar2=None, op0=ALU.mult), reads=[Bang, Bc], writes=[Bang])
        sintab(sn, Bsn, 0.0)
        sintab(cs, Bcs, math.pi / 2)
        P.op("dve", lambda e: e.tensor_scalar(out=sn[:], in0=sn[:], scalar1=ropecs[:, 1:2], scalar2=None, op0=ALU.mult), reads=[Bsn, Bc], writes=[Bsn])
        for j in range(3):
            pt, pb = inproj((16 + j) * 128, 128)
            P.op("act", lambda e, pt=pt, j=j: e.activation(out=c32[:, j, :], in_=pt[:], func=AF.Identity, bias=bAs[:, 16 + j:17 + j], scale=1.0),
                 reads=[pb, Bc], writes=[Bc32])
        rms_norm(3, 384.0)
        if "dil" in phases:
            for g in range(3):
                d = DILS[g]
                for t in range(3):
                    blk = 7 + g * 3 + t
                    pt, pb = inproj(blk * 128, 128)
                    s = ddi % 2
                    ddi += 1
                    P.op("act", lambda e, pt=pt, s=s, d=d, blk=blk, t=t: e.activation(
                        out=dd16[s][:].rearrange("p (r j) -> p r j", r=d), in_=pt[:].rearrange("p (j r) -> p r j", r=d),
                        func=AF.Identity, bias=bAs[:, blk:blk + 1], scale=(0.125 if t == 0 else 1.0)), reads=[pb, Bc], writes=[Bdd[s]])
                    P.dma("sp", "dd%d" % s, dsub[g][t][:, :, c0 // d:(c0 + TT) // d], dd16[s][:].rearrange("p (r j) -> p r j", r=d),
                          reads=[Bdd[s]], writes=[Bdsub])
        pt, pb = ps.next()
        for j in range(3):
            P.op("pe", lambda e, j=j, pt=pt: e.matmul(pt[:], lhsT=wuq16[:, j, 0:128], rhs=cn16[:, j, :], start=(j == 0), stop=(j == 2)),
                 reads=[Bwu, Bcn], writes=[pb])
        P.op("act", lambda e, pt=pt: e.activation(out=q1s[:], in_=pt[:], func=AF.Copy, scale=QSCALE), reads=[pb], writes=[Bq1s])
        P.dma("sp", "q1s", qd1[:, c0:c0 + TT], q1s[:], reads=[Bq1s], writes=[Bqd])
        pA, pbA = ps.next()
        for j in range(3):
            P.op("pe", lambda e, j=j, pA=pA: e.matmul(pA[0:64, :], lhsT=wuq16[:, j, 128:192], rhs=cn16[:, j, :], start=(j == 0), stop=(j == 2)),
                 reads=[Bwu, Bcn], writes=[pbA])
        pB, pbB = ps.next()
        for j in range(3):
            P.op("pe", lambda e, j=j, pB=pB: e.matmul(pB[0:64, :], lhsT=wuq16[:, j, 192:256], rhs=cn16[:, j, :], start=(j == 0), stop=(j == 2)),
                 reads=[Bwu, Bcn], writes=[pbB])
        P.op("dve", lambda e, pA=pA: e.scalar_tensor_tensor(out=t1[:], in0=pA[0:64, :], scalar=QSCALE, in1=cs[:], op0=ALU.mult, op1=ALU.mult),
             reads=[pbA, Bcs], writes=[Bt1])
        P.op("dve", lambda e, pB=pB: e.scalar_tensor_tensor(out=t2[:], in0=pB[0:64, :], scalar=QSCALE, in1=sn[:], op0=ALU.mult, op1=ALU.mult),
             reads=[pbB, Bsn], writes=[Bt2])
        P.op("pool", lambda e: e.tensor_tensor(out=q2s[:], in0=t1[:], in1=t2[:], op=ALU.add), reads=[Bt1, Bt2], writes=[Bq2s])
        P.dma("sp", "q2s", qd2[:, c0:c0 + TT], q2s[:], reads=[Bq2s], writes=[Bqd])
        for j in range(2):
            pt, pb = inproj((19 + j) * 128, 128)
            P.op("act", lambda e, pt=pt, j=j: e.activation(out=c32[:, j, :], in_=pt[:], func=AF.Identity, bias=bAs[:, 19 + j:20 + j], scale=1.0),
                 reads=[pb, Bc], writes=[Bc32])
        rms_norm(2, 256.0)
        if "hg" in phases:
            for h in range(2):
                pt, pb = inproj(h * 128, 128)
                P.op("act", lambda e, pt=pt, h=h: e.activation(out=qs[h][:], in_=pt[:], func=AF.Silu, bias=bAs[:, h:h + 1], scale=1.0),
                     reads=[pb, Bc], writes=[Bqs[h]])
            pt, pb = inproj(6 * 128, 128)
            P.op("act", lambda e, pt=pt: e.activation(out=hv16[:], in_=pt[:], func=AF.Identity, bias=bAs[:, 6:7], scale=1.0), reads=[pb, Bc], writes=[Bhv16])
            P.dma("sp", "hv16", hv_d[:, c0:c0 + TT], hv16[:], reads=[Bhv16], writes=[Bhqk])
            for dr_ in range(2):
                for h in range(2):
                    idx = h * 2 + dr_
                    blk = 2 + dr_ * 2 + h
                    pt, pb = inproj(blk * 128, 128)
                    P.op("act", lambda e, pt=pt, blk=blk: e.activation(out=sig[:], in_=pt[:], func=AF.Sigmoid, bias=bAs[:, blk:blk + 1], scale=1.0),
                         reads=[pb, Bc], writes=[Bsig])
                    P.op("dve", lambda e, idx=idx: e.tensor_scalar(out=ff[:], in0=sig[:], scalar1=lbt[:, idx, 1:2], scalar2=lbt[:, idx, 0:1], op0=ALU.mult, op1=ALU.add),
                         reads=[Bsig, Bc], writes=[Bff])
                    P.op("act", lambda e: e.activation(out=ff[:], in_=ff[:], func=AF.Ln), reads=[Bff], writes=[Bff])
                    P.op("pool", lambda e: e.tensor_scalar(out=ff[:], in0=ff[:], scalar1=LN_MINF, scalar2=None, op0=ALU.max), reads=[Bff], writes=[Bff])
                    if dr_ == 0:
                        P.op("dve", lambda e: e.tensor_tensor_scan(out=bb[:], data0=smask[:], data1=ff[:], initial=0.0, op0=ALU.mult, op1=ALU.add),
                             reads=[Bff, Bc], writes=[Bbb])
                        mcol, bcol = 31, 63
                    else:
                        P.op("dve", lambda e: e.tensor_tensor_scan(out=bb[:, ::-1], data0=smask[:], data1=ff[:, ::-1], initial=0.0, op0=ALU.mult, op1=ALU.add),
                             reads=[Bff, Bc], writes=[Bbb])
                        mcol, bcol = 32, 0
                    b3 = bb[:].rearrange("p (c t) -> p c t", t=64)
                    P.op("pool", lambda e, h=h, dr_=dr_, tt=tt, b3=b3, mcol=mcol: e.tensor_copy(out=mT[h][dr_][:, tt * 8:(tt + 1) * 8], in_=b3[:, :, mcol]),
                         reads=[Bbb], writes=[BmB])
                    P.op("pool", lambda e, h=h, dr_=dr_, tt=tt, b3=b3, bcol=bcol: e.tensor_copy(out=BTt[h][dr_][:, tt * 8:(tt + 1) * 8], in_=b3[:, :, bcol]),
                         reads=[Bbb], writes=[BmB])
                    mb = b3[:, :, mcol:mcol + 1]
                    mbc = bass.AP(mb.tensor, mb.offset, [list(mb.ap[0]), list(mb.ap[1]), [0, 64]])
                    P.op("dve", lambda e, b3=b3, mbc=mbc: e.tensor_tensor(out=eq[:].rearrange("p (c t) -> p c t", t=64), in0=b3, in1=mbc, op=ALU.subtract),
                         reads=[Bbb], writes=[Beq])
                    P.op("act", lambda e: e.activation(out=ek[:], in_=eq[:], func=AF.Exp, scale=-1.0), reads=[Beq], writes=[Bek])
                    P.op("act", lambda e: e.activation(out=eq[:], in_=eq[:], func=AF.Exp), reads=[Beq], writes=[Beq])
                    s = hi % 2
                    hi += 1
                    P.op("dve", lambda e, s=s, h=h: e.tensor_tensor(out=hq16[s][:], in0=qs[h][:], in1=eq[:], op=ALU.mult), reads=[Bqs[h], Beq], writes=[Bhq16[s]])
                    P.dma("sp", "hq16_%d" % s, hq_d[h][dr_][:, c0:c0 + TT], hq16[s][:], reads=[Bhq16[s]], writes=[Bhqk])
                    P.op("dve", lambda e, idx=idx: e.tensor_scalar(out=kk[:], in0=sig[:], scalar1=lbt[:, idx, 2:3], scalar2=lbt[:, idx, 1:2], op0=ALU.mult, op1=ALU.add),
                         reads=[Bsig, Bc], writes=[Bkk])
                    P.op("pool", lambda e, s=s: e.tensor_tensor(out=hk16[s][:], in0=kk[:], in1=ek[:], op=ALU.mult), reads=[Bkk, Bek], writes=[Bhk16[s]])
                    P.dma("sp", "hk16_%d" % s, hk_d[h][dr_][:, c0:c0 + TT], hk16[s][:], reads=[Bhk16[s]], writes=[Bhqk])

        pt, pb = ps.next()
        for j in range(2):
            P.op("pe", lambda e, j=j, pt=pt: e.matmul(pt[:], lhsT=wukv16[:, j, 0:128], rhs=cn16[:, j, :], start=(j == 0), stop=(j == 1)),
                 reads=[Bwu, Bcn], writes=[pb])
        P.op("act", lambda e, pt=pt, c0=c0: e.activation(out=K1T[:, c0:c0 + TT], in_=pt[:], func=AF.Copy), reads=[pb], writes=[BK])
        pt, pb = ps.next()
        for i in range(4):
            for j in range(2):
                P.op("pe", lambda e, i=i, j=j, pt=pt: e.matmul(pt[:, i * 128:(i + 1) * 128], lhsT=cn16[:, j, i * 128:(i + 1) * 128], rhs=wukv16[:, j, 128:256],
                                                          start=(j == 0), stop=(j == 1)), reads=[Bwu, Bcn], writes=[pb])
        P.op("act", lambda e, pt=pt, tt=tt: e.activation(out=Vtok[:, tt * 4:(tt + 1) * 4, :], in_=pt[:].rearrange("p (a b) -> p a b", a=4), func=AF.Copy),
             reads=[pb], writes=[BV])
        pA, pbA = inproj(21 * 128, 64)
        pB, pbB = inproj(21 * 128 + 64, 64)
        P.op("dve", lambda e, pA=pA: e.scalar_tensor_tensor(out=t1[:], in0=pA[0:64, :], scalar=bAs[0:64, 21:22], in1=cs[:], op0=ALU.add, op1=ALU.mult),
             reads=[pbA, Bcs, Bc], writes=[Bt1])
        P.op("dve", lambda e, pB=pB: e.scalar_tensor_tensor(out=t2[:], in0=pB[0:64, :], scalar=bAs[0:64, 22:23], in1=sn[:], op0=ALU.add, op1=ALU.mult),
             reads=[pbB, Bsn, Bc], writes=[Bt2])
        P.op("pool", lambda e, c0=c0: e.tensor_tensor(out=K2T[:, c0:c0 + TT], in0=t1[:], in1=t2[:], op=ALU.add), reads=[Bt1, Bt2], writes=[BK])
    P.barrier()
    es1.close()
    es2 = ExitStack()
    sb = lambda n, s, dt=F32: es2.enter_context(nc.sbuf_tensor("%s_A%d" % (n, layer), s, dt))
    if "mla" in phases:
        pT = [sb("pT%d" % i, [128, TT], BF16) for i in range(4)]; BpT = [Buf("pT%d" % i) for i in range(4)]
        dacc = sb("dacc", [128, TT]); Bdacc = Buf("dacc")
        daccs = [sb("daccs%d" % i, [128, TT]) for i in range(3)]; Bdaccs = [Buf("daccs%d" % i) for i in range(3)]
        rden = sb("rden", [128, TT]); Brden = Buf("rden")
        oc = sb("oc", [128, TT], BF16); Boc = Buf("oc")
        po_t = nc.alloc_psum_tensor("po_mla", [128, 512], F32) if False else None
        pi = 0
        Q1 = [sb("Q1_%d" % i, [128, TT], BF16) for i in range(2)]; Q2 = [sb("Q2_%d" % i, [64, TT], BF16) for i in range(2)]
        BQs = [Buf("Qs%d" % i) for i in range(2)]
        for qt in range(NT):
            q0 = qt * TT
            qi = qt % 2
            P.dma("sp", "Q1_%d" % qi, Q1[qi][:], qd1[:, q0:q0 + TT], reads=[Bqd], writes=[BQs[qi]])
            P.dma("sp", "Q2_%d" % qi, Q2[qi][:], qd2[:, q0:q0 + TT], reads=[Bqd], writes=[BQs[qi]])
            BQ = BQs[qi]
            po, pbo = ps.next()

            def mla_a(kb, qi=qi, BQ=BQ, po=po):
                nonlocal pi
                k0 = kb * 128
                pt, pb = ps.next()
                if pt is po:
                    pt, pb = ps.next()
                P.op("pe", lambda e, pt=pt, k0=k0, qi=qi: e.matmul(pt[:], lhsT=K1T[:, k0:k0 + 128], rhs=Q1[qi][:], start=True, stop=False),
                     reads=[BK, BQ], writes=[pb])
                P.op("pe", lambda e, pt=pt, k0=k0, qi=qi: e.matmul(pt[:], lhsT=K2T[:, k0:k0 + 128], rhs=Q2[qi][:], start=False, stop=True),
                     reads=[BK, BQ], writes=[pb])
                s = pi % 4
                pi += 1
                P.op("act", lambda e, pt=pt, s=s: e.activation(out=pT[s][:], in_=pt[:], func=AF.Exp), reads=[pb], writes=[BpT[s]])
                return s

            def mla_b(kb, s, po=po, pbo=pbo):
                P.op("pe", lambda e, po=po, kb=kb, s=s: e.matmul(po[:], lhsT=Vtok[:, kb, :], rhs=pT[s][:], start=(kb == 0), stop=(kb == S // 128 - 1)),
                     reads=[BV, BpT[s]], writes=[pbo])
                ai = kb % 3
                eng = "pool" if ai == 2 else "dve"
                if kb < 3:
                    P.op(eng, lambda e, s=s, ai=ai: e.tensor_copy(out=daccs[ai][:], in_=pT[s][:]), reads=[BpT[s]], writes=[Bdaccs[ai]])
                else:
                    P.op(eng, lambda e, s=s, ai=ai: e.tensor_tensor(out=daccs[ai][:], in0=daccs[ai][:], in1=pT[s][:], op=ALU.add),
                         reads=[BpT[s], Bdaccs[ai]], writes=[Bdaccs[ai]])

            pend = []
            for kb in range(S // 128):
                pend.append((kb, mla_a(kb)))
                if len(pend) > 2:
                    mla_b(*pend.pop(0))
            while pend:
                mla_b(*pend.pop(0))
            P.op("dve", lambda e: e.tensor_tensor(out=dacc[:], in0=daccs[0][:], in1=daccs[1][:], op=ALU.add), reads=[Bdaccs[0], Bdaccs[1]], writes=[Bdacc])
            P.op("dve", lambda e: e.tensor_tensor(out=dacc[:], in0=dacc[:], in1=daccs[2][:], op=ALU.add), reads=[Bdacc, Bdaccs[2]], writes=[Bdacc])
            pd, pbd = ps.next()
            if pd is po:
                pd, pbd = ps.next()
            P.op("pe", lambda e, pd=pd: e.matmul(pd[:], lhsT=ones32[:], rhs=dacc[:], start=True, stop=True), reads=[Bc, Bdacc], writes=[pbd])
            P.op("dve", lambda e, pd=pd: e.reciprocal(out=rden[:], in_=pd[:]), reads=[pbd], writes=[Brden])
            P.op("dve", lambda e, po=po: e.tensor_tensor(out=oc[:], in0=po[:], in1=rden[:], op=ALU.mult), reads=[pbo, Brden], writes=[Boc])
            P.dma("sp", "oc", osrc[1024 + (q0 // 2048) * 128:1024 + (q0 // 2048) * 128 + 128, (q0 % 2048):(q0 % 2048) + TT], oc[:], reads=[Boc], writes=[BoT])


    P.barrier()
    if "cc_o" in io:
        io["cc_o"]((4, 5))
    es2.close()
    es2 = ExitStack()
    sb = lambda n, s, dt=F32: es2.enter_context(nc.sbuf_tensor("%s_A%d" % (n, layer), s, dt))
    if "dil" in phases:
        RG = 2048
        ets = sb("ets", [128, 18, 256]); Bets = Buf("ets")
        P.dma("sp", "ets", ets[:], etab, writes=[Bets])
        NSET = 2
        Qs_ = [sb("Qs%d" % i, [128, RG], BF16) for i in range(NSET)]; Ks_ = [sb("Ks%d" % i, [128, RG + 128], BF16) for i in range(NSET)]
        Vs_ = [sb("Vs%d" % i, [128, RG + 128], BF16) for i in range(NSET)]
        BQs_l = [Buf("Qs%d" % i) for i in range(NSET)]; BKs_l = [Buf("Ks%d" % i) for i in range(NSET)]; BVs_l = [Buf("Vs%d" % i) for i in range(NSET)]
        Vp_ = [sb("Vp%d" % i, [128, 17, 2, 65], BF16) for i in range(NSET)]; BVp_l = [[Buf("Vp%d_%d" % (i, j)) for j in range(17)] for i in range(NSET)]
        NU = 4
        pe32 = [sb("pe32_%d" % i, [128, 256]) for i in range(NU)]; Bpe = [Buf("pe32_%d" % i) for i in range(NU)]
        pt16 = [sb("pt16_%d" % i, [128, 256], BF16) for i in range(NU)]; Bpt16 = [Buf("pt16_%d" % i) for i in range(NU)]
        acc = [sb("dacc%d" % h, [65, RG]) for h in range(2)]; Bacc = [Buf("dacc%d" % h) for h in range(2)]
        obd = sb("obd", [64, 512], BF16); Bobd = Buf("obd")
        for i in range(NSET):
            P.op("pool", lambda e, i=i: e.memset(Vp_[i][:], 1.0), writes=BVp_l[i])
        ui = 0
        si = 0
        for rg in range(S // RG):
            R0 = rg * RG
            for h in range(2):
                P.op("pool", lambda e, h=h: e.memset(acc[h][:], 0.0), writes=[Bacc[h]])
            for g in range(3):
                d = DILS[g]
                J = S // d
                nj = RG // d
                nb = nj // 128
                j0 = R0 // d
                for r in range(d):
                    ss = si % NSET
                    si += 1
                    Qs, Ks, Vs, Vp = Qs_[ss], Ks_[ss], Vs_[ss], Vp_[ss]
                    BQs_, BKs, BVs, BVp = BQs_l[ss], BKs_l[ss], BVs_l[ss], BVp_l[ss]
                    lo = j0 - 64
                    hi_ = j0 + nj + 64
                    clo = max(lo, 0)
                    chi = min(hi_, J)
                    if lo < 0:
                        P.op("pool", lambda e, Ks=Ks: e.memset(Ks[:, 0:64], 0.0), writes=[BKs])
                        P.op("pool", lambda e, Vs=Vs: e.memset(Vs[:, 0:64], 0.0), writes=[BVs])
                    if hi_ > J:
                        P.op("pool", lambda e, nj=nj, Ks=Ks: e.memset(Ks[:, nj + 64:nj + 128], 0.0), writes=[BKs])
                        P.op("pool", lambda e, nj=nj, Vs=Vs: e.memset(Vs[:, nj + 64:nj + 128], 0.0), writes=[BVs])
                    P.dma("sp", "Qs%d" % ss, Qs[:, 0:nj], dsub[g][0][:, r, j0:j0 + nj], reads=[Bdsub], writes=[BQs_])
                    P.dma("sp", "Ks%d" % ss, Ks[:, clo - lo:chi - lo], dsub[g][1][:, r, clo:chi], reads=[Bdsub], writes=[BKs])
                    P.dma("sp", "Vs%d" % ss, Vs[:, clo - lo:chi - lo], dsub[g][2][:, r, clo:chi], reads=[Bdsub], writes=[BVs])
                    for n in range(nb + 1):
                        ptr, pbr = ps.next()
                        ptr16 = ptr[:].bitcast(BF16)
                        P.op("pe", lambda e, n=n, ptr16=ptr16, Vs=Vs: e.transpose(out=ptr16[:, 0:128], in_=Vs[:, n * 128:(n + 1) * 128], identity=id16[:]),
                             reads=[BVs, Bc], writes=[pbr])
                        P.op("act", lambda e, n=n, ptr16=ptr16, Vp=Vp: e.activation(out=Vp[:, n, :, 0:64], in_=ptr16[:, 0:128].rearrange("p (h v) -> p h v", h=2), func=AF.Copy),
                             reads=[pbr], writes=[BVp[n]])

                    def dil_a(qb, h, Qs=Qs, Ks=Ks, BQs_=BQs_, BKs=BKs, g=g, j0=j0, J=J):
                        nonlocal ui
                        jb = j0 + qb * 128
                        var = 1 if jb == 0 else (2 if jb + 128 == J else 0)
                        u = ui % NU
                        ui += 1
                        pt, pb = ps.next()
                        for kc in range(2):
                            P.op("pe", lambda e, pt=pt, kc=kc, qb=qb, h=h: e.matmul(pt[:, kc * 128:(kc + 1) * 128],
                                 lhsT=Ks[h * 64:(h + 1) * 64, (qb + kc) * 128:(qb + kc + 1) * 128], rhs=Qs[h * 64:(h + 1) * 64, qb * 128:(qb + 1) * 128],
                                 start=True, stop=True), reads=[BKs, BQs_], writes=[pb])
                        P.op("act", lambda e, pt=pt, u=u: e.activation(out=pe32[u][:], in_=pt[:, 0:256], func=AF.Exp), reads=[pb], writes=[Bpe[u]])
                        ei = (g * 2 + h) * 3 + var
                        P.op("dve", lambda e, u=u, ei=ei: e.tensor_tensor(out=pt16[u][:], in0=pe32[u][:], in1=ets[:, ei, :], op=ALU.mult),
                             reads=[Bpe[u], Bets], writes=[Bpt16[u]])
                        return (qb, h, u)

                    def dil_b(qb, h, u, Vp=Vp, BVp=BVp, d=d, r=r):
                        po, pbo = ps.next()
                        for kc in range(2):
                            P.op("pe", lambda e, po=po, kc=kc, qb=qb, h=h, u=u: e.matmul(po[0:65, 0:128], lhsT=Vp[:, qb + kc, h, :], rhs=pt16[u][:, kc * 128:(kc + 1) * 128],
                                 start=(kc == 0), stop=(kc == 1)), reads=[BVp[qb + kc], Bpt16[u]], writes=[pbo])
                        st = qb * 128 * d + r
                        av = acc[h][:, st:st + 127 * d + 1:d]
                        P.op("dve", lambda e, po=po, av=av: e.tensor_tensor(out=av, in0=av, in1=po[0:65, 0:128], op=ALU.add), reads=[pbo, Bacc[h]], writes=[Bacc[h]])

                    pend = []
                    for qb in range(nb):
                        for h in range(2):
                            pend.append(dil_a(qb, h))
                            if len(pend) > 2:
                                dil_b(*pend.pop(0))
                    while pend:
                        dil_b(*pend.pop(0))
            for h in range(2):
                P.op("dve", lambda e, h=h: e.reciprocal(out=acc[h][64:65, :], in_=acc[h][64:65, :]), reads=[Bacc[h]], writes=[Bacc[h]])
                for cc in range(RG // 512):
                    pt, pb = ps.next()
                    P.op("pe", lambda e, pt=pt, h=h, cc=cc: e.matmul(pt[0:64, :], lhsT=ones32[64:65, 0:64], rhs=acc[h][64:65, cc * 512:(cc + 1) * 512], start=True, stop=True),
                         reads=[Bc, Bacc[h]], writes=[pb])
                    P.op("dve", lambda e, pt=pt, h=h, cc=cc: e.tensor_tensor(out=obd[:], in0=acc[h][0:64, cc * 512:(cc + 1) * 512], in1=pt[0:64, :], op=ALU.mult),
                         reads=[pb, Bacc[h]], writes=[Bobd])
                    P.dma("sp", "obd", osrc[512 + rg * 128 + h * 64:512 + rg * 128 + (h + 1) * 64, cc * 512:(cc + 1) * 512], obd[:], reads=[Bobd], writes=[BoT])

    P.barrier()
    if "cc_o" in io:
        io["cc_o"]((2, 3))
    es2.close()
    es2 = ExitStack()
    sb = lambda n, s, dt=F32: es2.enter_context(nc.sbuf_tensor("%s_A%d" % (n, layer), s, dt))
    if "hg" in phases:
        oac = [sb("oac%d" % h, [64, S]) for h in range(2)]; Boac = [Buf("oac%d" % h) for h in range(2)]
        for h in range(2):
            P.op("pool", lambda e, h=h: e.memset(oac[h][:], 0.0), writes=[Boac[h]])
        gam = [[sb("gam%d%d" % (h, d), [128, 128]) for d in range(2)] for h in range(2)]; Bgam = Buf("gam")
        for h in range(2):
            for d in range(2):
                if d == 0:
                    dst, mn, bc, mc = gam[h][d][:, 0:127], mT[h][d][:, 1:128], BTt[h][d][:, 0:127], mT[h][d][:, 0:127]
                else:
                    dst, mn, bc, mc = gam[h][d][:, 1:128], mT[h][d][:, 0:127], BTt[h][d][:, 1:128], mT[h][d][:, 1:128]
                P.op("dve", lambda e, dst=dst, mn=mn, bc=bc: e.tensor_tensor(out=dst, in0=mn, in1=bc, op=ALU.add), reads=[BmB], writes=[Bgam])
                P.op("dve", lambda e, dst=dst, mc=mc: e.tensor_tensor(out=dst, in0=dst, in1=mc, op=ALU.subtract), reads=[BmB, Bgam], writes=[Bgam])
                P.op("act", lambda e, dst=dst: e.activation(out=dst, in_=dst, func=AF.Exp), reads=[Bgam], writes=[Bgam])
        ch = [(h, d) for d in range(2) for h in range(2)]
        qT = {c: sb("hqT%d%d" % c, [128, TT], BF16) for c in ch}; kT = {c: sb("hkT%d%d" % c, [128, TT], BF16) for c in ch}
        vT = {c: sb("hvT%d%d" % c, [128, TT], BF16) for c in ch}
        Bld = {c: Buf("hld%d%d" % c) for c in ch}
        kvtok = {c: sb("kvtok%d%d" % c, [128, 256], BF16) for c in ch}; Bkv = {c: Buf("kvtok%d%d" % c) for c in ch}
        at16 = {c: sb("at16%d%d" % c, [128, 128], BF16) for c in ch}; Bat = {c: Buf("at%d%d" % c) for c in ch}
        S32 = {c: sb("S32%d%d" % c, [128, 64]) for c in ch}; S16 = {c: sb("S16%d%d" % c, [128, 64], BF16) for c in ch}
        Sh = {c: sb("Sh%d%d" % c, [128, 64]) for c in ch}
        BS32 = {c: Buf("S32%d%d" % c) for c in ch}; BS16 = {c: Buf("S16%d%d" % c) for c in ch}; BSh = {c: Buf("Sh%d%d" % c) for c in ch}
        for c in ch:
            P.op("pool", lambda e, c=c: e.memset(S32[c][:], 0.0), writes=[BS32[c]])
            P.op("pool", lambda e, c=c: e.memset(S16[c][:], 0.0), writes=[BS16[c]])
        for step in range(NT):
            for c in ch:
                h, d = c
                ti = step if d == 0 else NT - 1 - step
                c0 = ti * TT
                P.dma("sp", "hl%d%d" % c, qT[c][:], hq_d[h][d][:, c0:c0 + TT], reads=[Bhqk], writes=[Bld[c]])
                P.dma("sp", "hl%d%d" % c, kT[c][:], hk_d[h][d][:, c0:c0 + TT], reads=[Bhqk], writes=[Bld[c]])
                P.dma("sp", "hl%d%d" % c, vT[c][:], hv_d[:, c0:c0 + TT], reads=[Bhqk], writes=[Bld[c]])
            for pp in range(4):
                inf = {}
                for c in ch:
                    h, d = c
                    ti = step if d == 0 else NT - 1 - step
                    pr = pp if d == 0 else 3 - pp
                    inf[c] = (h, d, ti, pr, pr * 128)
                trb = {}
                for ci, c in enumerate(ch):
                    h, d, ti, pr, p0 = inf[c]
                    bank, bb_ = ps.t[ci], ps.b[ci]
                    b16 = bank[:].bitcast(BF16)
                    P.op("pe", lambda e, c=c, p0=p0, b16=b16: e.transpose(out=b16[:, 0:128], in_=kT[c][:, p0:p0 + 128], identity=id16[:]),
                         reads=[Bld[c], Bc], writes=[bb_])
                    P.op("pe", lambda e, c=c, p0=p0, b16=b16: e.transpose(out=b16[:, 128:256], in_=vT[c][:, p0:p0 + 128], identity=id16[:]),
                         reads=[Bld[c], Bc], writes=[bb_])
                    trb[c] = (b16, bb_)
                for c in ch:
                    b16, bb_ = trb[c]
                    P.op("act", lambda e, c=c, b16=b16: e.activation(out=kvtok[c][:], in_=b16[:, 0:256], func=AF.Copy), reads=[bb_], writes=[Bkv[c]])
                pab = {}
                for ci, c in enumerate(ch):
                    h, d, ti, pr, p0 = inf[c]
                    pa, pba = ps.t[ci], ps.b[ci]
                    P.op("pe", lambda e, c=c, p0=p0, pa=pa: e.matmul(pa[:, 0:128], lhsT=kT[c][:, p0:p0 + 128], rhs=qT[c][:, p0:p0 + 128], start=True, stop=True),
                         reads=[Bld[c]], writes=[pba])
                    pab[c] = (pa, pba)
                for c in ch:
                    h, d, ti, pr, p0 = inf[c]
                    pa, pba = pab[c]
                    P.op("dve", lambda e, c=c, d=d, pa=pa: e.tensor_tensor(out=at16[c][:], in0=pa[:, 0:128], in1=msk[:, d, :], op=ALU.mult),
                         reads=[pba, Bc], writes=[Bat[c]])
                for cc in range(2):
                    pub = {}
                    for ci, c in enumerate(ch):
                        h, d, ti, pr, p0 = inf[c]
                        po, pbo = ps.t[4 + ci], ps.b[4 + ci]
                        ck = cc if d == 0 else 1 - cc
                        q0 = p0 + ck * 64
                        if cc == 0:
                            P.op("pe", lambda e, c=c, h=h, po=po: e.matmul(po[0:64, 0:128], lhsT=kvtok[c][:, 128 + h * 64:128 + (h + 1) * 64], rhs=at16[c][:], start=True, stop=False),
                                 reads=[Bkv[c], Bat[c]], writes=[pbo])
                        P.op("pe", lambda e, c=c, po=po, ck=ck, q0=q0, cc=cc: e.matmul(po[0:64, ck * 64:(ck + 1) * 64], lhsT=S16[c][:], rhs=qT[c][:, q0:q0 + 64],
                             start=False, stop=(cc == 1)), reads=[BS16[c], Bld[c]], writes=[pbo])
                        pu, pbu = ps.t[ci], ps.b[ci]
                        P.op("pe", lambda e, c=c, h=h, pu=pu, ck=ck: e.matmul(pu[:, 0:64], lhsT=kvtok[c][ck * 64:(ck + 1) * 64, 0:128],
                             rhs=kvtok[c][ck * 64:(ck + 1) * 64, 128 + h * 64:128 + (h + 1) * 64], start=True, stop=True), reads=[Bkv[c]], writes=[pbu])
                        pub[c] = (pu, pbu, ti * 8 + pr * 2 + ck)
                    for c in ch:
                        pu, pbu, cidx = pub[c]
                        P.op("dve", lambda e, c=c, pu=pu: e.tensor_tensor(out=Sh[c][:], in0=pu[:, 0:64], in1=S32[c][:], op=ALU.add), reads=[pbu, BS32[c]], writes=[BSh[c]])
                    for c in ch:
                        h, d = c
                        pu, pbu, cidx = pub[c]
                        P.op("dve", lambda e, c=c, h=h, d=d, cidx=cidx: e.tensor_scalar(out=S32[c][:], in0=Sh[c][:], scalar1=gam[h][d][:, cidx:cidx + 1], scalar2=None, op0=ALU.mult),
                             reads=[BSh[c], Bgam], writes=[BS32[c]])
                        P.op("act", lambda e, c=c, h=h, d=d, cidx=cidx: e.activation(out=S16[c][:], in_=Sh[c][:], func=AF.Copy, scale=gam[h][d][:, cidx:cidx + 1]),
                             reads=[BSh[c], Bgam], writes=[BS16[c]])
                for ci, c in enumerate(ch):
                    h, d, ti, pr, p0 = inf[c]
                    po, pbo = ps.t[4 + ci], ps.b[4 + ci]
                    t0_ = ti * TT + p0
                    P.op("dve", lambda e, h=h, po=po, t0_=t0_: e.tensor_tensor(out=oac[h][:, t0_:t0_ + 128], in0=oac[h][:, t0_:t0_ + 128], in1=po[0:64, 0:128], op=ALU.add),
                         reads=[pbo, Boac[h]], writes=[Boac[h]])
        o16c = [sb("o16c%d" % i, [64, 2048], BF16) for i in range(2)]; Bo16c = [Buf("o16c%d" % i) for i in range(2)]
        for h in range(2):
            for tq in range(4):
                u = (h * 4 + tq) % 2
                P.op("act" if u == 0 else "dve", (lambda e, h=h, tq=tq, u=u: e.activation(out=o16c[u][:], in_=oac[h][:, tq * 2048:(tq + 1) * 2048], func=AF.Copy)) if u == 0 else
                     (lambda e, h=h, tq=tq, u=u: e.tensor_copy(out=o16c[u][:], in_=oac[h][:, tq * 2048:(tq + 1) * 2048])), reads=[Boac[h]], writes=[Bo16c[u]])
                P.dma("sp", "o16c%d" % u, osrc[tq * 128 + h * 64:tq * 128 + (h + 1) * 64, :], o16c[u][:], reads=[Bo16c[u]], writes=[BoT])

    P.barrier()
    es2.close()
    es0.close()


RG4 = [[0, 1, 2, 3], [4, 5, 6, 7]]


def build_fused():
    nc = bass.Bass("TRN2", target_bir_lowering=False)
    dr = lambda n, s, kind="ExternalInput", dt=F32: nc.dram_tensor(n, s, dt, kind=kind).ap()
    shared = {"pos": dr("pos", [1, S], dt=I32), "etab": dr("etab", [128, 18, 256]), "ropec": dr("ropec", [64, 2]),
              "ident": dr("ident", [128, 128]), "masks": dr("masks", [128, 2, 128]), "scanmask": dr("scanmask", [128, 512]),
              "lbraw": dr("lbraw", [128, 4, 2])}
    xT = dr("xT", [1024, S]); xTq = dr("xTq", [1024, 2048]); oidx = dr("oidx", [128, 96], dt=I32)
    ioA, ioB = [], []
    for l in range(2):
        a = dict(shared)
        a.update({"wA": dr("wA%d" % l, [1024, 2816]), "bA": dr("bA%d" % l, [128, 23]), "wuq": dr("wuq%d" % l, [384, 256]), "gq": dr("gq%d" % l, [128, 3]),
                  "wukv": dr("wukv%d" % l, [256, 256]), "gkv": dr("gkv%d" % l, [128, 2])})
        ioA.append(a)
        ioB.append({"wg": dr("wg%d" % l, [1024, 4608]), "bg": dr("bg%d" % l, [128, 36]), "wbr": dr("wbr%d" % l, [1536, 1024]), "wo": dr("wo%d" % l, [1024, 1024]),
                    "hgn": dr("hgn%d" % l, [128, 4]), "lng": dr("lng%d" % l, [128, 8]), "lnb": dr("lnb%d" % l, [128, 8]), "oidx": oidx})
    outT = dr("outT", [1024, 2048], kind="ExternalOutput")
    cco_src = [nc.dram_tensor("cco_src%d" % l, [1536, 2048], BF16) for l in range(2)]
    cco_dst = [nc.dram_tensor("cco_dst%d" % l, [6 * 1024, 2048], BF16) for l in range(2)]
    ccx_src = nc.dram_tensor("ccx_src", [4 * 1024, 512], BF16)
    ccx_dst = nc.dram_tensor("ccx_dst", [4 * 4096, 512], BF16)
    xn32 = nc.dram_tensor("xn32", [1024, 2048], F32).ap()
    scr = make_scratch(nc)
    P = Prog(nc)
    ps = PsumPool(nc)
    Bxn = Buf("xn32"); Bxg = Buf("xg"); Bnone = Buf("none")
    Bods = [Buf("cco_dst%d" % l) for l in range(2)]
    Bccx = [Buf("ccx_src%d" % j) for j in range(4)]
    Bxgs = [Buf("xg%d" % j) for j in range(4)]

    def cc_o(l):
        def go(chunks):
            for k in chunks:
                P.dma("pool", "cc_o%d" % l, None, None, reads=[Bnone], writes=[Bods[l]], inc=1,
                      fn=(lambda e, l=l, k=k: e.collective_compute("AllGather", ALU.bypass, replica_groups=RG4,
                                                                 ins=[cco_src[l].ap()[k * 256:(k + 1) * 256, :].opt()],
                                                                 outs=[cco_dst[l].ap()[k * 1024:(k + 1) * 1024, :].opt()])))
        return go

    def cc_x(j):
        P.dma("pool", "cc_x", None, None, reads=[Bccx[j]], writes=[Bxgs[j]], inc=1,
              fn=(lambda e, j=j: e.collective_compute("AllGather", ALU.bypass, replica_groups=RG4,
                                                    ins=[ccx_src.ap()[j * 1024:(j + 1) * 1024, :].opt()],
                                                    outs=[ccx_dst.ap()[j * 4096:(j + 1) * 4096, :].opt()])))

    for l in range(2):
        ioA[l]["osrc"] = cco_src[l].ap()
        ioA[l]["cc_o"] = cc_o(l)
        if l == 0:
            ioA[l]["xT"] = xT
            emit_A(nc, P, ps, l, ioA[l], scr)
        else:
            ioA[l]["Bxg"] = Bxgs
            emit_A(nc, P, ps, l, ioA[l], scr, xsrc16=ccx_dst.ap())
        P.barrier()
        Bod = Bods[l]
        cc_o(l)((0, 1))
        b = ioB[l]
        b["orows"] = cco_dst[l].ap().rearrange("r (a c) -> (r a) c", c=256)
        b["Bodst"] = Bod
        if l == 0:
            b["x32src"] = xTq; b["Bxsrc"] = Bnone; b["out32"] = xn32; b["out16"] = ccx_src.ap(); b["Bccx"] = Bccx; b["cc_x"] = cc_x
        else:
            b["x32src"] = xn32; b["Bxsrc"] = Bxn; b["out32"] = outT
        emit_B(nc, P, ps, l, b)
        P.barrier()
    P.barrier()
    P.emit()
    return nc


SPL = [1024,1024,1024,512,512] + [512]*10 + [384,256,64,512,3072]
NAMES = ['hg_q','hg_f_fwd','hg_f_bwd','hg_i','hg_g','dil_q0','dil_k0','dil_v0','dil_q1','dil_k1','dil_v1','dil_q2','dil_k2','dil_v2','dil_g','mla_cq','mla_ckv','mla_kr','mla_g','merge']
OFF = dict(zip(NAMES, [int(v) for v in np.cumsum([0]+SPL[:-1])]))
def a_cols(hq):
    ar = np.arange
    c = []
    c += [OFF['hg_q'] + (2*hq)*128 + ar(128), OFF['hg_q'] + (2*hq+1)*128 + ar(128)]
    c += [OFF['hg_f_fwd'] + (2*hq)*128 + ar(128), OFF['hg_f_fwd'] + (2*hq+1)*128 + ar(128)]
    c += [OFF['hg_f_bwd'] + (2*hq)*128 + ar(128), OFF['hg_f_bwd'] + (2*hq+1)*128 + ar(128)]
    c += [OFF['hg_i'] + hq*128 + ar(128)]
    for g in range(3):
        for t in 'qkv':
            c += [OFF['dil_%s%d' % (t, g)] + hq*128 + ar(128)]
    c += [OFF['mla_cq'] + ar(384), OFF['mla_ckv'] + ar(256)]
    kr = OFF['mla_kr'] + ar(64)
    c += [kr, np.concatenate([kr[32:], kr[:32]])]
    return np.concatenate(c)
def etab_np(hq):
    slopes = 2.0 ** (-8.0 * (np.arange(24) + 1) / 24)
    kk = np.arange(128)[:, None]; qq = np.arange(128)[None, :]
    E = np.zeros((128, 18, 256), np.float32)
    for g, d in enumerate((1, 4, 16)):
        for hh in range(2):
            sl = slopes[g*8 + 2*hq + hh]
            for var in range(3):
                for kc in range(2):
                    rel = (kk + 128*kc - 64) - qq
                    e = np.where(np.abs(rel) <= 64, np.exp(-sl * d * np.abs(rel)), 0.0)
                    if var == 1 and kc == 0: e = np.where(kk < 64, 0.0, e)
                    if var == 2 and kc == 1: e = np.where(kk >= 64, 0.0, e)
                    E[:, (g*2+hh)*3 + var, kc*128:(kc+1)*128] = e
    return E
def a_inputs(inp, l, b, hq, xT_b):
    cols = a_cols(hq)
    w_in = inp['w_in'][l]; b_in = inp['b_in'][l]
    bsel = b_in[cols]
    bA = np.zeros((128, 23), np.float32)
    bA[:, :22] = bsel.reshape(22, 128).T
    bA[:64, 22] = bsel[21*128+64: 22*128]
    lbraw = np.zeros((128, 4, 2), np.float32)
    for h in range(2):
        for d, nm in enumerate(('hg_lb_fwd', 'hg_lb_bwd')):
            lbraw[:, h*2+d, :] = inp[nm][:, (2*hq+h)*128:(2*hq+h+1)*128].T
    wuq = inp['w_uq'][l]
    qc = hq*192 + np.arange(192)
    rope = qc[128:]
    wuq_sel = np.concatenate([wuq[:, qc[:128]], wuq[:, rope], wuq[:, np.concatenate([rope[32:], rope[:32]])]], 1)
    wukv = inp['w_ukv'][l]
    wukv_sel = wukv[:, hq*256:(hq+1)*256]
    inv = (1.0 / (10000.0 ** (np.arange(32, dtype=np.float32) / 32))).astype(np.float32)
    ropec = np.zeros((64, 2), np.float32); ropec[:, 0] = np.concatenate([inv, inv]); ropec[:32, 1] = -1.0; ropec[32:, 1] = 1.0
    masks = np.zeros((128, 2, 128), np.float32)
    ss = np.arange(128)[:, None]; tq = np.arange(128)[None, :]
    same = (ss // 64) == (tq // 64)
    masks[:, 0, :] = (same & (ss <= tq)).astype(np.float32)
    masks[:, 1, :] = (same & (ss >= tq)).astype(np.float32)
    sm = np.ones((128, 512), np.float32); sm[:, ::64] = 0.0
    return {"pos": np.ascontiguousarray(inp['positions'][b:b+1].astype(np.int32)), "wA": np.ascontiguousarray(w_in[:, cols]), "bA": bA, "lbraw": lbraw,
            "wuq": np.ascontiguousarray(wuq_sel), "gq": np.ascontiguousarray(inp['mla_q_norm'][l].reshape(3,128).T),
            "wukv": np.ascontiguousarray(wukv_sel), "gkv": np.ascontiguousarray(inp['mla_kv_norm'][l].reshape(2,128).T),
            "etab": etab_np(hq), "ropec": ropec, "ident": np.eye(128, dtype=np.float32), "masks": masks, "scanmask": sm}


_CACHE = {}


def _b_inputs(inp, l):
    w_in = inp['w_in'][l]; b_in = inp['b_in'][l]
    cols = np.concatenate([np.arange(OFF['hg_g'], OFF['hg_g'] + 512), np.arange(OFF['dil_g'], OFF['dil_g'] + 512),
                           np.arange(OFF['mla_g'], OFF['mla_g'] + 512), np.arange(OFF['merge'], OFF['merge'] + 3072)])
    return {"wg%d" % l: np.ascontiguousarray(w_in[:, cols]), "bg%d" % l: np.ascontiguousarray(b_in[cols].reshape(36, 128).T),
            "wbr%d" % l: np.ascontiguousarray(inp['w_branch'][l].reshape(1536, 1024)), "wo%d" % l: np.ascontiguousarray(inp['w_out'][l]),
            "hgn%d" % l: np.ascontiguousarray(inp['hg_norm'][l].reshape(4, 128).T),
            "lng%d" % l: np.ascontiguousarray(inp['ln_g'][l].reshape(8, 128).T), "lnb%d" % l: np.ascontiguousarray(inp['ln_b'][l].reshape(8, 128).T)}


def _oidx(tq):
    p = np.arange(128)[:, None, None, None]; tt = np.arange(8)[None, :, None, None]
    n = np.arange(3)[None, None, :, None]; r = np.arange(4)[None, None, None, :]
    rho = n * 512 + tq * 128 + p
    g = (rho // 256) * 1024 + r * 256 + (rho % 256)
    v = g * 8 + tt
    return np.ascontiguousarray(v.reshape(128, 96).astype(np.int32))


def kernel(**inputs):
    inp = {k: np.asarray(v) for k, v in inputs.items()}
    inp['positions'] = inp['positions'].astype(np.int32)
    for k in inp:
        if k != 'positions':
            inp[k] = inp[k].astype(np.float32, copy=False)
    B = 2
    xT = [np.ascontiguousarray(inp['x'][b].T) for b in range(B)]
    if "nc" not in _CACHE:
        _CACHE["nc"] = build_fused()
    nc = _CACHE["nc"]
    bl = [_b_inputs(inp, l) for l in range(2)]
    in_maps = []
    for c in range(8):
        b, q = c // 4, c % 4
        m = {"xT": xT[b], "xTq": np.ascontiguousarray(xT[b][:, q * 2048:(q + 1) * 2048]), "oidx": _oidx(q)}
        for l in range(2):
            a = a_inputs(inp, l, b, q, None)
            for k in ("pos", "etab", "ropec", "ident", "masks", "scanmask", "lbraw"):
                m[k] = a[k]
            for k in ("wA", "bA", "wuq", "gq", "wukv", "gkv"):
                m["%s%d" % (k, l)] = a[k]
            m.update(bl[l])
        in_maps.append(m)
    res = run_bass_kernel_spmd(nc, in_maps, core_ids=list(range(8))).results
    out = np.empty((B, 8192, 1024), np.float32)
    for c in range(8):
        b, q = c // 4, c % 4
        out[b, q * 2048:(q + 1) * 2048, :] = np.asarray(res[c]["outT"]).T
    return out
```

```python
import math
from contextlib import ExitStack
import numpy as np
from concourse.bass_utils import run_bass_kernel_spmd
import concourse.bass as bass
import concourse.mybir as mybir

F32 = mybir.dt.float32
BF16 = mybir.dt.bfloat16
I32 = mybir.dt.int32
AF = mybir.ActivationFunctionType
ALU = mybir.AluOpType
AX = mybir.AxisListType


class Buf:
    __slots__ = ("name", "w", "r")

    def __init__(self, name=""):
        self.name = name
        self.w = {}
        self.r = {}


class _Eng:
    def __init__(self, name, sem):
        self.name = name
        self.sem = sem
        self.count = 0
        self.waited = {}
        self.items = []


class Prog:
    ENGS = ("pe", "act", "dve", "pool", "sp")

    def __init__(self, nc):
        self.nc = nc
        self.e = {n: _Eng(n, nc.alloc_semaphore("prog_" + n)) for n in self.ENGS}
        self.chan = {}
        self.nops = 0
        self.retired = []

    def _need(self, eng, waits, ev, raw):
        sem, val, en = ev
        if en == eng.name:
            if eng.name == "pe" or not raw:
                return
        k = id(sem)
        if eng.waited.get(k, 0) >= val:
            return
        if k not in waits or waits[k][1] < val:
            waits[k] = (sem, val)

    def _deps(self, eng, reads, writes):
        waits = {}
        for b in reads:
            for ev in b.w.values():
                self._need(eng, waits, ev, True)
        for b in writes:
            for ev in b.w.values():
                self._need(eng, waits, ev, False)
            for ev in b.r.values():
                self._need(eng, waits, ev, False)
        for k, (sem, val) in waits.items():
            eng.waited[k] = val
        return list(waits.values())

    @staticmethod
    def _mark(ev, reads, writes):
        k = id(ev[0])
        for b in reads:
            o = b.r.get(k)
            if o is None or o[1] < ev[1]:
                b.r[k] = ev
        for b in writes:
            o = b.w.get(k)
            if o is None or o[1] < ev[1]:
                b.w[k] = ev

    def op(self, engname, fn, reads=(), writes=()):
        eng = self.e[engname]
        waits = self._deps(eng, reads, writes)
        if eng.count >= 30000:
            self.retired.append((eng.sem, eng.count))
            eng.sem = self.nc.alloc_semaphore("prog_%s_%d" % (engname, self.nops))
            eng.count = 0
        eng.count += 1
        ev = (eng.sem, eng.count, eng.name)
        eng.items.append((waits, fn, (eng.sem, 1)))
        self._mark(ev, reads, writes)
        self.nops += 1
        return ev

    def dma(self, qname, chan, out, in_, reads=(), writes=(), fn=None, inc=16):
        eng = self.e[qname]
        waits = self._deps(eng, reads, writes)
        if chan not in self.chan:
            self.chan[chan] = [self.nc.alloc_semaphore("ch_" + chan), 0]
        c = self.chan[chan]
        if c[1] >= 30000:
            self.retired.append((c[0], c[1]))
            c[0] = self.nc.alloc_semaphore("ch_%s_%d" % (chan, self.nops))
            c[1] = 0
        c[1] += inc
        ev = (c[0], c[1], "dma")
        if fn is None:
            fn = (lambda e, o=out, i=in_: e.dma_start(out=o, in_=i))
        eng.items.append((waits, fn, (c[0], inc)))
        self._mark(ev, reads, writes)
        self.nops += 1
        return ev

    def barrier(self):
        evs = list(self.retired)
        for n in self.ENGS:
            if self.e[n].count > 0:
                evs.append((self.e[n].sem, self.e[n].count))
        for c in self.chan.values():
            evs.append((c[0], c[1]))
        for n in self.ENGS:
            eng = self.e[n]
            waits = []
            for sem, val in evs:
                if eng.waited.get(id(sem), 0) < val:
                    waits.append((sem, val))
                    eng.waited[id(sem)] = val
            eng.items.append((waits, None, None))

    def wait_all(self, engname, bufs):
        eng = self.e[engname]
        waits = self._deps(eng, bufs, bufs)
        eng.items.append((waits, None, None))

    def emit(self):
        nc = self.nc
        with nc.Block() as block:
            def run(eng, h):
                for waits, fn, inc in eng.items:
                    for sem, val in waits:
                        h.wait_ge(sem, val)
                    if fn is not None:
                        ins = fn(h)
                        ins.then_inc(inc[0], inc[1])

            @block.tensor
            def _(h):
                run(self.e["pe"], h)

            @block.scalar
            def _(h):
                run(self.e["act"], h)

            @block.vector
            def _(h):
                run(self.e["dve"], h)

            @block.gpsimd
            def _(h):
                run(self.e["pool"], h)

            @block.sync
            def _(h):
                run(self.e["sp"], h)


ALPHA = 4.0 ** 0.25
LN_EPS = 1e-5


class PsumPool:
    def __init__(self, nc, n=8):
        self.t = [nc.alloc_psum_tensor("psb%d" % i, [128, 512], F32) for i in range(n)]
        self.b = [Buf("psb%d" % i) for i in range(n)]
        self.i = 0
        self.n = n

    def next(self):
        i = self.i
        self.i = (i + 1) % self.n
        return self.t[i], self.b[i]


def load_cast_weight(P, nc, q, dram2d, dst16, dstbuf, kchunks, ncols, stage, stage_bufs, ctr, colsplit):
    v = dram2d.rearrange("(k p) c -> p k c", p=128)
    for k in range(kchunks):
        for c0 in range(0, ncols, colsplit):
            cw = min(colsplit, ncols - c0)
            s = ctr[0] % len(stage)
            ctr[0] += 1
            P.dma(q, "wst%d" % s, stage[s][:, 0:cw], v[:, k, c0:c0 + cw], writes=[stage_bufs[s]])
            eng = "dve" if (ctr[0] % 2 == 0) else "pool"
            P.op(eng, (lambda e, s=s, k=k, c0=c0, cw=cw: e.tensor_copy(out=dst16[:, k, c0:c0 + cw], in_=stage[s][:, 0:cw])),
                 reads=[stage_bufs[s]], writes=[dstbuf])


def emit_B(nc, P, ps, layer, io):
    T = 2048
    TT = 256
    NT = T // TT
    xT = io["x32src"]; wg = io["wg"]; bg = io["bg"]; wbr = io["wbr"]; wo = io["wo"]; hgn = io["hgn"]; lng = io["lng"]; lnb = io["lnb"]
    orows = io["orows"]; oidx = io["oidx"]; Bodst = io["Bodst"]; Bxsrc = io["Bxsrc"]
    out32 = io["out32"]; out16 = io.get("out16")
    esb = ExitStack()
    sb = lambda n, s, dt=F32: esb.enter_context(nc.sbuf_tensor("%s_B%d" % (n, layer), s, dt))
    wg16 = sb("wg16", [128, 8, 4608], BF16); Bwg = Buf("wg16")
    wbr16 = sb("wbr16", [128, 12, 1024], BF16); Bwbr = Buf("wbr16")
    wo16 = sb("wo16", [128, 8, 1024], BF16); Bwo = Buf("wo16")
    stage = [sb("wstage%d" % i, [128, 1152], F32) for i in range(2)]
    stage_b = [Buf("wstage%d" % i) for i in range(2)]
    bgs = sb("bgs", [128, 36]); hgns = sb("hgns", [128, 4]); lngs = sb("lngs", [128, 8]); lnbs = sb("lnbs", [128, 8])
    oix = sb("oix", [128, 96], I32)
    Bc = Buf("consts")
    ones32 = sb("ones32", [128, 128]); Bones = Buf("ones")
    epsr = sb("epsr", [128, 1]); epsl = sb("epsl", [128, 1])
    x32s = [sb("x32_%d" % i, [128, 8, TT]) for i in range(2)]; Bx32s = [Buf("x32_%d" % i) for i in range(2)]
    x16 = sb("x16", [128, 8, TT], BF16); Bx16 = Buf("x16")
    o32s = [sb("o16_%d" % i, [128, 12, TT], BF16) for i in range(2)]; Bo32s = [Buf("o16_%d" % i) for i in range(2)]
    y16 = sb("y16", [128, 12, TT], BF16); By16 = [Buf("y16_%d" % i) for i in range(12)]
    gt = [sb("gt%d" % i, [128, TT]) for i in range(2)]; Bgt = [Buf("gt%d" % i) for i in range(2)]
    sq = sb("sq", [128, 8, TT]); Bsq = Buf("sq")
    rstd = sb("rstd", [128, TT]); Brstd = Buf("rstd")
    tmp = sb("tmp", [128, TT]); Btmp = Buf("tmp")
    sg = [sb("sg%d" % i, [128, 3, TT]) for i in range(2)]; Bsg = [Buf("sg%d" % i) for i in range(2)]
    mm = sb("mm", [128, TT]); Bmm = Buf("mm")
    tt2 = sb("tt2", [128, TT]); Btt2 = Buf("tt2")
    mg16 = sb("mg16", [128, 8, TT], BF16); Bmg = [Buf("mg%d" % i) for i in range(8)]
    r32s = [sb("r32_%d" % i, [128, 8, TT]) for i in range(2)]; Brs = [[Buf("r%d_%d" % (j, i)) for i in range(8)] for j in range(2)]
    tmpl = sb("tmpl", [128, TT]); Btmpl = Buf("tmpl"); rstdl = sb("rstdl", [128, TT]); Brstdl = Buf("rstdl")
    mean = sb("mean", [128, TT]); Bmean = Buf("mean")
    ob = [sb("ob%d" % i, [128, TT]) for i in range(2)]; Bob = [Buf("ob%d" % i) for i in range(2)]
    ob16 = [sb("ob16_%d" % i, [128, TT], BF16) for i in range(2)]; Bob16 = [Buf("ob16_%d" % i) for i in range(2)]
    Bout = Buf("xnT")
    xnT = out32

    P.dma("sp", "c0", bgs[:], bg, writes=[Bc])
    P.dma("sp", "c4", oix[:], oidx, writes=[Bc])
    P.dma("sp", "c1", hgns[:], hgn, writes=[Bc])
    P.dma("sp", "c2", lngs[:], lng, writes=[Bc])
    P.dma("sp", "c3", lnbs[:], lnb, writes=[Bc])
    P.op("pool", lambda e: e.memset(ones32[:], 1.0), writes=[Bones])
    P.op("pool", lambda e: e.memset(epsr[:], RMS_EPS), writes=[Bc])
    P.op("pool", lambda e: e.memset(epsl[:], LN_EPS), writes=[Bc])
    ctr = [0]
    load_cast_weight(P, nc, "sp", wg, wg16, Bwg, 8, 4608, stage, stage_b, ctr, 1152)
    load_cast_weight(P, nc, "sp", wbr, wbr16, Bwbr, 12, 1024, stage, stage_b, ctr, 1024)
    load_cast_weight(P, nc, "sp", wo, wo16, Bwo, 8, 1024, stage, stage_b, ctr, 1024)

    xv = xT.rearrange("(k p) t -> p k t", p=128)
    outv = xnT.rearrange("(k p) t -> p k t", p=128)
    gi = 0

    def load_tile(t):
        cc0 = t * TT
        xb, bxb, ob_, bob = x32s[t % 2], Bx32s[t % 2], o32s[t % 2], Bo32s[t % 2]
        P.dma("act", "x32_%d" % (t % 2), xb[:], xv[:, :, cc0:cc0 + TT], reads=[Bxsrc], writes=[bxb])
        for blk in range(12):
            P.dma("pool", "o16g%d" % (t % 2), None, None, reads=[Bodst, Bc], writes=[bob],
                  fn=(lambda e, blk=blk, t=t, ob_=ob_: e.indirect_dma_start(out=ob_[:, blk, :], out_offset=None, in_=orows,
                                                                         in_offset=bass.IndirectOffsetOnAxis(ap=oix[:, t * 12 + blk:t * 12 + blk + 1], axis=0))))

    def part1(tt):
        nonlocal gi
        c0 = tt * TT
        if tt == 0:
            load_tile(0)
        if tt + 1 < NT:
            load_tile(tt + 1)
        x32, Bx32, o32, Bo32 = x32s[tt % 2], Bx32s[tt % 2], o32s[tt % 2], Bo32s[tt % 2]
        r32, Br = r32s[tt % 2], Brs[tt % 2]
        P.op("pool", lambda e, x32=x32: e.tensor_copy(out=x16[:], in_=x32[:]), reads=[Bx32], writes=[Bx16])
        P.op("act", lambda e, o32=o32: e.activation(out=sq[:, 0:4, :], in_=o32[:, 0:4, :], func=AF.Square), reads=[Bo32], writes=[Bsq])
        pt, pb = ps.next()
        for j in range(4):
            P.op("pe", lambda e, j=j, pt=pt: e.matmul(pt[:, 0:TT], lhsT=ones32[:], rhs=sq[:, j, :], start=(j == 0), stop=(j == 3)),
                 reads=[Bones, Bsq], writes=[pb])
        P.op("act", lambda e, pt=pt: e.activation(out=tmp[:], in_=pt[:, 0:TT], func=AF.Sqrt, bias=epsr[:, 0:1], scale=1.0 / 512.0), reads=[pb, Bc], writes=[Btmp])
        P.op("dve", lambda e: e.reciprocal(out=rstd[:], in_=tmp[:]), reads=[Btmp], writes=[Brstd])
        for blk in range(12):
            pt, pb = ps.next()
            for k in range(8):
                P.op("pe", lambda e, k=k, pt=pt, blk=blk: e.matmul(pt[:, 0:TT], lhsT=wg16[:, k, blk * 128:(blk + 1) * 128], rhs=x16[:, k, :],
                                                              start=(k == 0), stop=(k == 7)), reads=[Bwg, Bx16], writes=[pb])
            g = gi % 2
            gi += 1
            P.op("act", lambda e, pt=pt, blk=blk, g=g: e.activation(out=gt[g][:], in_=pt[:, 0:TT], func=AF.Silu, bias=bgs[:, blk:blk + 1], scale=1.0),
                 reads=[pb, Bc], writes=[Bgt[g]])
            if blk < 4:
                P.op("dve", lambda e, g=g: e.tensor_tensor(out=gt[g][:], in0=gt[g][:], in1=rstd[:], op=ALU.mult),
                     reads=[Bgt[g], Brstd], writes=[Bgt[g]])
                P.op("dve", lambda e, g=g, blk=blk, o32=o32: e.scalar_tensor_tensor(out=y16[:, blk, :], in0=o32[:, blk, :], scalar=hgns[:, blk:blk + 1],
                                                                           in1=gt[g][:], op0=ALU.mult, op1=ALU.mult),
                     reads=[Bo32, Bgt[g], Bc], writes=[By16[blk]])
            else:
                P.op("dve", lambda e, g=g, blk=blk, o32=o32: e.tensor_tensor(out=y16[:, blk, :], in0=o32[:, blk, :], in1=gt[g][:], op=ALU.mult),
                     reads=[Bo32, Bgt[g]], writes=[By16[blk]])
        for db in range(8):
            s = db % 2
            pbs = []
            for n in range(3):
                pt, pb = ps.next()
                col = 1536 + n * 1024 + db * 128
                for k in range(8):
                    P.op("pe", lambda e, k=k, pt=pt, col=col: e.matmul(pt[:, 0:TT], lhsT=wg16[:, k, col:col + 128], rhs=x16[:, k, :],
                                                                  start=(k == 0), stop=(k == 7)), reads=[Bwg, Bx16], writes=[pb])
                bi = 12 + n * 8 + db
                P.op("act", lambda e, pt=pt, n=n, s=s, bi=bi: e.activation(out=sg[s][:, n, :], in_=pt[:, 0:TT], func=AF.Sigmoid, bias=bgs[:, bi:bi + 1], scale=1.0),
                     reads=[pb, Bc], writes=[Bsg[s]])
            for n in range(3):
                pt, pb = ps.next()
                for j in range(4):
                    P.op("pe", lambda e, j=j, n=n, pt=pt, db=db: e.matmul(pt[:, 0:TT], lhsT=wbr16[:, n * 4 + j, db * 128:(db + 1) * 128], rhs=y16[:, n * 4 + j, :],
                                                                     start=(j == 0), stop=(j == 3)), reads=[Bwbr, By16[n * 4 + j]], writes=[pb])
                pbs.append((pt, pb))
            P.op("dve", lambda e, s=s, p0=pbs[0][0]: e.tensor_tensor(out=mm[:], in0=p0[:, 0:TT], in1=sg[s][:, 0, :], op=ALU.mult),
                 reads=[pbs[0][1], Bsg[s]], writes=[Bmm])
            P.op("dve", lambda e, s=s, p1=pbs[1][0]: e.tensor_tensor(out=tt2[:], in0=p1[:, 0:TT], in1=sg[s][:, 1, :], op=ALU.mult),
                 reads=[pbs[1][1], Bsg[s]], writes=[Btt2])
            P.op("pool", lambda e: e.tensor_tensor(out=mm[:], in0=mm[:], in1=tt2[:], op=ALU.add), reads=[Bmm, Btt2], writes=[Bmm])
            P.op("dve", lambda e, s=s, p2=pbs[2][0]: e.tensor_tensor(out=tt2[:], in0=p2[:, 0:TT], in1=sg[s][:, 2, :], op=ALU.mult),
                 reads=[pbs[2][1], Bsg[s]], writes=[Btt2])
            P.op("pool", lambda e, db=db: e.tensor_tensor(out=mg16[:, db, :], in0=mm[:], in1=tt2[:], op=ALU.add),
                 reads=[Bmm, Btt2], writes=[Bmg[db]])
        for eb in range(8):
            pt, pb = ps.next()
            for d in range(8):
                P.op("pe", lambda e, d=d, pt=pt, eb=eb: e.matmul(pt[:, 0:TT], lhsT=wo16[:, d, eb * 128:(eb + 1) * 128], rhs=mg16[:, d, :],
                                                            start=(d == 0), stop=(d == 7)), reads=[Bwo, Bmg[d]], writes=[pb])
            P.op("dve", lambda e, pt=pt, eb=eb, x32=x32: e.scalar_tensor_tensor(out=r32[:, eb, :], in0=x32[:, eb, :], scalar=ALPHA, in1=pt[:, 0:TT],
                                                                        op0=ALU.mult, op1=ALU.add), reads=[Bx32, pb], writes=[Br[eb]])

    def part2(tt):
        c0 = tt * TT
        r32, Br = r32s[tt % 2], Brs[tt % 2]
        pt, pb = ps.next()
        for eb in range(8):
            P.op("pe", lambda e, eb=eb, pt=pt: e.matmul(pt[:, 0:TT], lhsT=ones32[:], rhs=r32[:, eb, :], start=(eb == 0), stop=(eb == 7)),
                 reads=[Bones, Br[eb]], writes=[pb])
        P.op("act", lambda e, pt=pt: e.activation(out=mean[:], in_=pt[:, 0:TT], func=AF.Copy, scale=1.0 / 1024.0), reads=[pb], writes=[Bmean])
        for eb in range(8):
            P.op("dve", lambda e, eb=eb: e.tensor_tensor(out=r32[:, eb, :], in0=r32[:, eb, :], in1=mean[:], op=ALU.subtract),
                 reads=[Br[eb], Bmean], writes=[Br[eb]])
        P.op("act", lambda e: e.activation(out=sq[:], in_=r32[:], func=AF.Square), reads=Br, writes=[Bsq])
        pt, pb = ps.next()
        for eb in range(8):
            P.op("pe", lambda e, eb=eb, pt=pt: e.matmul(pt[:, 0:TT], lhsT=ones32[:], rhs=sq[:, eb, :], start=(eb == 0), stop=(eb == 7)),
                 reads=[Bones, Bsq], writes=[pb])
        P.op("act", lambda e, pt=pt: e.activation(out=tmpl[:], in_=pt[:, 0:TT], func=AF.Sqrt, bias=epsl[:, 0:1], scale=1.0 / 1024.0), reads=[pb, Bc], writes=[Btmpl])
        P.op("dve", lambda e: e.reciprocal(out=rstdl[:], in_=tmpl[:]), reads=[Btmpl], writes=[Brstdl])
        for eb in range(8):
            s = eb % 2
            P.op("dve", lambda e, eb=eb: e.tensor_tensor(out=r32[:, eb, :], in0=r32[:, eb, :], in1=rstdl[:], op=ALU.mult),
                 reads=[Br[eb], Brstdl], writes=[Br[eb]])
            P.op("act", lambda e, eb=eb, s=s: e.activation(out=ob[s][:], in_=r32[:, eb, :], func=AF.Identity, bias=lnbs[:, eb:eb + 1], scale=lngs[:, eb:eb + 1]),
                 reads=[Br[eb], Bc], writes=[Bob[s]])
            P.dma("sp", "ob%d" % s, outv[:, eb, c0:c0 + TT], ob[s][:], reads=[Bob[s]], writes=[Bout])
            if out16 is not None:
                P.op("pool", lambda e, s=s: e.tensor_copy(out=ob16[s][:], in_=ob[s][:]), reads=[Bob[s]], writes=[Bob16[s]])
                jx = c0 // 512
                P.dma("sp", "ob16_%d" % s, out16[jx * 1024 + eb * 128:jx * 1024 + (eb + 1) * 128, (c0 % 512):(c0 % 512) + TT], ob16[s][:],
                      reads=[Bob16[s]], writes=[Bout, io["Bccx"][jx]])
        if out16 is not None and (c0 % 512) + TT == 512:
            io["cc_x"](c0 // 512)

    part1(0)
    for tt in range(NT):
        if tt + 1 < NT:
            part1(tt + 1)
        part2(tt)
    P.barrier()
    esb.close()


RMS_EPS = 1e-6
S = 8192
TT = 512
NT = S // TT
QSCALE = 192.0 ** -0.5
TWO_PI = 2.0 * math.pi
C1 = 6.28125
C2 = TWO_PI - C1
DILS = (1, 4, 16)
LN_MINF = math.log(1e-6)


def make_scratch(nc):
    dr = lambda n, s, dt=BF16: nc.dram_tensor(n, s, dt, kind="Internal").ap()
    scr = {}
    scr["dsub"] = [[dr("dsub%d_%d" % (g, t), [128, DILS[g], S // DILS[g]]) for t in range(3)] for g in range(3)]
    scr["hq_d"] = [[dr("hq%d_%d" % (h, d), [128, S]) for d in range(2)] for h in range(2)]
    scr["hk_d"] = [[dr("hk%d_%d" % (h, d), [128, S]) for d in range(2)] for h in range(2)]
    scr["hv_d"] = dr("hv", [128, S])
    scr["qd1"] = dr("qd1", [128, S]); scr["qd2"] = dr("qd2", [64, S])
    return scr


def emit_A(nc, P, ps, layer, io, scr, xsrc16=None, phases=("mla", "dil", "hg")):
    debug = False
    xT = io.get("xT"); pos = io["pos"]
    wA = io["wA"]; bA = io["bA"]; lbraw = io["lbraw"]
    wuq = io["wuq"]; gq = io["gq"]; wukv = io["wukv"]; gkv = io["gkv"]
    etab = io["etab"]; ropec = io["ropec"]; ident = io["ident"]; masks = io["masks"]; scanmask = io["scanmask"]
    osrc = io["osrc"]
    dsub = scr["dsub"]; hq_d = scr["hq_d"]; hk_d = scr["hk_d"]; hv_d = scr["hv_d"]; qd1 = scr["qd1"]; qd2 = scr["qd2"]
    Bdsub = Buf("dsub"); Bhqk = Buf("hqk"); BoT = Buf("oT"); Bqd = Buf("qd")
    es0 = ExitStack()
    sb = lambda n, s, dt=F32: es0.enter_context(nc.sbuf_tensor("%s_A%d" % (n, layer), s, dt))

    Bc = Buf("consts")
    bAs = sb("bAs", [128, 23]); lbr = sb("lbr", [128, 4, 2]); lbt = sb("lbt", [128, 4, 3])
    gqs = sb("gqs", [128, 3]); gkvs = sb("gkvs", [128, 2]); ropecs = sb("ropecs", [64, 2])
    ones32 = sb("ones32", [128, 128]); epsr = sb("epsr", [128, 1]); id32 = sb("id32", [128, 128]); id16 = sb("id16", [128, 128], BF16)
    ones16 = sb("ones16", [128, 128], BF16)
    msk = sb("msk", [128, 2, 128]); smask = sb("smask", [128, TT])
    P.dma("sp", "c0", bAs[:], bA, writes=[Bc])
    P.dma("sp", "c1", lbr[:], lbraw, writes=[Bc])
    P.dma("sp", "c2", gqs[:], gq, writes=[Bc])
    P.dma("sp", "c3", gkvs[:], gkv, writes=[Bc])
    P.dma("sp", "c4", ropecs[:], ropec, writes=[Bc])
    P.dma("sp", "c5", id32[:], ident, writes=[Bc])
    P.dma("sp", "c6", msk[:], masks, writes=[Bc])
    P.dma("sp", "c7", smask[:], scanmask, writes=[Bc])
    P.op("pool", lambda e: e.memset(ones32[:], 1.0), writes=[Bc])
    P.op("pool", lambda e: e.memset(ones16[:], 1.0), writes=[Bc])
    P.op("pool", lambda e: e.memset(epsr[:], RMS_EPS), writes=[Bc])
    P.op("pool", lambda e: e.tensor_copy(out=id16[:], in_=id32[:]), reads=[Bc], writes=[Bc])
    for blk in (7, 10, 13):
        P.op("dve", lambda e, blk=blk: e.tensor_scalar(out=bAs[:, blk:blk + 1], in0=bAs[:, blk:blk + 1], scalar1=0.125, scalar2=None, op0=ALU.mult),
             reads=[Bc], writes=[Bc])
    if layer == 0:
        P.op("dve", lambda e: e.memset(lbt[:, :, 0], 0.0), writes=[Bc])
    else:
        P.op("dve", lambda e: e.tensor_tensor(out=lbt[:, :, 1], in0=lbr[:, :, 1], in1=lbr[:, :, 0], op=ALU.subtract), reads=[Bc], writes=[Bc])
        P.op("act", lambda e: e.activation(out=lbt[:, :, 0], in_=lbt[:, :, 1], func=AF.Sigmoid), reads=[Bc], writes=[Bc])
    P.op("dve", lambda e: e.tensor_scalar(out=lbt[:, :, 1], in0=lbt[:, :, 0], scalar1=-1.0, scalar2=1.0, op0=ALU.mult, op1=ALU.add), reads=[Bc], writes=[Bc])
    P.op("dve", lambda e: e.tensor_scalar(out=lbt[:, :, 2], in0=lbt[:, :, 1], scalar1=-1.0, scalar2=None, op0=ALU.mult), reads=[Bc], writes=[Bc])

    K1T = sb("K1T", [128, S], BF16); K2T = sb("K2T", [64, S], BF16)
    Vtok = sb("Vtok", [128, S // 128, 128], BF16)
    BQ = Buf("Q"); BK = Buf("K"); BV = Buf("V")
    mT = [[sb("mT%d%d" % (h, d), [128, 128]) for d in range(2)] for h in range(2)]
    BTt = [[sb("BT%d%d" % (h, d), [128, 128]) for d in range(2)] for h in range(2)]
    BmB = Buf("mB")

    es1 = ExitStack()
    sb = lambda n, s, dt=F32: es1.enter_context(nc.sbuf_tensor("%s_A%d" % (n, layer), s, dt))
    win16 = sb("win16", [128, 8, 2816], BF16); Bwin = Buf("win16")
    stage = [sb("wstage%d" % i, [128, 704], F32) for i in range(2)]
    stage_b = [Buf("wstage%d" % i) for i in range(2)]
    wv = wA.rearrange("(k p) c -> p k c", p=128)
    ci = 0
    for k in range(8):
        for c0 in (0, 704, 1408, 2112):
            s = ci % 2
            P.dma("sp", "wst%d" % s, stage[s][:], wv[:, k, c0:c0 + 704], writes=[stage_b[s]])
            P.op("dve" if ci % 2 == 0 else "pool", lambda e, s=s, k=k, c0=c0: e.tensor_copy(out=win16[:, k, c0:c0 + 704], in_=stage[s][:]),
                 reads=[stage_b[s]], writes=[Bwin])
            ci += 1
    wuq16 = sb("wuq16", [128, 3, 256], BF16); wukv16 = sb("wukv16", [128, 2, 256], BF16); Bwu = Buf("wu")
    wuv = wuq.rearrange("(k p) c -> p k c", p=128)
    wkv = wukv.rearrange("(k p) c -> p k c", p=128)
    for j in range(3):
        s = ci % 2
        P.dma("sp", "wst%d" % s, stage[s][:, 0:256], wuv[:, j, :], writes=[stage_b[s]])
        P.op("dve", lambda e, s=s, j=j: e.tensor_scalar(out=wuq16[:, j, :], in0=stage[s][:, 0:256], scalar1=gqs[:, j:j + 1], scalar2=None, op0=ALU.mult),
             reads=[stage_b[s], Bc], writes=[Bwu])
        ci += 1
    for j in range(2):
        s = ci % 2
        P.dma("sp", "wst%d" % s, stage[s][:, 0:256], wkv[:, j, :], writes=[stage_b[s]])
        P.op("dve", lambda e, s=s, j=j: e.tensor_scalar(out=wukv16[:, j, :], in0=stage[s][:, 0:256], scalar1=gkvs[:, j:j + 1], scalar2=None, op0=ALU.mult),
             reads=[stage_b[s], Bc], writes=[Bwu])
        ci += 1

    x32s = [sb("x32_%d" % i, [128, 4, TT]) for i in range(2)]; Bx32s = [Buf("x32_%d" % i) for i in range(2)]
    x16s = [sb("x16_%d" % i, [128, 8, TT], BF16) for i in range(2)]; Bx16s = [Buf("x16_%d" % i) for i in range(2)]
    xcur = [None, None]
    posi = sb("posi", [64, TT], I32); Bposi = Buf("posi")
    ang = sb("ang", [64, TT]); Bang = Buf("ang")
    ru = sb("ru", [64, TT]); Bru = Buf("ru"); rki = sb("rki", [64, TT], I32); Brki = Buf("rki"); rkf = sb("rkf", [64, TT]); Brkf = Buf("rkf")
    cs = sb("cs", [64, TT]); sn = sb("sn", [64, TT]); Bcs = Buf("cs"); Bsn = Buf("sn")
    c32 = sb("c32", [128, 3, TT]); Bc32 = Buf("c32"); csq = sb("csq", [128, 3, TT]); Bcsq = Buf("csq")
    cn16 = sb("cn16", [128, 3, TT], BF16); Bcn = Buf("cn16")
    rt = sb("rt", [128, TT]); Brt = Buf("rt"); rr = sb("rr", [128, TT]); Brr = Buf("rr")
    t1 = sb("t1", [64, TT]); t2 = sb("t2", [64, TT]); Bt1 = Buf("t1"); Bt2 = Buf("t2")
    q1s = sb("q1s", [128, TT], BF16); q2s = sb("q2s", [64, TT], BF16); Bq1s = Buf("q1s"); Bq2s = Buf("q2s")
    dd16 = [sb("dd16_%d" % i, [128, TT], BF16) for i in range(2)]; Bdd = [Buf("dd16_%d" % i) for i in range(2)]
    qs = [sb("qs%d" % h, [128, TT]) for h in range(2)]; Bqs = [Buf("qs%d" % h) for h in range(2)]
    sig = sb("sig", [128, TT]); Bsig = Buf("sig"); ff = sb("ff", [128, TT]); Bff = Buf("ff")
    bb = sb("bb", [128, TT]); Bbb = Buf("bb"); eq = sb("eq", [128, TT]); Beq = Buf("eq"); ek = sb("ek", [128, TT]); Bek = Buf("ek")
    kk = sb("kk", [128, TT]); Bkk = Buf("kk")
    hq16 = [sb("hq16_%d" % i, [128, TT], BF16) for i in range(2)]; Bhq16 = [Buf("hq16_%d" % i) for i in range(2)]
    hk16 = [sb("hk16_%d" % i, [128, TT], BF16) for i in range(2)]; Bhk16 = [Buf("hk16_%d" % i) for i in range(2)]
    hv16 = sb("hv16", [128, TT], BF16); Bhv16 = Buf("hv16")

    if xsrc16 is None:
        xv = xT.rearrange("(k p) t -> p k t", p=128)
    else:
        xg = xsrc16.rearrange("(j r k p) t -> p j r k t", j=4, r=4, k=8, p=128)

    def inproj(col, m, rhs_cols=None):
        pt, pb = ps.next()
        xx, bxx = xcur[0], xcur[1]
        for k in range(8):
            P.op("pe", lambda e, k=k, pt=pt, xx=xx: e.matmul(pt[0:m, :], lhsT=win16[:, k, col:col + m], rhs=xx[:, k, :], start=(k == 0), stop=(k == 7)),
                 reads=[Bwin, bxx], writes=[pb])
        return pt, pb

    def sintab(dst, Bdst, shift):
        P.op("dve", lambda e: e.tensor_scalar(out=ru[:], in0=ang[:], scalar1=1.0 / TWO_PI, scalar2=shift / TWO_PI + 0.5, op0=ALU.mult, op1=ALU.add),
             reads=[Bang], writes=[Bru])
        P.op("dve", lambda e: e.tensor_copy(out=rki[:], in_=ru[:]), reads=[Bru], writes=[Brki])
        P.op("dve", lambda e: e.tensor_copy(out=rkf[:], in_=rki[:]), reads=[Brki], writes=[Brkf])
        P.op("dve", lambda e: e.tensor_scalar(out=ru[:], in0=ang[:], scalar1=shift, scalar2=None, op0=ALU.add), reads=[Bang], writes=[Bru])
        P.op("dve", lambda e: e.scalar_tensor_tensor(out=ru[:], in0=rkf[:], scalar=-C1, in1=ru[:], op0=ALU.mult, op1=ALU.add),
             reads=[Brkf, Bru], writes=[Bru])
        P.op("dve", lambda e: e.scalar_tensor_tensor(out=ru[:], in0=rkf[:], scalar=-C2, in1=ru[:], op0=ALU.mult, op1=ALU.add),
             reads=[Brkf, Bru], writes=[Bru])
        P.op("dve", lambda e: e.tensor_scalar(out=rkf[:], in0=ru[:], scalar1=math.pi, scalar2=None, op0=ALU.is_gt), reads=[Bru], writes=[Brkf])
        P.op("dve", lambda e: e.scalar_tensor_tensor(out=ru[:], in0=rkf[:], scalar=-TWO_PI, in1=ru[:], op0=ALU.mult, op1=ALU.add),
             reads=[Brkf, Bru], writes=[Bru])
        P.op("dve", lambda e: e.tensor_scalar(out=rkf[:], in0=ru[:], scalar1=-math.pi, scalar2=None, op0=ALU.is_lt), reads=[Bru], writes=[Brkf])
        P.op("dve", lambda e: e.scalar_tensor_tensor(out=ru[:], in0=rkf[:], scalar=TWO_PI, in1=ru[:], op0=ALU.mult, op1=ALU.add),
             reads=[Brkf, Bru], writes=[Bru])
        P.op("dve", lambda e: e.tensor_scalar(out=ru[:], in0=ru[:], scalar1=math.pi, scalar2=-math.pi, op0=ALU.min, op1=ALU.max), reads=[Bru], writes=[Bru])
        P.op("act", lambda e: e.activation(out=dst[:], in_=ru[:], func=AF.Sin), reads=[Bru], writes=[Bdst])

    def rms_norm(nblk, rank):
        P.op("act", lambda e: e.activation(out=csq[:, 0:nblk, :], in_=c32[:, 0:nblk, :], func=AF.Square), reads=[Bc32], writes=[Bcsq])
        pt, pb = ps.next()
        for j in range(nblk):
            P.op("pe", lambda e, j=j, pt=pt: e.matmul(pt[:], lhsT=ones32[:], rhs=csq[:, j, :], start=(j == 0), stop=(j == nblk - 1)),
                 reads=[Bc, Bcsq], writes=[pb])
        P.op("act", lambda e, pt=pt: e.activation(out=rt[:], in_=pt[:], func=AF.Sqrt, bias=epsr[:, 0:1], scale=1.0 / rank), reads=[pb, Bc], writes=[Brt])
        P.op("dve", lambda e: e.reciprocal(out=rr[:], in_=rt[:]), reads=[Brt], writes=[Brr])
        for j in range(nblk):
            P.op("dve", lambda e, j=j: e.tensor_tensor(out=cn16[:, j, :], in0=c32[:, j, :], in1=rr[:], op=ALU.mult), reads=[Bc32, Brr], writes=[Bcn])

    ddi = 0
    hi = 0

    def load_x(tt):
        c0 = tt * TT
        xb, bxb = x16s[tt % 2], Bx16s[tt % 2]
        if xsrc16 is None:
            for hf in range(2):
                P.dma("sp", "x32_%d" % hf, x32s[hf][:], xv[:, hf * 4:(hf + 1) * 4, c0:c0 + TT], writes=[Bx32s[hf]])
                P.op("pool", lambda e, hf=hf, xb=xb: e.tensor_copy(out=xb[:, hf * 4:(hf + 1) * 4, :], in_=x32s[hf][:]), reads=[Bx32s[hf]], writes=[bxb])
        else:
            P.dma("sp", "x32_0", xb[:], xg[:, (c0 % 2048) // 512, c0 // 2048, :, :], reads=[io["Bxg"][(c0 % 2048) // 512]], writes=[bxb])

    load_x(0)
    for tt in range(NT):
        c0 = tt * TT
        xcur[0], xcur[1] = x16s[tt % 2], Bx16s[tt % 2]
        if tt + 1 < NT:
            load_x(tt + 1)
        if "ropecache" in io and layer > 0:
            P.dma("sp", "csld", cs[:], io["ropecache"][0][:, c0:c0 + TT], reads=[io["Bropecache"]], writes=[Bcs])
            P.dma("sp", "snld", sn[:], io["ropecache"][1][:, c0:c0 + TT], reads=[io["Bropecache"]], writes=[Bsn])
        else:
            P.dma("sp", "posi", posi[:], pos[0:1, c0:c0 + TT].partition_broadcast(64), writes=[Bposi])
            P.op("dve", lambda e: e.tensor_copy(out=ang[:], in_=posi[:]), reads=[Bposi], writes=[Bang])
            P.op("dve", lambda e: e.tensor_scalar(out=ang[:], in0=ang[:], scalar1=ropecs[:, 0:1], scalar2=None, op0=ALU.mult), reads=[Bang, Bc], writes=[Bang])
            sintab(sn, Bsn, 0.0)
            sintab(cs, Bcs, math.pi / 2)
            P.op("dve", lambda e: e.tensor_scalar(out=sn[:], in0=sn[:], scalar1=ropecs[:, 1:2], scalar2=None, op0=ALU.mult), reads=[Bsn, Bc], writes=[Bsn])

            if "ropecache" in io:
                P.dma("sp", "csst", io["ropecache"][0][:, c0:c0 + TT], cs[:], reads=[Bcs], writes=[io["Bropecache"]])
                P.dma("sp", "snst", io["ropecache"][1][:, c0:c0 + TT], sn[:], reads=[Bsn], writes=[io["Bropecache"]])
        for j in range(3):
            pt, pb = inproj((16 + j) * 128, 128)
            P.op("act", lambda e, pt=pt, j=j: e.activation(out=c32[:, j, :], in_=pt[:], func=AF.Identity, bias=bAs[:, 16 + j:17 + j], scale=1.0),
                 reads=[pb, Bc], writes=[Bc32])
        rms_norm(3, 384.0)
        if "dil" in phases:
            for g in range(3):
                d = DILS[g]
                for t in range(3):
                    blk = 7 + g * 3 + t
                    pt, pb = inproj(blk * 128, 128)
                    s = ddi % 2
                    ddi += 1
                    P.op("act", lambda e, pt=pt, s=s, d=d, blk=blk, t=t: e.activation(
                        out=dd16[s][:].rearrange("p (r j) -> p r j", r=d), in_=pt[:].rearrange("p (j r) -> p r j", r=d),
                        func=AF.Identity, bias=bAs[:, blk:blk + 1], scale=(0.125 if t == 0 else 1.0)), reads=[pb, Bc], writes=[Bdd[s]])
                    P.dma("sp", "dd%d" % s, dsub[g][t][:, :, c0 // d:(c0 + TT) // d], dd16[s][:].rearrange("p (r j) -> p r j", r=d),
                          reads=[Bdd[s]], writes=[Bdsub])
        pt, pb = ps.next()
        for j in range(3):
            P.op("pe", lambda e, j=j, pt=pt: e.matmul(pt[:], lhsT=wuq16[:, j, 0:128], rhs=cn16[:, j, :], start=(j == 0), stop=(j == 2)),
                 reads=[Bwu, Bcn], writes=[pb])
        P.op("act", lambda e, pt=pt: e.activation(out=q1s[:], in_=pt[:], func=AF.Copy, scale=QSCALE), reads=[pb], writes=[Bq1s])
        P.dma("sp", "q1s", qd1[:, c0:c0 + TT], q1s[:], reads=[Bq1s], writes=[Bqd])
        pA, pbA = ps.next()
        for j in range(3):
            P.op("pe", lambda e, j=j, pA=pA: e.matmul(pA[0:64, :], lhsT=wuq16[:, j, 128:192], rhs=cn16[:, j, :], start=(j == 0), stop=(j == 2)),
                 reads=[Bwu, Bcn], writes=[pbA])
        pB, pbB = ps.next()
        for j in range(3):
            P.op("pe", lambda e, j=j, pB=pB: e.matmul(pB[0:64, :], lhsT=wuq16[:, j, 192:256], rhs=cn16[:, j, :], start=(j == 0), stop=(j == 2)),
                 reads=[Bwu, Bcn], writes=[pbB])
        P.op("dve", lambda e, pA=pA: e.scalar_tensor_tensor(out=t1[:], in0=pA[0:64, :], scalar=QSCALE, in1=cs[:], op0=ALU.mult, op1=ALU.mult),
             reads=[pbA, Bcs], writes=[Bt1])
        P.op("dve", lambda e, pB=pB: e.scalar_tensor_tensor(out=t2[:], in0=pB[0:64, :], scalar=QSCALE, in1=sn[:], op0=ALU.mult, op1=ALU.mult),
             reads=[pbB, Bsn], writes=[Bt2])
        P.op("pool", lambda e: e.tensor_tensor(out=q2s[:], in0=t1[:], in1=t2[:], op=ALU.add), reads=[Bt1, Bt2], writes=[Bq2s])
        P.dma("sp", "q2s", qd2[:, c0:c0 + TT], q2s[:], reads=[Bq2s], writes=[Bqd])
        for j in range(2):
            pt, pb = inproj((19 + j) * 128, 128)
            P.op("act", lambda e, pt=pt, j=j: e.activation(out=c32[:, j, :], in_=pt[:], func=AF.Identity, bias=bAs[:, 19 + j:20 + j], scale=1.0),
                 reads=[pb, Bc], writes=[Bc32])
        rms_norm(2, 256.0)
        if "hg" in phases:
            for h in range(2):
                pt, pb = inproj(h * 128, 128)
                P.op("act", lambda e, pt=pt, h=h: e.activation(out=qs[h][:], in_=pt[:], func=AF.Silu, bias=bAs[:, h:h + 1], scale=1.0),
                     reads=[pb, Bc], writes=[Bqs[h]])
            pt, pb = inproj(6 * 128, 128)
            P.op("act", lambda e, pt=pt: e.activation(out=hv16[:], in_=pt[:], func=AF.Identity, bias=bAs[:, 6:7], scale=1.0), reads=[pb, Bc], writes=[Bhv16])
            P.dma("sp", "hv16", hv_d[:, c0:c0 + TT], hv16[:], reads=[Bhv16], writes=[Bhqk])
            for dr_ in range(2):
                for h in range(2):
                    idx = h * 2 + dr_
                    blk = 2 + dr_ * 2 + h
                    pt, pb = inproj(blk * 128, 128)
                    P.op("act", lambda e, pt=pt, blk=blk: e.activation(out=sig[:], in_=pt[:], func=AF.Sigmoid, bias=bAs[:, blk:blk + 1], scale=1.0),
                         reads=[pb, Bc], writes=[Bsig])
                    P.op("dve", lambda e, idx=idx: e.tensor_scalar(out=ff[:], in0=sig[:], scalar1=lbt[:, idx, 1:2], scalar2=lbt[:, idx, 0:1], op0=ALU.mult, op1=ALU.add),
                         reads=[Bsig, Bc], writes=[Bff])
                    P.op("act", lambda e: e.activation(out=ff[:], in_=ff[:], func=AF.Ln), reads=[Bff], writes=[Bff])
                    P.op("pool", lambda e: e.tensor_scalar(out=ff[:], in0=ff[:], scalar1=LN_MINF, scalar2=None, op0=ALU.max), reads=[Bff], writes=[Bff])
                    if dr_ == 0:
                        P.op("dve", lambda e: e.tensor_tensor_scan(out=bb[:], data0=smask[:], data1=ff[:], initial=0.0, op0=ALU.mult, op1=ALU.add),
                             reads=[Bff, Bc], writes=[Bbb])
                        mcol, bcol = 31, 63
                    else:
                        P.op("dve", lambda e: e.tensor_tensor_scan(out=bb[:, ::-1], data0=smask[:], data1=ff[:, ::-1], initial=0.0, op0=ALU.mult, op1=ALU.add),
                             reads=[Bff, Bc], writes=[Bbb])
                        mcol, bcol = 32, 0
                    b3 = bb[:].rearrange("p (c t) -> p c t", t=64)
                    P.op("pool", lambda e, h=h, dr_=dr_, tt=tt, b3=b3, mcol=mcol: e.tensor_copy(out=mT[h][dr_][:, tt * 8:(tt + 1) * 8], in_=b3[:, :, mcol]),
                         reads=[Bbb], writes=[BmB])
                    P.op("pool", lambda e, h=h, dr_=dr_, tt=tt, b3=b3, bcol=bcol: e.tensor_copy(out=BTt[h][dr_][:, tt * 8:(tt + 1) * 8], in_=b3[:, :, bcol]),
                         reads=[Bbb], writes=[BmB])
                    mb = b3[:, :, mcol:mcol + 1]
                    mbc = bass.AP(mb.tensor, mb.offset, [list(mb.ap[0]), list(mb.ap[1]), [0, 64]])
                    P.op("dve", lambda e, b3=b3, mbc=mbc: e.tensor_tensor(out=eq[:].rearrange("p (c t) -> p c t", t=64), in0=b3, in1=mbc, op=ALU.subtract),
                         reads=[Bbb], writes=[Beq])
                    P.op("act", lambda e: e.activation(out=ek[:], in_=eq[:], func=AF.Exp, scale=-1.0), reads=[Beq], writes=[Bek])
                    P.op("act", lambda e: e.activation(out=eq[:], in_=eq[:], func=AF.Exp), reads=[Beq], writes=[Beq])
                    s = hi % 2
                    hi += 1
                    P.op("dve", lambda e, s=s, h=h: e.tensor_tensor(out=hq16[s][:], in0=qs[h][:], in1=eq[:], op=ALU.mult), reads=[Bqs[h], Beq], writes=[Bhq16[s]])
                    P.dma("sp", "hq16_%d" % s, hq_d[h][dr_][:, c0:c0 + TT], hq16[s][:], reads=[Bhq16[s]], writes=[Bhqk])
                    P.op("dve", lambda e, idx=idx: e.tensor_scalar(out=kk[:], in0=sig[:], scalar1=lbt[:, idx, 2:3], scalar2=lbt[:, idx, 1:2], op0=ALU.mult, op1=ALU.add),
                         reads=[Bsig, Bc], writes=[Bkk])
                    P.op("pool", lambda e, s=s: e.tensor_tensor(out=hk16[s][:], in0=kk[:], in1=ek[:], op=ALU.mult), reads=[Bkk, Bek], writes=[Bhk16[s]])
                    P.dma("sp", "hk16_%d" % s, hk_d[h][dr_][:, c0:c0 + TT], hk16[s][:], reads=[Bhk16[s]], writes=[Bhqk])

        pt, pb = ps.next()
        for j in range(2):
            P.op("pe", lambda e, j=j, pt=pt: e.matmul(pt[:], lhsT=wukv16[:, j, 0:128], rhs=cn16[:, j, :], start=(j == 0), stop=(j == 1)),
                 reads=[Bwu, Bcn], writes=[pb])
        P.op("act", lambda e, pt=pt, c0=c0: e.activation(out=K1T[:, c0:c0 + TT], in_=pt[:], func=AF.Copy), reads=[pb], writes=[BK])
        pt, pb = ps.next()
        for i in range(4):
            for j in range(2):
                P.op("pe", lambda e, i=i, j=j, pt=pt: e.matmul(pt[:, i * 128:(i + 1) * 128], lhsT=cn16[:, j, i * 128:(i + 1) * 128], rhs=wukv16[:, j, 128:256],
                                                          start=(j == 0), stop=(j == 1)), reads=[Bwu, Bcn], writes=[pb])
        P.op("act", lambda e, pt=pt, tt=tt: e.activation(out=Vtok[:, tt * 4:(tt + 1) * 4, :], in_=pt[:].rearrange("p (a b) -> p a b", a=4), func=AF.Copy),
             reads=[pb], writes=[BV])
        pA, pbA = inproj(21 * 128, 64)
        pB, pbB = inproj(21 * 128 + 64, 64)
        P.op("dve", lambda e, pA=pA: e.scalar_tensor_tensor(out=t1[:], in0=pA[0:64, :], scalar=bAs[0:64, 21:22], in1=cs[:], op0=ALU.add, op1=ALU.mult),
             reads=[pbA, Bcs, Bc], writes=[Bt1])
        P.op("dve", lambda e, pB=pB: e.scalar_tensor_tensor(out=t2[:], in0=pB[0:64, :], scalar=bAs[0:64, 22:23], in1=sn[:], op0=ALU.add, op1=ALU.mult),
             reads=[pbB, Bsn, Bc], writes=[Bt2])
        P.op("pool", lambda e, c0=c0: e.tensor_tensor(out=K2T[:, c0:c0 + TT], in0=t1[:], in1=t2[:], op=ALU.add), reads=[Bt1, Bt2], writes=[BK])
    P.barrier()
    es1.close()
    es2 = ExitStack()
    sb = lambda n, s, dt=F32: es2.enter_context(nc.sbuf_tensor("%s_A%d" % (n, layer), s, dt))
    if "mla" in phases:
        pT = [sb("pT%d" % i, [128, TT], BF16) for i in range(4)]; BpT = [Buf("pT%d" % i) for i in range(4)]
        dacc = sb("dacc", [128, TT]); Bdacc = Buf("dacc")
        daccs = [sb("daccs%d" % i, [128, TT]) for i in range(3)]; Bdaccs = [Buf("daccs%d" % i) for i in range(3)]
        rden = sb("rden", [128, TT]); Brden = Buf("rden")
        oc = sb("oc", [128, TT], BF16); Boc = Buf("oc")
        po_t = nc.alloc_psum_tensor("po_mla", [128, 512], F32) if False else None
        pi = 0
        Q1 = [sb("Q1_%d" % i, [128, TT], BF16) for i in range(2)]; Q2 = [sb("Q2_%d" % i, [64, TT], BF16) for i in range(2)]
        BQs = [Buf("Qs%d" % i) for i in range(2)]
        for qt in range(NT):
            q0 = qt * TT
            qi = qt % 2
            P.dma("sp", "Q1_%d" % qi, Q1[qi][:], qd1[:, q0:q0 + TT], reads=[Bqd], writes=[BQs[qi]])
            P.dma("sp", "Q2_%d" % qi, Q2[qi][:], qd2[:, q0:q0 + TT], reads=[Bqd], writes=[BQs[qi]])
            BQ = BQs[qi]
            po, pbo = ps.next()

            def mla_a(kb, qi=qi, BQ=BQ, po=po):
                nonlocal pi
                k0 = kb * 128
                pt, pb = ps.next()
                if pt is po:
                    pt, pb = ps.next()
                P.op("pe", lambda e, pt=pt, k0=k0, qi=qi: e.matmul(pt[:], lhsT=K1T[:, k0:k0 + 128], rhs=Q1[qi][:], start=True, stop=False),
                     reads=[BK, BQ], writes=[pb])
                P.op("pe", lambda e, pt=pt, k0=k0, qi=qi: e.matmul(pt[:], lhsT=K2T[:, k0:k0 + 128], rhs=Q2[qi][:], start=False, stop=True),
                     reads=[BK, BQ], writes=[pb])
                s = pi % 4
                pi += 1
                P.op("act", lambda e, pt=pt, s=s: e.activation(out=pT[s][:], in_=pt[:], func=AF.Exp), reads=[pb], writes=[BpT[s]])
                return s

            def mla_b(kb, s, po=po, pbo=pbo):
                P.op("pe", lambda e, po=po, kb=kb, s=s: e.matmul(po[:], lhsT=Vtok[:, kb, :], rhs=pT[s][:], start=(kb == 0), stop=(kb == S // 128 - 1)),
                     reads=[BV, BpT[s]], writes=[pbo])
                ai = kb % 3
                eng = "pool" if ai == 2 else "dve"
                if kb < 3:
                    P.op(eng, lambda e, s=s, ai=ai: e.tensor_copy(out=daccs[ai][:], in_=pT[s][:]), reads=[BpT[s]], writes=[Bdaccs[ai]])
                else:
                    P.op(eng, lambda e, s=s, ai=ai: e.tensor_tensor(out=daccs[ai][:], in0=daccs[ai][:], in1=pT[s][:], op=ALU.add),
                         reads=[BpT[s], Bdaccs[ai]], writes=[Bdaccs[ai]])

            pend = []
            for kb in range(S // 128):
                pend.append((kb, mla_a(kb)))
                if len(pend) > 2:
                    mla_b(*pend.pop(0))
            while pend:
                mla_b(*pend.pop(0))
            P.op("dve", lambda e: e.tensor_tensor(out=dacc[:], in0=daccs[0][:], in1=daccs[1][:], op=ALU.add), reads=[Bdaccs[0], Bdaccs[1]], writes=[Bdacc])
            P.op("dve", lambda e: e.tensor_tensor(out=dacc[:], in0=dacc[:], in1=daccs[2][:], op=ALU.add), reads=[Bdacc, Bdaccs[2]], writes=[Bdacc])
            pd, pbd = ps.next()
            if pd is po:
                pd, pbd = ps.next()
            P.op("pe", lambda e, pd=pd: e.matmul(pd[:], lhsT=ones32[:], rhs=dacc[:], start=True, stop=True), reads=[Bc, Bdacc], writes=[pbd])
            P.op("dve", lambda e, pd=pd: e.reciprocal(out=rden[:], in_=pd[:]), reads=[pbd], writes=[Brden])
            P.op("dve", lambda e, po=po: e.tensor_tensor(out=oc[:], in0=po[:], in1=rden[:], op=ALU.mult), reads=[pbo, Brden], writes=[Boc])
            P.dma("sp", "oc", osrc[1024 + (q0 // 2048) * 128:1024 + (q0 // 2048) * 128 + 128, (q0 % 2048):(q0 % 2048) + TT], oc[:], reads=[Boc], writes=[BoT])


    P.barrier()
    if "cc_o" in io:
        io["cc_o"]((4, 5))
    es2.close()
    es2 = ExitStack()
    sb = lambda n, s, dt=F32: es2.enter_context(nc.sbuf_tensor("%s_A%d" % (n, layer), s, dt))
    if "dil" in phases:
        RG = 2048
        ets = sb("ets", [128, 18, 256]); Bets = Buf("ets")
        P.dma("sp", "ets", ets[:], etab, writes=[Bets])
        NSET = 2
        Qs_ = [sb("Qs%d" % i, [128, RG], BF16) for i in range(NSET)]; Ks_ = [sb("Ks%d" % i, [128, RG + 128], BF16) for i in range(NSET)]
        Vs_ = [sb("Vs%d" % i, [128, RG + 128], BF16) for i in range(NSET)]
        BQs_l = [Buf("Qs%d" % i) for i in range(NSET)]; BKs_l = [Buf("Ks%d" % i) for i in range(NSET)]; BVs_l = [Buf("Vs%d" % i) for i in range(NSET)]
        Vp_ = [sb("Vp%d" % i, [128, 17, 2, 65], BF16) for i in range(NSET)]; BVp_l = [[Buf("Vp%d_%d" % (i, j)) for j in range(17)] for i in range(NSET)]
        NU = 4
        pe32 = [sb("pe32_%d" % i, [128, 256]) for i in range(NU)]; Bpe = [Buf("pe32_%d" % i) for i in range(NU)]
        pt16 = [sb("pt16_%d" % i, [128, 256], BF16) for i in range(NU)]; Bpt16 = [Buf("pt16_%d" % i) for i in range(NU)]
        acc = [sb("dacc%d" % h, [65, RG]) for h in range(2)]; Bacc = [Buf("dacc%d" % h) for h in range(2)]
        obd = sb("obd", [64, 512], BF16); Bobd = Buf("obd")
        for i in range(NSET):
            P.op("pool", lambda e, i=i: e.memset(Vp_[i][:], 1.0), writes=BVp_l[i])
        ui = 0
        si = 0
        for rg in range(S // RG):
            R0 = rg * RG
            for h in range(2):
                P.op("pool", lambda e, h=h: e.memset(acc[h][:], 0.0), writes=[Bacc[h]])
            for g in range(3):
                d = DILS[g]
                J = S // d
                nj = RG // d
                nb = nj // 128
                j0 = R0 // d
                for r in range(d):
                    ss = si % NSET
                    si += 1
                    Qs, Ks, Vs, Vp = Qs_[ss], Ks_[ss], Vs_[ss], Vp_[ss]
                    BQs_, BKs, BVs, BVp = BQs_l[ss], BKs_l[ss], BVs_l[ss], BVp_l[ss]
                    lo = j0 - 64
                    hi_ = j0 + nj + 64
                    clo = max(lo, 0)
                    chi = min(hi_, J)
                    if lo < 0:
                        P.op("pool", lambda e, Ks=Ks: e.memset(Ks[:, 0:64], 0.0), writes=[BKs])
                        P.op("pool", lambda e, Vs=Vs: e.memset(Vs[:, 0:64], 0.0), writes=[BVs])
                    if hi_ > J:
                        P.op("pool", lambda e, nj=nj, Ks=Ks: e.memset(Ks[:, nj + 64:nj + 128], 0.0), writes=[BKs])
                        P.op("pool", lambda e, nj=nj, Vs=Vs: e.memset(Vs[:, nj + 64:nj + 128], 0.0), writes=[BVs])
                    P.dma("sp", "Qs%d" % ss, Qs[:, 0:nj], dsub[g][0][:, r, j0:j0 + nj], reads=[Bdsub], writes=[BQs_])
                    P.dma("sp", "Ks%d" % ss, Ks[:, clo - lo:chi - lo], dsub[g][1][:, r, clo:chi], reads=[Bdsub], writes=[BKs])
                    P.dma("sp", "Vs%d" % ss, Vs[:, clo - lo:chi - lo], dsub[g][2][:, r, clo:chi], reads=[Bdsub], writes=[BVs])
                    for n in range(nb + 1):
                        ptr, pbr = ps.next()
                        ptr16 = ptr[:].bitcast(BF16)
                        P.op("pe", lambda e, n=n, ptr16=ptr16, Vs=Vs: e.transpose(out=ptr16[:, 0:128], in_=Vs[:, n * 128:(n + 1) * 128], identity=id16[:]),
                             reads=[BVs, Bc], writes=[pbr])
                        P.op("act", lambda e, n=n, ptr16=ptr16, Vp=Vp: e.activation(out=Vp[:, n, :, 0:64], in_=ptr16[:, 0:128].rearrange("p (h v) -> p h v", h=2), func=AF.Copy),
                             reads=[pbr], writes=[BVp[n]])

                    def dil_a(qb, h, Qs=Qs, Ks=Ks, BQs_=BQs_, BKs=BKs, g=g, j0=j0, J=J):
                        nonlocal ui
                        jb = j0 + qb * 128
                        var = 1 if jb == 0 else (2 if jb + 128 == J else 0)
                        u = ui % NU
                        ui += 1
                        pt, pb = ps.next()
                        for kc in range(2):
                            P.op("pe", lambda e, pt=pt, kc=kc, qb=qb, h=h: e.matmul(pt[:, kc * 128:(kc + 1) * 128],
                                 lhsT=Ks[h * 64:(h + 1) * 64, (qb + kc) * 128:(qb + kc + 1) * 128], rhs=Qs[h * 64:(h + 1) * 64, qb * 128:(qb + 1) * 128],
                                 start=True, stop=True), reads=[BKs, BQs_], writes=[pb])
                        P.op("act", lambda e, pt=pt, u=u: e.activation(out=pe32[u][:], in_=pt[:, 0:256], func=AF.Exp), reads=[pb], writes=[Bpe[u]])
                        ei = (g * 2 + h) * 3 + var
                        P.op("dve", lambda e, u=u, ei=ei: e.tensor_tensor(out=pt16[u][:], in0=pe32[u][:], in1=ets[:, ei, :], op=ALU.mult),
                             reads=[Bpe[u], Bets], writes=[Bpt16[u]])
                        return (qb, h, u)

                    def dil_b(qb, h, u, Vp=Vp, BVp=BVp, d=d, r=r):
                        po, pbo = ps.next()
                        for kc in range(2):
                            P.op("pe", lambda e, po=po, kc=kc, qb=qb, h=h, u=u: e.matmul(po[0:65, 0:128], lhsT=Vp[:, qb + kc, h, :], rhs=pt16[u][:, kc * 128:(kc + 1) * 128],
                                 start=(kc == 0), stop=(kc == 1)), reads=[BVp[qb + kc], Bpt16[u]], writes=[pbo])
                        st = qb * 128 * d + r
                        av = acc[h][:, st:st + 127 * d + 1:d]
                        P.op("dve", lambda e, po=po, av=av: e.tensor_tensor(out=av, in0=av, in1=po[0:65, 0:128], op=ALU.add), reads=[pbo, Bacc[h]], writes=[Bacc[h]])

                    pend = []
                    for qb in range(nb):
                        for h in range(2):
                            pend.append(dil_a(qb, h))
                            if len(pend) > 2:
                                dil_b(*pend.pop(0))
                    while pend:
                        dil_b(*pend.pop(0))
            for h in range(2):
                P.op("dve", lambda e, h=h: e.reciprocal(out=acc[h][64:65, :], in_=acc[h][64:65, :]), reads=[Bacc[h]], writes=[Bacc[h]])
                for cc in range(RG // 512):
                    pt, pb = ps.next()
                    P.op("pe", lambda e, pt=pt, h=h, cc=cc: e.matmul(pt[0:64, :], lhsT=ones32[64:65, 0:64], rhs=acc[h][64:65, cc * 512:(cc + 1) * 512], start=True, stop=True),
                         reads=[Bc, Bacc[h]], writes=[pb])
                    P.op("dve", lambda e, pt=pt, h=h, cc=cc: e.tensor_tensor(out=obd[:], in0=acc[h][0:64, cc * 512:(cc + 1) * 512], in1=pt[0:64, :], op=ALU.mult),
                         reads=[pb, Bacc[h]], writes=[Bobd])
                    P.dma("sp", "obd", osrc[512 + rg * 128 + h * 64:512 + rg * 128 + (h + 1) * 64, cc * 512:(cc + 1) * 512], obd[:], reads=[Bobd], writes=[BoT])

    P.barrier()
    if "cc_o" in io:
        io["cc_o"]((2, 3))
    es2.close()
    es2 = ExitStack()
    sb = lambda n, s, dt=F32: es2.enter_context(nc.sbuf_tensor("%s_A%d" % (n, layer), s, dt))
    if "hg" in phases:
        oac = [sb("oac%d" % h, [64, S]) for h in range(2)]; Boac = [Buf("oac%d" % h) for h in range(2)]
        for h in range(2):
            P.op("pool", lambda e, h=h: e.memset(oac[h][:], 0.0), writes=[Boac[h]])
        gam = [[sb("gam%d%d" % (h, d), [128, 128]) for d in range(2)] for h in range(2)]; Bgam = Buf("gam")
        for h in range(2):
            for d in range(2):
                P.op("pool", lambda e, h=h, d=d: e.memset(gam[h][d][:], 1.0), writes=[Bgam])
        for h in range(2):
            for d in range(2):
                if d == 0:
                    dst, mn, bc, mc = gam[h][d][:, 0:127], mT[h][d][:, 1:128], BTt[h][d][:, 0:127], mT[h][d][:, 0:127]
                else:
                    dst, mn, bc, mc = gam[h][d][:, 1:128], mT[h][d][:, 0:127], BTt[h][d][:, 1:128], mT[h][d][:, 1:128]
                P.op("dve", lambda e, dst=dst, mn=mn, bc=bc: e.tensor_tensor(out=dst, in0=mn, in1=bc, op=ALU.add), reads=[BmB], writes=[Bgam])
                P.op("dve", lambda e, dst=dst, mc=mc: e.tensor_tensor(out=dst, in0=dst, in1=mc, op=ALU.subtract), reads=[BmB, Bgam], writes=[Bgam])
                P.op("act", lambda e, dst=dst: e.activation(out=dst, in_=dst, func=AF.Exp), reads=[Bgam], writes=[Bgam])
        ch = [(h, d) for d in range(2) for h in range(2)]
        qT = {c: sb("hqT%d%d" % c, [128, TT], BF16) for c in ch}; kT = {c: sb("hkT%d%d" % c, [128, TT], BF16) for c in ch}
        vT = {c: sb("hvT%d%d" % c, [128, TT], BF16) for c in ch}
        Bld = {c: Buf("hld%d%d" % c) for c in ch}
        kvtok = {c: sb("kvtok%d%d" % c, [128, 256], BF16) for c in ch}; Bkv = {c: Buf("kvtok%d%d" % c) for c in ch}
        at16 = {c: sb("at16%d%d" % c, [128, 128], BF16) for c in ch}; Bat = {c: Buf("at%d%d" % c) for c in ch}
        S32 = {c: sb("S32%d%d" % c, [128, 64]) for c in ch}; S16 = {c: sb("S16%d%d" % c, [128, 64], BF16) for c in ch}
        Sh = {c: sb("Sh%d%d" % c, [128, 64]) for c in ch}
        BS32 = {c: Buf("S32%d%d" % c) for c in ch}; BS16 = {c: Buf("S16%d%d" % c) for c in ch}; BSh = {c: Buf("Sh%d%d" % c) for c in ch}
        for c in ch:
            P.op("pool", lambda e, c=c: e.memset(S32[c][:], 0.0), writes=[BS32[c]])
            P.op("pool", lambda e, c=c: e.memset(S16[c][:], 0.0), writes=[BS16[c]])
        for step in range(NT):
            for c in ch:
                h, d = c
                ti = step if d == 0 else NT - 1 - step
                c0 = ti * TT
                P.dma("sp", "hl%d%d" % c, qT[c][:], hq_d[h][d][:, c0:c0 + TT], reads=[Bhqk], writes=[Bld[c]])
                P.dma("sp", "hl%d%d" % c, kT[c][:], hk_d[h][d][:, c0:c0 + TT], reads=[Bhqk], writes=[Bld[c]])
                P.dma("sp", "hl%d%d" % c, vT[c][:], hv_d[:, c0:c0 + TT], reads=[Bhqk], writes=[Bld[c]])
            for pp in range(4):
                inf = {}
                for c in ch:
                    h, d = c
                    ti = step if d == 0 else NT - 1 - step
                    pr = pp if d == 0 else 3 - pp
                    inf[c] = (h, d, ti, pr, pr * 128)
                trb = {}
                for ci, c in enumerate(ch):
                    h, d, ti, pr, p0 = inf[c]
                    bank, bb_ = ps.t[ci], ps.b[ci]
                    b16 = bank[:].bitcast(BF16)
                    P.op("pe", lambda e, c=c, p0=p0, b16=b16: e.transpose(out=b16[:, 0:128], in_=kT[c][:, p0:p0 + 128], identity=id16[:]),
                         reads=[Bld[c], Bc], writes=[bb_])
                    P.op("pe", lambda e, c=c, p0=p0, b16=b16: e.transpose(out=b16[:, 128:256], in_=vT[c][:, p0:p0 + 128], identity=id16[:]),
                         reads=[Bld[c], Bc], writes=[bb_])
                    trb[c] = (b16, bb_)
                for c in ch:
                    b16, bb_ = trb[c]
                    P.op("act", lambda e, c=c, b16=b16: e.activation(out=kvtok[c][:], in_=b16[:, 0:256], func=AF.Copy), reads=[bb_], writes=[Bkv[c]])
                pab = {}
                for ci, c in enumerate(ch):
                    h, d, ti, pr, p0 = inf[c]
                    pa, pba = ps.t[ci], ps.b[ci]
                    P.op("pe", lambda e, c=c, p0=p0, pa=pa: e.matmul(pa[:, 0:128], lhsT=kT[c][:, p0:p0 + 128], rhs=qT[c][:, p0:p0 + 128], start=True, stop=True),
                         reads=[Bld[c]], writes=[pba])
                    pab[c] = (pa, pba)
                for c in ch:
                    h, d, ti, pr, p0 = inf[c]
                    pa, pba = pab[c]
                    P.op("dve", lambda e, c=c, d=d, pa=pa: e.tensor_tensor(out=at16[c][:], in0=pa[:, 0:128], in1=msk[:, d, :], op=ALU.mult),
                         reads=[pba, Bc], writes=[Bat[c]])
                for cc in range(2):
                    pub = {}
                    for ci, c in enumerate(ch):
                        h, d, ti, pr, p0 = inf[c]
                        po, pbo = ps.t[4 + ci], ps.b[4 + ci]
                        ck = cc if d == 0 else 1 - cc
                        q0 = p0 + ck * 64
                        if cc == 0:
                            P.op("pe", lambda e, c=c, h=h, po=po: e.matmul(po[0:64, 0:128], lhsT=kvtok[c][:, 128 + h * 64:128 + (h + 1) * 64], rhs=at16[c][:], start=True, stop=False),
                                 reads=[Bkv[c], Bat[c]], writes=[pbo])
                        P.op("pe", lambda e, c=c, po=po, ck=ck, q0=q0, cc=cc: e.matmul(po[0:64, ck * 64:(ck + 1) * 64], lhsT=S16[c][:], rhs=qT[c][:, q0:q0 + 64],
                             start=False, stop=(cc == 1)), reads=[BS16[c], Bld[c]], writes=[pbo])
                        pu, pbu = ps.t[ci], ps.b[ci]
                        P.op("pe", lambda e, c=c, h=h, pu=pu, ck=ck: e.matmul(pu[:, 0:64], lhsT=kvtok[c][ck * 64:(ck + 1) * 64, 0:128],
                             rhs=kvtok[c][ck * 64:(ck + 1) * 64, 128 + h * 64:128 + (h + 1) * 64], start=True, stop=True), reads=[Bkv[c]], writes=[pbu])
                        pub[c] = (pu, pbu, ti * 8 + pr * 2 + ck)
                    for c in ch:
                        pu, pbu, cidx = pub[c]
                        P.op("dve", lambda e, c=c, pu=pu: e.tensor_tensor(out=Sh[c][:], in0=pu[:, 0:64], in1=S32[c][:], op=ALU.add), reads=[pbu, BS32[c]], writes=[BSh[c]])
                    for c in ch:
                        h, d = c
                        pu, pbu, cidx = pub[c]
                        P.op("pool", lambda e, c=c, h=h, d=d, cidx=cidx: e.tensor_scalar(out=S32[c][:], in0=Sh[c][:], scalar1=gam[h][d][:, cidx:cidx + 1], scalar2=None, op0=ALU.mult),
                             reads=[BSh[c], Bgam], writes=[BS32[c]])
                        P.op("act", lambda e, c=c, h=h, d=d, cidx=cidx: e.activation(out=S16[c][:], in_=Sh[c][:], func=AF.Copy, scale=gam[h][d][:, cidx:cidx + 1]),
                             reads=[BSh[c], Bgam], writes=[BS16[c]])
                for ci, c in enumerate(ch):
                    h, d, ti, pr, p0 = inf[c]
                    po, pbo = ps.t[4 + ci], ps.b[4 + ci]
                    t0_ = ti * TT + p0
                    P.op("dve", lambda e, h=h, po=po, t0_=t0_: e.tensor_tensor(out=oac[h][:, t0_:t0_ + 128], in0=oac[h][:, t0_:t0_ + 128], in1=po[0:64, 0:128], op=ALU.add),
                         reads=[pbo, Boac[h]], writes=[Boac[h]])
        o16c = [sb("o16c%d" % i, [64, 2048], BF16) for i in range(2)]; Bo16c = [Buf("o16c%d" % i) for i in range(2)]
        for h in range(2):
            for tq in range(4):
                u = (h * 4 + tq) % 2
                P.op("act" if u == 0 else "dve", (lambda e, h=h, tq=tq, u=u: e.activation(out=o16c[u][:], in_=oac[h][:, tq * 2048:(tq + 1) * 2048], func=AF.Copy)) if u == 0 else
                     (lambda e, h=h, tq=tq, u=u: e.tensor_copy(out=o16c[u][:], in_=oac[h][:, tq * 2048:(tq + 1) * 2048])), reads=[Boac[h]], writes=[Bo16c[u]])
                P.dma("sp", "o16c%d" % u, osrc[tq * 128 + h * 64:tq * 128 + (h + 1) * 64, :], o16c[u][:], reads=[Bo16c[u]], writes=[BoT])

    P.barrier()
    es2.close()
    es0.close()


RG4 = [[0, 1, 2, 3], [4, 5, 6, 7]]


def build_fused():
    nc = bass.Bass("TRN2", target_bir_lowering=False)
    dr = lambda n, s, kind="ExternalInput", dt=F32: nc.dram_tensor(n, s, dt, kind=kind).ap()
    shared = {"pos": dr("pos", [1, S], dt=I32), "etab": dr("etab", [128, 18, 256]), "ropec": dr("ropec", [64, 2]),
              "ident": dr("ident", [128, 128]), "masks": dr("masks", [128, 2, 128]), "scanmask": dr("scanmask", [128, 512]),
              "lbraw": dr("lbraw", [128, 4, 2])}
    xT = dr("xT", [1024, S]); xTq = dr("xTq", [1024, 2048]); oidx = dr("oidx", [128, 96], dt=I32)
    ioA, ioB = [], []
    for l in range(2):
        a = dict(shared)
        a.update({"wA": dr("wA%d" % l, [1024, 2816]), "bA": dr("bA%d" % l, [128, 23]), "wuq": dr("wuq%d" % l, [384, 256]), "gq": dr("gq%d" % l, [128, 3]),
                  "wukv": dr("wukv%d" % l, [256, 256]), "gkv": dr("gkv%d" % l, [128, 2])})
        ioA.append(a)
        ioB.append({"wg": dr("wg%d" % l, [1024, 4608]), "bg": dr("bg%d" % l, [128, 36]), "wbr": dr("wbr%d" % l, [1536, 1024]), "wo": dr("wo%d" % l, [1024, 1024]),
                    "hgn": dr("hgn%d" % l, [128, 4]), "lng": dr("lng%d" % l, [128, 8]), "lnb": dr("lnb%d" % l, [128, 8]), "oidx": oidx})
    outT = dr("outT", [1024, 2048], kind="ExternalOutput")
    cco_src = [nc.dram_tensor("cco_src%d" % l, [1536, 2048], BF16) for l in range(2)]
    cco_dst = [nc.dram_tensor("cco_dst%d" % l, [6 * 1024, 2048], BF16) for l in range(2)]
    ccx_src = nc.dram_tensor("ccx_src", [4 * 1024, 512], BF16)
    ccx_dst = nc.dram_tensor("ccx_dst", [4 * 4096, 512], BF16)
    xn32 = nc.dram_tensor("xn32", [1024, 2048], F32).ap()
    scr = make_scratch(nc)
    ropecache = [nc.dram_tensor("ropec_%d" % i, [64, S], F32).ap() for i in range(2)]
    Bropecache = Buf("ropecache")
    P = Prog(nc)
    ps = PsumPool(nc)
    Bxn = Buf("xn32"); Bxg = Buf("xg"); Bnone = Buf("none")
    Bods = [Buf("cco_dst%d" % l) for l in range(2)]
    Bccx = [Buf("ccx_src%d" % j) for j in range(4)]
    Bxgs = [Buf("xg%d" % j) for j in range(4)]

    def cc_o(l):
        def go(chunks):
            for k in chunks:
                P.dma("pool", "cc_o%d" % l, None, None, reads=[Bnone], writes=[Bods[l]], inc=1,
                      fn=(lambda e, l=l, k=k: e.collective_compute("AllGather", ALU.bypass, replica_groups=RG4,
                                                                 ins=[cco_src[l].ap()[k * 256:(k + 1) * 256, :].opt()],
                                                                 outs=[cco_dst[l].ap()[k * 1024:(k + 1) * 1024, :].opt()])))
        return go

    def cc_x(j):
        P.dma("pool", "cc_x", None, None, reads=[Bccx[j]], writes=[Bxgs[j]], inc=1,
              fn=(lambda e, j=j: e.collective_compute("AllGather", ALU.bypass, replica_groups=RG4,
                                                    ins=[ccx_src.ap()[j * 1024:(j + 1) * 1024, :].opt()],
                                                    outs=[ccx_dst.ap()[j * 4096:(j + 1) * 4096, :].opt()])))

    for l in range(2):
        ioA[l]["osrc"] = cco_src[l].ap()
        ioA[l]["cc_o"] = cc_o(l)
        ioA[l]["ropecache"] = ropecache; ioA[l]["Bropecache"] = Bropecache
        if l == 0:
            ioA[l]["xT"] = xT
            emit_A(nc, P, ps, l, ioA[l], scr)
        else:
            ioA[l]["Bxg"] = Bxgs
            emit_A(nc, P, ps, l, ioA[l], scr, xsrc16=ccx_dst.ap())
        P.barrier()
        Bod = Bods[l]
        cc_o(l)((0, 1))
        b = ioB[l]
        b["orows"] = cco_dst[l].ap().rearrange("r (a c) -> (r a) c", c=256)
        b["Bodst"] = Bod
        if l == 0:
            b["x32src"] = xTq; b["Bxsrc"] = Bnone; b["out32"] = xn32; b["out16"] = ccx_src.ap(); b["Bccx"] = Bccx; b["cc_x"] = cc_x
        else:
            b["x32src"] = xn32; b["Bxsrc"] = Bxn; b["out32"] = outT
        emit_B(nc, P, ps, l, b)
        P.barrier()
    P.barrier()
    P.emit()
    return nc


SPL = [1024,1024,1024,512,512] + [512]*10 + [384,256,64,512,3072]
NAMES = ['hg_q','hg_f_fwd','hg_f_bwd','hg_i','hg_g','dil_q0','dil_k0','dil_v0','dil_q1','dil_k1','dil_v1','dil_q2','dil_k2','dil_v2','dil_g','mla_cq','mla_ckv','mla_kr','mla_g','merge']
OFF = dict(zip(NAMES, [int(v) for v in np.cumsum([0]+SPL[:-1])]))
def a_cols(hq):
    ar = np.arange
    c = []
    c += [OFF['hg_q'] + (2*hq)*128 + ar(128), OFF['hg_q'] + (2*hq+1)*128 + ar(128)]
    c += [OFF['hg_f_fwd'] + (2*hq)*128 + ar(128), OFF['hg_f_fwd'] + (2*hq+1)*128 + ar(128)]
    c += [OFF['hg_f_bwd'] + (2*hq)*128 + ar(128), OFF['hg_f_bwd'] + (2*hq+1)*128 + ar(128)]
    c += [OFF['hg_i'] + hq*128 + ar(128)]
    for g in range(3):
        for t in 'qkv':
            c += [OFF['dil_%s%d' % (t, g)] + hq*128 + ar(128)]
    c += [OFF['mla_cq'] + ar(384), OFF['mla_ckv'] + ar(256)]
    kr = OFF['mla_kr'] + ar(64)
    c += [kr, np.concatenate([kr[32:], kr[:32]])]
    return np.concatenate(c)
def etab_np(hq):
    slopes = 2.0 ** (-8.0 * (np.arange(24) + 1) / 24)
    kk = np.arange(128)[:, None]; qq = np.arange(128)[None, :]
    E = np.zeros((128, 18, 256), np.float32)
    for g, d in enumerate((1, 4, 16)):
        for hh in range(2):
            sl = slopes[g*8 + 2*hq + hh]
            for var in range(3):
                for kc in range(2):
                    rel = (kk + 128*kc - 64) - qq
                    e = np.where(np.abs(rel) <= 64, np.exp(-sl * d * np.abs(rel)), 0.0)
                    if var == 1 and kc == 0: e = np.where(kk < 64, 0.0, e)
                    if var == 2 and kc == 1: e = np.where(kk >= 64, 0.0, e)
                    E[:, (g*2+hh)*3 + var, kc*128:(kc+1)*128] = e
    return E
def a_inputs(inp, l, b, hq, xT_b):
    cols = a_cols(hq)
    w_in = inp['w_in'][l]; b_in = inp['b_in'][l]
    bsel = b_in[cols]
    bA = np.zeros((128, 23), np.float32)
    bA[:, :22] = bsel.reshape(22, 128).T
    bA[:64, 22] = bsel[21*128+64: 22*128]
    lbraw = np.zeros((128, 4, 2), np.float32)
    for h in range(2):
        for d, nm in enumerate(('hg_lb_fwd', 'hg_lb_bwd')):
            lbraw[:, h*2+d, :] = inp[nm][:, (2*hq+h)*128:(2*hq+h+1)*128].T
    wuq = inp['w_uq'][l]
    qc = hq*192 + np.arange(192)
    rope = qc[128:]
    wuq_sel = np.concatenate([wuq[:, qc[:128]], wuq[:, rope], wuq[:, np.concatenate([rope[32:], rope[:32]])]], 1)
    wukv = inp['w_ukv'][l]
    wukv_sel = wukv[:, hq*256:(hq+1)*256]
    inv = (1.0 / (10000.0 ** (np.arange(32, dtype=np.float32) / 32))).astype(np.float32)
    ropec = np.zeros((64, 2), np.float32); ropec[:, 0] = np.concatenate([inv, inv]); ropec[:32, 1] = -1.0; ropec[32:, 1] = 1.0
    masks = np.zeros((128, 2, 128), np.float32)
    ss = np.arange(128)[:, None]; tq = np.arange(128)[None, :]
    same = (ss // 64) == (tq // 64)
    masks[:, 0, :] = (same & (ss <= tq)).astype(np.float32)
    masks[:, 1, :] = (same & (ss >= tq)).astype(np.float32)
    sm = np.ones((128, 512), np.float32); sm[:, ::64] = 0.0
    return {"pos": np.ascontiguousarray(inp['positions'][b:b+1].astype(np.int32)), "wA": np.ascontiguousarray(w_in[:, cols]), "bA": bA, "lbraw": lbraw,
            "wuq": np.ascontiguousarray(wuq_sel), "gq": np.ascontiguousarray(inp['mla_q_norm'][l].reshape(3,128).T),
            "wukv": np.ascontiguousarray(wukv_sel), "gkv": np.ascontiguousarray(inp['mla_kv_norm'][l].reshape(2,128).T),
            "etab": etab_np(hq), "ropec": ropec, "ident": np.eye(128, dtype=np.float32), "masks": masks, "scanmask": sm}


_CACHE = {}


def _b_inputs(inp, l):
    w_in = inp['w_in'][l]; b_in = inp['b_in'][l]
    cols = np.concatenate([np.arange(OFF['hg_g'], OFF['hg_g'] + 512), np.arange(OFF['dil_g'], OFF['dil_g'] + 512),
                           np.arange(OFF['mla_g'], OFF['mla_g'] + 512), np.arange(OFF['merge'], OFF['merge'] + 3072)])
    return {"wg%d" % l: np.ascontiguousarray(w_in[:, cols]), "bg%d" % l: np.ascontiguousarray(b_in[cols].reshape(36, 128).T),
            "wbr%d" % l: np.ascontiguousarray(inp['w_branch'][l].reshape(1536, 1024)), "wo%d" % l: np.ascontiguousarray(inp['w_out'][l]),
            "hgn%d" % l: np.ascontiguousarray(inp['hg_norm'][l].reshape(4, 128).T),
            "lng%d" % l: np.ascontiguousarray(inp['ln_g'][l].reshape(8, 128).T), "lnb%d" % l: np.ascontiguousarray(inp['ln_b'][l].reshape(8, 128).T)}


def _oidx(tq):
    p = np.arange(128)[:, None, None, None]; tt = np.arange(8)[None, :, None, None]
    n = np.arange(3)[None, None, :, None]; r = np.arange(4)[None, None, None, :]
    rho = n * 512 + tq * 128 + p
    g = (rho // 256) * 1024 + r * 256 + (rho % 256)
    v = g * 8 + tt
    return np.ascontiguousarray(v.reshape(128, 96).astype(np.int32))


def kernel(**inputs):
    inp = {k: np.asarray(v) for k, v in inputs.items()}
    inp['positions'] = inp['positions'].astype(np.int32)
    for k in inp:
        if k != 'positions':
            inp[k] = inp[k].astype(np.float32, copy=False)
    B = 2
    xT = [np.ascontiguousarray(inp['x'][b].T) for b in range(B)]
    if "nc" not in _CACHE:
        _CACHE["nc"] = build_fused()
    nc = _CACHE["nc"]
    bl = [_b_inputs(inp, l) for l in range(2)]
    in_maps = []
    for c in range(8):
        b, q = c // 4, c % 4
        m = {"xT": xT[b], "xTq": np.ascontiguousarray(xT[b][:, q * 2048:(q + 1) * 2048]), "oidx": _oidx(q)}
        for l in range(2):
            a = a_inputs(inp, l, b, q, None)
            for k in ("pos", "etab", "ropec", "ident", "masks", "scanmask", "lbraw"):
                m[k] = a[k]
            for k in ("wA", "bA", "wuq", "gq", "wukv", "gkv"):
                m["%s%d" % (k, l)] = a[k]
            m.update(bl[l])
        in_maps.append(m)
    res = run_bass_kernel_spmd(nc, in_maps, core_ids=list(range(8))).results
    out = np.empty((B, 8192, 1024), np.float32)
    for c in range(8):
        b, q = c // 4, c % 4
        out[b, q * 2048:(q + 1) * 2048, :] = np.asarray(res[c]["outT"]).T
    return out
```

```python
import math
from contextlib import ExitStack
import numpy as np
from concourse.bass_utils import run_bass_kernel_spmd
import concourse.bass as bass
import concourse.mybir as mybir

F32 = mybir.dt.float32
BF16 = mybir.dt.bfloat16
I32 = mybir.dt.int32
AF = mybir.ActivationFunctionType
ALU = mybir.AluOpType
AX = mybir.AxisListType


class Buf:
    __slots__ = ("name", "w", "r")

    def __init__(self, name=""):
        self.name = name
        self.w = {}
        self.r = {}


class _Eng:
    def __init__(self, name, sem):
        self.name = name
        self.sem = sem
        self.count = 0
        self.waited = {}
        self.items = []


class Prog:
    ENGS = ("pe", "act", "dve", "pool", "sp")

    def __init__(self, nc):
        self.nc = nc
        self.e = {n: _Eng(n, nc.alloc_semaphore("prog_" + n)) for n in self.ENGS}
        self.chan = {}
        self.nops = 0
        self.retired = []

    def _need(self, eng, waits, ev, raw):
        sem, val, en = ev
        if en == eng.name:
            if eng.name == "pe" or not raw:
                return
        k = id(sem)
        if eng.waited.get(k, 0) >= val:
            return
        if k not in waits or waits[k][1] < val:
            waits[k] = (sem, val)

    def _deps(self, eng, reads, writes):
        waits = {}
        for b in reads:
            for ev in b.w.values():
                self._need(eng, waits, ev, True)
        for b in writes:
            for ev in b.w.values():
                self._need(eng, waits, ev, False)
            for ev in b.r.values():
                self._need(eng, waits, ev, False)
        for k, (sem, val) in waits.items():
            eng.waited[k] = val
        return list(waits.values())

    @staticmethod
    def _mark(ev, reads, writes):
        k = id(ev[0])
        for b in reads:
            o = b.r.get(k)
            if o is None or o[1] < ev[1]:
                b.r[k] = ev
        for b in writes:
            o = b.w.get(k)
            if o is None or o[1] < ev[1]:
                b.w[k] = ev

    def op(self, engname, fn, reads=(), writes=()):
        eng = self.e[engname]
        waits = self._deps(eng, reads, writes)
        if eng.count >= 30000:
            self.retired.append((eng.sem, eng.count))
            eng.sem = self.nc.alloc_semaphore("prog_%s_%d" % (engname, self.nops))
            eng.count = 0
        eng.count += 1
        ev = (eng.sem, eng.count, eng.name)
        eng.items.append((waits, fn, (eng.sem, 1)))
        self._mark(ev, reads, writes)
        self.nops += 1
        return ev

    def dma(self, qname, chan, out, in_, reads=(), writes=(), fn=None, inc=16):
        eng = self.e[qname]
        waits = self._deps(eng, reads, writes)
        if chan not in self.chan:
            self.chan[chan] = [self.nc.alloc_semaphore("ch_" + chan), 0]
        c = self.chan[chan]
        if c[1] >= 30000:
            self.retired.append((c[0], c[1]))
            c[0] = self.nc.alloc_semaphore("ch_%s_%d" % (chan, self.nops))
            c[1] = 0
        c[1] += inc
        ev = (c[0], c[1], "dma")
        if fn is None:
            fn = (lambda e, o=out, i=in_: e.dma_start(out=o, in_=i))
        eng.items.append((waits, fn, (c[0], inc)))
        self._mark(ev, reads, writes)
        self.nops += 1
        return ev

    def barrier(self):
        evs = list(self.retired)
        for n in self.ENGS:
            if self.e[n].count > 0:
                evs.append((self.e[n].sem, self.e[n].count))
        for c in self.chan.values():
            evs.append((c[0], c[1]))
        for n in self.ENGS:
            eng = self.e[n]
            waits = []
            for sem, val in evs:
                if eng.waited.get(id(sem), 0) < val:
                    waits.append((sem, val))
                    eng.waited[id(sem)] = val
            eng.items.append((waits, None, None))

    def wait_all(self, engname, bufs):
        eng = self.e[engname]
        waits = self._deps(eng, bufs, bufs)
        eng.items.append((waits, None, None))

    def emit(self):
        nc = self.nc
        with nc.Block() as block:
            def run(eng, h):
                for waits, fn, inc in eng.items:
                    for sem, val in waits:
                        h.wait_ge(sem, val)
                    if fn is not None:
                        ins = fn(h)
                        ins.then_inc(inc[0], inc[1])

            @block.tensor
            def _(h):
                run(self.e["pe"], h)

            @block.scalar
            def _(h):
                run(self.e["act"], h)

            @block.vector
            def _(h):
                run(self.e["dve"], h)

            @block.gpsimd
            def _(h):
                run(self.e["pool"], h)

            @block.sync
            def _(h):
                run(self.e["sp"], h)


ALPHA = 4.0 ** 0.25
LN_EPS = 1e-5


class PsumPool:
    def __init__(self, nc, n=8):
        self.t = [nc.alloc_psum_tensor("psb%d" % i, [128, 512], F32) for i in range(n)]
        self.b = [Buf("psb%d" % i) for i in range(n)]
        self.i = 0
        self.n = n

    def next(self):
        i = self.i
        self.i = (i + 1) % self.n
        return self.t[i], self.b[i]


def load_cast_weight(P, nc, q, dram2d, dst16, dstbuf, kchunks, ncols, stage, stage_bufs, ctr, colsplit):
    v = dram2d.rearrange("(k p) c -> p k c", p=128)
    for k in range(kchunks):
        for c0 in range(0, ncols, colsplit):
            cw = min(colsplit, ncols - c0)
            s = ctr[0] % len(stage)
            ctr[0] += 1
            P.dma(q, "wst%d" % s, stage[s][:, 0:cw], v[:, k, c0:c0 + cw], writes=[stage_bufs[s]])
            eng = ("dve", "act", "pool")[ctr[0] % 3]
            if eng == "act":
                P.op(eng, (lambda e, s=s, k=k, c0=c0, cw=cw: e.activation(out=dst16[:, k, c0:c0 + cw], in_=stage[s][:, 0:cw], func=AF.Copy)),
                     reads=[stage_bufs[s]], writes=[dstbuf])
            else:
                P.op(eng, (lambda e, s=s, k=k, c0=c0, cw=cw: e.tensor_copy(out=dst16[:, k, c0:c0 + cw], in_=stage[s][:, 0:cw])),
                     reads=[stage_bufs[s]], writes=[dstbuf])


def emit_B(nc, P, ps, layer, io):
    T = 2048
    TT = 256
    NT = T // TT
    xT = io["x32src"]; wg = io["wg"]; bg = io["bg"]; wbr = io["wbr"]; wo = io["wo"]; hgn = io["hgn"]; lng = io["lng"]; lnb = io["lnb"]
    orows = io["orows"]; oidx = io["oidx"]; Bodst = io["Bodst"]; Bxsrc = io["Bxsrc"]
    out32 = io["out32"]; out16 = io.get("out16")
    esb = ExitStack()
    sb = lambda n, s, dt=F32: esb.enter_context(nc.sbuf_tensor("%s_B%d" % (n, layer), s, dt))
    wg16 = sb("wg16", [128, 8, 4608], BF16); Bwg = Buf("wg16")
    wbr16 = sb("wbr16", [128, 12, 1024], BF16); Bwbr = Buf("wbr16")
    wo16 = sb("wo16", [128, 8, 1024], BF16); Bwo = Buf("wo16")
    stage = [sb("wstage%d" % i, [128, 1152], F32) for i in range(2)]
    stage_b = [Buf("wstage%d" % i) for i in range(2)]
    bgs = sb("bgs", [128, 36]); hgns = sb("hgns", [128, 4]); lngs = sb("lngs", [128, 8]); lnbs = sb("lnbs", [128, 8])
    oix = sb("oix", [128, 96], I32)
    Bc = Buf("consts")
    ones32 = sb("ones32", [128, 128]); Bones = Buf("ones")
    epsr = sb("epsr", [128, 1]); epsl = sb("epsl", [128, 1])
    x32s = [sb("x32_%d" % i, [128, 8, TT]) for i in range(2)]; Bx32s = [Buf("x32_%d" % i) for i in range(2)]
    x16 = sb("x16", [128, 8, TT], BF16); Bx16 = Buf("x16")
    o32s = [sb("o16_%d" % i, [128, 12, TT], BF16) for i in range(2)]; Bo32s = [Buf("o16_%d" % i) for i in range(2)]
    y16 = sb("y16", [128, 12, TT], BF16); By16 = [Buf("y16_%d" % i) for i in range(12)]
    gt = [sb("gt%d" % i, [128, TT]) for i in range(2)]; Bgt = [Buf("gt%d" % i) for i in range(2)]
    sq = sb("sq", [128, 8, TT]); Bsq = Buf("sq")
    rstd = sb("rstd", [128, TT]); Brstd = Buf("rstd")
    tmp = sb("tmp", [128, TT]); Btmp = Buf("tmp")
    sg = [sb("sg%d" % i, [128, 3, TT]) for i in range(2)]; Bsg = [Buf("sg%d" % i) for i in range(2)]
    mm = sb("mm", [128, TT]); Bmm = Buf("mm")
    tt2 = sb("tt2", [128, TT]); Btt2 = Buf("tt2")
    mg16 = sb("mg16", [128, 8, TT], BF16); Bmg = [Buf("mg%d" % i) for i in range(8)]
    r32s = [sb("r32_%d" % i, [128, 8, TT]) for i in range(2)]; Brs = [[Buf("r%d_%d" % (j, i)) for i in range(8)] for j in range(2)]
    tmpl = sb("tmpl", [128, TT]); Btmpl = Buf("tmpl"); rstdl = sb("rstdl", [128, TT]); Brstdl = Buf("rstdl")
    mean = sb("mean", [128, TT]); Bmean = Buf("mean")
    ob = [sb("ob%d" % i, [128, TT]) for i in range(2)]; Bob = [Buf("ob%d" % i) for i in range(2)]
    ob16 = [sb("ob16_%d" % i, [128, TT], BF16) for i in range(2)]; Bob16 = [Buf("ob16_%d" % i) for i in range(2)]
    Bout = Buf("xnT")
    xnT = out32

    P.dma("sp", "c0", bgs[:], bg, writes=[Bc])
    P.dma("sp", "c4", oix[:], oidx, writes=[Bc])
    P.dma("sp", "c1", hgns[:], hgn, writes=[Bc])
    P.dma("sp", "c2", lngs[:], lng, writes=[Bc])
    P.dma("sp", "c3", lnbs[:], lnb, writes=[Bc])
    P.op("pool", lambda e: e.memset(ones32[:], 1.0), writes=[Bones])
    P.op("pool", lambda e: e.memset(epsr[:], RMS_EPS), writes=[Bc])
    P.op("pool", lambda e: e.memset(epsl[:], LN_EPS), writes=[Bc])
    ctr = [0]
    load_cast_weight(P, nc, "sp", wg, wg16, Bwg, 8, 4608, stage, stage_b, ctr, 1152)
    load_cast_weight(P, nc, "sp", wbr, wbr16, Bwbr, 12, 1024, stage, stage_b, ctr, 1024)
    load_cast_weight(P, nc, "sp", wo, wo16, Bwo, 8, 1024, stage, stage_b, ctr, 1024)

    xv = xT.rearrange("(k p) t -> p k t", p=128)
    outv = xnT.rearrange("(k p) t -> p k t", p=128)
    gi = 0

    def load_tile(t):
        cc0 = t * TT
        xb, bxb, ob_, bob = x32s[t % 2], Bx32s[t % 2], o32s[t % 2], Bo32s[t % 2]
        P.dma("act", "x32_%d" % (t % 2), xb[:], xv[:, :, cc0:cc0 + TT], reads=[Bxsrc], writes=[bxb])
        for blk in range(12):
            P.dma("pool", "o16g%d" % (t % 2), None, None, reads=[Bodst, Bc], writes=[bob],
                  fn=(lambda e, blk=blk, t=t, ob_=ob_: e.indirect_dma_start(out=ob_[:, blk, :], out_offset=None, in_=orows,
                                                                         in_offset=bass.IndirectOffsetOnAxis(ap=oix[:, t * 12 + blk:t * 12 + blk + 1], axis=0))))

    def part1(tt):
        nonlocal gi
        c0 = tt * TT
        if tt == 0:
            load_tile(0)
        if tt + 1 < NT:
            load_tile(tt + 1)
        x32, Bx32, o32, Bo32 = x32s[tt % 2], Bx32s[tt % 2], o32s[tt % 2], Bo32s[tt % 2]
        r32, Br = r32s[tt % 2], Brs[tt % 2]
        P.op("pool", lambda e, x32=x32: e.tensor_copy(out=x16[:], in_=x32[:]), reads=[Bx32], writes=[Bx16])
        P.op("act", lambda e, o32=o32: e.activation(out=sq[:, 0:4, :], in_=o32[:, 0:4, :], func=AF.Square), reads=[Bo32], writes=[Bsq])
        pt, pb = ps.next()
        for j in range(4):
            P.op("pe", lambda e, j=j, pt=pt: e.matmul(pt[:, 0:TT], lhsT=ones32[:], rhs=sq[:, j, :], start=(j == 0), stop=(j == 3)),
                 reads=[Bones, Bsq], writes=[pb])
        P.op("act", lambda e, pt=pt: e.activation(out=tmp[:], in_=pt[:, 0:TT], func=AF.Sqrt, bias=epsr[:, 0:1], scale=1.0 / 512.0), reads=[pb, Bc], writes=[Btmp])
        P.op("dve", lambda e: e.reciprocal(out=rstd[:], in_=tmp[:]), reads=[Btmp], writes=[Brstd])
        for blk in range(12):
            pt, pb = ps.next()
            for k in range(8):
                P.op("pe", lambda e, k=k, pt=pt, blk=blk: e.matmul(pt[:, 0:TT], lhsT=wg16[:, k, blk * 128:(blk + 1) * 128], rhs=x16[:, k, :],
                                                              start=(k == 0), stop=(k == 7)), reads=[Bwg, Bx16], writes=[pb])
            g = gi % 2
            gi += 1
            P.op("act", lambda e, pt=pt, blk=blk, g=g: e.activation(out=gt[g][:], in_=pt[:, 0:TT], func=AF.Silu, bias=bgs[:, blk:blk + 1], scale=1.0),
                 reads=[pb, Bc], writes=[Bgt[g]])
            if blk < 4:
                P.op("dve", lambda e, g=g: e.tensor_tensor(out=gt[g][:], in0=gt[g][:], in1=rstd[:], op=ALU.mult),
                     reads=[Bgt[g], Brstd], writes=[Bgt[g]])
                P.op("dve", lambda e, g=g, blk=blk, o32=o32: e.scalar_tensor_tensor(out=y16[:, blk, :], in0=o32[:, blk, :], scalar=hgns[:, blk:blk + 1],
                                                                           in1=gt[g][:], op0=ALU.mult, op1=ALU.mult),
                     reads=[Bo32, Bgt[g], Bc], writes=[By16[blk]])
            else:
                P.op("dve", lambda e, g=g, blk=blk, o32=o32: e.tensor_tensor(out=y16[:, blk, :], in0=o32[:, blk, :], in1=gt[g][:], op=ALU.mult),
                     reads=[Bo32, Bgt[g]], writes=[By16[blk]])
        for db in range(8):
            s = db % 2
            pbs = []
            for n in range(3):
                pt, pb = ps.next()
                col = 1536 + n * 1024 + db * 128
                for k in range(8):
                    P.op("pe", lambda e, k=k, pt=pt, col=col: e.matmul(pt[:, 0:TT], lhsT=wg16[:, k, col:col + 128], rhs=x16[:, k, :],
                                                                  start=(k == 0), stop=(k == 7)), reads=[Bwg, Bx16], writes=[pb])
                bi = 12 + n * 8 + db
                P.op("act", lambda e, pt=pt, n=n, s=s, bi=bi: e.activation(out=sg[s][:, n, :], in_=pt[:, 0:TT], func=AF.Sigmoid, bias=bgs[:, bi:bi + 1], scale=1.0),
                     reads=[pb, Bc], writes=[Bsg[s]])
            for n in range(3):
                pt, pb = ps.next()
                for j in range(4):
                    P.op("pe", lambda e, j=j, n=n, pt=pt, db=db: e.matmul(pt[:, 0:TT], lhsT=wbr16[:, n * 4 + j, db * 128:(db + 1) * 128], rhs=y16[:, n * 4 + j, :],
                                                                     start=(j == 0), stop=(j == 3)), reads=[Bwbr, By16[n * 4 + j]], writes=[pb])
                pbs.append((pt, pb))
            P.op("dve", lambda e, s=s, p0=pbs[0][0]: e.tensor_tensor(out=mm[:], in0=p0[:, 0:TT], in1=sg[s][:, 0, :], op=ALU.mult),
                 reads=[pbs[0][1], Bsg[s]], writes=[Bmm])
            P.op("dve", lambda e, s=s, p1=pbs[1][0]: e.tensor_tensor(out=tt2[:], in0=p1[:, 0:TT], in1=sg[s][:, 1, :], op=ALU.mult),
                 reads=[pbs[1][1], Bsg[s]], writes=[Btt2])
            P.op("pool", lambda e: e.tensor_tensor(out=mm[:], in0=mm[:], in1=tt2[:], op=ALU.add), reads=[Bmm, Btt2], writes=[Bmm])
            P.op("dve", lambda e, s=s, p2=pbs[2][0]: e.tensor_tensor(out=tt2[:], in0=p2[:, 0:TT], in1=sg[s][:, 2, :], op=ALU.mult),
                 reads=[pbs[2][1], Bsg[s]], writes=[Btt2])
            P.op("pool", lambda e, db=db: e.tensor_tensor(out=mg16[:, db, :], in0=mm[:], in1=tt2[:], op=ALU.add),
                 reads=[Bmm, Btt2], writes=[Bmg[db]])
        for eb in range(8):
            pt, pb = ps.next()
            for d in range(8):
                P.op("pe", lambda e, d=d, pt=pt, eb=eb: e.matmul(pt[:, 0:TT], lhsT=wo16[:, d, eb * 128:(eb + 1) * 128], rhs=mg16[:, d, :],
                                                            start=(d == 0), stop=(d == 7)), reads=[Bwo, Bmg[d]], writes=[pb])
            P.op("dve", lambda e, pt=pt, eb=eb, x32=x32: e.scalar_tensor_tensor(out=r32[:, eb, :], in0=x32[:, eb, :], scalar=ALPHA, in1=pt[:, 0:TT],
                                                                        op0=ALU.mult, op1=ALU.add), reads=[Bx32, pb], writes=[Br[eb]])

    def part2(tt):
        c0 = tt * TT
        r32, Br = r32s[tt % 2], Brs[tt % 2]
        pt, pb = ps.next()
        for eb in range(8):
            P.op("pe", lambda e, eb=eb, pt=pt: e.matmul(pt[:, 0:TT], lhsT=ones32[:], rhs=r32[:, eb, :], start=(eb == 0), stop=(eb == 7)),
                 reads=[Bones, Br[eb]], writes=[pb])
        P.op("act", lambda e, pt=pt: e.activation(out=mean[:], in_=pt[:, 0:TT], func=AF.Copy, scale=1.0 / 1024.0), reads=[pb], writes=[Bmean])
        for eb in range(8):
            P.op("dve", lambda e, eb=eb: e.tensor_tensor(out=r32[:, eb, :], in0=r32[:, eb, :], in1=mean[:], op=ALU.subtract),
                 reads=[Br[eb], Bmean], writes=[Br[eb]])
        P.op("act", lambda e: e.activation(out=sq[:], in_=r32[:], func=AF.Square), reads=Br, writes=[Bsq])
        pt, pb = ps.next()
        for eb in range(8):
            P.op("pe", lambda e, eb=eb, pt=pt: e.matmul(pt[:, 0:TT], lhsT=ones32[:], rhs=sq[:, eb, :], start=(eb == 0), stop=(eb == 7)),
                 reads=[Bones, Bsq], writes=[pb])
        P.op("act", lambda e, pt=pt: e.activation(out=tmpl[:], in_=pt[:, 0:TT], func=AF.Sqrt, bias=epsl[:, 0:1], scale=1.0 / 1024.0), reads=[pb, Bc], writes=[Btmpl])
        P.op("dve", lambda e: e.reciprocal(out=rstdl[:], in_=tmpl[:]), reads=[Btmpl], writes=[Brstdl])
        for eb in range(8):
            s = eb % 2
            P.op("dve", lambda e, eb=eb: e.tensor_tensor(out=r32[:, eb, :], in0=r32[:, eb, :], in1=rstdl[:], op=ALU.mult),
                 reads=[Br[eb], Brstdl], writes=[Br[eb]])
            P.op("act", lambda e, eb=eb, s=s: e.activation(out=ob[s][:], in_=r32[:, eb, :], func=AF.Identity, bias=lnbs[:, eb:eb + 1], scale=lngs[:, eb:eb + 1]),
                 reads=[Br[eb], Bc], writes=[Bob[s]])
            P.dma("sp", "ob%d" % s, outv[:, eb, c0:c0 + TT], ob[s][:], reads=[Bob[s]], writes=[Bout])
            if out16 is not None:
                P.op("pool", lambda e, s=s: e.tensor_copy(out=ob16[s][:], in_=ob[s][:]), reads=[Bob[s]], writes=[Bob16[s]])
                jx = c0 // 512
                P.dma("sp", "ob16_%d" % s, out16[jx * 1024 + eb * 128:jx * 1024 + (eb + 1) * 128, (c0 % 512):(c0 % 512) + TT], ob16[s][:],
                      reads=[Bob16[s]], writes=[Bout, io["Bccx"][jx]])
        if out16 is not None and (c0 % 512) + TT == 512:
            io["cc_x"](c0 // 512)

    part1(0)
    for tt in range(NT):
        if tt + 1 < NT:
            part1(tt + 1)
        part2(tt)
    P.barrier()
    esb.close()


RMS_EPS = 1e-6
S = 8192
TT = 512
NT = S // TT
QSCALE = 192.0 ** -0.5
TWO_PI = 2.0 * math.pi
C1 = 6.28125
C2 = TWO_PI - C1
DILS = (1, 4, 16)
LN_MINF = math.log(1e-6)


def make_scratch(nc):
    dr = lambda n, s, dt=BF16: nc.dram_tensor(n, s, dt, kind="Internal").ap()
    scr = {}
    scr["dsub"] = [[dr("dsub%d_%d" % (g, t), [128, DILS[g], S // DILS[g]]) for t in range(3)] for g in range(3)]
    scr["hq_d"] = [[dr("hq%d_%d" % (h, d), [128, S]) for d in range(2)] for h in range(2)]
    scr["hk_d"] = [[dr("hk%d_%d" % (h, d), [128, S]) for d in range(2)] for h in range(2)]
    scr["hv_d"] = dr("hv", [128, S])
    scr["qd1"] = dr("qd1", [128, S]); scr["qd2"] = dr("qd2", [64, S])
    return scr


def emit_A(nc, P, ps, layer, io, scr, xsrc16=None, phases=("mla", "dil", "hg")):
    debug = False
    xT = io.get("xT"); pos = io["pos"]
    wA = io["wA"]; bA = io["bA"]; lbraw = io["lbraw"]
    wuq = io["wuq"]; gq = io["gq"]; wukv = io["wukv"]; gkv = io["gkv"]
    etab = io["etab"]; ropec = io["ropec"]; ident = io["ident"]; masks = io["masks"]; scanmask = io["scanmask"]
    osrc = io["osrc"]
    dsub = scr["dsub"]; hq_d = scr["hq_d"]; hk_d = scr["hk_d"]; hv_d = scr["hv_d"]; qd1 = scr["qd1"]; qd2 = scr["qd2"]
    Bdsub = Buf("dsub"); Bhqk = Buf("hqk"); BoT = Buf("oT"); Bqd = Buf("qd")
    es0 = ExitStack()
    sb = lambda n, s, dt=F32: es0.enter_context(nc.sbuf_tensor("%s_A%d" % (n, layer), s, dt))

    Bc = Buf("consts")
    bAs = sb("bAs", [128, 23]); lbr = sb("lbr", [128, 4, 2]); lbt = sb("lbt", [128, 4, 3])
    gqs = sb("gqs", [128, 3]); gkvs = sb("gkvs", [128, 2]); ropecs = sb("ropecs", [64, 2])
    ones32 = sb("ones32", [128, 128]); epsr = sb("epsr", [128, 1]); id32 = sb("id32", [128, 128]); id16 = sb("id16", [128, 128], BF16)
    ones16 = sb("ones16", [128, 128], BF16)
    msk = sb("msk", [128, 2, 128]); smask = sb("smask", [128, TT])
    P.dma("sp", "c0", bAs[:], bA, writes=[Bc])
    P.dma("sp", "c1", lbr[:], lbraw, writes=[Bc])
    P.dma("sp", "c2", gqs[:], gq, writes=[Bc])
    P.dma("sp", "c3", gkvs[:], gkv, writes=[Bc])
    P.dma("sp", "c4", ropecs[:], ropec, writes=[Bc])
    P.dma("sp", "c5", id32[:], ident, writes=[Bc])
    P.dma("sp", "c6", msk[:], masks, writes=[Bc])
    P.dma("sp", "c7", smask[:], scanmask, writes=[Bc])
    P.op("pool", lambda e: e.memset(ones32[:], 1.0), writes=[Bc])
    P.op("pool", lambda e: e.memset(ones16[:], 1.0), writes=[Bc])
    P.op("pool", lambda e: e.memset(epsr[:], RMS_EPS), writes=[Bc])
    P.op("pool", lambda e: e.tensor_copy(out=id16[:], in_=id32[:]), reads=[Bc], writes=[Bc])
    for blk in (7, 10, 13):
        P.op("dve", lambda e, blk=blk: e.tensor_scalar(out=bAs[:, blk:blk + 1], in0=bAs[:, blk:blk + 1], scalar1=0.125, scalar2=None, op0=ALU.mult),
             reads=[Bc], writes=[Bc])
    if layer == 0:
        P.op("dve", lambda e: e.memset(lbt[:, :, 0], 0.0), writes=[Bc])
    else:
        P.op("dve", lambda e: e.tensor_tensor(out=lbt[:, :, 1], in0=lbr[:, :, 1], in1=lbr[:, :, 0], op=ALU.subtract), reads=[Bc], writes=[Bc])
        P.op("act", lambda e: e.activation(out=lbt[:, :, 0], in_=lbt[:, :, 1], func=AF.Sigmoid), reads=[Bc], writes=[Bc])
    P.op("dve", lambda e: e.tensor_scalar(out=lbt[:, :, 1], in0=lbt[:, :, 0], scalar1=-1.0, scalar2=1.0, op0=ALU.mult, op1=ALU.add), reads=[Bc], writes=[Bc])
    P.op("dve", lambda e: e.tensor_scalar(out=lbt[:, :, 2], in0=lbt[:, :, 1], scalar1=-1.0, scalar2=None, op0=ALU.mult), reads=[Bc], writes=[Bc])

    K1T = sb("K1T", [128, S], BF16); K2T = sb("K2T", [64, S], BF16)
    Vtok = sb("Vtok", [128, S // 128, 128], BF16)
    BQ = Buf("Q"); BK = Buf("K"); BV = Buf("V")
    mT = [[sb("mT%d%d" % (h, d), [128, 128]) for d in range(2)] for h in range(2)]
    BTt = [[sb("BT%d%d" % (h, d), [128, 128]) for d in range(2)] for h in range(2)]
    BmB = Buf("mB")

    es1 = ExitStack()
    sb = lambda n, s, dt=F32: es1.enter_context(nc.sbuf_tensor("%s_A%d" % (n, layer), s, dt))
    win16 = sb("win16", [128, 8, 2816], BF16); Bwin = Buf("win16")
    stage = [sb("wstage%d" % i, [128, 704], F32) for i in range(2)]
    stage_b = [Buf("wstage%d" % i) for i in range(2)]
    wv = wA.rearrange("(k p) c -> p k c", p=128)
    ci = 0
    for k in range(8):
        for c0 in (0, 704, 1408, 2112):
            s = ci % 2
            P.dma("sp", "wst%d" % s, stage[s][:], wv[:, k, c0:c0 + 704], writes=[stage_b[s]])
            P.op("dve" if ci % 2 == 0 else "pool", lambda e, s=s, k=k, c0=c0: e.tensor_copy(out=win16[:, k, c0:c0 + 704], in_=stage[s][:]),
                 reads=[stage_b[s]], writes=[Bwin])
            ci += 1
    wuq16 = sb("wuq16", [128, 3, 256], BF16); wukv16 = sb("wukv16", [128, 2, 256], BF16); Bwu = Buf("wu")
    wuv = wuq.rearrange("(k p) c -> p k c", p=128)
    wkv = wukv.rearrange("(k p) c -> p k c", p=128)
    for j in range(3):
        s = ci % 2
        P.dma("sp", "wst%d" % s, stage[s][:, 0:256], wuv[:, j, :], writes=[stage_b[s]])
        P.op("dve", lambda e, s=s, j=j: e.tensor_scalar(out=wuq16[:, j, :], in0=stage[s][:, 0:256], scalar1=gqs[:, j:j + 1], scalar2=None, op0=ALU.mult),
             reads=[stage_b[s], Bc], writes=[Bwu])
        ci += 1
    for j in range(2):
        s = ci % 2
        P.dma("sp", "wst%d" % s, stage[s][:, 0:256], wkv[:, j, :], writes=[stage_b[s]])
        P.op("dve", lambda e, s=s, j=j: e.tensor_scalar(out=wukv16[:, j, :], in0=stage[s][:, 0:256], scalar1=gkvs[:, j:j + 1], scalar2=None, op0=ALU.mult),
             reads=[stage_b[s], Bc], writes=[Bwu])
        ci += 1

    x32s = [sb("x32_%d" % i, [128, 4, TT]) for i in range(2)]; Bx32s = [Buf("x32_%d" % i) for i in range(2)]
    x16s = [sb("x16_%d" % i, [128, 8, TT], BF16) for i in range(2)]; Bx16s = [Buf("x16_%d" % i) for i in range(2)]
    xcur = [None, None]
    posi = sb("posi", [64, TT], I32); Bposi = Buf("posi")
    ang = sb("ang", [64, TT]); Bang = Buf("ang")
    ru = sb("ru", [64, TT]); Bru = Buf("ru"); rki = sb("rki", [64, TT], I32); Brki = Buf("rki"); rkf = sb("rkf", [64, TT]); Brkf = Buf("rkf")
    cs = sb("cs", [64, TT]); sn = sb("sn", [64, TT]); Bcs = Buf("cs"); Bsn = Buf("sn")
    c32 = sb("c32", [128, 3, TT]); Bc32 = Buf("c32"); csq = sb("csq", [128, 3, TT]); Bcsq = Buf("csq")
    cn16 = sb("cn16", [128, 3, TT], BF16); Bcn = Buf("cn16")
    rt = sb("rt", [128, TT]); Brt = Buf("rt"); rr = sb("rr", [128, TT]); Brr = Buf("rr")
    t1 = sb("t1", [64, TT]); t2 = sb("t2", [64, TT]); Bt1 = Buf("t1"); Bt2 = Buf("t2")
    q1s = sb("q1s", [128, TT], BF16); q2s = sb("q2s", [64, TT], BF16); Bq1s = Buf("q1s"); Bq2s = Buf("q2s")
    dd16 = [sb("dd16_%d" % i, [128, TT], BF16) for i in range(2)]; Bdd = [Buf("dd16_%d" % i) for i in range(2)]
    qs = [sb("qs%d" % h, [128, TT]) for h in range(2)]; Bqs = [Buf("qs%d" % h) for h in range(2)]
    sig = sb("sig", [128, TT]); Bsig = Buf("sig"); ff = sb("ff", [128, TT]); Bff = Buf("ff")
    bb = sb("bb", [128, TT]); Bbb = Buf("bb"); eq = sb("eq", [128, TT]); Beq = Buf("eq"); ek = sb("ek", [128, TT]); Bek = Buf("ek")
    kk = sb("kk", [128, TT]); Bkk = Buf("kk")
    hq16 = [sb("hq16_%d" % i, [128, TT], BF16) for i in range(2)]; Bhq16 = [Buf("hq16_%d" % i) for i in range(2)]
    hk16 = [sb("hk16_%d" % i, [128, TT], BF16) for i in range(2)]; Bhk16 = [Buf("hk16_%d" % i) for i in range(2)]
    hv16 = sb("hv16", [128, TT], BF16); Bhv16 = Buf("hv16")

    if xsrc16 is None:
        xv = xT.rearrange("(k p) t -> p k t", p=128)
    else:
        xg = xsrc16.rearrange("(j r k p) t -> p j r k t", j=4, r=4, k=8, p=128)

    def inproj(col, m, rhs_cols=None):
        pt, pb = ps.next()
        xx, bxx = xcur[0], xcur[1]
        for k in range(8):
            P.op("pe", lambda e, k=k, pt=pt, xx=xx: e.matmul(pt[0:m, :], lhsT=win16[:, k, col:col + m], rhs=xx[:, k, :], start=(k == 0), stop=(k == 7)),
                 reads=[Bwin, bxx], writes=[pb])
        return pt, pb

    def sintab(dst, Bdst, shift):
        P.op("dve", lambda e: e.tensor_scalar(out=ru[:], in0=ang[:], scalar1=1.0 / TWO_PI, scalar2=shift / TWO_PI + 0.5, op0=ALU.mult, op1=ALU.add),
             reads=[Bang], writes=[Bru])
        P.op("dve", lambda e: e.tensor_copy(out=rki[:], in_=ru[:]), reads=[Bru], writes=[Brki])
        P.op("dve", lambda e: e.tensor_copy(out=rkf[:], in_=rki[:]), reads=[Brki], writes=[Brkf])
        P.op("dve", lambda e: e.tensor_scalar(out=ru[:], in0=ang[:], scalar1=shift, scalar2=None, op0=ALU.add), reads=[Bang], writes=[Bru])
        P.op("dve", lambda e: e.scalar_tensor_tensor(out=ru[:], in0=rkf[:], scalar=-C1, in1=ru[:], op0=ALU.mult, op1=ALU.add),
             reads=[Brkf, Bru], writes=[Bru])
        P.op("dve", lambda e: e.scalar_tensor_tensor(out=ru[:], in0=rkf[:], scalar=-C2, in1=ru[:], op0=ALU.mult, op1=ALU.add),
             reads=[Brkf, Bru], writes=[Bru])
        P.op("dve", lambda e: e.tensor_scalar(out=rkf[:], in0=ru[:], scalar1=math.pi, scalar2=None, op0=ALU.is_gt), reads=[Bru], writes=[Brkf])
        P.op("dve", lambda e: e.scalar_tensor_tensor(out=ru[:], in0=rkf[:], scalar=-TWO_PI, in1=ru[:], op0=ALU.mult, op1=ALU.add),
             reads=[Brkf, Bru], writes=[Bru])
        P.op("dve", lambda e: e.tensor_scalar(out=rkf[:], in0=ru[:], scalar1=-math.pi, scalar2=None, op0=ALU.is_lt), reads=[Bru], writes=[Brkf])
        P.op("dve", lambda e: e.scalar_tensor_tensor(out=ru[:], in0=rkf[:], scalar=TWO_PI, in1=ru[:], op0=ALU.mult, op1=ALU.add),
             reads=[Brkf, Bru], writes=[Bru])
        P.op("dve", lambda e: e.tensor_scalar(out=ru[:], in0=ru[:], scalar1=math.pi, scalar2=-math.pi, op0=ALU.min, op1=ALU.max), reads=[Bru], writes=[Bru])
        P.op("act", lambda e: e.activation(out=dst[:], in_=ru[:], func=AF.Sin), reads=[Bru], writes=[Bdst])

    def rms_norm(nblk, rank):
        P.op("act", lambda e: e.activation(out=csq[:, 0:nblk, :], in_=c32[:, 0:nblk, :], func=AF.Square), reads=[Bc32], writes=[Bcsq])
        pt, pb = ps.next()
        for j in range(nblk):
            P.op("pe", lambda e, j=j, pt=pt: e.matmul(pt[:], lhsT=ones32[:], rhs=csq[:, j, :], start=(j == 0), stop=(j == nblk - 1)),
                 reads=[Bc, Bcsq], writes=[pb])
        P.op("act", lambda e, pt=pt: e.activation(out=rt[:], in_=pt[:], func=AF.Sqrt, bias=epsr[:, 0:1], scale=1.0 / rank), reads=[pb, Bc], writes=[Brt])
        P.op("dve", lambda e: e.reciprocal(out=rr[:], in_=rt[:]), reads=[Brt], writes=[Brr])
        for j in range(nblk):
            P.op("dve", lambda e, j=j: e.tensor_tensor(out=cn16[:, j, :], in0=c32[:, j, :], in1=rr[:], op=ALU.mult), reads=[Bc32, Brr], writes=[Bcn])

    ddi = 0
    hi = 0

    def load_x(tt):
        c0 = tt * TT
        xb, bxb = x16s[tt % 2], Bx16s[tt % 2]
        if xsrc16 is None:
            for hf in range(2):
                P.dma("sp", "x32_%d" % hf, x32s[hf][:], xv[:, hf * 4:(hf + 1) * 4, c0:c0 + TT], writes=[Bx32s[hf]])
                P.op("pool", lambda e, hf=hf, xb=xb: e.tensor_copy(out=xb[:, hf * 4:(hf + 1) * 4, :], in_=x32s[hf][:]), reads=[Bx32s[hf]], writes=[bxb])
        else:
            P.dma("sp", "x32_0", xb[:], xg[:, (c0 % 2048) // 512, c0 // 2048, :, :], reads=[io["Bxg"][(c0 % 2048) // 512]], writes=[bxb])

    load_x(0)
    for tt in range(NT):
        c0 = tt * TT
        xcur[0], xcur[1] = x16s[tt % 2], Bx16s[tt % 2]
        if tt + 1 < NT:
            load_x(tt + 1)
        if "ropecache" in io and layer > 0:
            P.dma("sp", "csld", cs[:], io["ropecache"][0][:, c0:c0 + TT], reads=[io["Bropecache"]], writes=[Bcs])
            P.dma("sp", "snld", sn[:], io["ropecache"][1][:, c0:c0 + TT], reads=[io["Bropecache"]], writes=[Bsn])
        else:
            P.dma("sp", "posi", posi[:], pos[0:1, c0:c0 + TT].partition_broadcast(64), writes=[Bposi])
            P.op("dve", lambda e: e.tensor_copy(out=ang[:], in_=posi[:]), reads=[Bposi], writes=[Bang])
            P.op("dve", lambda e: e.tensor_scalar(out=ang[:], in0=ang[:], scalar1=ropecs[:, 0:1], scalar2=None, op0=ALU.mult), reads=[Bang, Bc], writes=[Bang])
            sintab(sn, Bsn, 0.0)
            sintab(cs, Bcs, math.pi / 2)
            P.op("dve", lambda e: e.tensor_scalar(out=sn[:], in0=sn[:], scalar1=ropecs[:, 1:2], scalar2=None, op0=ALU.mult), reads=[Bsn, Bc], writes=[Bsn])

            if "ropecache" in io:
                P.dma("sp", "csst", io["ropecache"][0][:, c0:c0 + TT], cs[:], reads=[Bcs], writes=[io["Bropecache"]])
                P.dma("sp", "snst", io["ropecache"][1][:, c0:c0 + TT], sn[:], reads=[Bsn], writes=[io["Bropecache"]])
        for j in range(3):
            pt, pb = inproj((16 + j) * 128, 128)
            P.op("act", lambda e, pt=pt, j=j: e.activation(out=c32[:, j, :], in_=pt[:], func=AF.Identity, bias=bAs[:, 16 + j:17 + j], scale=1.0),
                 reads=[pb, Bc], writes=[Bc32])
        rms_norm(3, 384.0)
        if "dil" in phases:
            for g in range(3):
                d = DILS[g]
                for t in range(3):
                    blk = 7 + g * 3 + t
                    pt, pb = inproj(blk * 128, 128)
                    s = ddi % 2
                    ddi += 1
                    P.op("act", lambda e, pt=pt, s=s, d=d, blk=blk, t=t: e.activation(
                        out=dd16[s][:].rearrange("p (r j) -> p r j", r=d), in_=pt[:].rearrange("p (j r) -> p r j", r=d),
                        func=AF.Identity, bias=bAs[:, blk:blk + 1], scale=(0.125 if t == 0 else 1.0)), reads=[pb, Bc], writes=[Bdd[s]])
                    P.dma("sp", "dd%d" % s, dsub[g][t][:, :, c0 // d:(c0 + TT) // d], dd16[s][:].rearrange("p (r j) -> p r j", r=d),
                          reads=[Bdd[s]], writes=[Bdsub])
        pt, pb = ps.next()
        for j in range(3):
            P.op("pe", lambda e, j=j, pt=pt: e.matmul(pt[:], lhsT=wuq16[:, j, 0:128], rhs=cn16[:, j, :], start=(j == 0), stop=(j == 2)),
                 reads=[Bwu, Bcn], writes=[pb])
        P.op("act", lambda e, pt=pt: e.activation(out=q1s[:], in_=pt[:], func=AF.Copy, scale=QSCALE), reads=[pb], writes=[Bq1s])
        P.dma("sp", "q1s", qd1[:, c0:c0 + TT], q1s[:], reads=[Bq1s], writes=[Bqd])
        pA, pbA = ps.next()
        for j in range(3):
            P.op("pe", lambda e, j=j, pA=pA: e.matmul(pA[0:64, :], lhsT=wuq16[:, j, 128:192], rhs=cn16[:, j, :], start=(j == 0), stop=(j == 2)),
                 reads=[Bwu, Bcn], writes=[pbA])
        pB, pbB = ps.next()
        for j in range(3):
            P.op("pe", lambda e, j=j, pB=pB: e.matmul(pB[0:64, :], lhsT=wuq16[:, j, 192:256], rhs=cn16[:, j, :], start=(j == 0), stop=(j == 2)),
                 reads=[Bwu, Bcn], writes=[pbB])
        P.op("dve", lambda e, pA=pA: e.scalar_tensor_tensor(out=t1[:], in0=pA[0:64, :], scalar=QSCALE, in1=cs[:], op0=ALU.mult, op1=ALU.mult),
             reads=[pbA, Bcs], writes=[Bt1])
        P.op("dve", lambda e, pB=pB: e.scalar_tensor_tensor(out=t2[:], in0=pB[0:64, :], scalar=QSCALE, in1=sn[:], op0=ALU.mult, op1=ALU.mult),
             reads=[pbB, Bsn], writes=[Bt2])
        P.op("pool", lambda e: e.tensor_tensor(out=q2s[:], in0=t1[:], in1=t2[:], op=ALU.add), reads=[Bt1, Bt2], writes=[Bq2s])
        P.dma("sp", "q2s", qd2[:, c0:c0 + TT], q2s[:], reads=[Bq2s], writes=[Bqd])
        for j in range(2):
            pt, pb = inproj((19 + j) * 128, 128)
            P.op("act", lambda e, pt=pt, j=j: e.activation(out=c32[:, j, :], in_=pt[:], func=AF.Identity, bias=bAs[:, 19 + j:20 + j], scale=1.0),
                 reads=[pb, Bc], writes=[Bc32])
        rms_norm(2, 256.0)
        if "hg" in phases:
            for h in range(2):
                pt, pb = inproj(h * 128, 128)
                P.op("act", lambda e, pt=pt, h=h: e.activation(out=qs[h][:], in_=pt[:], func=AF.Silu, bias=bAs[:, h:h + 1], scale=1.0),
                     reads=[pb, Bc], writes=[Bqs[h]])
            pt, pb = inproj(6 * 128, 128)
            P.op("act", lambda e, pt=pt: e.activation(out=hv16[:], in_=pt[:], func=AF.Identity, bias=bAs[:, 6:7], scale=1.0), reads=[pb, Bc], writes=[Bhv16])
            P.dma("sp", "hv16", hv_d[:, c0:c0 + TT], hv16[:], reads=[Bhv16], writes=[Bhqk])
            for dr_ in range(2):
                for h in range(2):
                    idx = h * 2 + dr_
                    blk = 2 + dr_ * 2 + h
                    pt, pb = inproj(blk * 128, 128)
                    P.op("act", lambda e, pt=pt, blk=blk: e.activation(out=sig[:], in_=pt[:], func=AF.Sigmoid, bias=bAs[:, blk:blk + 1], scale=1.0),
                         reads=[pb, Bc], writes=[Bsig])
                    P.op("dve", lambda e, idx=idx: e.tensor_scalar(out=ff[:], in0=sig[:], scalar1=lbt[:, idx, 1:2], scalar2=lbt[:, idx, 0:1], op0=ALU.mult, op1=ALU.add),
                         reads=[Bsig, Bc], writes=[Bff])
                    P.op("act", lambda e: e.activation(out=ff[:], in_=ff[:], func=AF.Ln), reads=[Bff], writes=[Bff])
                    P.op("pool", lambda e: e.tensor_scalar(out=ff[:], in0=ff[:], scalar1=LN_MINF, scalar2=None, op0=ALU.max), reads=[Bff], writes=[Bff])
                    if dr_ == 0:
                        P.op("dve", lambda e: e.tensor_tensor_scan(out=bb[:], data0=smask[:], data1=ff[:], initial=0.0, op0=ALU.mult, op1=ALU.add),
                             reads=[Bff, Bc], writes=[Bbb])
                        mcol, bcol = 31, 63
                    else:
                        P.op("dve", lambda e: e.tensor_tensor_scan(out=bb[:, ::-1], data0=smask[:], data1=ff[:, ::-1], initial=0.0, op0=ALU.mult, op1=ALU.add),
                             reads=[Bff, Bc], writes=[Bbb])
                        mcol, bcol = 32, 0
                    b3 = bb[:].rearrange("p (c t) -> p c t", t=64)
                    P.op("pool", lambda e, h=h, dr_=dr_, tt=tt, b3=b3, mcol=mcol: e.tensor_copy(out=mT[h][dr_][:, tt * 8:(tt + 1) * 8], in_=b3[:, :, mcol]),
                         reads=[Bbb], writes=[BmB])
                    P.op("pool", lambda e, h=h, dr_=dr_, tt=tt, b3=b3, bcol=bcol: e.tensor_copy(out=BTt[h][dr_][:, tt * 8:(tt + 1) * 8], in_=b3[:, :, bcol]),
                         reads=[Bbb], writes=[BmB])
                    mb = b3[:, :, mcol:mcol + 1]
                    mbc = bass.AP(mb.tensor, mb.offset, [list(mb.ap[0]), list(mb.ap[1]), [0, 64]])
                    P.op("dve", lambda e, b3=b3, mbc=mbc: e.tensor_tensor(out=eq[:].rearrange("p (c t) -> p c t", t=64), in0=b3, in1=mbc, op=ALU.subtract),
                         reads=[Bbb], writes=[Beq])
                    P.op("act", lambda e: e.activation(out=ek[:], in_=eq[:], func=AF.Exp, scale=-1.0), reads=[Beq], writes=[Bek])
                    P.op("act", lambda e: e.activation(out=eq[:], in_=eq[:], func=AF.Exp), reads=[Beq], writes=[Beq])
                    s = hi % 2
                    hi += 1
                    P.op("dve", lambda e, s=s, h=h: e.tensor_tensor(out=hq16[s][:], in0=qs[h][:], in1=eq[:], op=ALU.mult), reads=[Bqs[h], Beq], writes=[Bhq16[s]])
                    P.dma("sp", "hq16_%d" % s, hq_d[h][dr_][:, c0:c0 + TT], hq16[s][:], reads=[Bhq16[s]], writes=[Bhqk])
                    P.op("dve", lambda e, idx=idx: e.tensor_scalar(out=kk[:], in0=sig[:], scalar1=lbt[:, idx, 2:3], scalar2=lbt[:, idx, 1:2], op0=ALU.mult, op1=ALU.add),
                         reads=[Bsig, Bc], writes=[Bkk])
                    P.op("pool", lambda e, s=s: e.tensor_tensor(out=hk16[s][:], in0=kk[:], in1=ek[:], op=ALU.mult), reads=[Bkk, Bek], writes=[Bhk16[s]])
                    P.dma("sp", "hk16_%d" % s, hk_d[h][dr_][:, c0:c0 + TT], hk16[s][:], reads=[Bhk16[s]], writes=[Bhqk])

        pt, pb = ps.next()
        for j in range(2):
            P.op("pe", lambda e, j=j, pt=pt: e.matmul(pt[:], lhsT=wukv16[:, j, 0:128], rhs=cn16[:, j, :], start=(j == 0), stop=(j == 1)),
                 reads=[Bwu, Bcn], writes=[pb])
        P.op("act", lambda e, pt=pt, c0=c0: e.activation(out=K1T[:, c0:c0 + TT], in_=pt[:], func=AF.Copy), reads=[pb], writes=[BK])
        pt, pb = ps.next()
        for i in range(4):
            for j in range(2):
                P.op("pe", lambda e, i=i, j=j, pt=pt: e.matmul(pt[:, i * 128:(i + 1) * 128], lhsT=cn16[:, j, i * 128:(i + 1) * 128], rhs=wukv16[:, j, 128:256],
                                                          start=(j == 0), stop=(j == 1)), reads=[Bwu, Bcn], writes=[pb])
        P.op("act", lambda e, pt=pt, tt=tt: e.activation(out=Vtok[:, tt * 4:(tt + 1) * 4, :], in_=pt[:].rearrange("p (a b) -> p a b", a=4), func=AF.Copy),
             reads=[pb], writes=[BV])
        pA, pbA = inproj(21 * 128, 64)
        pB, pbB = inproj(21 * 128 + 64, 64)
        P.op("dve", lambda e, pA=pA: e.scalar_tensor_tensor(out=t1[:], in0=pA[0:64, :], scalar=bAs[0:64, 21:22], in1=cs[:], op0=ALU.add, op1=ALU.mult),
             reads=[pbA, Bcs, Bc], writes=[Bt1])
        P.op("dve", lambda e, pB=pB: e.scalar_tensor_tensor(out=t2[:], in0=pB[0:64, :], scalar=bAs[0:64, 22:23], in1=sn[:], op0=ALU.add, op1=ALU.mult),
             reads=[pbB, Bsn, Bc], writes=[Bt2])
        P.op("pool", lambda e, c0=c0: e.tensor_tensor(out=K2T[:, c0:c0 + TT], in0=t1[:], in1=t2[:], op=ALU.add), reads=[Bt1, Bt2], writes=[BK])
    P.barrier()
    es1.close()
    es2 = ExitStack()
    sb = lambda n, s, dt=F32: es2.enter_context(nc.sbuf_tensor("%s_A%d" % (n, layer), s, dt))
    if "mla" in phases:
        pT = [sb("pT%d" % i, [128, TT], BF16) for i in range(4)]; BpT = [Buf("pT%d" % i) for i in range(4)]
        dacc = sb("dacc", [128, TT]); Bdacc = Buf("dacc")
        daccs = [sb("daccs%d" % i, [128, TT]) for i in range(3)]; Bdaccs = [Buf("daccs%d" % i) for i in range(3)]
        rden = sb("rden", [128, TT]); Brden = Buf("rden")
        oc = sb("oc", [128, TT], BF16); Boc = Buf("oc")
        po_t = nc.alloc_psum_tensor("po_mla", [128, 512], F32) if False else None
        pi = 0
        Q1 = [sb("Q1_%d" % i, [128, TT], BF16) for i in range(2)]; Q2 = [sb("Q2_%d" % i, [64, TT], BF16) for i in range(2)]
        BQs = [Buf("Qs%d" % i) for i in range(2)]
        for qt in range(NT):
            q0 = qt * TT
            qi = qt % 2
            P.dma("sp", "Q1_%d" % qi, Q1[qi][:], qd1[:, q0:q0 + TT], reads=[Bqd], writes=[BQs[qi]])
            P.dma("sp", "Q2_%d" % qi, Q2[qi][:], qd2[:, q0:q0 + TT], reads=[Bqd], writes=[BQs[qi]])
            BQ = BQs[qi]
            po, pbo = ps.next()

            def mla_a(kb, qi=qi, BQ=BQ, po=po):
                nonlocal pi
                k0 = kb * 128
                pt, pb = ps.next()
                if pt is po:
                    pt, pb = ps.next()
                P.op("pe", lambda e, pt=pt, k0=k0, qi=qi: e.matmul(pt[:], lhsT=K1T[:, k0:k0 + 128], rhs=Q1[qi][:], start=True, stop=False),
                     reads=[BK, BQ], writes=[pb])
                P.op("pe", lambda e, pt=pt, k0=k0, qi=qi: e.matmul(pt[:], lhsT=K2T[:, k0:k0 + 128], rhs=Q2[qi][:], start=False, stop=True),
                     reads=[BK, BQ], writes=[pb])
                s = pi % 4
                pi += 1
                P.op("act", lambda e, pt=pt, s=s: e.activation(out=pT[s][:], in_=pt[:], func=AF.Exp), reads=[pb], writes=[BpT[s]])
                return s

            def mla_b(kb, s, po=po, pbo=pbo):
                P.op("pe", lambda e, po=po, kb=kb, s=s: e.matmul(po[:], lhsT=Vtok[:, kb, :], rhs=pT[s][:], start=(kb == 0), stop=(kb == S // 128 - 1)),
                     reads=[BV, BpT[s]], writes=[pbo])
                ai = kb % 3
                eng = "pool" if ai == 2 else "dve"
                if kb < 3:
                    P.op(eng, lambda e, s=s, ai=ai: e.tensor_copy(out=daccs[ai][:], in_=pT[s][:]), reads=[BpT[s]], writes=[Bdaccs[ai]])
                else:
                    P.op(eng, lambda e, s=s, ai=ai: e.tensor_tensor(out=daccs[ai][:], in0=daccs[ai][:], in1=pT[s][:], op=ALU.add),
                         reads=[BpT[s], Bdaccs[ai]], writes=[Bdaccs[ai]])

            pend = []
            for kb in range(S // 128):
                pend.append((kb, mla_a(kb)))
                if len(pend) > 2:
                    mla_b(*pend.pop(0))
            while pend:
                mla_b(*pend.pop(0))
            P.op("dve", lambda e: e.tensor_tensor(out=dacc[:], in0=daccs[0][:], in1=daccs[1][:], op=ALU.add), reads=[Bdaccs[0], Bdaccs[1]], writes=[Bdacc])
            P.op("dve", lambda e: e.tensor_tensor(out=dacc[:], in0=dacc[:], in1=daccs[2][:], op=ALU.add), reads=[Bdacc, Bdaccs[2]], writes=[Bdacc])
            pd, pbd = ps.next()
            if pd is po:
                pd, pbd = ps.next()
            P.op("pe", lambda e, pd=pd: e.matmul(pd[:], lhsT=ones32[:], rhs=dacc[:], start=True, stop=True), reads=[Bc, Bdacc], writes=[pbd])
            P.op("dve", lambda e, pd=pd: e.reciprocal(out=rden[:], in_=pd[:]), reads=[pbd], writes=[Brden])
            P.op("dve", lambda e, po=po: e.tensor_tensor(out=oc[:], in0=po[:], in1=rden[:], op=ALU.mult), reads=[pbo, Brden], writes=[Boc])
            P.dma("sp", "oc", osrc[1024 + (q0 // 2048) * 128:1024 + (q0 // 2048) * 128 + 128, (q0 % 2048):(q0 % 2048) + TT], oc[:], reads=[Boc], writes=[BoT])


    P.barrier()
    if "cc_o" in io:
        io["cc_o"]((4, 5))
    es2.close()
    es2 = ExitStack()
    sb = lambda n, s, dt=F32: es2.enter_context(nc.sbuf_tensor("%s_A%d" % (n, layer), s, dt))
    if "dil" in phases:
        RG = 2048
        ets = sb("ets", [128, 18, 256]); Bets = Buf("ets")
        P.dma("sp", "ets", ets[:], etab, writes=[Bets])
        NSET = 2
        Qs_ = [sb("Qs%d" % i, [128, RG], BF16) for i in range(NSET)]; Ks_ = [sb("Ks%d" % i, [128, RG + 128], BF16) for i in range(NSET)]
        Vs_ = [sb("Vs%d" % i, [128, RG + 128], BF16) for i in range(NSET)]
        BQs_l = [Buf("Qs%d" % i) for i in range(NSET)]; BKs_l = [Buf("Ks%d" % i) for i in range(NSET)]; BVs_l = [Buf("Vs%d" % i) for i in range(NSET)]
        Vp_ = [sb("Vp%d" % i, [128, 17, 2, 65], BF16) for i in range(NSET)]; BVp_l = [[Buf("Vp%d_%d" % (i, j)) for j in range(17)] for i in range(NSET)]
        NU = 4
        pe32 = [sb("pe32_%d" % i, [128, 256]) for i in range(NU)]; Bpe = [Buf("pe32_%d" % i) for i in range(NU)]
        pt16 = [sb("pt16_%d" % i, [128, 256], BF16) for i in range(NU)]; Bpt16 = [Buf("pt16_%d" % i) for i in range(NU)]
        acc = [sb("dacc%d" % h, [65, RG]) for h in range(2)]; Bacc = [Buf("dacc%d" % h) for h in range(2)]
        obd = sb("obd", [64, 512], BF16); Bobd = Buf("obd")
        for i in range(NSET):
            P.op("pool", lambda e, i=i: e.memset(Vp_[i][:], 1.0), writes=BVp_l[i])
        ui = 0
        si = 0
        for rg in range(S // RG):
            R0 = rg * RG
            for h in range(2):
                P.op("pool", lambda e, h=h: e.memset(acc[h][:], 0.0), writes=[Bacc[h]])
            for g in range(3):
                d = DILS[g]
                J = S // d
                nj = RG // d
                nb = nj // 128
                j0 = R0 // d
                for r in range(d):
                    ss = si % NSET
                    si += 1
                    Qs, Ks, Vs, Vp = Qs_[ss], Ks_[ss], Vs_[ss], Vp_[ss]
                    BQs_, BKs, BVs, BVp = BQs_l[ss], BKs_l[ss], BVs_l[ss], BVp_l[ss]
                    lo = j0 - 64
                    hi_ = j0 + nj + 64
                    clo = max(lo, 0)
                    chi = min(hi_, J)
                    if lo < 0:
                        P.op("pool", lambda e, Ks=Ks: e.memset(Ks[:, 0:64], 0.0), writes=[BKs])
                        P.op("pool", lambda e, Vs=Vs: e.memset(Vs[:, 0:64], 0.0), writes=[BVs])
                    if hi_ > J:
                        P.op("pool", lambda e, nj=nj, Ks=Ks: e.memset(Ks[:, nj + 64:nj + 128], 0.0), writes=[BKs])
                        P.op("pool", lambda e, nj=nj, Vs=Vs: e.memset(Vs[:, nj + 64:nj + 128], 0.0), writes=[BVs])
                    P.dma("sp", "Qs%d" % ss, Qs[:, 0:nj], dsub[g][0][:, r, j0:j0 + nj], reads=[Bdsub], writes=[BQs_])
                    P.dma("sp", "Ks%d" % ss, Ks[:, clo - lo:chi - lo], dsub[g][1][:, r, clo:chi], reads=[Bdsub], writes=[BKs])
                    P.dma("sp", "Vs%d" % ss, Vs[:, clo - lo:chi - lo], dsub[g][2][:, r, clo:chi], reads=[Bdsub], writes=[BVs])
                    for n in range(nb + 1):
                        ptr, pbr = ps.next()
                        ptr16 = ptr[:].bitcast(BF16)
                        P.op("pe", lambda e, n=n, ptr16=ptr16, Vs=Vs: e.transpose(out=ptr16[:, 0:128], in_=Vs[:, n * 128:(n + 1) * 128], identity=id16[:]),
                             reads=[BVs, Bc], writes=[pbr])
                        P.op("act", lambda e, n=n, ptr16=ptr16, Vp=Vp: e.activation(out=Vp[:, n, :, 0:64], in_=ptr16[:, 0:128].rearrange("p (h v) -> p h v", h=2), func=AF.Copy),
                             reads=[pbr], writes=[BVp[n]])

                    def dil_a(qb, h, Qs=Qs, Ks=Ks, BQs_=BQs_, BKs=BKs, g=g, j0=j0, J=J):
                        nonlocal ui
                        jb = j0 + qb * 128
                        var = 1 if jb == 0 else (2 if jb + 128 == J else 0)
                        u = ui % NU
                        ui += 1
                        pt, pb = ps.next()
                        for kc in range(2):
                            P.op("pe", lambda e, pt=pt, kc=kc, qb=qb, h=h: e.matmul(pt[:, kc * 128:(kc + 1) * 128],
                                 lhsT=Ks[h * 64:(h + 1) * 64, (qb + kc) * 128:(qb + kc + 1) * 128], rhs=Qs[h * 64:(h + 1) * 64, qb * 128:(qb + 1) * 128],
                                 start=True, stop=True), reads=[BKs, BQs_], writes=[pb])
                        P.op("act", lambda e, pt=pt, u=u: e.activation(out=pe32[u][:], in_=pt[:, 0:256], func=AF.Exp), reads=[pb], writes=[Bpe[u]])
                        ei = (g * 2 + h) * 3 + var
                        P.op("dve", lambda e, u=u, ei=ei: e.tensor_tensor(out=pt16[u][:], in0=pe32[u][:], in1=ets[:, ei, :], op=ALU.mult),
                             reads=[Bpe[u], Bets], writes=[Bpt16[u]])
                        return (qb, h, u)

                    def dil_b(qb, h, u, Vp=Vp, BVp=BVp, d=d, r=r):
                        po, pbo = ps.next()
                        for kc in range(2):
                            P.op("pe", lambda e, po=po, kc=kc, qb=qb, h=h, u=u: e.matmul(po[0:65, 0:128], lhsT=Vp[:, qb + kc, h, :], rhs=pt16[u][:, kc * 128:(kc + 1) * 128],
                                 start=(kc == 0), stop=(kc == 1)), reads=[BVp[qb + kc], Bpt16[u]], writes=[pbo])
                        st = qb * 128 * d + r
                        av = acc[h][:, st:st + 127 * d + 1:d]
                        P.op("dve", lambda e, po=po, av=av: e.tensor_tensor(out=av, in0=av, in1=po[0:65, 0:128], op=ALU.add), reads=[pbo, Bacc[h]], writes=[Bacc[h]])

                    pend = []
                    for qb in range(nb):
                        for h in range(2):
                            pend.append(dil_a(qb, h))
                            if len(pend) > 2:
                                dil_b(*pend.pop(0))
                    while pend:
                        dil_b(*pend.pop(0))
            for h in range(2):
                P.op("dve", lambda e, h=h: e.reciprocal(out=acc[h][64:65, :], in_=acc[h][64:65, :]), reads=[Bacc[h]], writes=[Bacc[h]])
                for cc in range(RG // 512):
                    pt, pb = ps.next()
                    P.op("pe", lambda e, pt=pt, h=h, cc=cc: e.matmul(pt[0:64, :], lhsT=ones32[64:65, 0:64], rhs=acc[h][64:65, cc * 512:(cc + 1) * 512], start=True, stop=True),
                         reads=[Bc, Bacc[h]], writes=[pb])
                    P.op("dve", lambda e, pt=pt, h=h, cc=cc: e.tensor_tensor(out=obd[:], in0=acc[h][0:64, cc * 512:(cc + 1) * 512], in1=pt[0:64, :], op=ALU.mult),
                         reads=[pb, Bacc[h]], writes=[Bobd])
                    P.dma("sp", "obd", osrc[512 + rg * 128 + h * 64:512 + rg * 128 + (h + 1) * 64, cc * 512:(cc + 1) * 512], obd[:], reads=[Bobd], writes=[BoT])

    P.barrier()
    if "cc_o" in io:
        io["cc_o"]((2, 3))
    es2.close()
    es2 = ExitStack()
    sb = lambda n, s, dt=F32: es2.enter_context(nc.sbuf_tensor("%s_A%d" % (n, layer), s, dt))
    if "hg" in phases:
        oac = [sb("oac%d" % h, [64, S]) for h in range(2)]; Boac = [Buf("oac%d" % h) for h in range(2)]
        for h in range(2):
            P.op("pool", lambda e, h=h: e.memset(oac[h][:], 0.0), writes=[Boac[h]])
        gam = [[sb("gam%d%d" % (h, d), [128, 128]) for d in range(2)] for h in range(2)]; Bgam = Buf("gam")
        for h in range(2):
            for d in range(2):
                P.op("pool", lambda e, h=h, d=d: e.memset(gam[h][d][:], 1.0), writes=[Bgam])
        for h in range(2):
            for d in range(2):
                if d == 0:
                    dst, mn, bc, mc = gam[h][d][:, 0:127], mT[h][d][:, 1:128], BTt[h][d][:, 0:127], mT[h][d][:, 0:127]
                else:
                    dst, mn, bc, mc = gam[h][d][:, 1:128], mT[h][d][:, 0:127], BTt[h][d][:, 1:128], mT[h][d][:, 1:128]
                P.op("dve", lambda e, dst=dst, mn=mn, bc=bc: e.tensor_tensor(out=dst, in0=mn, in1=bc, op=ALU.add), reads=[BmB], writes=[Bgam])
                P.op("dve", lambda e, dst=dst, mc=mc: e.tensor_tensor(out=dst, in0=dst, in1=mc, op=ALU.subtract), reads=[BmB, Bgam], writes=[Bgam])
                P.op("act", lambda e, dst=dst: e.activation(out=dst, in_=dst, func=AF.Exp), reads=[Bgam], writes=[Bgam])
        ch = [(h, d) for d in range(2) for h in range(2)]
        qT = {(c, u): sb("hqT%d%d_%d" % (c + (u,)), [128, TT], BF16) for c in ch for u in range(2)}
        kT = {(c, u): sb("hkT%d%d_%d" % (c + (u,)), [128, TT], BF16) for c in ch for u in range(2)}
        vT = {(c, u): sb("hvT%d%d_%d" % (c + (u,)), [128, TT], BF16) for c in ch for u in range(2)}
        Bld = {(c, u): Buf("hld") for c in ch for u in range(2)}
        kvtok = {(c, u): sb("kvtok%d%d_%d" % (c + (u,)), [128, 256], BF16) for c in ch for u in range(2)}
        Bkv = {(c, u): Buf("kvtok") for c in ch for u in range(2)}
        at16 = {(c, u): sb("at16%d%d_%d" % (c + (u,)), [128, 128], BF16) for c in ch for u in range(2)}
        Bat = {(c, u): Buf("at") for c in ch for u in range(2)}
        S32 = {c: sb("S32%d%d" % c, [128, 64]) for c in ch}; S16 = {c: sb("S16%d%d" % c, [128, 64], BF16) for c in ch}
        Sh = {c: sb("Sh%d%d" % c, [128, 64]) for c in ch}
        BS32 = {c: Buf("S32%d%d" % c) for c in ch}; BS16 = {c: Buf("S16%d%d" % c) for c in ch}; BSh = {c: Buf("Sh%d%d" % c) for c in ch}
        for c in ch:
            P.op("pool", lambda e, c=c: e.memset(S32[c][:], 0.0), writes=[BS32[c]])
            P.op("pool", lambda e, c=c: e.memset(S16[c][:], 0.0), writes=[BS16[c]])

        def hg_load(step):
            u = step % 2
            for c in ch:
                h, d = c
                ti = step if d == 0 else NT - 1 - step
                c0 = ti * TT
                P.dma("sp", "hl%d%d_%d" % (c + (u,)), qT[(c, u)][:], hq_d[h][d][:, c0:c0 + TT], reads=[Bhqk], writes=[Bld[(c, u)]])
                P.dma("sp", "hl%d%d_%d" % (c + (u,)), kT[(c, u)][:], hk_d[h][d][:, c0:c0 + TT], reads=[Bhqk], writes=[Bld[(c, u)]])
                P.dma("sp", "hl%d%d_%d" % (c + (u,)), vT[(c, u)][:], hv_d[:, c0:c0 + TT], reads=[Bhqk], writes=[Bld[(c, u)]])

        def hg_info(step, pp, c):
            h, d = c
            ti = step if d == 0 else NT - 1 - step
            pr = pp if d == 0 else 3 - pp
            return h, d, ti, pr, pr * 128

        def hg_prep(step, pp):
            u = step % 2
            w = (step * 4 + pp) % 2
            for ci, c in enumerate(ch):
                h, d, ti, pr, p0 = hg_info(step, pp, c)
                bank, bb_ = ps.t[ci], ps.b[ci]
                b16 = bank[:].bitcast(BF16)
                P.op("pe", lambda e, c=c, p0=p0, b16=b16, u=u: e.transpose(out=b16[:, 0:128], in_=kT[(c, u)][:, p0:p0 + 128], identity=id16[:]),
                     reads=[Bld[(c, u)], Bc], writes=[bb_])
                P.op("pe", lambda e, c=c, p0=p0, b16=b16, u=u: e.transpose(out=b16[:, 128:256], in_=vT[(c, u)][:, p0:p0 + 128], identity=id16[:]),
                     reads=[Bld[(c, u)], Bc], writes=[bb_])
            for ci, c in enumerate(ch):
                b16 = ps.t[ci][:].bitcast(BF16)
                P.op("act", lambda e, c=c, b16=b16, w=w: e.activation(out=kvtok[(c, w)][:], in_=b16[:, 0:256], func=AF.Copy), reads=[ps.b[ci]], writes=[Bkv[(c, w)]])
            for ci, c in enumerate(ch):
                h, d, ti, pr, p0 = hg_info(step, pp, c)
                pa, pba = ps.t[ci], ps.b[ci]
                P.op("pe", lambda e, c=c, p0=p0, pa=pa, u=u: e.matmul(pa[:, 0:128], lhsT=kT[(c, u)][:, p0:p0 + 128], rhs=qT[(c, u)][:, p0:p0 + 128], start=True, stop=True),
                     reads=[Bld[(c, u)]], writes=[pba])
            for ci, c in enumerate(ch):
                h, d = c
                pa, pba = ps.t[ci], ps.b[ci]
                P.op("dve", lambda e, c=c, d=d, pa=pa, w=w: e.tensor_tensor(out=at16[(c, w)][:], in0=pa[:, 0:128], in1=msk[:, d, :], op=ALU.mult),
                     reads=[pba, Bc], writes=[Bat[(c, w)]])
            pi_, pbi = ps.t[4 + w], ps.b[4 + w]
            for ci, c in enumerate(ch):
                h, d = c
                P.op("pe", lambda e, c=c, h=h, ci=ci, pi_=pi_, w=w: e.matmul(pi_[0:64, ci * 128:(ci + 1) * 128], lhsT=kvtok[(c, w)][:, 128 + h * 64:128 + (h + 1) * 64],
                     rhs=at16[(c, w)][:], start=True, stop=True), reads=[Bkv[(c, w)], Bat[(c, w)]], writes=[pbi])
            for ci, c in enumerate(ch):
                h, d, ti, pr, p0 = hg_info(step, pp, c)
                t0_ = ti * TT + p0
                P.op("dve", lambda e, h=h, ci=ci, pi_=pi_, t0_=t0_: e.tensor_tensor(out=oac[h][:, t0_:t0_ + 128], in0=oac[h][:, t0_:t0_ + 128], in1=pi_[0:64, ci * 128:(ci + 1) * 128], op=ALU.add),
                     reads=[pbi, Boac[h]], writes=[Boac[h]])

        def hg_chain(step, pp):
            u = step % 2
            w = (step * 4 + pp) % 2
            pj, pbj = ps.t[6 + w], ps.b[6 + w]
            for cc in range(2):
                pub = {}
                for ci, c in enumerate(ch):
                    h, d, ti, pr, p0 = hg_info(step, pp, c)
                    ck = cc if d == 0 else 1 - cc
                    q0 = p0 + ck * 64
                    P.op("pe", lambda e, c=c, ci=ci, pj=pj, ck=ck, q0=q0, u=u: e.matmul(pj[0:64, ci * 128 + ck * 64:ci * 128 + (ck + 1) * 64], lhsT=S16[c][:], rhs=qT[(c, u)][:, q0:q0 + 64],
                         start=True, stop=True), reads=[BS16[c], Bld[(c, u)]], writes=[pbj])
                    pu, pbu = ps.t[ci], ps.b[ci]
                    P.op("pe", lambda e, c=c, h=h, pu=pu, ck=ck, w=w: e.matmul(pu[:, 0:64], lhsT=kvtok[(c, w)][ck * 64:(ck + 1) * 64, 0:128],
                         rhs=kvtok[(c, w)][ck * 64:(ck + 1) * 64, 128 + h * 64:128 + (h + 1) * 64], start=True, stop=True), reads=[Bkv[(c, w)]], writes=[pbu])
                    pub[c] = (pu, pbu, ti * 8 + pr * 2 + ck)
                for c in ch:
                    pu, pbu, cidx = pub[c]
                    P.op("dve", lambda e, c=c, pu=pu: e.tensor_tensor(out=Sh[c][:], in0=pu[:, 0:64], in1=S32[c][:], op=ALU.add), reads=[pbu, BS32[c]], writes=[BSh[c]])
                for c in ch:
                    h, d = c
                    pu, pbu, cidx = pub[c]
                    P.op("pool", lambda e, c=c, h=h, d=d, cidx=cidx: e.tensor_scalar(out=S32[c][:], in0=Sh[c][:], scalar1=gam[h][d][:, cidx:cidx + 1], scalar2=None, op0=ALU.mult),
                         reads=[BSh[c], Bgam], writes=[BS32[c]])
                    P.op("act", lambda e, c=c, h=h, d=d, cidx=cidx: e.activation(out=S16[c][:], in_=Sh[c][:], func=AF.Copy, scale=gam[h][d][:, cidx:cidx + 1]),
                         reads=[BSh[c], Bgam], writes=[BS16[c]])
            for ci, c in enumerate(ch):
                h, d, ti, pr, p0 = hg_info(step, pp, c)
                t0_ = ti * TT + p0
                P.op("dve", lambda e, h=h, ci=ci, pj=pj, t0_=t0_: e.tensor_tensor(out=oac[h][:, t0_:t0_ + 128], in0=oac[h][:, t0_:t0_ + 128], in1=pj[0:64, ci * 128:(ci + 1) * 128], op=ALU.add),
                     reads=[pbj, Boac[h]], writes=[Boac[h]])

        seq = [(st, pp) for st in range(NT) for pp in range(4)]
        hg_load(0)
        hg_load(1)
        hg_prep(0, 0)
        for i, (st, pp) in enumerate(seq):
            if i + 1 < len(seq):
                nst, npp = seq[i + 1]
                hg_prep(nst, npp)
            hg_chain(st, pp)
            if pp == 3 and st + 2 < NT:
                hg_load(st + 2)
        o16c = [sb("o16c%d" % i, [64, 2048], BF16) for i in range(2)]; Bo16c = [Buf("o16c%d" % i) for i in range(2)]
        for h in range(2):
            for tq in range(4):
                u = (h * 4 + tq) % 2
                P.op("act" if u == 0 else "dve", (lambda e, h=h, tq=tq, u=u: e.activation(out=o16c[u][:], in_=oac[h][:, tq * 2048:(tq + 1) * 2048], func=AF.Copy)) if u == 0 else
                     (lambda e, h=h, tq=tq, u=u: e.tensor_copy(out=o16c[u][:], in_=oac[h][:, tq * 2048:(tq + 1) * 2048])), reads=[Boac[h]], writes=[Bo16c[u]])
                P.dma("sp", "o16c%d" % u, osrc[tq * 128 + h * 64:tq * 128 + (h + 1) * 64, :], o16c[u][:], reads=[Bo16c[u]], writes=[BoT])

    P.barrier()
    es2.close()
    es0.close()


RG4 = [[0, 1, 2, 3], [4, 5, 6, 7]]


def build_fused():
    nc = bass.Bass("TRN2", target_bir_lowering=False)
    dr = lambda n, s, kind="ExternalInput", dt=F32: nc.dram_tensor(n, s, dt, kind=kind).ap()
    shared = {"pos": dr("pos", [1, S], dt=I32), "etab": dr("etab", [128, 18, 256]), "ropec": dr("ropec", [64, 2]),
              "ident": dr("ident", [128, 128]), "masks": dr("masks", [128, 2, 128]), "scanmask": dr("scanmask", [128, 512]),
              "lbraw": dr("lbraw", [128, 4, 2])}
    xT = dr("xT", [1024, S]); xTq = dr("xTq", [1024, 2048]); oidx = dr("oidx", [128, 96], dt=I32)
    ioA, ioB = [], []
    for l in range(2):
        a = dict(shared)
        a.update({"wA": dr("wA%d" % l, [1024, 2816]), "bA": dr("bA%d" % l, [128, 23]), "wuq": dr("wuq%d" % l, [384, 256]), "gq": dr("gq%d" % l, [128, 3]),
                  "wukv": dr("wukv%d" % l, [256, 256]), "gkv": dr("gkv%d" % l, [128, 2])})
        ioA.append(a)
        ioB.append({"wg": dr("wg%d" % l, [1024, 4608]), "bg": dr("bg%d" % l, [128, 36]), "wbr": dr("wbr%d" % l, [1536, 1024]), "wo": dr("wo%d" % l, [1024, 1024]),
                    "hgn": dr("hgn%d" % l, [128, 4]), "lng": dr("lng%d" % l, [128, 8]), "lnb": dr("lnb%d" % l, [128, 8]), "oidx": oidx})
    outT = dr("outT", [1024, 2048], kind="ExternalOutput")
    cco_src = [nc.dram_tensor("cco_src%d" % l, [1536, 2048], BF16) for l in range(2)]
    cco_dst = [nc.dram_tensor("cco_dst%d" % l, [6 * 1024, 2048], BF16) for l in range(2)]
    ccx_src = nc.dram_tensor("ccx_src", [4 * 1024, 512], BF16)
    ccx_dst = nc.dram_tensor("ccx_dst", [4 * 4096, 512], BF16)
    xn32 = nc.dram_tensor("xn32", [1024, 2048], F32).ap()
    scr = make_scratch(nc)
    ropecache = [nc.dram_tensor("ropec_%d" % i, [64, S], F32).ap() for i in range(2)]
    Bropecache = Buf("ropecache")
    P = Prog(nc)
    ps = PsumPool(nc)
    Bxn = Buf("xn32"); Bxg = Buf("xg"); Bnone = Buf("none")
    Bods = [Buf("cco_dst%d" % l) for l in range(2)]
    Bccx = [Buf("ccx_src%d" % j) for j in range(4)]
    Bxgs = [Buf("xg%d" % j) for j in range(4)]

    def cc_o(l):
        def go(chunks):
            for k in chunks:
                P.dma("pool", "cc_o%d" % l, None, None, reads=[Bnone], writes=[Bods[l]], inc=1,
                      fn=(lambda e, l=l, k=k: e.collective_compute("AllGather", ALU.bypass, replica_groups=RG4,
                                                                 ins=[cco_src[l].ap()[k * 256:(k + 1) * 256, :].opt()],
                                                                 outs=[cco_dst[l].ap()[k * 1024:(k + 1) * 1024, :].opt()])))
        return go

    def cc_x(j):
        P.dma("pool", "cc_x", None, None, reads=[Bccx[j]], writes=[Bxgs[j]], inc=1,
              fn=(lambda e, j=j: e.collective_compute("AllGather", ALU.bypass, replica_groups=RG4,
                                                    ins=[ccx_src.ap()[j * 1024:(j + 1) * 1024, :].opt()],
                                                    outs=[ccx_dst.ap()[j * 4096:(j + 1) * 4096, :].opt()])))

    for l in range(2):
        ioA[l]["osrc"] = cco_src[l].ap()
        ioA[l]["cc_o"] = cc_o(l)
        ioA[l]["ropecache"] = ropecache; ioA[l]["Bropecache"] = Bropecache
        if l == 0:
            ioA[l]["xT"] = xT
            emit_A(nc, P, ps, l, ioA[l], scr)
        else:
            ioA[l]["Bxg"] = Bxgs
            emit_A(nc, P, ps, l, ioA[l], scr, xsrc16=ccx_dst.ap())
        P.barrier()
        Bod = Bods[l]
        cc_o(l)((0, 1))
        b = ioB[l]
        b["orows"] = cco_dst[l].ap().rearrange("r (a c) -> (r a) c", c=256)
        b["Bodst"] = Bod
        if l == 0:
            b["x32src"] = xTq; b["Bxsrc"] = Bnone; b["out32"] = xn32; b["out16"] = ccx_src.ap(); b["Bccx"] = Bccx; b["cc_x"] = cc_x
        else:
            b["x32src"] = xn32; b["Bxsrc"] = Bxn; b["out32"] = outT
        emit_B(nc, P, ps, l, b)
        P.barrier()
    P.barrier()
    P.emit()
    return nc


SPL = [1024,1024,1024,512,512] + [512]*10 + [384,256,64,512,3072]
NAMES = ['hg_q','hg_f_fwd','hg_f_bwd','hg_i','hg_g','dil_q0','dil_k0','dil_v0','dil_q1','dil_k1','dil_v1','dil_q2','dil_k2','dil_v2','dil_g','mla_cq','mla_ckv','mla_kr','mla_g','merge']
OFF = dict(zip(NAMES, [int(v) for v in np.cumsum([0]+SPL[:-1])]))
def a_cols(hq):
    ar = np.arange
    c = []
    c += [OFF['hg_q'] + (2*hq)*128 + ar(128), OFF['hg_q'] + (2*hq+1)*128 + ar(128)]
    c += [OFF['hg_f_fwd'] + (2*hq)*128 + ar(128), OFF['hg_f_fwd'] + (2*hq+1)*128 + ar(128)]
    c += [OFF['hg_f_bwd'] + (2*hq)*128 + ar(128), OFF['hg_f_bwd'] + (2*hq+1)*128 + ar(128)]
    c += [OFF['hg_i'] + hq*128 + ar(128)]
    for g in range(3):
        for t in 'qkv':
            c += [OFF['dil_%s%d' % (t, g)] + hq*128 + ar(128)]
    c += [OFF['mla_cq'] + ar(384), OFF['mla_ckv'] + ar(256)]
    kr = OFF['mla_kr'] + ar(64)
    c += [kr, np.concatenate([kr[32:], kr[:32]])]
    return np.concatenate(c)
def etab_np(hq):
    slopes = 2.0 ** (-8.0 * (np.arange(24) + 1) / 24)
    kk = np.arange(128)[:, None]; qq = np.arange(128)[None, :]
    E = np.zeros((128, 18, 256), np.float32)
    for g, d in enumerate((1, 4, 16)):
        for hh in range(2):
            sl = slopes[g*8 + 2*hq + hh]
            for var in range(3):
                for kc in range(2):
                    rel = (kk + 128*kc - 64) - qq
                    e = np.where(np.abs(rel) <= 64, np.exp(-sl * d * np.abs(rel)), 0.0)
                    if var == 1 and kc == 0: e = np.where(kk < 64, 0.0, e)
                    if var == 2 and kc == 1: e = np.where(kk >= 64, 0.0, e)
                    E[:, (g*2+hh)*3 + var, kc*128:(kc+1)*128] = e
    return E
def a_inputs(inp, l, b, hq, xT_b):
    cols = a_cols(hq)
    w_in = inp['w_in'][l]; b_in = inp['b_in'][l]
    bsel = b_in[cols]
    bA = np.zeros((128, 23), np.float32)
    bA[:, :22] = bsel.reshape(22, 128).T
    bA[:64, 22] = bsel[21*128+64: 22*128]
    lbraw = np.zeros((128, 4, 2), np.float32)
    for h in range(2):
        for d, nm in enumerate(('hg_lb_fwd', 'hg_lb_bwd')):
            lbraw[:, h*2+d, :] = inp[nm][:, (2*hq+h)*128:(2*hq+h+1)*128].T
    wuq = inp['w_uq'][l]
    qc = hq*192 + np.arange(192)
    rope = qc[128:]
    wuq_sel = np.concatenate([wuq[:, qc[:128]], wuq[:, rope], wuq[:, np.concatenate([rope[32:], rope[:32]])]], 1)
    wukv = inp['w_ukv'][l]
    wukv_sel = wukv[:, hq*256:(hq+1)*256]
    inv = (1.0 / (10000.0 ** (np.arange(32, dtype=np.float32) / 32))).astype(np.float32)
    ropec = np.zeros((64, 2), np.float32); ropec[:, 0] = np.concatenate([inv, inv]); ropec[:32, 1] = -1.0; ropec[32:, 1] = 1.0
    masks = np.zeros((128, 2, 128), np.float32)
    ss = np.arange(128)[:, None]; tq = np.arange(128)[None, :]
    same = (ss // 64) == (tq // 64)
    masks[:, 0, :] = (same & (ss <= tq)).astype(np.float32)
    masks[:, 1, :] = (same & (ss >= tq)).astype(np.float32)
    sm = np.ones((128, 512), np.float32); sm[:, ::64] = 0.0
    return {"pos": np.ascontiguousarray(inp['positions'][b:b+1].astype(np.int32)), "wA": np.ascontiguousarray(w_in[:, cols]), "bA": bA, "lbraw": lbraw,
            "wuq": np.ascontiguousarray(wuq_sel), "gq": np.ascontiguousarray(inp['mla_q_norm'][l].reshape(3,128).T),
            "wukv": np.ascontiguousarray(wukv_sel), "gkv": np.ascontiguousarray(inp['mla_kv_norm'][l].reshape(2,128).T),
            "etab": etab_np(hq), "ropec": ropec, "ident": np.eye(128, dtype=np.float32), "masks": masks, "scanmask": sm}


_CACHE = {}


def _b_inputs(inp, l):
    w_in = inp['w_in'][l]; b_in = inp['b_in'][l]
    cols = np.concatenate([np.arange(OFF['hg_g'], OFF['hg_g'] + 512), np.arange(OFF['dil_g'], OFF['dil_g'] + 512),
                           np.arange(OFF['mla_g'], OFF['mla_g'] + 512), np.arange(OFF['merge'], OFF['merge'] + 3072)])
    return {"wg%d" % l: np.ascontiguousarray(w_in[:, cols]), "bg%d" % l: np.ascontiguousarray(b_in[cols].reshape(36, 128).T),
            "wbr%d" % l: np.ascontiguousarray(inp['w_branch'][l].reshape(1536, 1024)), "wo%d" % l: np.ascontiguousarray(inp['w_out'][l]),
            "hgn%d" % l: np.ascontiguousarray(inp['hg_norm'][l].reshape(4, 128).T),
            "lng%d" % l: np.ascontiguousarray(inp['ln_g'][l].reshape(8, 128).T), "lnb%d" % l: np.ascontiguousarray(inp['ln_b'][l].reshape(8, 128).T)}


def _oidx(tq):
    p = np.arange(128)[:, None, None, None]; tt = np.arange(8)[None, :, None, None]
    n = np.arange(3)[None, None, :, None]; r = np.arange(4)[None, None, None, :]
    rho = n * 512 + tq * 128 + p
    g = (rho // 256) * 1024 + r * 256 + (rho % 256)
    v = g * 8 + tt
    return np.ascontiguousarray(v.reshape(128, 96).astype(np.int32))


def kernel(**inputs):
    inp = {k: np.asarray(v) for k, v in inputs.items()}
    inp['positions'] = inp['positions'].astype(np.int32)
    for k in inp:
        if k != 'positions':
            inp[k] = inp[k].astype(np.float32, copy=False)
    B = 2
    xT = [np.ascontiguousarray(inp['x'][b].T) for b in range(B)]
    if "nc" not in _CACHE:
        _CACHE["nc"] = build_fused()
    nc = _CACHE["nc"]
    bl = [_b_inputs(inp, l) for l in range(2)]
    in_maps = []
    for c in range(8):
        b, q = c // 4, c % 4
        m = {"xT": xT[b], "xTq": np.ascontiguousarray(xT[b][:, q * 2048:(q + 1) * 2048]), "oidx": _oidx(q)}
        for l in range(2):
            a = a_inputs(inp, l, b, q, None)
            for k in ("pos", "etab", "ropec", "ident", "masks", "scanmask", "lbraw"):
                m[k] = a[k]
            for k in ("wA", "bA", "wuq", "gq", "wukv", "gkv"):
                m["%s%d" % (k, l)] = a[k]
            m.update(bl[l])
        in_maps.append(m)
    res = run_bass_kernel_spmd(nc, in_maps, core_ids=list(range(8))).results
    out = np.empty((B, 8192, 1024), np.float32)
    for c in range(8):
        b, q = c // 4, c % 4
        out[b, q * 2048:(q + 1) * 2048, :] = np.asarray(res[c]["outT"]).T
    return out
```

```python
import math
from contextlib import ExitStack
import numpy as np
from concourse.bass_utils import run_bass_kernel_spmd
import concourse.bass as bass
import concourse.mybir as mybir

F32 = mybir.dt.float32
BF16 = mybir.dt.bfloat16
I32 = mybir.dt.int32
AF = mybir.ActivationFunctionType
ALU = mybir.AluOpType
AX = mybir.AxisListType


class Buf:
    __slots__ = ("name", "w", "r")

    def __init__(self, name=""):
        self.name = name
        self.w = {}
        self.r = {}


class _Eng:
    def __init__(self, name, sem):
        self.name = name
        self.sem = sem
        self.count = 0
        self.waited = {}
        self.items = []


class Prog:
    ENGS = ("pe", "act", "dve", "pool", "sp")

    def __init__(self, nc):
        self.nc = nc
        self.e = {n: _Eng(n, nc.alloc_semaphore("prog_" + n)) for n in self.ENGS}
        self.chan = {}
        self.nops = 0
        self.retired = []

    def _need(self, eng, waits, ev, raw):
        sem, val, en = ev
        if en == eng.name:
            if eng.name == "pe" or not raw:
                return
        k = id(sem)
        if eng.waited.get(k, 0) >= val:
            return
        if k not in waits or waits[k][1] < val:
            waits[k] = (sem, val)

    def _deps(self, eng, reads, writes):
        waits = {}
        for b in reads:
            for ev in b.w.values():
                self._need(eng, waits, ev, True)
        for b in writes:
            for ev in b.w.values():
                self._need(eng, waits, ev, False)
            for ev in b.r.values():
                self._need(eng, waits, ev, False)
        for k, (sem, val) in waits.items():
            eng.waited[k] = val
        return list(waits.values())

    @staticmethod
    def _mark(ev, reads, writes):
        k = id(ev[0])
        for b in reads:
            o = b.r.get(k)
            if o is None or o[1] < ev[1]:
                b.r[k] = ev
        for b in writes:
            o = b.w.get(k)
            if o is None or o[1] < ev[1]:
                b.w[k] = ev

    def op(self, engname, fn, reads=(), writes=()):
        eng = self.e[engname]
        waits = self._deps(eng, reads, writes)
        if eng.count >= 30000:
            self.retired.append((eng.sem, eng.count))
            eng.sem = self.nc.alloc_semaphore("prog_%s_%d" % (engname, self.nops))
            eng.count = 0
        eng.count += 1
        ev = (eng.sem, eng.count, eng.name)
        eng.items.append((waits, fn, (eng.sem, 1)))
        self._mark(ev, reads, writes)
        self.nops += 1
        return ev

    def dma(self, qname, chan, out, in_, reads=(), writes=(), fn=None, inc=16):
        eng = self.e[qname]
        waits = self._deps(eng, reads, writes)
        if chan not in self.chan:
            self.chan[chan] = [self.nc.alloc_semaphore("ch_" + chan), 0]
        c = self.chan[chan]
        if c[1] >= 30000:
            self.retired.append((c[0], c[1]))
            c[0] = self.nc.alloc_semaphore("ch_%s_%d" % (chan, self.nops))
            c[1] = 0
        c[1] += inc
        ev = (c[0], c[1], "dma")
        if fn is None:
            fn = (lambda e, o=out, i=in_: e.dma_start(out=o, in_=i))
        eng.items.append((waits, fn, (c[0], inc)))
        self._mark(ev, reads, writes)
        self.nops += 1
        return ev

    def barrier(self):
        evs = list(self.retired)
        for n in self.ENGS:
            if self.e[n].count > 0:
                evs.append((self.e[n].sem, self.e[n].count))
        for c in self.chan.values():
            evs.append((c[0], c[1]))
        for n in self.ENGS:
            eng = self.e[n]
            waits = []
            for sem, val in evs:
                if eng.waited.get(id(sem), 0) < val:
                    waits.append((sem, val))
                    eng.waited[id(sem)] = val
            eng.items.append((waits, None, None))

    def wait_all(self, engname, bufs):
        eng = self.e[engname]
        waits = self._deps(eng, bufs, bufs)
        eng.items.append((waits, None, None))

    def emit(self):
        nc = self.nc
        with nc.Block() as block:
            def run(eng, h):
                for waits, fn, inc in eng.items:
                    for sem, val in waits:
                        h.wait_ge(sem, val)
                    if fn is not None:
                        ins = fn(h)
                        ins.then_inc(inc[0], inc[1])

            @block.tensor
            def _(h):
                run(self.e["pe"], h)

            @block.scalar
            def _(h):
                run(self.e["act"], h)

            @block.vector
            def _(h):
                run(self.e["dve"], h)

            @block.gpsimd
            def _(h):
                run(self.e["pool"], h)

            @block.sync
            def _(h):
                run(self.e["sp"], h)


ALPHA = 4.0 ** 0.25
LN_EPS = 1e-5


class PsumPool:
    def __init__(self, nc, n=8):
        self.t = [nc.alloc_psum_tensor("psb%d" % i, [128, 512], F32) for i in range(n)]
        self.b = [Buf("psb%d" % i) for i in range(n)]
        self.i = 0
        self.n = n

    def next(self):
        i = self.i
        self.i = (i + 1) % self.n
        return self.t[i], self.b[i]


def load_cast_weight(P, nc, q, dram2d, dst16, dstbuf, kchunks, ncols, stage, stage_bufs, ctr, colsplit):
    v = dram2d.rearrange("(k p) c -> p k c", p=128)
    for k in range(kchunks):
        for c0 in range(0, ncols, colsplit):
            cw = min(colsplit, ncols - c0)
            s = ctr[0] % len(stage)
            ctr[0] += 1
            P.dma(q, "wst%d" % s, stage[s][:, 0:cw], v[:, k, c0:c0 + cw], writes=[stage_bufs[s]])
            eng = ("dve", "act", "pool")[ctr[0] % 3]
            if eng == "act":
                P.op(eng, (lambda e, s=s, k=k, c0=c0, cw=cw: e.activation(out=dst16[:, k, c0:c0 + cw], in_=stage[s][:, 0:cw], func=AF.Copy)),
                     reads=[stage_bufs[s]], writes=[dstbuf])
            else:
                P.op(eng, (lambda e, s=s, k=k, c0=c0, cw=cw: e.tensor_copy(out=dst16[:, k, c0:c0 + cw], in_=stage[s][:, 0:cw])),
                     reads=[stage_bufs[s]], writes=[dstbuf])


def emit_B(nc, P, ps, layer, io):
    T = 2048
    TT = 256
    NT = T // TT
    xT = io["x32src"]; wg = io["wg"]; bg = io["bg"]; wbr = io["wbr"]; wo = io["wo"]; hgn = io["hgn"]; lng = io["lng"]; lnb = io["lnb"]
    orows = io["orows"]; oidx = io["oidx"]; Bodst = io["Bodst"]; Bxsrc = io["Bxsrc"]
    out32 = io["out32"]; out16 = io.get("out16")
    esb = ExitStack()
    sb = lambda n, s, dt=F32: esb.enter_context(nc.sbuf_tensor("%s_B%d" % (n, layer), s, dt))
    wg16 = sb("wg16", [128, 8, 4608], BF16); Bwg = Buf("wg16")
    wbr16 = sb("wbr16", [128, 12, 1024], BF16); Bwbr = Buf("wbr16")
    wo16 = sb("wo16", [128, 8, 1024], BF16); Bwo = Buf("wo16")
    stage = [sb("wstage%d" % i, [128, 1152], F32) for i in range(2)]
    stage_b = [Buf("wstage%d" % i) for i in range(2)]
    bgs = sb("bgs", [128, 36]); hgns = sb("hgns", [128, 4]); lngs = sb("lngs", [128, 8]); lnbs = sb("lnbs", [128, 8])
    oix = sb("oix", [128, 96], I32)
    Bc = Buf("consts")
    ones32 = sb("ones32", [128, 128]); Bones = Buf("ones")
    epsr = sb("epsr", [128, 1]); epsl = sb("epsl", [128, 1])
    x32s = [sb("x32_%d" % i, [128, 8, TT]) for i in range(2)]; Bx32s = [Buf("x32_%d" % i) for i in range(2)]
    x16 = sb("x16", [128, 8, TT], BF16); Bx16 = Buf("x16")
    o32s = [sb("o16_%d" % i, [128, 12, TT], BF16) for i in range(2)]; Bo32s = [Buf("o16_%d" % i) for i in range(2)]
    y16 = sb("y16", [128, 12, TT], BF16); By16 = [Buf("y16_%d" % i) for i in range(12)]
    gt = [sb("gt%d" % i, [128, TT]) for i in range(2)]; Bgt = [Buf("gt%d" % i) for i in range(2)]
    sq = sb("sq", [128, 8, TT]); Bsq = Buf("sq")
    rstd = sb("rstd", [128, TT]); Brstd = Buf("rstd")
    tmp = sb("tmp", [128, TT]); Btmp = Buf("tmp")
    sg = [sb("sg%d" % i, [128, 3, TT]) for i in range(2)]; Bsg = [Buf("sg%d" % i) for i in range(2)]
    mm = sb("mm", [128, TT]); Bmm = Buf("mm")
    tt2 = sb("tt2", [128, TT]); Btt2 = Buf("tt2")
    mg16 = sb("mg16", [128, 8, TT], BF16); Bmg = [Buf("mg%d" % i) for i in range(8)]
    r32s = [sb("r32_%d" % i, [128, 8, TT]) for i in range(2)]; Brs = [[Buf("r%d_%d" % (j, i)) for i in range(8)] for j in range(2)]
    tmpl = sb("tmpl", [128, TT]); Btmpl = Buf("tmpl"); rstdl = sb("rstdl", [128, TT]); Brstdl = Buf("rstdl")
    mean = sb("mean", [128, TT]); Bmean = Buf("mean")
    ob = [sb("ob%d" % i, [128, TT]) for i in range(2)]; Bob = [Buf("ob%d" % i) for i in range(2)]
    ob16 = [sb("ob16_%d" % i, [128, TT], BF16) for i in range(2)]; Bob16 = [Buf("ob16_%d" % i) for i in range(2)]
    Bout = Buf("xnT")
    xnT = out32

    P.dma("sp", "c0", bgs[:], bg, writes=[Bc])
    P.dma("sp", "c4", oix[:], oidx, writes=[Bc])
    P.dma("sp", "c1", hgns[:], hgn, writes=[Bc])
    P.dma("sp", "c2", lngs[:], lng, writes=[Bc])
    P.dma("sp", "c3", lnbs[:], lnb, writes=[Bc])
    P.op("pool", lambda e: e.memset(ones32[:], 1.0), writes=[Bones])
    P.op("pool", lambda e: e.memset(epsr[:], RMS_EPS), writes=[Bc])
    P.op("pool", lambda e: e.memset(epsl[:], LN_EPS), writes=[Bc])
    ctr = [0]
    load_cast_weight(P, nc, "sp", wg, wg16, Bwg, 8, 4608, stage, stage_b, ctr, 1152)
    load_cast_weight(P, nc, "sp", wbr, wbr16, Bwbr, 12, 1024, stage, stage_b, ctr, 1024)
    load_cast_weight(P, nc, "sp", wo, wo16, Bwo, 8, 1024, stage, stage_b, ctr, 1024)

    xv = xT.rearrange("(k p) t -> p k t", p=128)
    outv = xnT.rearrange("(k p) t -> p k t", p=128)
    gi = 0

    def load_tile(t):
        cc0 = t * TT
        xb, bxb, ob_, bob = x32s[t % 2], Bx32s[t % 2], o32s[t % 2], Bo32s[t % 2]
        P.dma("act", "x32_%d" % (t % 2), xb[:], xv[:, :, cc0:cc0 + TT], reads=[Bxsrc], writes=[bxb])
        for blk in range(12):
            P.dma("pool", "o16g%d" % (t % 2), None, None, reads=[Bodst, Bc], writes=[bob],
                  fn=(lambda e, blk=blk, t=t, ob_=ob_: e.indirect_dma_start(out=ob_[:, blk, :], out_offset=None, in_=orows,
                                                                         in_offset=bass.IndirectOffsetOnAxis(ap=oix[:, t * 12 + blk:t * 12 + blk + 1], axis=0))))

    def part1(tt):
        nonlocal gi
        c0 = tt * TT
        if tt == 0:
            load_tile(0)
        if tt + 1 < NT:
            load_tile(tt + 1)
        x32, Bx32, o32, Bo32 = x32s[tt % 2], Bx32s[tt % 2], o32s[tt % 2], Bo32s[tt % 2]
        r32, Br = r32s[tt % 2], Brs[tt % 2]
        P.op("pool", lambda e, x32=x32: e.tensor_copy(out=x16[:], in_=x32[:]), reads=[Bx32], writes=[Bx16])
        P.op("act", lambda e, o32=o32: e.activation(out=sq[:, 0:4, :], in_=o32[:, 0:4, :], func=AF.Square), reads=[Bo32], writes=[Bsq])
        pt, pb = ps.next()
        for j in range(4):
            P.op("pe", lambda e, j=j, pt=pt: e.matmul(pt[:, 0:TT], lhsT=ones32[:], rhs=sq[:, j, :], start=(j == 0), stop=(j == 3)),
                 reads=[Bones, Bsq], writes=[pb])
        P.op("act", lambda e, pt=pt: e.activation(out=tmp[:], in_=pt[:, 0:TT], func=AF.Sqrt, bias=epsr[:, 0:1], scale=1.0 / 512.0), reads=[pb, Bc], writes=[Btmp])
        P.op("dve", lambda e: e.reciprocal(out=rstd[:], in_=tmp[:]), reads=[Btmp], writes=[Brstd])
        for blk in range(12):
            pt, pb = ps.next()
            for k in range(8):
                P.op("pe", lambda e, k=k, pt=pt, blk=blk: e.matmul(pt[:, 0:TT], lhsT=wg16[:, k, blk * 128:(blk + 1) * 128], rhs=x16[:, k, :],
                                                              start=(k == 0), stop=(k == 7)), reads=[Bwg, Bx16], writes=[pb])
            g = gi % 2
            gi += 1
            P.op("act", lambda e, pt=pt, blk=blk, g=g: e.activation(out=gt[g][:], in_=pt[:, 0:TT], func=AF.Silu, bias=bgs[:, blk:blk + 1], scale=1.0),
                 reads=[pb, Bc], writes=[Bgt[g]])
            if blk < 4:
                P.op("dve", lambda e, g=g: e.tensor_tensor(out=gt[g][:], in0=gt[g][:], in1=rstd[:], op=ALU.mult),
                     reads=[Bgt[g], Brstd], writes=[Bgt[g]])
                P.op("dve", lambda e, g=g, blk=blk, o32=o32: e.scalar_tensor_tensor(out=y16[:, blk, :], in0=o32[:, blk, :], scalar=hgns[:, blk:blk + 1],
                                                                           in1=gt[g][:], op0=ALU.mult, op1=ALU.mult),
                     reads=[Bo32, Bgt[g], Bc], writes=[By16[blk]])
            else:
                P.op("dve", lambda e, g=g, blk=blk, o32=o32: e.tensor_tensor(out=y16[:, blk, :], in0=o32[:, blk, :], in1=gt[g][:], op=ALU.mult),
                     reads=[Bo32, Bgt[g]], writes=[By16[blk]])
        for db in range(8):
            s = db % 2
            pbs = []
            for n in range(3):
                pt, pb = ps.next()
                col = 1536 + n * 1024 + db * 128
                for k in range(8):
                    P.op("pe", lambda e, k=k, pt=pt, col=col: e.matmul(pt[:, 0:TT], lhsT=wg16[:, k, col:col + 128], rhs=x16[:, k, :],
                                                                  start=(k == 0), stop=(k == 7)), reads=[Bwg, Bx16], writes=[pb])
                bi = 12 + n * 8 + db
                P.op("act", lambda e, pt=pt, n=n, s=s, bi=bi: e.activation(out=sg[s][:, n, :], in_=pt[:, 0:TT], func=AF.Sigmoid, bias=bgs[:, bi:bi + 1], scale=1.0),
                     reads=[pb, Bc], writes=[Bsg[s]])
            for n in range(3):
                pt, pb = ps.next()
                for j in range(4):
                    P.op("pe", lambda e, j=j, n=n, pt=pt, db=db: e.matmul(pt[:, 0:TT], lhsT=wbr16[:, n * 4 + j, db * 128:(db + 1) * 128], rhs=y16[:, n * 4 + j, :],
                                                                     start=(j == 0), stop=(j == 3)), reads=[Bwbr, By16[n * 4 + j]], writes=[pb])
                pbs.append((pt, pb))
            P.op("dve", lambda e, s=s, p0=pbs[0][0]: e.tensor_tensor(out=mm[:], in0=p0[:, 0:TT], in1=sg[s][:, 0, :], op=ALU.mult),
                 reads=[pbs[0][1], Bsg[s]], writes=[Bmm])
            P.op("dve", lambda e, s=s, p1=pbs[1][0]: e.tensor_tensor(out=tt2[:], in0=p1[:, 0:TT], in1=sg[s][:, 1, :], op=ALU.mult),
                 reads=[pbs[1][1], Bsg[s]], writes=[Btt2])
            P.op("pool", lambda e: e.tensor_tensor(out=mm[:], in0=mm[:], in1=tt2[:], op=ALU.add), reads=[Bmm, Btt2], writes=[Bmm])
            P.op("dve", lambda e, s=s, p2=pbs[2][0]: e.tensor_tensor(out=tt2[:], in0=p2[:, 0:TT], in1=sg[s][:, 2, :], op=ALU.mult),
                 reads=[pbs[2][1], Bsg[s]], writes=[Btt2])
            P.op("pool", lambda e, db=db: e.tensor_tensor(out=mg16[:, db, :], in0=mm[:], in1=tt2[:], op=ALU.add),
                 reads=[Bmm, Btt2], writes=[Bmg[db]])
        for eb in range(8):
            pt, pb = ps.next()
            for d in range(8):
                P.op("pe", lambda e, d=d, pt=pt, eb=eb: e.matmul(pt[:, 0:TT], lhsT=wo16[:, d, eb * 128:(eb + 1) * 128], rhs=mg16[:, d, :],
                                                            start=(d == 0), stop=(d == 7)), reads=[Bwo, Bmg[d]], writes=[pb])
            P.op("dve", lambda e, pt=pt, eb=eb, x32=x32: e.scalar_tensor_tensor(out=r32[:, eb, :], in0=x32[:, eb, :], scalar=ALPHA, in1=pt[:, 0:TT],
                                                                        op0=ALU.mult, op1=ALU.add), reads=[Bx32, pb], writes=[Br[eb]])

    def part2(tt):
        c0 = tt * TT
        r32, Br = r32s[tt % 2], Brs[tt % 2]
        pt, pb = ps.next()
        for eb in range(8):
            P.op("pe", lambda e, eb=eb, pt=pt: e.matmul(pt[:, 0:TT], lhsT=ones32[:], rhs=r32[:, eb, :], start=(eb == 0), stop=(eb == 7)),
                 reads=[Bones, Br[eb]], writes=[pb])
        P.op("act", lambda e, pt=pt: e.activation(out=mean[:], in_=pt[:, 0:TT], func=AF.Copy, scale=1.0 / 1024.0), reads=[pb], writes=[Bmean])
        for eb in range(8):
            P.op("dve", lambda e, eb=eb: e.tensor_tensor(out=r32[:, eb, :], in0=r32[:, eb, :], in1=mean[:], op=ALU.subtract),
                 reads=[Br[eb], Bmean], writes=[Br[eb]])
        P.op("act", lambda e: e.activation(out=sq[:], in_=r32[:], func=AF.Square), reads=Br, writes=[Bsq])
        pt, pb = ps.next()
        for eb in range(8):
            P.op("pe", lambda e, eb=eb, pt=pt: e.matmul(pt[:, 0:TT], lhsT=ones32[:], rhs=sq[:, eb, :], start=(eb == 0), stop=(eb == 7)),
                 reads=[Bones, Bsq], writes=[pb])
        P.op("act", lambda e, pt=pt: e.activation(out=tmpl[:], in_=pt[:, 0:TT], func=AF.Sqrt, bias=epsl[:, 0:1], scale=1.0 / 1024.0), reads=[pb, Bc], writes=[Btmpl])
        P.op("dve", lambda e: e.reciprocal(out=rstdl[:], in_=tmpl[:]), reads=[Btmpl], writes=[Brstdl])
        for eb in range(8):
            s = eb % 2
            P.op("dve", lambda e, eb=eb: e.tensor_tensor(out=r32[:, eb, :], in0=r32[:, eb, :], in1=rstdl[:], op=ALU.mult),
                 reads=[Br[eb], Brstdl], writes=[Br[eb]])
            P.op("act", lambda e, eb=eb, s=s: e.activation(out=ob[s][:], in_=r32[:, eb, :], func=AF.Identity, bias=lnbs[:, eb:eb + 1], scale=lngs[:, eb:eb + 1]),
                 reads=[Br[eb], Bc], writes=[Bob[s]])
            P.dma("sp", "ob%d" % s, outv[:, eb, c0:c0 + TT], ob[s][:], reads=[Bob[s]], writes=[Bout])
            if out16 is not None:
                P.op("pool", lambda e, s=s: e.tensor_copy(out=ob16[s][:], in_=ob[s][:]), reads=[Bob[s]], writes=[Bob16[s]])
                jx = c0 // 512
                P.dma("sp", "ob16_%d" % s, out16[jx * 1024 + eb * 128:jx * 1024 + (eb + 1) * 128, (c0 % 512):(c0 % 512) + TT], ob16[s][:],
                      reads=[Bob16[s]], writes=[Bout, io["Bccx"][jx]])
        if out16 is not None and (c0 % 512) + TT == 512:
            io["cc_x"](c0 // 512)

    part1(0)
    for tt in range(NT):
        if tt + 1 < NT:
            part1(tt + 1)
        part2(tt)
    P.barrier()
    esb.close()


RMS_EPS = 1e-6
S = 8192
TT = 512
NT = S // TT
QSCALE = 192.0 ** -0.5
TWO_PI = 2.0 * math.pi
C1 = 6.28125
C2 = TWO_PI - C1
DILS = (1, 4, 16)
LN_MINF = math.log(1e-6)


def make_scratch(nc):
    dr = lambda n, s, dt=BF16: nc.dram_tensor(n, s, dt, kind="Internal").ap()
    scr = {}
    scr["dsub"] = [[dr("dsub%d_%d" % (g, t), [128, DILS[g], S // DILS[g]]) for t in range(3)] for g in range(3)]
    scr["hq_d"] = [[dr("hq%d_%d" % (h, d), [128, S]) for d in range(2)] for h in range(2)]
    scr["hk_d"] = [[dr("hk%d_%d" % (h, d), [128, S]) for d in range(2)] for h in range(2)]
    scr["hv_d"] = dr("hv", [128, S])
    scr["qd1"] = dr("qd1", [128, S]); scr["qd2"] = dr("qd2", [64, S])
    return scr


def emit_A(nc, P, ps, layer, io, scr, xsrc16=None, phases=("mla", "dil", "hg")):
    debug = False
    xT = io.get("xT"); pos = io["pos"]
    wA = io["wA"]; bA = io["bA"]; lbraw = io["lbraw"]
    wuq = io["wuq"]; gq = io["gq"]; wukv = io["wukv"]; gkv = io["gkv"]
    etab = io["etab"]; ropec = io["ropec"]; ident = io["ident"]; masks = io["masks"]; scanmask = io["scanmask"]
    osrc = io["osrc"]
    dsub = scr["dsub"]; hq_d = scr["hq_d"]; hk_d = scr["hk_d"]; hv_d = scr["hv_d"]; qd1 = scr["qd1"]; qd2 = scr["qd2"]
    Bdsub = Buf("dsub"); Bhqk = Buf("hqk"); BoT = Buf("oT"); Bqd = Buf("qd")
    es0 = ExitStack()
    sb = lambda n, s, dt=F32: es0.enter_context(nc.sbuf_tensor("%s_A%d" % (n, layer), s, dt))

    Bc = Buf("consts")
    bAs = sb("bAs", [128, 23]); lbr = sb("lbr", [128, 4, 2]); lbt = sb("lbt", [128, 4, 3])
    gqs = sb("gqs", [128, 3]); gkvs = sb("gkvs", [128, 2]); ropecs = sb("ropecs", [64, 2])
    ones32 = sb("ones32", [128, 128]); epsr = sb("epsr", [128, 1]); id32 = sb("id32", [128, 128]); id16 = sb("id16", [128, 128], BF16)
    ones16 = sb("ones16", [128, 128], BF16)
    msk = sb("msk", [128, 2, 128]); smask = sb("smask", [128, TT])
    P.dma("sp", "c0", bAs[:], bA, writes=[Bc])
    P.dma("sp", "c1", lbr[:], lbraw, writes=[Bc])
    P.dma("sp", "c2", gqs[:], gq, writes=[Bc])
    P.dma("sp", "c3", gkvs[:], gkv, writes=[Bc])
    P.dma("sp", "c4", ropecs[:], ropec, writes=[Bc])
    P.dma("sp", "c5", id32[:], ident, writes=[Bc])
    P.dma("sp", "c6", msk[:], masks, writes=[Bc])
    P.dma("sp", "c7", smask[:], scanmask, writes=[Bc])
    P.op("pool", lambda e: e.memset(ones32[:], 1.0), writes=[Bc])
    P.op("pool", lambda e: e.memset(ones16[:], 1.0), writes=[Bc])
    P.op("pool", lambda e: e.memset(epsr[:], RMS_EPS), writes=[Bc])
    P.op("pool", lambda e: e.tensor_copy(out=id16[:], in_=id32[:]), reads=[Bc], writes=[Bc])
    for blk in (7, 10, 13):
        P.op("dve", lambda e, blk=blk: e.tensor_scalar(out=bAs[:, blk:blk + 1], in0=bAs[:, blk:blk + 1], scalar1=0.125, scalar2=None, op0=ALU.mult),
             reads=[Bc], writes=[Bc])
    if layer == 0:
        P.op("dve", lambda e: e.memset(lbt[:, :, 0], 0.0), writes=[Bc])
    else:
        P.op("dve", lambda e: e.tensor_tensor(out=lbt[:, :, 1], in0=lbr[:, :, 1], in1=lbr[:, :, 0], op=ALU.subtract), reads=[Bc], writes=[Bc])
        P.op("act", lambda e: e.activation(out=lbt[:, :, 0], in_=lbt[:, :, 1], func=AF.Sigmoid), reads=[Bc], writes=[Bc])
    P.op("dve", lambda e: e.tensor_scalar(out=lbt[:, :, 1], in0=lbt[:, :, 0], scalar1=-1.0, scalar2=1.0, op0=ALU.mult, op1=ALU.add), reads=[Bc], writes=[Bc])
    P.op("dve", lambda e: e.tensor_scalar(out=lbt[:, :, 2], in0=lbt[:, :, 1], scalar1=-1.0, scalar2=None, op0=ALU.mult), reads=[Bc], writes=[Bc])

    K1T = sb("K1T", [128, S], BF16); K2T = sb("K2T", [64, S], BF16)
    Vtok = sb("Vtok", [128, S // 128, 128], BF16)
    BQ = Buf("Q"); BK = Buf("K"); BV = Buf("V")
    mT = [[sb("mT%d%d" % (h, d), [128, 128]) for d in range(2)] for h in range(2)]
    BTt = [[sb("BT%d%d" % (h, d), [128, 128]) for d in range(2)] for h in range(2)]
    BmB = Buf("mB")

    es1 = ExitStack()
    sb = lambda n, s, dt=F32: es1.enter_context(nc.sbuf_tensor("%s_A%d" % (n, layer), s, dt))
    win16 = sb("win16", [128, 8, 2816], BF16); Bwin = Buf("win16")
    stage = [sb("wstage%d" % i, [128, 704], F32) for i in range(2)]
    stage_b = [Buf("wstage%d" % i) for i in range(2)]
    wv = wA.rearrange("(k p) c -> p k c", p=128)
    ci = 0
    for k in range(8):
        for c0 in (0, 704, 1408, 2112):
            s = ci % 2
            P.dma("sp", "wst%d" % s, stage[s][:], wv[:, k, c0:c0 + 704], writes=[stage_b[s]])
            P.op("dve" if ci % 2 == 0 else "pool", lambda e, s=s, k=k, c0=c0: e.tensor_copy(out=win16[:, k, c0:c0 + 704], in_=stage[s][:]),
                 reads=[stage_b[s]], writes=[Bwin])
            ci += 1
    wuq16 = sb("wuq16", [128, 3, 256], BF16); wukv16 = sb("wukv16", [128, 2, 256], BF16); Bwu = Buf("wu")
    wuv = wuq.rearrange("(k p) c -> p k c", p=128)
    wkv = wukv.rearrange("(k p) c -> p k c", p=128)
    for j in range(3):
        s = ci % 2
        P.dma("sp", "wst%d" % s, stage[s][:, 0:256], wuv[:, j, :], writes=[stage_b[s]])
        P.op("dve", lambda e, s=s, j=j: e.tensor_scalar(out=wuq16[:, j, :], in0=stage[s][:, 0:256], scalar1=gqs[:, j:j + 1], scalar2=None, op0=ALU.mult),
             reads=[stage_b[s], Bc], writes=[Bwu])
        ci += 1
    for j in range(2):
        s = ci % 2
        P.dma("sp", "wst%d" % s, stage[s][:, 0:256], wkv[:, j, :], writes=[stage_b[s]])
        P.op("dve", lambda e, s=s, j=j: e.tensor_scalar(out=wukv16[:, j, :], in0=stage[s][:, 0:256], scalar1=gkvs[:, j:j + 1], scalar2=None, op0=ALU.mult),
             reads=[stage_b[s], Bc], writes=[Bwu])
        ci += 1

    x32s = [sb("x32_%d" % i, [128, 4, TT]) for i in range(2)]; Bx32s = [Buf("x32_%d" % i) for i in range(2)]
    x16s = [sb("x16_%d" % i, [128, 8, TT], BF16) for i in range(2)]; Bx16s = [Buf("x16_%d" % i) for i in range(2)]
    xcur = [None, None]
    posi = sb("posi", [64, TT], I32); Bposi = Buf("posi")
    ang = sb("ang", [64, TT]); Bang = Buf("ang")
    ru = sb("ru", [64, TT]); Bru = Buf("ru"); rki = sb("rki", [64, TT], I32); Brki = Buf("rki"); rkf = sb("rkf", [64, TT]); Brkf = Buf("rkf")
    cs = sb("cs", [64, TT]); sn = sb("sn", [64, TT]); Bcs = Buf("cs"); Bsn = Buf("sn")
    c32 = sb("c32", [128, 3, TT]); Bc32 = Buf("c32"); csq = sb("csq", [128, 3, TT]); Bcsq = Buf("csq")
    cn16 = sb("cn16", [128, 3, TT], BF16); Bcn = Buf("cn16")
    rt = sb("rt", [128, TT]); Brt = Buf("rt"); rr = sb("rr", [128, TT]); Brr = Buf("rr")
    t1 = sb("t1", [64, TT]); t2 = sb("t2", [64, TT]); Bt1 = Buf("t1"); Bt2 = Buf("t2")
    q1s = sb("q1s", [128, TT], BF16); q2s = sb("q2s", [64, TT], BF16); Bq1s = Buf("q1s"); Bq2s = Buf("q2s")
    dd16 = [sb("dd16_%d" % i, [128, TT], BF16) for i in range(2)]; Bdd = [Buf("dd16_%d" % i) for i in range(2)]
    qs = [sb("qs%d" % h, [128, TT]) for h in range(2)]; Bqs = [Buf("qs%d" % h) for h in range(2)]
    sig = sb("sig", [128, TT]); Bsig = Buf("sig"); ff = sb("ff", [128, TT]); Bff = Buf("ff")
    bb = sb("bb", [128, TT]); Bbb = Buf("bb"); eq = sb("eq", [128, TT]); Beq = Buf("eq"); ek = sb("ek", [128, TT]); Bek = Buf("ek")
    kk = sb("kk", [128, TT]); Bkk = Buf("kk")
    hq16 = [sb("hq16_%d" % i, [128, TT], BF16) for i in range(2)]; Bhq16 = [Buf("hq16_%d" % i) for i in range(2)]
    hk16 = [sb("hk16_%d" % i, [128, TT], BF16) for i in range(2)]; Bhk16 = [Buf("hk16_%d" % i) for i in range(2)]
    hv16 = sb("hv16", [128, TT], BF16); Bhv16 = Buf("hv16")

    if xsrc16 is None:
        xv = xT.rearrange("(k p) t -> p k t", p=128)
    else:
        xg = xsrc16.rearrange("(j r k p) t -> p j r k t", j=4, r=4, k=8, p=128)

    def inproj(col, m, rhs_cols=None):
        pt, pb = ps.next()
        xx, bxx = xcur[0], xcur[1]
        for k in range(8):
            P.op("pe", lambda e, k=k, pt=pt, xx=xx: e.matmul(pt[0:m, :], lhsT=win16[:, k, col:col + m], rhs=xx[:, k, :], start=(k == 0), stop=(k == 7)),
                 reads=[Bwin, bxx], writes=[pb])
        return pt, pb

    def sintab(dst, Bdst, shift):
        P.op("dve", lambda e: e.tensor_scalar(out=ru[:], in0=ang[:], scalar1=1.0 / TWO_PI, scalar2=shift / TWO_PI + 0.5, op0=ALU.mult, op1=ALU.add),
             reads=[Bang], writes=[Bru])
        P.op("dve", lambda e: e.tensor_copy(out=rki[:], in_=ru[:]), reads=[Bru], writes=[Brki])
        P.op("dve", lambda e: e.tensor_copy(out=rkf[:], in_=rki[:]), reads=[Brki], writes=[Brkf])
        P.op("dve", lambda e: e.tensor_scalar(out=ru[:], in0=ang[:], scalar1=shift, scalar2=None, op0=ALU.add), reads=[Bang], writes=[Bru])
        P.op("dve", lambda e: e.scalar_tensor_tensor(out=ru[:], in0=rkf[:], scalar=-C1, in1=ru[:], op0=ALU.mult, op1=ALU.add),
             reads=[Brkf, Bru], writes=[Bru])
        P.op("dve", lambda e: e.scalar_tensor_tensor(out=ru[:], in0=rkf[:], scalar=-C2, in1=ru[:], op0=ALU.mult, op1=ALU.add),
             reads=[Brkf, Bru], writes=[Bru])
        P.op("dve", lambda e: e.tensor_scalar(out=rkf[:], in0=ru[:], scalar1=math.pi, scalar2=None, op0=ALU.is_gt), reads=[Bru], writes=[Brkf])
        P.op("dve", lambda e: e.scalar_tensor_tensor(out=ru[:], in0=rkf[:], scalar=-TWO_PI, in1=ru[:], op0=ALU.mult, op1=ALU.add),
             reads=[Brkf, Bru], writes=[Bru])
        P.op("dve", lambda e: e.tensor_scalar(out=rkf[:], in0=ru[:], scalar1=-math.pi, scalar2=None, op0=ALU.is_lt), reads=[Bru], writes=[Brkf])
        P.op("dve", lambda e: e.scalar_tensor_tensor(out=ru[:], in0=rkf[:], scalar=TWO_PI, in1=ru[:], op0=ALU.mult, op1=ALU.add),
             reads=[Brkf, Bru], writes=[Bru])
        P.op("dve", lambda e: e.tensor_scalar(out=ru[:], in0=ru[:], scalar1=math.pi, scalar2=-math.pi, op0=ALU.min, op1=ALU.max), reads=[Bru], writes=[Bru])
        P.op("act", lambda e: e.activation(out=dst[:], in_=ru[:], func=AF.Sin), reads=[Bru], writes=[Bdst])

    def rms_norm(nblk, rank):
        P.op("act", lambda e: e.activation(out=csq[:, 0:nblk, :], in_=c32[:, 0:nblk, :], func=AF.Square), reads=[Bc32], writes=[Bcsq])
        pt, pb = ps.next()
        for j in range(nblk):
            P.op("pe", lambda e, j=j, pt=pt: e.matmul(pt[:], lhsT=ones32[:], rhs=csq[:, j, :], start=(j == 0), stop=(j == nblk - 1)),
                 reads=[Bc, Bcsq], writes=[pb])
        P.op("act", lambda e, pt=pt: e.activation(out=rt[:], in_=pt[:], func=AF.Sqrt, bias=epsr[:, 0:1], scale=1.0 / rank), reads=[pb, Bc], writes=[Brt])
        P.op("dve", lambda e: e.reciprocal(out=rr[:], in_=rt[:]), reads=[Brt], writes=[Brr])
        for j in range(nblk):
            P.op("dve", lambda e, j=j: e.tensor_tensor(out=cn16[:, j, :], in0=c32[:, j, :], in1=rr[:], op=ALU.mult), reads=[Bc32, Brr], writes=[Bcn])

    ddi = 0
    hi = 0

    def load_x(tt):
        c0 = tt * TT
        xb, bxb = x16s[tt % 2], Bx16s[tt % 2]
        if xsrc16 is None:
            for hf in range(2):
                P.dma("sp", "x32_%d" % hf, x32s[hf][:], xv[:, hf * 4:(hf + 1) * 4, c0:c0 + TT], writes=[Bx32s[hf]])
                P.op("pool", lambda e, hf=hf, xb=xb: e.tensor_copy(out=xb[:, hf * 4:(hf + 1) * 4, :], in_=x32s[hf][:]), reads=[Bx32s[hf]], writes=[bxb])
        else:
            P.dma("sp", "xg_%d" % (tt % 2), xb[:], xg[:, (c0 % 2048) // 512, c0 // 2048, :, :], reads=[io["Bxg"][(c0 % 2048) // 512]], writes=[bxb])

    load_x(0)
    for tt in range(NT):
        c0 = tt * TT
        xcur[0], xcur[1] = x16s[tt % 2], Bx16s[tt % 2]
        if tt + 1 < NT:
            load_x(tt + 1)
        if "ropecache" in io and layer > 0:
            P.dma("sp", "csld", cs[:], io["ropecache"][0][:, c0:c0 + TT], reads=[io["Bropecache"]], writes=[Bcs])
            P.dma("sp", "snld", sn[:], io["ropecache"][1][:, c0:c0 + TT], reads=[io["Bropecache"]], writes=[Bsn])
        else:
            P.dma("sp", "posi", posi[:], pos[0:1, c0:c0 + TT].partition_broadcast(64), writes=[Bposi])
            P.op("dve", lambda e: e.tensor_copy(out=ang[:], in_=posi[:]), reads=[Bposi], writes=[Bang])
            P.op("dve", lambda e: e.tensor_scalar(out=ang[:], in0=ang[:], scalar1=ropecs[:, 0:1], scalar2=None, op0=ALU.mult), reads=[Bang, Bc], writes=[Bang])
            sintab(sn, Bsn, 0.0)
            sintab(cs, Bcs, math.pi / 2)
            P.op("dve", lambda e: e.tensor_scalar(out=sn[:], in0=sn[:], scalar1=ropecs[:, 1:2], scalar2=None, op0=ALU.mult), reads=[Bsn, Bc], writes=[Bsn])

            if "ropecache" in io:
                P.dma("sp", "csst", io["ropecache"][0][:, c0:c0 + TT], cs[:], reads=[Bcs], writes=[io["Bropecache"]])
                P.dma("sp", "snst", io["ropecache"][1][:, c0:c0 + TT], sn[:], reads=[Bsn], writes=[io["Bropecache"]])
        for j in range(3):
            pt, pb = inproj((16 + j) * 128, 128)
            P.op("act", lambda e, pt=pt, j=j: e.activation(out=c32[:, j, :], in_=pt[:], func=AF.Identity, bias=bAs[:, 16 + j:17 + j], scale=1.0),
                 reads=[pb, Bc], writes=[Bc32])
        rms_norm(3, 384.0)
        if "dil" in phases:
            for g in range(3):
                d = DILS[g]
                for t in range(3):
                    blk = 7 + g * 3 + t
                    pt, pb = inproj(blk * 128, 128)
                    s = ddi % 2
                    ddi += 1
                    P.op("act", lambda e, pt=pt, s=s, d=d, blk=blk, t=t: e.activation(
                        out=dd16[s][:].rearrange("p (r j) -> p r j", r=d), in_=pt[:].rearrange("p (j r) -> p r j", r=d),
                        func=AF.Identity, bias=bAs[:, blk:blk + 1], scale=(0.125 if t == 0 else 1.0)), reads=[pb, Bc], writes=[Bdd[s]])
                    P.dma("sp", "dd%d" % s, dsub[g][t][:, :, c0 // d:(c0 + TT) // d], dd16[s][:].rearrange("p (r j) -> p r j", r=d),
                          reads=[Bdd[s]], writes=[Bdsub])
        pt, pb = ps.next()
        for j in range(3):
            P.op("pe", lambda e, j=j, pt=pt: e.matmul(pt[:], lhsT=wuq16[:, j, 0:128], rhs=cn16[:, j, :], start=(j == 0), stop=(j == 2)),
                 reads=[Bwu, Bcn], writes=[pb])
        P.op("act", lambda e, pt=pt: e.activation(out=q1s[:], in_=pt[:], func=AF.Copy, scale=QSCALE), reads=[pb], writes=[Bq1s])
        P.dma("sp", "q1s", qd1[:, c0:c0 + TT], q1s[:], reads=[Bq1s], writes=[Bqd])
        pA, pbA = ps.next()
        for j in range(3):
            P.op("pe", lambda e, j=j, pA=pA: e.matmul(pA[0:64, :], lhsT=wuq16[:, j, 128:192], rhs=cn16[:, j, :], start=(j == 0), stop=(j == 2)),
                 reads=[Bwu, Bcn], writes=[pbA])
        pB, pbB = ps.next()
        for j in range(3):
            P.op("pe", lambda e, j=j, pB=pB: e.matmul(pB[0:64, :], lhsT=wuq16[:, j, 192:256], rhs=cn16[:, j, :], start=(j == 0), stop=(j == 2)),
                 reads=[Bwu, Bcn], writes=[pbB])
        P.op("dve", lambda e, pA=pA: e.scalar_tensor_tensor(out=t1[:], in0=pA[0:64, :], scalar=QSCALE, in1=cs[:], op0=ALU.mult, op1=ALU.mult),
             reads=[pbA, Bcs], writes=[Bt1])
        P.op("dve", lambda e, pB=pB: e.scalar_tensor_tensor(out=t2[:], in0=pB[0:64, :], scalar=QSCALE, in1=sn[:], op0=ALU.mult, op1=ALU.mult),
             reads=[pbB, Bsn], writes=[Bt2])
        P.op("pool", lambda e: e.tensor_tensor(out=q2s[:], in0=t1[:], in1=t2[:], op=ALU.add), reads=[Bt1, Bt2], writes=[Bq2s])
        P.dma("sp", "q2s", qd2[:, c0:c0 + TT], q2s[:], reads=[Bq2s], writes=[Bqd])
        for j in range(2):
            pt, pb = inproj((19 + j) * 128, 128)
            P.op("act", lambda e, pt=pt, j=j: e.activation(out=c32[:, j, :], in_=pt[:], func=AF.Identity, bias=bAs[:, 19 + j:20 + j], scale=1.0),
                 reads=[pb, Bc], writes=[Bc32])
        rms_norm(2, 256.0)
        if "hg" in phases:
            for h in range(2):
                pt, pb = inproj(h * 128, 128)
                P.op("act", lambda e, pt=pt, h=h: e.activation(out=qs[h][:], in_=pt[:], func=AF.Silu, bias=bAs[:, h:h + 1], scale=1.0),
                     reads=[pb, Bc], writes=[Bqs[h]])
            pt, pb = inproj(6 * 128, 128)
            P.op("act", lambda e, pt=pt: e.activation(out=hv16[:], in_=pt[:], func=AF.Identity, bias=bAs[:, 6:7], scale=1.0), reads=[pb, Bc], writes=[Bhv16])
            P.dma("sp", "hv16", hv_d[:, c0:c0 + TT], hv16[:], reads=[Bhv16], writes=[Bhqk])
            for dr_ in range(2):
                for h in range(2):
                    idx = h * 2 + dr_
                    blk = 2 + dr_ * 2 + h
                    pt, pb = inproj(blk * 128, 128)
                    P.op("act", lambda e, pt=pt, blk=blk: e.activation(out=sig[:], in_=pt[:], func=AF.Sigmoid, bias=bAs[:, blk:blk + 1], scale=1.0),
                         reads=[pb, Bc], writes=[Bsig])
                    P.op("dve", lambda e, idx=idx: e.tensor_scalar(out=ff[:], in0=sig[:], scalar1=lbt[:, idx, 1:2], scalar2=lbt[:, idx, 0:1], op0=ALU.mult, op1=ALU.add),
                         reads=[Bsig, Bc], writes=[Bff])
                    P.op("act", lambda e: e.activation(out=ff[:], in_=ff[:], func=AF.Ln), reads=[Bff], writes=[Bff])
                    P.op("pool", lambda e: e.tensor_scalar(out=ff[:], in0=ff[:], scalar1=LN_MINF, scalar2=None, op0=ALU.max), reads=[Bff], writes=[Bff])
                    if dr_ == 0:
                        P.op("dve", lambda e: e.tensor_tensor_scan(out=bb[:], data0=smask[:], data1=ff[:], initial=0.0, op0=ALU.mult, op1=ALU.add),
                             reads=[Bff, Bc], writes=[Bbb])
                        mcol, bcol = 31, 63
                    else:
                        P.op("dve", lambda e: e.tensor_tensor_scan(out=bb[:, ::-1], data0=smask[:], data1=ff[:, ::-1], initial=0.0, op0=ALU.mult, op1=ALU.add),
                             reads=[Bff, Bc], writes=[Bbb])
                        mcol, bcol = 32, 0
                    b3 = bb[:].rearrange("p (c t) -> p c t", t=64)
                    P.op("pool", lambda e, h=h, dr_=dr_, tt=tt, b3=b3, mcol=mcol: e.tensor_copy(out=mT[h][dr_][:, tt * 8:(tt + 1) * 8], in_=b3[:, :, mcol]),
                         reads=[Bbb], writes=[BmB])
                    P.op("pool", lambda e, h=h, dr_=dr_, tt=tt, b3=b3, bcol=bcol: e.tensor_copy(out=BTt[h][dr_][:, tt * 8:(tt + 1) * 8], in_=b3[:, :, bcol]),
                         reads=[Bbb], writes=[BmB])
                    mb = b3[:, :, mcol:mcol + 1]
                    mbc = bass.AP(mb.tensor, mb.offset, [list(mb.ap[0]), list(mb.ap[1]), [0, 64]])
                    P.op("dve", lambda e, b3=b3, mbc=mbc: e.tensor_tensor(out=eq[:].rearrange("p (c t) -> p c t", t=64), in0=b3, in1=mbc, op=ALU.subtract),
                         reads=[Bbb], writes=[Beq])
                    P.op("act", lambda e: e.activation(out=ek[:], in_=eq[:], func=AF.Exp, scale=-1.0), reads=[Beq], writes=[Bek])
                    P.op("act", lambda e: e.activation(out=eq[:], in_=eq[:], func=AF.Exp), reads=[Beq], writes=[Beq])
                    s = hi % 2
                    hi += 1
                    P.op("dve", lambda e, s=s, h=h: e.tensor_tensor(out=hq16[s][:], in0=qs[h][:], in1=eq[:], op=ALU.mult), reads=[Bqs[h], Beq], writes=[Bhq16[s]])
                    P.dma("sp", "hq16_%d" % s, hq_d[h][dr_][:, c0:c0 + TT], hq16[s][:], reads=[Bhq16[s]], writes=[Bhqk])
                    P.op("dve", lambda e, idx=idx: e.tensor_scalar(out=kk[:], in0=sig[:], scalar1=lbt[:, idx, 2:3], scalar2=lbt[:, idx, 1:2], op0=ALU.mult, op1=ALU.add),
                         reads=[Bsig, Bc], writes=[Bkk])
                    P.op("pool", lambda e, s=s: e.tensor_tensor(out=hk16[s][:], in0=kk[:], in1=ek[:], op=ALU.mult), reads=[Bkk, Bek], writes=[Bhk16[s]])
                    P.dma("sp", "hk16_%d" % s, hk_d[h][dr_][:, c0:c0 + TT], hk16[s][:], reads=[Bhk16[s]], writes=[Bhqk])

        pt, pb = ps.next()
        for j in range(2):
            P.op("pe", lambda e, j=j, pt=pt: e.matmul(pt[:], lhsT=wukv16[:, j, 0:128], rhs=cn16[:, j, :], start=(j == 0), stop=(j == 1)),
                 reads=[Bwu, Bcn], writes=[pb])
        P.op("act", lambda e, pt=pt, c0=c0: e.activation(out=K1T[:, c0:c0 + TT], in_=pt[:], func=AF.Copy), reads=[pb], writes=[BK])
        pt, pb = ps.next()
        for i in range(4):
            for j in range(2):
                P.op("pe", lambda e, i=i, j=j, pt=pt: e.matmul(pt[:, i * 128:(i + 1) * 128], lhsT=cn16[:, j, i * 128:(i + 1) * 128], rhs=wukv16[:, j, 128:256],
                                                          start=(j == 0), stop=(j == 1)), reads=[Bwu, Bcn], writes=[pb])
        P.op("act", lambda e, pt=pt, tt=tt: e.activation(out=Vtok[:, tt * 4:(tt + 1) * 4, :], in_=pt[:].rearrange("p (a b) -> p a b", a=4), func=AF.Copy),
             reads=[pb], writes=[BV])
        pA, pbA = inproj(21 * 128, 64)
        pB, pbB = inproj(21 * 128 + 64, 64)
        P.op("dve", lambda e, pA=pA: e.scalar_tensor_tensor(out=t1[:], in0=pA[0:64, :], scalar=bAs[0:64, 21:22], in1=cs[:], op0=ALU.add, op1=ALU.mult),
             reads=[pbA, Bcs, Bc], writes=[Bt1])
        P.op("dve", lambda e, pB=pB: e.scalar_tensor_tensor(out=t2[:], in0=pB[0:64, :], scalar=bAs[0:64, 22:23], in1=sn[:], op0=ALU.add, op1=ALU.mult),
             reads=[pbB, Bsn, Bc], writes=[Bt2])
        P.op("pool", lambda e, c0=c0: e.tensor_tensor(out=K2T[:, c0:c0 + TT], in0=t1[:], in1=t2[:], op=ALU.add), reads=[Bt1, Bt2], writes=[BK])
    P.barrier()
    es1.close()
    es2 = ExitStack()
    sb = lambda n, s, dt=F32: es2.enter_context(nc.sbuf_tensor("%s_A%d" % (n, layer), s, dt))
    if "mla" in phases:
        pT = [sb("pT%d" % i, [128, TT], BF16) for i in range(4)]; BpT = [Buf("pT%d" % i) for i in range(4)]
        dacc = sb("dacc", [128, TT]); Bdacc = Buf("dacc")
        daccs = [sb("daccs%d" % i, [128, TT]) for i in range(3)]; Bdaccs = [Buf("daccs%d" % i) for i in range(3)]
        rden = sb("rden", [128, TT]); Brden = Buf("rden")
        oc = sb("oc", [128, TT], BF16); Boc = Buf("oc")
        po_t = nc.alloc_psum_tensor("po_mla", [128, 512], F32) if False else None
        pi = 0
        Q1 = [sb("Q1_%d" % i, [128, TT], BF16) for i in range(2)]; Q2 = [sb("Q2_%d" % i, [64, TT], BF16) for i in range(2)]
        BQs = [Buf("Qs%d" % i) for i in range(2)]
        for qt in range(NT):
            q0 = qt * TT
            qi = qt % 2
            P.dma("sp", "Q1_%d" % qi, Q1[qi][:], qd1[:, q0:q0 + TT], reads=[Bqd], writes=[BQs[qi]])
            P.dma("sp", "Q2_%d" % qi, Q2[qi][:], qd2[:, q0:q0 + TT], reads=[Bqd], writes=[BQs[qi]])
            BQ = BQs[qi]
            po, pbo = ps.next()

            def mla_a(kb, qi=qi, BQ=BQ, po=po):
                nonlocal pi
                k0 = kb * 128
                pt, pb = ps.next()
                if pt is po:
                    pt, pb = ps.next()
                P.op("pe", lambda e, pt=pt, k0=k0, qi=qi: e.matmul(pt[:], lhsT=K1T[:, k0:k0 + 128], rhs=Q1[qi][:], start=True, stop=False),
                     reads=[BK, BQ], writes=[pb])
                P.op("pe", lambda e, pt=pt, k0=k0, qi=qi: e.matmul(pt[:], lhsT=K2T[:, k0:k0 + 128], rhs=Q2[qi][:], start=False, stop=True),
                     reads=[BK, BQ], writes=[pb])
                s = pi % 4
                pi += 1
                P.op("act", lambda e, pt=pt, s=s: e.activation(out=pT[s][:], in_=pt[:], func=AF.Exp), reads=[pb], writes=[BpT[s]])
                return s

            def mla_b(kb, s, po=po, pbo=pbo):
                P.op("pe", lambda e, po=po, kb=kb, s=s: e.matmul(po[:], lhsT=Vtok[:, kb, :], rhs=pT[s][:], start=(kb == 0), stop=(kb == S // 128 - 1)),
                     reads=[BV, BpT[s]], writes=[pbo])
                ai = kb % 3
                eng = "pool" if ai == 2 else "dve"
                if kb < 3:
                    P.op(eng, lambda e, s=s, ai=ai: e.tensor_copy(out=daccs[ai][:], in_=pT[s][:]), reads=[BpT[s]], writes=[Bdaccs[ai]])
                else:
                    P.op(eng, lambda e, s=s, ai=ai: e.tensor_tensor(out=daccs[ai][:], in0=daccs[ai][:], in1=pT[s][:], op=ALU.add),
                         reads=[BpT[s], Bdaccs[ai]], writes=[Bdaccs[ai]])

            pend = []
            for kb in range(S // 128):
                pend.append((kb, mla_a(kb)))
                if len(pend) > 2:
                    mla_b(*pend.pop(0))
            while pend:
                mla_b(*pend.pop(0))
            P.op("dve", lambda e: e.tensor_tensor(out=dacc[:], in0=daccs[0][:], in1=daccs[1][:], op=ALU.add), reads=[Bdaccs[0], Bdaccs[1]], writes=[Bdacc])
            P.op("dve", lambda e: e.tensor_tensor(out=dacc[:], in0=dacc[:], in1=daccs[2][:], op=ALU.add), reads=[Bdacc, Bdaccs[2]], writes=[Bdacc])
            pd, pbd = ps.next()
            if pd is po:
                pd, pbd = ps.next()
            P.op("pe", lambda e, pd=pd: e.matmul(pd[:], lhsT=ones32[:], rhs=dacc[:], start=True, stop=True), reads=[Bc, Bdacc], writes=[pbd])
            P.op("dve", lambda e, pd=pd: e.reciprocal(out=rden[:], in_=pd[:]), reads=[pbd], writes=[Brden])
            P.op("dve", lambda e, po=po: e.tensor_tensor(out=oc[:], in0=po[:], in1=rden[:], op=ALU.mult), reads=[pbo, Brden], writes=[Boc])
            P.dma("sp", "oc", osrc[1024 + (q0 // 2048) * 128:1024 + (q0 // 2048) * 128 + 128, (q0 % 2048):(q0 % 2048) + TT], oc[:], reads=[Boc], writes=[BoT])


    P.barrier()
    if "cc_o" in io:
        io["cc_o"]((4, 5))
    es2.close()
    es2 = ExitStack()
    sb = lambda n, s, dt=F32: es2.enter_context(nc.sbuf_tensor("%s_A%d" % (n, layer), s, dt))
    if "dil" in phases:
        RG = 2048
        ets = sb("ets", [128, 18, 256]); Bets = Buf("ets")
        P.dma("sp", "ets", ets[:], etab, writes=[Bets])
        NSET = 2
        Qs_ = [sb("Qs%d" % i, [128, RG], BF16) for i in range(NSET)]; Ks_ = [sb("Ks%d" % i, [128, RG + 128], BF16) for i in range(NSET)]
        Vs_ = [sb("Vs%d" % i, [128, RG + 128], BF16) for i in range(NSET)]
        BQs_l = [Buf("Qs%d" % i) for i in range(NSET)]; BKs_l = [Buf("Ks%d" % i) for i in range(NSET)]; BVs_l = [Buf("Vs%d" % i) for i in range(NSET)]
        Vp_ = [sb("Vp%d" % i, [128, 17, 2, 65], BF16) for i in range(NSET)]; BVp_l = [[Buf("Vp%d_%d" % (i, j)) for j in range(17)] for i in range(NSET)]
        NU = 4
        pe32 = [sb("pe32_%d" % i, [128, 256]) for i in range(NU)]; Bpe = [Buf("pe32_%d" % i) for i in range(NU)]
        pt16 = [sb("pt16_%d" % i, [128, 256], BF16) for i in range(NU)]; Bpt16 = [Buf("pt16_%d" % i) for i in range(NU)]
        acc = [sb("dacc%d" % h, [65, RG]) for h in range(2)]; Bacc = [Buf("dacc%d" % h) for h in range(2)]
        obd = sb("obd", [64, 512], BF16); Bobd = Buf("obd")
        for i in range(NSET):
            P.op("pool", lambda e, i=i: e.memset(Vp_[i][:], 1.0), writes=BVp_l[i])
        ui = 0
        si = 0
        for rg in range(S // RG):
            R0 = rg * RG
            for h in range(2):
                P.op("pool", lambda e, h=h: e.memset(acc[h][:], 0.0), writes=[Bacc[h]])
            for g in range(3):
                d = DILS[g]
                J = S // d
                nj = RG // d
                nb = nj // 128
                j0 = R0 // d
                for r in range(d):
                    ss = si % NSET
                    si += 1
                    Qs, Ks, Vs, Vp = Qs_[ss], Ks_[ss], Vs_[ss], Vp_[ss]
                    BQs_, BKs, BVs, BVp = BQs_l[ss], BKs_l[ss], BVs_l[ss], BVp_l[ss]
                    lo = j0 - 64
                    hi_ = j0 + nj + 64
                    clo = max(lo, 0)
                    chi = min(hi_, J)
                    if lo < 0:
                        P.op("pool", lambda e, Ks=Ks: e.memset(Ks[:, 0:64], 0.0), writes=[BKs])
                        P.op("pool", lambda e, Vs=Vs: e.memset(Vs[:, 0:64], 0.0), writes=[BVs])
                    if hi_ > J:
                        P.op("pool", lambda e, nj=nj, Ks=Ks: e.memset(Ks[:, nj + 64:nj + 128], 0.0), writes=[BKs])
                        P.op("pool", lambda e, nj=nj, Vs=Vs: e.memset(Vs[:, nj + 64:nj + 128], 0.0), writes=[BVs])
                    P.dma("sp", "Qs%d" % ss, Qs[:, 0:nj], dsub[g][0][:, r, j0:j0 + nj], reads=[Bdsub], writes=[BQs_])
                    P.dma("sp", "Ks%d" % ss, Ks[:, clo - lo:chi - lo], dsub[g][1][:, r, clo:chi], reads=[Bdsub], writes=[BKs])
                    P.dma("sp", "Vs%d" % ss, Vs[:, clo - lo:chi - lo], dsub[g][2][:, r, clo:chi], reads=[Bdsub], writes=[BVs])
                    for n in range(nb + 1):
                        ptr, pbr = ps.next()
                        ptr16 = ptr[:].bitcast(BF16)
                        P.op("pe", lambda e, n=n, ptr16=ptr16, Vs=Vs: e.transpose(out=ptr16[:, 0:128], in_=Vs[:, n * 128:(n + 1) * 128], identity=id16[:]),
                             reads=[BVs, Bc], writes=[pbr])
                        P.op("act", lambda e, n=n, ptr16=ptr16, Vp=Vp: e.activation(out=Vp[:, n, :, 0:64], in_=ptr16[:, 0:128].rearrange("p (h v) -> p h v", h=2), func=AF.Copy),
                             reads=[pbr], writes=[BVp[n]])

                    def dil_a(qb, h, Qs=Qs, Ks=Ks, BQs_=BQs_, BKs=BKs, g=g, j0=j0, J=J):
                        nonlocal ui
                        jb = j0 + qb * 128
                        var = 1 if jb == 0 else (2 if jb + 128 == J else 0)
                        u = ui % NU
                        ui += 1
                        pt, pb = ps.next()
                        for kc in range(2):
                            P.op("pe", lambda e, pt=pt, kc=kc, qb=qb, h=h: e.matmul(pt[:, kc * 128:(kc + 1) * 128],
                                 lhsT=Ks[h * 64:(h + 1) * 64, (qb + kc) * 128:(qb + kc + 1) * 128], rhs=Qs[h * 64:(h + 1) * 64, qb * 128:(qb + 1) * 128],
                                 start=True, stop=True), reads=[BKs, BQs_], writes=[pb])
                        P.op("act", lambda e, pt=pt, u=u: e.activation(out=pe32[u][:], in_=pt[:, 0:256], func=AF.Exp), reads=[pb], writes=[Bpe[u]])
                        ei = (g * 2 + h) * 3 + var
                        P.op("dve", lambda e, u=u, ei=ei: e.tensor_tensor(out=pt16[u][:], in0=pe32[u][:], in1=ets[:, ei, :], op=ALU.mult),
                             reads=[Bpe[u], Bets], writes=[Bpt16[u]])
                        return (qb, h, u)

                    def dil_b(qb, h, u, Vp=Vp, BVp=BVp, d=d, r=r):
                        po, pbo = ps.next()
                        for kc in range(2):
                            P.op("pe", lambda e, po=po, kc=kc, qb=qb, h=h, u=u: e.matmul(po[0:65, 0:128], lhsT=Vp[:, qb + kc, h, :], rhs=pt16[u][:, kc * 128:(kc + 1) * 128],
                                 start=(kc == 0), stop=(kc == 1)), reads=[BVp[qb + kc], Bpt16[u]], writes=[pbo])
                        st = qb * 128 * d + r
                        av = acc[h][:, st:st + 127 * d + 1:d]
                        P.op("dve", lambda e, po=po, av=av: e.tensor_tensor(out=av, in0=av, in1=po[0:65, 0:128], op=ALU.add), reads=[pbo, Bacc[h]], writes=[Bacc[h]])

                    pend = []
                    for qb in range(nb):
                        for h in range(2):
                            pend.append(dil_a(qb, h))
                            if len(pend) > 2:
                                dil_b(*pend.pop(0))
                    while pend:
                        dil_b(*pend.pop(0))
            for h in range(2):
                P.op("dve", lambda e, h=h: e.reciprocal(out=acc[h][64:65, :], in_=acc[h][64:65, :]), reads=[Bacc[h]], writes=[Bacc[h]])
                for cc in range(RG // 512):
                    pt, pb = ps.next()
                    P.op("pe", lambda e, pt=pt, h=h, cc=cc: e.matmul(pt[0:64, :], lhsT=ones32[64:65, 0:64], rhs=acc[h][64:65, cc * 512:(cc + 1) * 512], start=True, stop=True),
                         reads=[Bc, Bacc[h]], writes=[pb])
                    P.op("dve", lambda e, pt=pt, h=h, cc=cc: e.tensor_tensor(out=obd[:], in0=acc[h][0:64, cc * 512:(cc + 1) * 512], in1=pt[0:64, :], op=ALU.mult),
                         reads=[pb, Bacc[h]], writes=[Bobd])
                    P.dma("sp", "obd", osrc[512 + rg * 128 + h * 64:512 + rg * 128 + (h + 1) * 64, cc * 512:(cc + 1) * 512], obd[:], reads=[Bobd], writes=[BoT])

    P.barrier()
    if "cc_o" in io:
        io["cc_o"]((2, 3))
    es2.close()
    es2 = ExitStack()
    sb = lambda n, s, dt=F32: es2.enter_context(nc.sbuf_tensor("%s_A%d" % (n, layer), s, dt))
    if "hg" in phases:
        oac = [sb("oac%d" % h, [64, S]) for h in range(2)]; Boac = [Buf("oac%d" % h) for h in range(2)]
        for h in range(2):
            P.op("pool", lambda e, h=h: e.memset(oac[h][:], 0.0), writes=[Boac[h]])
        gam = [[sb("gam%d%d" % (h, d), [128, 128]) for d in range(2)] for h in range(2)]; Bgam = Buf("gam")
        for h in range(2):
            for d in range(2):
                P.op("pool", lambda e, h=h, d=d: e.memset(gam[h][d][:], 1.0), writes=[Bgam])
        for h in range(2):
            for d in range(2):
                if d == 0:
                    dst, mn, bc, mc = gam[h][d][:, 0:127], mT[h][d][:, 1:128], BTt[h][d][:, 0:127], mT[h][d][:, 0:127]
                else:
                    dst, mn, bc, mc = gam[h][d][:, 1:128], mT[h][d][:, 0:127], BTt[h][d][:, 1:128], mT[h][d][:, 1:128]
                P.op("dve", lambda e, dst=dst, mn=mn, bc=bc: e.tensor_tensor(out=dst, in0=mn, in1=bc, op=ALU.add), reads=[BmB], writes=[Bgam])
                P.op("dve", lambda e, dst=dst, mc=mc: e.tensor_tensor(out=dst, in0=dst, in1=mc, op=ALU.subtract), reads=[BmB, Bgam], writes=[Bgam])
                P.op("act", lambda e, dst=dst: e.activation(out=dst, in_=dst, func=AF.Exp), reads=[Bgam], writes=[Bgam])
        ch = [(h, d) for d in range(2) for h in range(2)]
        qT = {(c, u): sb("hqT%d%d_%d" % (c + (u,)), [128, TT], BF16) for c in ch for u in range(2)}
        kT = {(c, u): sb("hkT%d%d_%d" % (c + (u,)), [128, TT], BF16) for c in ch for u in range(2)}
        vT = {(c, u): sb("hvT%d%d_%d" % (c + (u,)), [128, TT], BF16) for c in ch for u in range(2)}
        Bld = {(c, u): Buf("hld") for c in ch for u in range(2)}
        kvtok = {(c, u): sb("kvtok%d%d_%d" % (c + (u,)), [128, 256], BF16) for c in ch for u in range(2)}
        Bkv = {(c, u): Buf("kvtok") for c in ch for u in range(2)}
        at16 = {(c, u): sb("at16%d%d_%d" % (c + (u,)), [128, 128], BF16) for c in ch for u in range(2)}
        Bat = {(c, u): Buf("at") for c in ch for u in range(2)}
        S32 = {c: sb("S32%d%d" % c, [128, 64]) for c in ch}; S16 = {c: sb("S16%d%d" % c, [128, 64], BF16) for c in ch}
        Sh = {c: sb("Sh%d%d" % c, [128, 64]) for c in ch}
        BS32 = {c: Buf("S32%d%d" % c) for c in ch}; BS16 = {c: Buf("S16%d%d" % c) for c in ch}; BSh = {c: Buf("Sh%d%d" % c) for c in ch}
        for c in ch:
            P.op("pool", lambda e, c=c: e.memset(S32[c][:], 0.0), writes=[BS32[c]])
            P.op("pool", lambda e, c=c: e.memset(S16[c][:], 0.0), writes=[BS16[c]])

        def hg_load(step):
            u = step % 2
            for c in ch:
                h, d = c
                ti = step if d == 0 else NT - 1 - step
                c0 = ti * TT
                P.dma("sp", "hl%d%d_%d" % (c + (u,)), qT[(c, u)][:], hq_d[h][d][:, c0:c0 + TT], reads=[Bhqk], writes=[Bld[(c, u)]])
                P.dma("sp", "hl%d%d_%d" % (c + (u,)), kT[(c, u)][:], hk_d[h][d][:, c0:c0 + TT], reads=[Bhqk], writes=[Bld[(c, u)]])
                P.dma("sp", "hl%d%d_%d" % (c + (u,)), vT[(c, u)][:], hv_d[:, c0:c0 + TT], reads=[Bhqk], writes=[Bld[(c, u)]])

        def hg_info(step, pp, c):
            h, d = c
            ti = step if d == 0 else NT - 1 - step
            pr = pp if d == 0 else 3 - pp
            return h, d, ti, pr, pr * 128

        def hg_prep(step, pp):
            u = step % 2
            w = (step * 4 + pp) % 2
            for ci, c in enumerate(ch):
                h, d, ti, pr, p0 = hg_info(step, pp, c)
                bank, bb_ = ps.t[ci], ps.b[ci]
                b16 = bank[:].bitcast(BF16)
                P.op("pe", lambda e, c=c, p0=p0, b16=b16, u=u: e.transpose(out=b16[:, 0:128], in_=kT[(c, u)][:, p0:p0 + 128], identity=id16[:]),
                     reads=[Bld[(c, u)], Bc], writes=[bb_])
                P.op("pe", lambda e, c=c, p0=p0, b16=b16, u=u: e.transpose(out=b16[:, 128:256], in_=vT[(c, u)][:, p0:p0 + 128], identity=id16[:]),
                     reads=[Bld[(c, u)], Bc], writes=[bb_])
            for ci, c in enumerate(ch):
                b16 = ps.t[ci][:].bitcast(BF16)
                P.op("act", lambda e, c=c, b16=b16, w=w: e.activation(out=kvtok[(c, w)][:], in_=b16[:, 0:256], func=AF.Copy), reads=[ps.b[ci]], writes=[Bkv[(c, w)]])
            for ci, c in enumerate(ch):
                h, d, ti, pr, p0 = hg_info(step, pp, c)
                pa, pba = ps.t[ci], ps.b[ci]
                P.op("pe", lambda e, c=c, p0=p0, pa=pa, u=u: e.matmul(pa[:, 0:128], lhsT=kT[(c, u)][:, p0:p0 + 128], rhs=qT[(c, u)][:, p0:p0 + 128], start=True, stop=True),
                     reads=[Bld[(c, u)]], writes=[pba])
            for ci, c in enumerate(ch):
                h, d = c
                pa, pba = ps.t[ci], ps.b[ci]
                P.op("dve", lambda e, c=c, d=d, pa=pa, w=w: e.tensor_tensor(out=at16[(c, w)][:], in0=pa[:, 0:128], in1=msk[:, d, :], op=ALU.mult),
                     reads=[pba, Bc], writes=[Bat[(c, w)]])
            pi_, pbi = ps.t[4 + w], ps.b[4 + w]
            for ci, c in enumerate(ch):
                h, d = c
                P.op("pe", lambda e, c=c, h=h, ci=ci, pi_=pi_, w=w: e.matmul(pi_[0:64, ci * 128:(ci + 1) * 128], lhsT=kvtok[(c, w)][:, 128 + h * 64:128 + (h + 1) * 64],
                     rhs=at16[(c, w)][:], start=True, stop=True), reads=[Bkv[(c, w)], Bat[(c, w)]], writes=[pbi])
            for ci, c in enumerate(ch):
                h, d, ti, pr, p0 = hg_info(step, pp, c)
                t0_ = ti * TT + p0
                P.op("dve", lambda e, h=h, ci=ci, pi_=pi_, t0_=t0_: e.tensor_tensor(out=oac[h][:, t0_:t0_ + 128], in0=oac[h][:, t0_:t0_ + 128], in1=pi_[0:64, ci * 128:(ci + 1) * 128], op=ALU.add),
                     reads=[pbi, Boac[h]], writes=[Boac[h]])

        def hg_chain(step, pp):
            u = step % 2
            w = (step * 4 + pp) % 2
            pj, pbj = ps.t[6 + w], ps.b[6 + w]
            for cc in range(2):
                pub = {}
                for ci, c in enumerate(ch):
                    h, d, ti, pr, p0 = hg_info(step, pp, c)
                    ck = cc if d == 0 else 1 - cc
                    q0 = p0 + ck * 64
                    P.op("pe", lambda e, c=c, ci=ci, pj=pj, ck=ck, q0=q0, u=u: e.matmul(pj[0:64, ci * 128 + ck * 64:ci * 128 + (ck + 1) * 64], lhsT=S16[c][:], rhs=qT[(c, u)][:, q0:q0 + 64],
                         start=True, stop=True), reads=[BS16[c], Bld[(c, u)]], writes=[pbj])
                    pu, pbu = ps.t[ci], ps.b[ci]
                    P.op("pe", lambda e, c=c, h=h, pu=pu, ck=ck, w=w: e.matmul(pu[:, 0:64], lhsT=kvtok[(c, w)][ck * 64:(ck + 1) * 64, 0:128],
                         rhs=kvtok[(c, w)][ck * 64:(ck + 1) * 64, 128 + h * 64:128 + (h + 1) * 64], start=True, stop=True), reads=[Bkv[(c, w)]], writes=[pbu])
                    pub[c] = (pu, pbu, ti * 8 + pr * 2 + ck)
                for c in ch:
                    pu, pbu, cidx = pub[c]
                    P.op("dve", lambda e, c=c, pu=pu: e.tensor_tensor(out=Sh[c][:], in0=pu[:, 0:64], in1=S32[c][:], op=ALU.add), reads=[pbu, BS32[c]], writes=[BSh[c]])
                for c in ch:
                    h, d = c
                    pu, pbu, cidx = pub[c]
                    P.op("pool", lambda e, c=c, h=h, d=d, cidx=cidx: e.tensor_scalar(out=S32[c][:], in0=Sh[c][:], scalar1=gam[h][d][:, cidx:cidx + 1], scalar2=None, op0=ALU.mult),
                         reads=[BSh[c], Bgam], writes=[BS32[c]])
                    P.op("act", lambda e, c=c, h=h, d=d, cidx=cidx: e.activation(out=S16[c][:], in_=Sh[c][:], func=AF.Copy, scale=gam[h][d][:, cidx:cidx + 1]),
                         reads=[BSh[c], Bgam], writes=[BS16[c]])
            for ci, c in enumerate(ch):
                h, d, ti, pr, p0 = hg_info(step, pp, c)
                t0_ = ti * TT + p0
                P.op("dve", lambda e, h=h, ci=ci, pj=pj, t0_=t0_: e.tensor_tensor(out=oac[h][:, t0_:t0_ + 128], in0=oac[h][:, t0_:t0_ + 128], in1=pj[0:64, ci * 128:(ci + 1) * 128], op=ALU.add),
                     reads=[pbj, Boac[h]], writes=[Boac[h]])

        seq = [(st, pp) for st in range(NT) for pp in range(4)]
        hg_load(0)
        hg_load(1)
        hg_prep(0, 0)
        for i, (st, pp) in enumerate(seq):
            if i + 1 < len(seq):
                nst, npp = seq[i + 1]
                hg_prep(nst, npp)
            hg_chain(st, pp)
            if pp == 3 and st + 2 < NT:
                hg_load(st + 2)
        o16c = [sb("o16c%d" % i, [64, 2048], BF16) for i in range(2)]; Bo16c = [Buf("o16c%d" % i) for i in range(2)]
        for h in range(2):
            for tq in range(4):
                u = (h * 4 + tq) % 2
                P.op("act" if u == 0 else "dve", (lambda e, h=h, tq=tq, u=u: e.activation(out=o16c[u][:], in_=oac[h][:, tq * 2048:(tq + 1) * 2048], func=AF.Copy)) if u == 0 else
                     (lambda e, h=h, tq=tq, u=u: e.tensor_copy(out=o16c[u][:], in_=oac[h][:, tq * 2048:(tq + 1) * 2048])), reads=[Boac[h]], writes=[Bo16c[u]])
                P.dma("sp", "o16c%d" % u, osrc[tq * 128 + h * 64:tq * 128 + (h + 1) * 64, :], o16c[u][:], reads=[Bo16c[u]], writes=[BoT])

    P.barrier()
    es2.close()
    es0.close()


RG4 = [[0, 1, 2, 3], [4, 5, 6, 7]]


def build_fused():
    nc = bass.Bass("TRN2", target_bir_lowering=False)
    dr = lambda n, s, kind="ExternalInput", dt=F32: nc.dram_tensor(n, s, dt, kind=kind).ap()
    shared = {"pos": dr("pos", [1, S], dt=I32), "etab": dr("etab", [128, 18, 256]), "ropec": dr("ropec", [64, 2]),
              "ident": dr("ident", [128, 128]), "masks": dr("masks", [128, 2, 128]), "scanmask": dr("scanmask", [128, 512]),
              "lbraw": dr("lbraw", [128, 4, 2])}
    xT = dr("xT", [1024, S]); xTq = dr("xTq", [1024, 2048]); oidx = dr("oidx", [128, 96], dt=I32)
    ioA, ioB = [], []
    for l in range(2):
        a = dict(shared)
        a.update({"wA": dr("wA%d" % l, [1024, 2816]), "bA": dr("bA%d" % l, [128, 23]), "wuq": dr("wuq%d" % l, [384, 256]), "gq": dr("gq%d" % l, [128, 3]),
                  "wukv": dr("wukv%d" % l, [256, 256]), "gkv": dr("gkv%d" % l, [128, 2])})
        ioA.append(a)
        ioB.append({"wg": dr("wg%d" % l, [1024, 4608]), "bg": dr("bg%d" % l, [128, 36]), "wbr": dr("wbr%d" % l, [1536, 1024]), "wo": dr("wo%d" % l, [1024, 1024]),
                    "hgn": dr("hgn%d" % l, [128, 4]), "lng": dr("lng%d" % l, [128, 8]), "lnb": dr("lnb%d" % l, [128, 8]), "oidx": oidx})
    outT = dr("outT", [1024, 2048], kind="ExternalOutput")
    cco_src = [nc.dram_tensor("cco_src%d" % l, [1536, 2048], BF16) for l in range(2)]
    cco_dst = [nc.dram_tensor("cco_dst%d" % l, [6 * 1024, 2048], BF16) for l in range(2)]
    ccx_src = nc.dram_tensor("ccx_src", [4 * 1024, 512], BF16)
    ccx_dst = nc.dram_tensor("ccx_dst", [4 * 4096, 512], BF16)
    xn32 = nc.dram_tensor("xn32", [1024, 2048], F32).ap()
    scr = make_scratch(nc)
    ropecache = [nc.dram_tensor("ropec_%d" % i, [64, S], F32).ap() for i in range(2)]
    Bropecache = Buf("ropecache")
    P = Prog(nc)
    ps = PsumPool(nc)
    Bxn = Buf("xn32"); Bxg = Buf("xg"); Bnone = Buf("none")
    Bods = [Buf("cco_dst%d" % l) for l in range(2)]
    Bccx = [Buf("ccx_src%d" % j) for j in range(4)]
    Bxgs = [Buf("xg%d" % j) for j in range(4)]

    def cc_o(l):
        def go(chunks):
            for k in chunks:
                P.dma("pool", "cc_o%d" % l, None, None, reads=[Bnone], writes=[Bods[l]], inc=1,
                      fn=(lambda e, l=l, k=k: e.collective_compute("AllGather", ALU.bypass, replica_groups=RG4,
                                                                 ins=[cco_src[l].ap()[k * 256:(k + 1) * 256, :].opt()],
                                                                 outs=[cco_dst[l].ap()[k * 1024:(k + 1) * 1024, :].opt()])))
        return go

    def cc_x(j):
        P.dma("pool", "cc_x", None, None, reads=[Bccx[j]], writes=[Bxgs[j]], inc=1,
              fn=(lambda e, j=j: e.collective_compute("AllGather", ALU.bypass, replica_groups=RG4,
                                                    ins=[ccx_src.ap()[j * 1024:(j + 1) * 1024, :].opt()],
                                                    outs=[ccx_dst.ap()[j * 4096:(j + 1) * 4096, :].opt()])))

    for l in range(2):
        ioA[l]["osrc"] = cco_src[l].ap()
        ioA[l]["cc_o"] = cc_o(l)
        ioA[l]["ropecache"] = ropecache; ioA[l]["Bropecache"] = Bropecache
        if l == 0:
            ioA[l]["xT"] = xT
            emit_A(nc, P, ps, l, ioA[l], scr)
        else:
            ioA[l]["Bxg"] = Bxgs
            emit_A(nc, P, ps, l, ioA[l], scr, xsrc16=ccx_dst.ap())
        P.barrier()
        Bod = Bods[l]
        cc_o(l)((0, 1))
        b = ioB[l]
        b["orows"] = cco_dst[l].ap().rearrange("r (a c) -> (r a) c", c=256)
        b["Bodst"] = Bod
        if l == 0:
            b["x32src"] = xTq; b["Bxsrc"] = Bnone; b["out32"] = xn32; b["out16"] = ccx_src.ap(); b["Bccx"] = Bccx; b["cc_x"] = cc_x
        else:
            b["x32src"] = xn32; b["Bxsrc"] = Bxn; b["out32"] = outT
        emit_B(nc, P, ps, l, b)
        P.barrier()
    P.barrier()
    P.emit()
    return nc


SPL = [1024,1024,1024,512,512] + [512]*10 + [384,256,64,512,3072]
NAMES = ['hg_q','hg_f_fwd','hg_f_bwd','hg_i','hg_g','dil_q0','dil_k0','dil_v0','dil_q1','dil_k1','dil_v1','dil_q2','dil_k2','dil_v2','dil_g','mla_cq','mla_ckv','mla_kr','mla_g','merge']
OFF = dict(zip(NAMES, [int(v) for v in np.cumsum([0]+SPL[:-1])]))
def a_cols(hq):
    ar = np.arange
    c = []
    c += [OFF['hg_q'] + (2*hq)*128 + ar(128), OFF['hg_q'] + (2*hq+1)*128 + ar(128)]
    c += [OFF['hg_f_fwd'] + (2*hq)*128 + ar(128), OFF['hg_f_fwd'] + (2*hq+1)*128 + ar(128)]
    c += [OFF['hg_f_bwd'] + (2*hq)*128 + ar(128), OFF['hg_f_bwd'] + (2*hq+1)*128 + ar(128)]
    c += [OFF['hg_i'] + hq*128 + ar(128)]
    for g in range(3):
        for t in 'qkv':
            c += [OFF['dil_%s%d' % (t, g)] + hq*128 + ar(128)]
    c += [OFF['mla_cq'] + ar(384), OFF['mla_ckv'] + ar(256)]
    kr = OFF['mla_kr'] + ar(64)
    c += [kr, np.concatenate([kr[32:], kr[:32]])]
    return np.concatenate(c)
def etab_np(hq):
    slopes = 2.0 ** (-8.0 * (np.arange(24) + 1) / 24)
    kk = np.arange(128)[:, None]; qq = np.arange(128)[None, :]
    E = np.zeros((128, 18, 256), np.float32)
    for g, d in enumerate((1, 4, 16)):
        for hh in range(2):
            sl = slopes[g*8 + 2*hq + hh]
            for var in range(3):
                for kc in range(2):
                    rel = (kk + 128*kc - 64) - qq
                    e = np.where(np.abs(rel) <= 64, np.exp(-sl * d * np.abs(rel)), 0.0)
                    if var == 1 and kc == 0: e = np.where(kk < 64, 0.0, e)
                    if var == 2 and kc == 1: e = np.where(kk >= 64, 0.0, e)
                    E[:, (g*2+hh)*3 + var, kc*128:(kc+1)*128] = e
    return E
def a_inputs(inp, l, b, hq, xT_b):
    cols = a_cols(hq)
    w_in = inp['w_in'][l]; b_in = inp['b_in'][l]
    bsel = b_in[cols]
    bA = np.zeros((128, 23), np.float32)
    bA[:, :22] = bsel.reshape(22, 128).T
    bA[:64, 22] = bsel[21*128+64: 22*128]
    lbraw = np.zeros((128, 4, 2), np.float32)
    for h in range(2):
        for d, nm in enumerate(('hg_lb_fwd', 'hg_lb_bwd')):
            lbraw[:, h*2+d, :] = inp[nm][:, (2*hq+h)*128:(2*hq+h+1)*128].T
    wuq = inp['w_uq'][l]
    qc = hq*192 + np.arange(192)
    rope = qc[128:]
    wuq_sel = np.concatenate([wuq[:, qc[:128]], wuq[:, rope], wuq[:, np.concatenate([rope[32:], rope[:32]])]], 1)
    wukv = inp['w_ukv'][l]
    wukv_sel = wukv[:, hq*256:(hq+1)*256]
    inv = (1.0 / (10000.0 ** (np.arange(32, dtype=np.float32) / 32))).astype(np.float32)
    ropec = np.zeros((64, 2), np.float32); ropec[:, 0] = np.concatenate([inv, inv]); ropec[:32, 1] = -1.0; ropec[32:, 1] = 1.0
    masks = np.zeros((128, 2, 128), np.float32)
    ss = np.arange(128)[:, None]; tq = np.arange(128)[None, :]
    same = (ss // 64) == (tq // 64)
    masks[:, 0, :] = (same & (ss <= tq)).astype(np.float32)
    masks[:, 1, :] = (same & (ss >= tq)).astype(np.float32)
    sm = np.ones((128, 512), np.float32); sm[:, ::64] = 0.0
    return {"pos": np.ascontiguousarray(inp['positions'][b:b+1].astype(np.int32)), "wA": np.ascontiguousarray(w_in[:, cols]), "bA": bA, "lbraw": lbraw,
            "wuq": np.ascontiguousarray(wuq_sel), "gq": np.ascontiguousarray(inp['mla_q_norm'][l].reshape(3,128).T),
            "wukv": np.ascontiguousarray(wukv_sel), "gkv": np.ascontiguousarray(inp['mla_kv_norm'][l].reshape(2,128).T),
            "etab": etab_np(hq), "ropec": ropec, "ident": np.eye(128, dtype=np.float32), "masks": masks, "scanmask": sm}


_CACHE = {}


def _b_inputs(inp, l):
    w_in = inp['w_in'][l]; b_in = inp['b_in'][l]
    cols = np.concatenate([np.arange(OFF['hg_g'], OFF['hg_g'] + 512), np.arange(OFF['dil_g'], OFF['dil_g'] + 512),
                           np.arange(OFF['mla_g'], OFF['mla_g'] + 512), np.arange(OFF['merge'], OFF['merge'] + 3072)])
    return {"wg%d" % l: np.ascontiguousarray(w_in[:, cols]), "bg%d" % l: np.ascontiguousarray(b_in[cols].reshape(36, 128).T),
            "wbr%d" % l: np.ascontiguousarray(inp['w_branch'][l].reshape(1536, 1024)), "wo%d" % l: np.ascontiguousarray(inp['w_out'][l]),
            "hgn%d" % l: np.ascontiguousarray(inp['hg_norm'][l].reshape(4, 128).T),
            "lng%d" % l: np.ascontiguousarray(inp['ln_g'][l].reshape(8, 128).T), "lnb%d" % l: np.ascontiguousarray(inp['ln_b'][l].reshape(8, 128).T)}


def _oidx(tq):
    p = np.arange(128)[:, None, None, None]; tt = np.arange(8)[None, :, None, None]
    n = np.arange(3)[None, None, :, None]; r = np.arange(4)[None, None, None, :]
    rho = n * 512 + tq * 128 + p
    g = (rho // 256) * 1024 + r * 256 + (rho % 256)
    v = g * 8 + tt
    return np.ascontiguousarray(v.reshape(128, 96).astype(np.int32))


def kernel(**inputs):
    inp = {k: np.asarray(v) for k, v in inputs.items()}
    inp['positions'] = inp['positions'].astype(np.int32)
    for k in inp:
        if k != 'positions':
            inp[k] = inp[k].astype(np.float32, copy=False)
    B = 2
    xT = [np.ascontiguousarray(inp['x'][b].T) for b in range(B)]
    if "nc" not in _CACHE:
        _CACHE["nc"] = build_fused()
    nc = _CACHE["nc"]
    bl = [_b_inputs(inp, l) for l in range(2)]
    in_maps = []
    for c in range(8):
        b, q = c // 4, c % 4
        m = {"xT": xT[b], "xTq": np.ascontiguousarray(xT[b][:, q * 2048:(q + 1) * 2048]), "oidx": _oidx(q)}
        for l in range(2):
            a = a_inputs(inp, l, b, q, None)
            for k in ("pos", "etab", "ropec", "ident", "masks", "scanmask", "lbraw"):
                m[k] = a[k]
            for k in ("wA", "bA", "wuq", "gq", "wukv", "gkv"):
                m["%s%d" % (k, l)] = a[k]
            m.update(bl[l])
        in_maps.append(m)
    res = run_bass_kernel_spmd(nc, in_maps, core_ids=list(range(8))).results
    out = np.empty((B, 8192, 1024), np.float32)
    for c in range(8):
        b, q = c // 4, c % 4
        out[b, q * 2048:(q + 1) * 2048, :] = np.asarray(res[c]["outT"]).T
    return out
```
